# Optimizing a Trainium2 kernel written in Bass

```python
import math
import jax, jax.numpy as jnp
from jax import lax
import numpy as np

D_MODEL = 1024
BATCH = 32
SEQ = 2048
DEPTH = 4

N_MIXERS = 2
D_FF = 4 * D_MODEL
EPS = 1e-6

DA_HEAD_DIM = 64
DA_V_DIM = 2 * DA_HEAD_DIM
DA_HEADS = D_MODEL // DA_V_DIM
DA_QK_WIDTH = DA_HEADS * 2 * DA_HEAD_DIM
DA_V_WIDTH = DA_HEADS * DA_V_DIM
Q_BLOCK = 128

GD_HEAD_DIM = 128
GD_HEADS = D_MODEL // GD_HEAD_DIM
GD_WIDTH = GD_HEADS * GD_HEAD_DIM
GD_CONV = 4
GD_CHUNK = 64
GD_IN_WIDTH = 4 * GD_WIDTH + 2 * GD_HEADS

kernel_name = "hybrid_diffattn_gdn_sqrelu"


def rms_norm(x, g):
    xf = x.astype(jnp.float32)
    y = xf * lax.rsqrt(jnp.mean(xf * xf, axis=-1, keepdims=True) + EPS)
    return (y * g.astype(jnp.float32)).astype(x.dtype)


def l2_norm(x):
    return x * lax.rsqrt(jnp.sum(x * x, axis=-1, keepdims=True) + EPS)


def lambda_init_fn(layer):
    return 0.8 - 0.6 * math.exp(-0.3 * layer)


def diff_attention(xn, w_in, q_norm, k_norm, lq1, lk1, lq2, lk2, sub_norm, w_out, layer):
    B, S, _ = xn.shape
    proj = xn @ w_in
    q = proj[..., :DA_QK_WIDTH]
    k = proj[..., DA_QK_WIDTH:2 * DA_QK_WIDTH]
    v = proj[..., 2 * DA_QK_WIDTH:]
    q = q.reshape(B, S, DA_HEADS, 2, DA_HEAD_DIM).transpose(0, 2, 3, 1, 4)
    k = k.reshape(B, S, DA_HEADS, 2, DA_HEAD_DIM).transpose(0, 2, 3, 1, 4)
    v = v.reshape(B, S, DA_HEADS, DA_V_DIM).transpose(0, 2, 1, 3)
    q = rms_norm(q, q_norm)
    k = rms_norm(k, k_norm)
    lam_init = lambda_init_fn(layer)
    lam = (jnp.exp(jnp.sum(lq1.astype(jnp.float32) * lk1.astype(jnp.float32)))
           - jnp.exp(jnp.sum(lq2.astype(jnp.float32) * lk2.astype(jnp.float32)))
           + lam_init)
    scale = DA_HEAD_DIM ** -0.5
    outs = []
    for blk in range(S // Q_BLOCK):
        s0 = blk * Q_BLOCK
        end = s0 + Q_BLOCK
        qb = q[:, :, :, s0:end]
        kb = k[:, :, :, :end]
        vb = v[:, :, :end]
        scores = jnp.einsum('bhcqd,bhckd->bhcqk', qb, kb).astype(jnp.float32) * scale
        causal = (s0 + jnp.arange(Q_BLOCK))[:, None] >= jnp.arange(end)[None, :]
        p = jax.nn.softmax(jnp.where(causal, scores, -jnp.inf), axis=-1)
        p_diff = (p[:, :, 0] - lam * p[:, :, 1]).astype(v.dtype)
        outs.append(jnp.einsum('bhqk,bhkd->bhqd', p_diff, vb))
    o = jnp.concatenate(outs, axis=2)
    o = rms_norm(o, sub_norm) * (1.0 - lam_init)
    o = o.transpose(0, 2, 1, 3).reshape(B, S, DA_V_WIDTH)
    return o @ w_out


def causal_depthwise_conv(x, w):
    C = x.shape[-1]
    return lax.conv_general_dilated(
        x, w[:, None, :].astype(x.dtype), window_strides=(1,),
        padding=[(w.shape[0] - 1, 0)],
        dimension_numbers=('NWC', 'WIO', 'NWC'),
        feature_group_count=C)


def gated_delta_rule(q, k, v, beta, g):
    B, H, S, dk = q.shape
    dv = v.shape[-1]
    C = GD_CHUNK
    N = S // C
    q = q.reshape(B, H, N, C, dk)
    k = k.reshape(B, H, N, C, dk)
    v = v.reshape(B, H, N, C, dv)
    beta = beta.reshape(B, H, N, C)
    g = jnp.cumsum(g.reshape(B, H, N, C), axis=-1)
    incl = jnp.tril(jnp.ones((C, C), dtype=bool))
    strict = jnp.tril(jnp.ones((C, C), dtype=bool), -1)
    decay = jnp.exp(jnp.where(incl, g[..., :, None] - g[..., None, :], -jnp.inf))
    kb = k * beta[..., None]
    vb = v * beta[..., None]
    L = jnp.where(strict, jnp.einsum('bhncd,bhnsd->bhncs', kb, k) * decay, 0.0)
    eye = jnp.eye(C, dtype=L.dtype)
    T = lax.linalg.triangular_solve(L + eye, jnp.broadcast_to(eye, L.shape),
                                    left_side=True, lower=True, unit_diagonal=True)
    u = jnp.einsum('bhncs,bhnse->bhnce', T, vb)
    w = jnp.einsum('bhncs,bhnsd->bhncd', T, kb * jnp.exp(g)[..., None])
    a_qk = jnp.where(incl, jnp.einsum('bhncd,bhnsd->bhncs', q, k) * decay, 0.0)
    q_dec = q * jnp.exp(g)[..., None]
    k_dec = k * jnp.exp(g[..., -1:] - g)[..., None]
    g_last = jnp.exp(g[..., -1])

    def step(state, xs):
        u_n, w_n, a_n, qd_n, kd_n, gl_n = xs
        v_new = u_n - jnp.einsum('bhcd,bhde->bhce', w_n, state)
        o_n = (jnp.einsum('bhcd,bhde->bhce', qd_n, state)
               + jnp.einsum('bhcs,bhse->bhce', a_n, v_new))
        state = state * gl_n[..., None, None] + jnp.einsum('bhcd,bhce->bhde', kd_n, v_new)
        return state, o_n

    xs = tuple(jnp.moveaxis(t, 2, 0) for t in (u, w, a_qk, q_dec, k_dec, g_last))
    state0 = jnp.zeros((B, H, dk, dv), jnp.float32)
    _, o = lax.scan(step, state0, xs)
    return jnp.moveaxis(o, 0, 2).reshape(B, H, S, dv)


def gated_deltanet(xn, w_in, conv_w, a_log, dt_bias, out_norm, w_out):
    B, S, _ = xn.shape
    proj = xn @ w_in
    qkv = jax.nn.silu(causal_depthwise_conv(proj[..., :3 * GD_WIDTH], conv_w))
    z = proj[..., 3 * GD_WIDTH:4 * GD_WIDTH]
    b = proj[..., 4 * GD_WIDTH:4 * GD_WIDTH + GD_HEADS]
    a = proj[..., 4 * GD_WIDTH + GD_HEADS:]

    def heads(t):
        return t.reshape(B, S, GD_HEADS, GD_HEAD_DIM).transpose(0, 2, 1, 3).astype(jnp.float32)

    q = l2_norm(heads(qkv[..., :GD_WIDTH])) * (GD_HEAD_DIM ** -0.5)
    k = l2_norm(heads(qkv[..., GD_WIDTH:2 * GD_WIDTH]))
    v = heads(qkv[..., 2 * GD_WIDTH:])
    beta = jax.nn.sigmoid(b.astype(jnp.float32)).transpose(0, 2, 1)
    g = (-jnp.exp(a_log.astype(jnp.float32))
         * jax.nn.softplus(a.astype(jnp.float32) + dt_bias.astype(jnp.float32))).transpose(0, 2, 1)
    o = gated_delta_rule(q, k, v, beta, g).transpose(0, 2, 1, 3)
    o = rms_norm(o, out_norm) * jax.nn.silu(z.reshape(B, S, GD_HEADS, GD_HEAD_DIM).astype(jnp.float32))
    return o.astype(xn.dtype).reshape(B, S, GD_WIDTH) @ w_out


def squared_relu_mlp(xn, w1, w2):
    h = jax.nn.relu(xn @ w1)
    return (h * h) @ w2


def setup_inputs(seed: int = 0) -> dict:
    key = jax.random.key(seed)
    ks = jax.random.split(key, 24)
    n_attn = (DEPTH + N_MIXERS - 1) // N_MIXERS
    n_delta = DEPTH // N_MIXERS
    f32 = jnp.float32

    def nrm(k, shape, scale):
        return jax.random.normal(k, shape, f32) * scale

    def gain(k, shape):
        return 1.0 + 0.02 * jax.random.normal(k, shape, f32)

    dt = jnp.exp(jax.random.uniform(ks[17], (n_delta, GD_HEADS), f32,
                                    minval=math.log(1e-3), maxval=math.log(1e-1)))
    return {
        "x": jax.random.normal(ks[0], (BATCH, SEQ, D_MODEL), f32),
        "mix_norm": gain(ks[1], (DEPTH, D_MODEL)),
        "mlp_norm": gain(ks[2], (DEPTH, D_MODEL)),
        "mlp_w_in": nrm(ks[3], (DEPTH, D_MODEL, D_FF), D_MODEL ** -0.5),
        "mlp_w_out": nrm(ks[4], (DEPTH, D_FF, D_MODEL), D_FF ** -0.5),
        "da_w_in": nrm(ks[5], (n_attn, D_MODEL, 2 * DA_QK_WIDTH + DA_V_WIDTH), D_MODEL ** -0.5),
        "da_q_norm": gain(ks[6], (n_attn, DA_HEAD_DIM)),
        "da_k_norm": gain(ks[7], (n_attn, DA_HEAD_DIM)),
        "da_lambda_q1": nrm(ks[8], (n_attn, DA_HEAD_DIM), 0.1),
        "da_lambda_k1": nrm(ks[9], (n_attn, DA_HEAD_DIM), 0.1),
        "da_lambda_q2": nrm(ks[10], (n_attn, DA_HEAD_DIM), 0.1),
        "da_lambda_k2": nrm(ks[11], (n_attn, DA_HEAD_DIM), 0.1),
        "da_sub_norm": gain(ks[12], (n_attn, DA_V_DIM)),
        "da_w_out": nrm(ks[13], (n_attn, DA_V_WIDTH, D_MODEL), DA_V_WIDTH ** -0.5),
        "gd_w_in": nrm(ks[14], (n_delta, D_MODEL, GD_IN_WIDTH), D_MODEL ** -0.5),
        "gd_conv_w": nrm(ks[15], (n_delta, GD_CONV, 3 * GD_WIDTH), GD_CONV ** -0.5),
        "gd_a_log": jnp.log(jax.random.uniform(ks[16], (n_delta, GD_HEADS), f32, minval=1.0, maxval=16.0)),
        "gd_dt_bias": dt + jnp.log(-jnp.expm1(-dt)),
        "gd_out_norm": gain(ks[18], (n_delta, GD_HEAD_DIM)),
        "gd_w_out": nrm(ks[19], (n_delta, GD_WIDTH, D_MODEL), GD_WIDTH ** -0.5),
    }


def reference(x, mix_norm, mlp_norm, mlp_w_in, mlp_w_out,
              da_w_in, da_q_norm, da_k_norm, da_lambda_q1, da_lambda_k1,
              da_lambda_q2, da_lambda_k2, da_sub_norm, da_w_out,
              gd_w_in, gd_conv_w, gd_a_log, gd_dt_bias, gd_out_norm, gd_w_out):
    for i in range(DEPTH):
        j = i // N_MIXERS
        h = rms_norm(x, mix_norm[i])
        if i % N_MIXERS == 0:
            h = diff_attention(h, da_w_in[j], da_q_norm[j], da_k_norm[j],
                               da_lambda_q1[j], da_lambda_k1[j], da_lambda_q2[j], da_lambda_k2[j],
                               da_sub_norm[j], da_w_out[j], i)
        else:
            h = gated_deltanet(h, gd_w_in[j], gd_conv_w[j], gd_a_log[j], gd_dt_bias[j],
                               gd_out_norm[j], gd_w_out[j])
        x = x + h
        x = x + squared_relu_mlp(rms_norm(x, mlp_norm[i]), mlp_w_in[i], mlp_w_out[i])
    return x
```

```python
import math
from contextlib import ExitStack

import numpy as np
import concourse.bass as bass
import concourse.mybir as mybir
from concourse.bass_utils import run_bass_kernel_spmd

F32 = mybir.dt.float32
BF16 = mybir.dt.bfloat16
AF = mybir.ActivationFunctionType
ALU = mybir.AluOpType
AX = mybir.AxisListType

D = 1024
S = 2048
NT = S // 128
DFF = 4096
DEPTH = 4
EPS = 1e-6
NCORES = 8
SEQ_PER_CORE = 4
GD_IN = 4 * 1024 + 16


class Op:
    __slots__ = ("id", "eng", "fn", "deps", "dma", "seq", "signal")


class Prog:
    ENGS = ("pe", "act", "dve", "pool", "sp")

    def __init__(self, nc, dry=False):
        self.nc = nc
        self.dry = dry
        self.ops = []
        self.by_eng = {e: [] for e in self.ENGS}
        self.lw = {}
        self.rd = {}
        self.dma_groups = {}
        self.group_all = set()
        self.psum_last = {}

    def add(self, eng, fn, reads=(), writes=(), dma=None):
        if self.dry:
            return None
        op = Op()
        op.id = len(self.ops)
        op.eng = eng
        op.fn = fn
        op.dma = dma
        op.seq = 0
        op.signal = False
        deps = set()
        for k in reads:
            w = self.lw.get(k)
            if w is not None:
                deps.add(w)
        for k in writes:
            w = self.lw.get(k)
            if w is not None:
                deps.add(w)
            for r in self.rd.get(k, ()):
                deps.add(r)
        for k in reads:
            self.rd.setdefault(k, []).append(op.id)
        for k in writes:
            self.lw[k] = op.id
            self.rd[k] = []
        for k in tuple(reads) + tuple(writes):
            if isinstance(k, tuple) and k[0] in ("pf", "pb"):
                last = self.psum_last.setdefault(k, {})
                for eng2, oid in last.items():
                    if eng2 != eng:
                        deps.add(oid)
                last[eng] = op.id
        deps.discard(op.id)
        if eng == "pe" and dma is None:
            deps = {d for d in deps if not (self.ops[d].eng == "pe" and self.ops[d].dma is None)}
        op.deps = deps
        self.ops.append(op)
        self.by_eng[eng].append(op)
        if dma is not None:
            self.dma_groups.setdefault(dma, []).append(op.id)
        return op.id

    def emit(self, es):
        nc = self.nc
        ops = self.ops
        for op in ops:
            for d in op.deps:
                ops[d].signal = True
        for e in self.ENGS:
            c = 0
            for op in self.by_eng[e]:
                if op.dma is None and op.signal:
                    c += 1
                    op.seq = c
        for g, ids in self.dma_groups.items():
            for i, oid in enumerate(ids):
                ops[oid].seq = i + 1
        eng_sem = {e: es.enter_context(nc.semaphore("s_" + e)) for e in self.ENGS}
        dma_sem = {g: es.enter_context(nc.semaphore("d_" + str(g))) for g in self.dma_groups}
        block = es.enter_context(nc.Block())

        def emit_engine(ename, e):
            waited = {}
            for op in self.by_eng[ename]:
                need = {}
                for d in op.deps:
                    dop = ops[d]
                    if dop.dma is not None:
                        sem = dma_sem[dop.dma]
                        if dop.dma in self.group_all:
                            val = 16 * len(self.dma_groups[dop.dma])
                        else:
                            val = 16 * dop.seq
                    else:
                        sem = eng_sem[dop.eng]
                        val = dop.seq
                    key = id(sem)
                    if key not in need or need[key][1] < val:
                        need[key] = (sem, val)
                for key, (sem, val) in need.items():
                    if waited.get(key, 0) < val:
                        e.wait_ge(sem, val)
                        waited[key] = val
                if op.fn is None:
                    continue
                inst = op.fn(e)
                if op.dma is not None:
                    inst.then_inc(dma_sem[op.dma], 16)
                elif op.signal:
                    inst.then_inc(eng_sem[ename], 1)

        @block.tensor
        def _(e):
            emit_engine("pe", e)

        @block.scalar
        def _(e):
            emit_engine("act", e)

        @block.vector
        def _(e):
            emit_engine("dve", e)

        @block.gpsimd
        def _(e):
            emit_engine("pool", e)

        @block.sync
        def _(e):
            emit_engine("sp", e)


class WStream:
    def __init__(self, P, ring, nslot, lookahead_list=None):
        self.P = P
        self.ring = ring
        self.nslot = nslot
        self.future = lookahead_list
        self.requests = []
        self.issued = 0
        self.released = set()

    def _pump(self):
        if self.P.dry:
            return
        lim = min(len(self.future), len(self.requests) + self.nslot - 1)
        while self.issued < lim:
            j = self.issued
            if j >= self.nslot and (j - self.nslot) not in self.released:
                break
            src, w = self.future[j]
            slot = j % self.nslot
            ring = self.ring
            self.P.add("pool", lambda e, slot=slot, src=src, w=w: e.dma_start(out=ring[:, slot, :, 0:w], in_=src),
                       writes=[("w", slot)], dma=("w", slot))
            self.issued += 1

    def get(self, src, w):
        i = len(self.requests)
        self.requests.append((src, w))
        if self.P.dry:
            return 0, ("w", 0), i
        self._pump()
        assert self.issued > i, "weight ring deadlock: release slabs before requesting more"
        slot = i % self.nslot
        return slot, ("w", slot), i

    def done(self, idx):
        self.released.add(idx)
        self._pump()


class Builder:
    def __init__(self, nc, P, ws_future, n_seq, layers, es):
        self.nc = nc
        self.P = P
        self.n_seq = n_seq
        self.layers = layers
        self.es = es
        self.ws_future = ws_future
        self.alloc()

    def sb(self, name, shape, dt):
        return self.es.enter_context(self.nc.sbuf_tensor(name, shape, dt))

    def carve(self, shape, dt):
        esz = 4 if dt == F32 else 2
        n = 1
        for s_ in shape:
            n *= s_
        nbytes = (n * esz + 3) // 4 * 4
        off = self._aoff
        assert off + nbytes <= self.ARENA_BYTES, "arena overflow"
        self._aoff = off + nbytes
        v = self.arena[:, off // 4:(off + nbytes) // 4]
        if dt != F32:
            v = v.bitcast(dt)
        v = v[:, 0:n]
        if len(shape) == 2:
            v = v.rearrange("p (a b) -> p a b", a=shape[0])
        elif len(shape) == 3:
            v = v.rearrange("p (a b c) -> p a b c", a=shape[0], b=shape[1])
        return v

    def alloc(self):
        nc = self.nc
        n_seq = self.n_seq
        d = {}
        d["x"] = nc.dram_tensor("x", [n_seq, S, D], F32, kind="ExternalInput").ap()
        d["out"] = nc.dram_tensor("out", [n_seq, S, D], F32, kind="ExternalOutput").ap()
        specs = [
            ("mix_norm", [4, D]), ("mlp_norm", [4, D]), ("mlp_w_in", [4, D, DFF]), ("mlp_w_out", [4, DFF, D]),
            ("da_w_in", [2, D, 3072]), ("da_q_norm", [2, 64]), ("da_k_norm", [2, 64]),
            ("da_lambda_q1", [2, 64]), ("da_lambda_k1", [2, 64]), ("da_lambda_q2", [2, 64]), ("da_lambda_k2", [2, 64]),
            ("da_sub_norm", [2, 128]), ("da_w_out", [2, D, D]),
            ("gd_w_in", [2, D, GD_IN]), ("gd_conv_w", [2, 4, 3072]), ("gd_a_log", [2, 8]), ("gd_dt_bias", [2, 8]),
            ("gd_out_norm", [2, 128]), ("gd_w_out", [2, D, D]),
        ]
        for name, shape in specs:
            d[name] = nc.dram_tensor(name, shape, F32, kind="ExternalInput").ap()
        self.d = d
        self.NSLOT = 4
        self.X = self.sb("X", [128, NT, D], F32)
        self.xnT = self.sb("xnT", [128, 8, S], BF16)
        self.U = self.sb("U", [128, 8, S], BF16)
        self.ring = self.sb("ring", [128, self.NSLOT, 8, 512], BF16)
        self.xs_f = self.sb("xs_f", [128, 2, 512], F32)
        self.xs = self.xs_f[:, :, :].rearrange("p a b -> p (a b)").bitcast(BF16).rearrange("p (a b) -> p a b", a=2)
        self.rtmp = self.xs_f
        self.ss = self.sb("ss", [128, NT], F32)
        self.rstd = self.sb("rstd", [128, NT], F32)
        self.epsc = self.sb("epsc", [128, 4], F32)
        self.identb = self.sb("identb", [128, 128], BF16)
        self.identf = self.sb("identf", [128, 128], F32)
        self.crow = self.xs_f[0:64, 1, 0:128]
        self.gcol = self.sb("gcol", [128, 64], F32)
        self.ARENA_BYTES = 35840
        self.arena = self.sb("arena", [128, self.ARENA_BYTES // 4], F32)
        self._aoff = 0
        self.qT = self.carve([2, S], BF16)
        self.kT = self.carve([2, S], BF16)
        self.vaug = self.carve([NT, 2, 130], BF16)
        self.pT = self.carve([3, 512], BF16)
        self.qraw = self.carve([2, 512], BF16)
        self.qsq = self.carve([2, 512], BF16)
        self.qrs = self.carve([1, 512], F32)
        self.osb = self.carve([2, 128], F32)
        self.obf = self.carve([2, 128], BF16)
        self.fsm = self.carve([2, 8], F32)
        self.da_arena_end = self._aoff
        self._aoff = 0
        g = {}
        g["BA"] = self.carve([NT, 16], F32)
        for nm in ("BETA", "GC", "EG", "EGL", "EKD"):
            g[nm] = self.carve([NT, 8], F32)
        g["GLB"] = g["BA"][:, :, :].rearrange("p a b -> p (a b)")[:, 0:NT * 8].rearrange("p (a b) -> p a b", a=NT)
        g["raw"] = self.carve([2, 516], F32)
        g["acc"] = self.carve([1, 512], F32)
        g["halo"] = self.carve([6, 4], F32)
        g["sT"] = self.carve([2, 4, 3 * 128], BF16)
        g["sq"] = self.carve([2, 512], BF16)
        g["zs"] = self.carve([4, 256], BF16)
        g["R"] = self.carve([4, 4], F32)
        g["SC"] = self.carve([8, 8], F32)
        g["Sf"] = self.carve([2, 128], F32)
        g["Sb"] = self.carve([2, 128], BF16)
        self.NCH = 2
        for c in range(self.NCH):
            g["kd", c] = self.carve([128], BF16)
            g["kbg", c] = self.carve([128], BF16)
            g["vb", c] = self.carve([128], BF16)
            g["E", c] = self.carve([256], F32)
            g["MA", c] = self.carve([256], BF16)
            g["Lb", c] = self.carve([128], BF16)
            g["QQ", c] = self.carve([256], BF16)
            g["TM", c] = self.carve([256], BF16)
            g["DD", c] = self.carve([2, 256], BF16)
            g["u", c] = self.carve([128], F32)
            g["wT", c] = self.carve([128], BF16)
            g["vn", c] = self.carve([128], BF16)
            g["o", c] = self.carve([128], F32)
            g["og", c] = self.carve([128], BF16)
            g["ps", c] = self.carve([8], F32)
        self.g = g
        self.gd_arena_end = self._aoff
        assert max(self.da_arena_end, self.gd_arena_end) <= self.ARENA_BYTES
        self.convw = self.sb("convw", [128, 2, 96], F32)
        self.gdc = self.sb("gdc", [128, 2, 4], F32)
        self.gsm = self.sb("gsm", [128, 2, 16], F32)
        self.NEGM = self.sb("NEGM", [128, 7, 256], BF16)
        self.trif = self.sb("trif", [128, 128], F32)
        self.sellast = self.sb("sellast", [128, 128], F32)
        self.onesf = self.sb("onesf", [128, 128], F32)
        self.masks = self.sb("masks", [128, 256], F32)
        self.onecol = self.sb("onecol", [128, 2], BF16)
        self.onesbd = self.sb("onesbd", [128, 128], BF16)
        self.negmask = self.sb("negmask", [128, 128], BF16)
        self.cda = self.sb("cda", [128, 16], F32)
        self.lamt = self.xs_f[:, 0, 256:512].rearrange("p (a b) -> p a b", a=4)
        self.lamp = self.sb("lamp", [128, 4], F32)
        self.scr_f = self.xs_f[:, 0, 0:128]
        self.scr_f2 = self.xs_f[:, 0, 128:256]
        self.PF = [self.es.enter_context(nc.psum_tensor("pf%d" % i, [128, 512], F32)) for i in range(6)]
        self.PB = [self.es.enter_context(nc.psum_tensor("pb%d" % i, [128, 1024], BF16)) for i in range(2)]
        self.ws = WStream(self.P, self.ring, self.NSLOT, self.ws_future)

    def setup_consts(self):
        P = self.P
        nc = self.nc
        identf, identb = self.identf, self.identb
        P.add("pool", lambda e: e.memset(identf[:], 0.0), writes=["identf"])
        P.add("pool", lambda e: e.memset(self.epsc[:], EPS), writes=["epsc"])
        P.add("pool", lambda e: e.affine_select(out=identf[:], in_=identf[:], pattern=[[-1, 128]],
                                                compare_op=ALU.not_equal, fill=1.0, base=0,
                                                channel_multiplier=1),
              reads=["identf"], writes=["identf"])
        P.add("dve", lambda e: e.tensor_copy(out=identb[:], in_=identf[:]), reads=["identf"], writes=["identb"])
        crow, gcol = self.crow, self.gcol
        mixr = self.d["mix_norm"].rearrange("l (kc p) -> (l kc) p", p=128)
        mlpr = self.d["mlp_norm"].rearrange("l (kc p) -> (l kc) p", p=128)
        P.add("sp", lambda e: e.dma_start(out=crow[0:32, :], in_=mixr), writes=[("xs", 1)], dma="c0")
        P.add("sp", lambda e: e.dma_start(out=crow[32:64, :], in_=mlpr), writes=[("xs", 1)], dma="c1")
        pf = self.PF[0]
        P.add("pe", lambda e: e.transpose(out=pf[:, 0:64], in_=crow[0:64, :], identity=identf[0:64, 0:64]),
              reads=[("xs", 1), "identf"], writes=[("pf", 0)])
        P.add("dve", lambda e: e.tensor_copy(out=gcol[:, :], in_=pf[:, 0:64]), reads=[("pf", 0)], writes=["gcol"])

    def setup_da_consts(self, j, l):
        P = self.P
        d = self.d
        cda, lamt, lamp = self.cda, self.lamt, self.lamp
        c0 = 5 * j
        col = lambda ap: ap.rearrange("(p o) -> p o", o=1)
        for half in range(2):
            P.add("sp", lambda e, half=half: e.dma_start(out=cda[64 * half:64 * half + 64, c0:c0 + 1], in_=col(d["da_q_norm"][j])),
                  writes=[("cda", j)], dma="cq%d%d" % (j, half))
            P.add("sp", lambda e, half=half: e.dma_start(out=cda[64 * half:64 * half + 64, c0 + 1:c0 + 2], in_=col(d["da_k_norm"][j])),
                  writes=[("cda", j)], dma="ck%d%d" % (j, half))
        P.add("sp", lambda e: e.dma_start(out=cda[:, c0 + 4:c0 + 5], in_=col(d["da_sub_norm"][j])),
              writes=[("cda", j)], dma="cs%d" % j)
        for i, nm in enumerate(["da_lambda_q1", "da_lambda_k1", "da_lambda_q2", "da_lambda_k2"]):
            P.add("sp", lambda e, i=i, nm=nm: e.dma_start(out=lamt[:, i, :], in_=d[nm][j:j + 1, :].partition_broadcast(128)),
                  writes=[("xs", 0)], dma="cl%d%d" % (j, i))
        lam_init = 0.8 - 0.6 * math.exp(-0.3 * l)
        for i in range(2):
            P.add("dve", lambda e, i=i: e.tensor_tensor(out=lamt[:, 2 * i, :], in0=lamt[:, 2 * i, :], in1=lamt[:, 2 * i + 1, :], op=ALU.mult),
                  reads=[("xs", 0)], writes=[("xs", 0)])
            P.add("dve", lambda e, i=i: e.reduce_sum(out=lamp[:, i:i + 1], in_=lamt[:, 2 * i, :], axis=AX.X),
                  reads=[("xs", 0)], writes=["lamp"])
        P.add("act", lambda e: e.activation(out=lamp[:, 0:2], in_=lamp[:, 0:2], func=AF.Exp), reads=["lamp"], writes=["lamp"])
        P.add("dve", lambda e: e.tensor_tensor(out=cda[:, c0 + 2:c0 + 3], in0=lamp[:, 0:1], in1=lamp[:, 1:2], op=ALU.subtract),
              reads=["lamp"], writes=[("cda", j)])
        P.add("dve", lambda e: e.tensor_scalar(out=cda[:, c0 + 2:c0 + 3], in0=cda[:, c0 + 2:c0 + 3], scalar1=lam_init, scalar2=None, op0=ALU.add),
              reads=[("cda", j)], writes=[("cda", j)])
        P.add("dve", lambda e: e.tensor_scalar(out=cda[:, c0 + 3:c0 + 4], in0=cda[:, c0 + 2:c0 + 3], scalar1=-1.0, scalar2=None, op0=ALU.mult),
              reads=[("cda", j)], writes=[("cda", j)])
        P.add("dve", lambda e: e.tensor_scalar(out=cda[:, c0:c0 + 1], in0=cda[:, c0:c0 + 1], scalar1=0.125, scalar2=None, op0=ALU.mult),
              reads=[("cda", j)], writes=[("cda", j)])
        P.add("dve", lambda e: e.tensor_scalar(out=cda[:, c0 + 4:c0 + 5], in0=cda[:, c0 + 4:c0 + 5], scalar1=1.0 - lam_init, scalar2=None, op0=ALU.mult),
              reads=[("cda", j)], writes=[("cda", j)])

    def setup_da_static(self):
        P = self.P
        onesbd, negmask, vaug = self.onesbd, self.negmask, self.vaug
        scr = self.scr_f
        P.add("pool", lambda e: e.memset(scr[:], 0.0), writes=[("xs", 0)])
        P.add("pool", lambda e: e.memset(scr[0:64, 0:64], 1.0 / 64), reads=[("xs", 0)], writes=[("xs", 0)])
        P.add("pool", lambda e: e.memset(scr[64:128, 64:128], 1.0 / 64), reads=[("xs", 0)], writes=[("xs", 0)])
        P.add("dve", lambda e: e.tensor_copy(out=onesbd[:], in_=scr[:]), reads=[("xs", 0)], writes=["onesbd"])
        scr2 = self.scr_f2
        P.add("pool", lambda e: e.memset(scr2[:], 0.0), writes=[("xs", 0)])
        P.add("pool", lambda e: e.affine_select(out=scr2[:], in_=scr2[:], pattern=[[1, 128]], compare_op=ALU.is_ge,
                                                fill=-30000.0, base=0, channel_multiplier=-1),
              reads=[("xs", 0)], writes=[("xs", 0)])
        P.add("dve", lambda e: e.tensor_copy(out=negmask[:], in_=scr2[:]), reads=[("xs", 0)], writes=["negmask"])

    def diffattn(self, l):
        P = self.P
        j = l // 2
        X, xnT, U, ring = self.X, self.xnT, self.U, self.ring
        qT, kT, vaug, pT = self.qT, self.kT, self.vaug, self.pT
        qraw, qsq, qrs = self.qraw, self.qsq, self.qrs
        cda, onesbd, negmask, identb = self.cda, self.onesbd, self.negmask, self.identb
        osb, obf, fsm = self.osb, self.obf, self.fsm
        PF, PB = self.PF, self.PB
        c0 = 5 * j
        win = self.d["da_w_in"][j].rearrange("(kc p) f -> p kc f", p=128)
        wout = self.d["da_w_out"][j].rearrange("(kc p) f -> p kc f", p=128)
        self.rmsnorm(8 * l)
        P.add("pool", lambda e: e.memset(vaug[:, :, :, 128:130], 1.0), reads=[("xnT", 0, 0)], writes=["vones"])
        cnt = {"pf": 0, "nb": 0, "pt": 0, "fin": 0}
        for hp in range(4):
            sq = self.ws.get(win[:, :, hp * 256:(hp + 1) * 256], 256)
            sk = self.ws.get(win[:, :, 1024 + hp * 256:1024 + (hp + 1) * 256], 256)
            sv = self.ws.get(win[:, :, 2048 + hp * 256:2048 + (hp + 1) * 256], 256)
            for which, slab, dst, gcolq in (("q", sq, qT, c0), ("k", sk, kT, c0 + 1)):
                slot, wkey, _ = slab
                for hh in range(2):
                    for tt in range(4):
                        bank = cnt["pf"] % 2
                        cnt["pf"] += 1
                        pf = PF[bank]
                        nb = cnt["nb"] % 2
                        cnt["nb"] += 1
                        for kc in range(8):
                            P.add("pe", lambda e, pf=pf, slot=slot, kc=kc, hh=hh, tt=tt: e.matmul(
                                pf[:, :], ring[:, slot, kc, hh * 128:(hh + 1) * 128], xnT[:, kc, tt * 512:(tt + 1) * 512],
                                start=(kc == 0), stop=(kc == 7)),
                                reads=[wkey] + [("xnT", t, kc) for t in range(4 * tt, 4 * tt + 4)], writes=[("pf", bank)])
                        P.add("act", lambda e, pf=pf, nb=nb: e.activation(out=qraw[:, nb, :], in_=pf[:, :], func=AF.Copy),
                              reads=[("pf", bank)], writes=[("qraw", nb)])
                        P.add("dve", lambda e, nb=nb: e.tensor_tensor(out=qsq[:, nb, :], in0=qraw[:, nb, :], in1=qraw[:, nb, :], op=ALU.mult),
                              reads=[("qraw", nb)], writes=[("qsq", nb)])
                        mbank = 2 + nb
                        pm = PF[mbank]
                        P.add("pe", lambda e, pm=pm, nb=nb: e.matmul(pm[:, :], onesbd[:, :], qsq[:, nb, :], start=True, stop=True),
                              reads=[("qsq", nb), "onesbd"], writes=[("pf", mbank)])
                        P.add("act", lambda e, pm=pm, nb=nb: e.activation(out=qrs[:, 0, :], in_=pm[:, :], func=AF.Sqrt, bias=self.epsc[:, 0:1], scale=1.0),
                              reads=[("pf", mbank), "epsc"], writes=[("qrs", 0)])
                        P.add("dve", lambda e, nb=nb: e.reciprocal(out=qrs[:, 0, :], in_=qrs[:, 0, :]),
                              reads=[("qrs", 0)], writes=[("qrs", 0)])
                        P.add("dve", lambda e, nb=nb, dst=dst, hh=hh, tt=tt, gcolq=gcolq: e.scalar_tensor_tensor(
                            out=dst[:, hh, tt * 512:(tt + 1) * 512], in0=qraw[:, nb, :], scalar=cda[:, gcolq:gcolq + 1],
                            in1=qrs[:, 0, :], op0=ALU.mult, op1=ALU.mult),
                            reads=[("qraw", nb), ("qrs", 0), ("cda", j)], writes=[(which + "T", hh, tt)])
            slot, wkey, _ = sv
            for t in range(NT):
                bank = cnt["pf"] % 2
                cnt["pf"] += 1
                pf = PF[bank]
                for kc in range(8):
                    P.add("pe", lambda e, pf=pf, slot=slot, kc=kc, t=t: e.matmul(
                        pf[:, 0:256], xnT[:, kc, t * 128:(t + 1) * 128], ring[:, slot, kc, 0:256],
                        start=(kc == 0), stop=(kc == 7)),
                        reads=[wkey, ("xnT", t, kc)], writes=[("pf", bank)])
                P.add("act", lambda e, pf=pf, t=t: e.activation(
                    out=vaug[:, t, :, 0:128], in_=pf[:, 0:256].rearrange("p (h d) -> p h d", h=2), func=AF.Copy),
                    reads=[("pf", bank), "vones"], writes=[("va", t)])
            for sl in (sq, sk, sv):
                self.ws.done(sl[2])
            for hh in range(2):
                h = 2 * hp + hh
                for qt in range(4):
                    first_in_bank = {}
                    for c in range(2):
                        pl, ph = 64 * c, 64 * c + 64
                        for kb in range(4 * qt + 4):
                            r = kb - 4 * qt
                            col0 = max(r, 0) * 128
                            bank = cnt["pf"] % 2
                            cnt["pf"] += 1
                            ps = PF[bank]
                            krd = [("kT", hh, kb // 4)]
                            qrd = [("qT", hh, qt)]
                            if r >= 0:
                                P.add("pe", lambda e, ps=ps, pl=pl, ph=ph, hh=hh, kb=kb, qt=qt, col0=col0: e.matmul(
                                    ps[:, col0:col0 + 128], kT[pl:ph, hh, kb * 128:(kb + 1) * 128],
                                    qT[pl:ph, hh, qt * 512 + col0:qt * 512 + col0 + 128], start=True, stop=False,
                                    skip_group_check=True),
                                    reads=krd + qrd, writes=[("pf", bank)])
                                P.add("pe", lambda e, ps=ps, col0=col0: e.matmul(
                                    ps[:, col0:col0 + 128], identb[:, :], negmask[:, :], start=False, stop=True,
                                    skip_group_check=True),
                                    reads=["identb", "negmask"], writes=[("pf", bank)])
                                if col0 + 128 < 512:
                                    P.add("pe", lambda e, ps=ps, pl=pl, ph=ph, hh=hh, kb=kb, qt=qt, col0=col0: e.matmul(
                                        ps[:, col0 + 128:512], kT[pl:ph, hh, kb * 128:(kb + 1) * 128],
                                        qT[pl:ph, hh, qt * 512 + col0 + 128:qt * 512 + 512], start=True, stop=True,
                                        skip_group_check=True),
                                        reads=krd + qrd, writes=[("pf", bank)])
                            else:
                                P.add("pe", lambda e, ps=ps, pl=pl, ph=ph, hh=hh, kb=kb, qt=qt: e.matmul(
                                    ps[:, :], kT[pl:ph, hh, kb * 128:(kb + 1) * 128],
                                    qT[pl:ph, hh, qt * 512:qt * 512 + 512], start=True, stop=True, skip_group_check=True),
                                    reads=krd + qrd, writes=[("pf", bank)])
                            pi = cnt["pt"] % 3
                            cnt["pt"] += 1
                            P.add("act", lambda e, ps=ps, pi=pi, col0=col0: e.activation(
                                out=pT[:, pi, col0:512], in_=ps[:, col0:512], func=AF.Exp),
                                reads=[("pf", bank)], writes=[("pT", pi)])
                            for rr in range(max(r, 0), 4):
                                abank = 2 + 2 * c + rr // 2
                                off = (rr % 2) * 256
                                st = abank not in first_in_bank
                                first_in_bank[abank] = True
                                P.add("pe", lambda e, abank=abank, off=off, pi=pi, rr=rr, kb=kb, hh=hh, st=st, qt=qt: e.matmul(
                                    PF[abank][:, off:off + 129], pT[:, pi, rr * 128:(rr + 1) * 128], vaug[:, kb, hh, 0:129],
                                    start=st, stop=(kb == 4 * qt + rr), skip_group_check=True),
                                    reads=[("pT", pi), ("va", kb), "vones"], writes=[("pf", abank)])
                    for rr in range(4):
                        fi = cnt["fin"] % 2
                        cnt["fin"] += 1
                        b0, b1 = 2 + rr // 2, 4 + rr // 2
                        off = (rr % 2) * 256
                        t = 4 * qt + rr
                        a0, a1 = PF[b0], PF[b1]
                        P.add("dve", lambda e, a0=a0, off=off, fi=fi: e.reciprocal(out=fsm[:, fi, 0:1], in_=a0[:, off + 128:off + 129]),
                              reads=[("pf", b0)], writes=[("fsm", fi)])
                        P.add("dve", lambda e, a1=a1, off=off, fi=fi: e.reciprocal(out=fsm[:, fi, 1:2], in_=a1[:, off + 128:off + 129]),
                              reads=[("pf", b1), ("fsm", fi)], writes=[("fsm", fi)])
                        P.add("dve", lambda e, fi=fi: e.tensor_tensor(out=fsm[:, fi, 2:3], in0=fsm[:, fi, 1:2], in1=cda[:, c0 + 3:c0 + 4], op=ALU.mult),
                              reads=[("fsm", fi), ("cda", j)], writes=[("fsm", fi)])
                        P.add("act", lambda e, a0=a0, off=off, fi=fi: e.activation(out=osb[:, fi, :], in_=a0[:, off:off + 128], func=AF.Copy,
                                                                                  scale=fsm[:, fi, 0:1]),
                              reads=[("pf", b0), ("fsm", fi)], writes=[("osb", fi)])
                        P.add("dve", lambda e, a1=a1, off=off, fi=fi: e.scalar_tensor_tensor(
                            out=osb[:, fi, :], in0=a1[:, off:off + 128], scalar=fsm[:, fi, 2:3], in1=osb[:, fi, :],
                            op0=ALU.mult, op1=ALU.add),
                            reads=[("pf", b1), ("fsm", fi), ("osb", fi)], writes=[("osb", fi)])
                        P.add("act", lambda e, fi=fi: e.activation(out=obf[:, fi, :], in_=osb[:, fi, :], func=AF.Square,
                                                                   accum_out=fsm[:, fi, 3:4]),
                              reads=[("osb", fi), ("fsm", fi)], writes=[("obf", fi), ("fsm", fi)])
                        P.add("act", lambda e, fi=fi: e.activation(out=fsm[:, fi, 4:5], in_=fsm[:, fi, 3:4], func=AF.Sqrt,
                                                                   bias=self.epsc[:, 0:1], scale=1.0 / 128),
                              reads=[("fsm", fi), "epsc"], writes=[("fsm", fi)])
                        P.add("dve", lambda e, fi=fi: e.reciprocal(out=fsm[:, fi, 4:5], in_=fsm[:, fi, 4:5]),
                              reads=[("fsm", fi)], writes=[("fsm", fi)])
                        P.add("act", lambda e, fi=fi: e.activation(out=obf[:, fi, :], in_=osb[:, fi, :], func=AF.Copy, scale=fsm[:, fi, 4:5]),
                              reads=[("osb", fi), ("fsm", fi)], writes=[("obf", fi)])
                        pb = PB[fi]
                        P.add("pe", lambda e, pb=pb, fi=fi: e.transpose(out=pb[:, 0:128], in_=obf[:, fi, :], identity=identb[:]),
                              reads=[("obf", fi), "identb"], writes=[("pb", fi)])
                        P.add("dve", lambda e, pb=pb, h=h, t=t: e.tensor_scalar(
                            out=U[:, h, t * 128:(t + 1) * 128], in0=pb[:, 0:128], scalar1=cda[:, c0 + 4:c0 + 5], scalar2=None, op0=ALU.mult),
                            reads=[("pb", fi), ("cda", j)], writes=[("U", h, t // 4)])
        slabs = [self.ws.get(wout[:, :, dh * 512:(dh + 1) * 512], 512) for dh in range(2)]
        for t in range(NT):
            for dh in range(2):
                slot, wkey, _ = slabs[dh]
                bank = cnt["pf"] % 2
                cnt["pf"] += 1
                pf = PF[bank]
                for hc in range(8):
                    P.add("pe", lambda e, pf=pf, slot=slot, hc=hc, t=t: e.matmul(
                        pf[:, :], U[:, hc, t * 128:(t + 1) * 128], ring[:, slot, hc, :], start=(hc == 0), stop=(hc == 7)),
                        reads=[wkey, ("U", hc, t // 4)], writes=[("pf", bank)])
                P.add("dve", lambda e, pf=pf, t=t, dh=dh: e.tensor_tensor(
                    out=X[:, t, dh * 512:(dh + 1) * 512], in0=X[:, t, dh * 512:(dh + 1) * 512], in1=pf[:, :], op=ALU.add),
                    reads=[("pf", bank), ("x", t)], writes=[("x", t)])
        for sl in slabs:
            self.ws.done(sl[2])

    def setup_gd_static(self):
        P = self.P
        trif, sellast, onesf, masks, onecol = self.trif, self.sellast, self.onesf, self.masks, self.onecol
        P.add("pool", lambda e: e.memset(onesf[:], 1.0), writes=["onesf"])
        P.add("pool", lambda e: e.memset(onecol[:], 1.0), writes=["onecol"])
        P.add("pool", lambda e: e.memset(trif[:], 1.0), writes=["trif"])
        P.add("pool", lambda e: e.affine_select(out=trif[:], in_=trif[:], pattern=[[1, 128]], compare_op=ALU.is_ge,
                                                fill=0.0, base=0, channel_multiplier=-1), reads=["trif"], writes=["trif"])
        P.add("pool", lambda e: e.memset(sellast[:], 1.0), writes=["sellast"])
        P.add("pool", lambda e: e.affine_select(out=sellast[:], in_=sellast[:], pattern=[[0, 128]], compare_op=ALU.is_ge,
                                                fill=0.0, base=-127, channel_multiplier=1), reads=["sellast"], writes=["sellast"])
        P.add("pool", lambda e: e.memset(masks[:], 0.0), writes=["masks"])
        P.add("pool", lambda e: e.affine_select(out=masks[:, 0:128], in_=masks[:, 0:128], pattern=[[1, 128]], compare_op=ALU.is_ge,
                                                fill=-30000.0, base=-1, channel_multiplier=-1), reads=["masks"], writes=["masks"])
        P.add("pool", lambda e: e.affine_select(out=masks[:, 128:256], in_=masks[:, 128:256], pattern=[[1, 128]], compare_op=ALU.is_ge,
                                                fill=-30000.0, base=0, channel_multiplier=-1), reads=["masks"], writes=["masks"])

    def setup_gd_levelmasks(self):
        P = self.P
        NEGM = self.NEGM
        scrA = lambda nb: self.xs_f[0:nb, 0, 0:128]
        scrC = lambda nb: self.xs_f[0:nb, 0, 128:256]
        K0 = [("xs", 0)]
        for k in range(7):
            B, half = 2 ** (k + 1), 2 ** k
            nb = 128 // B
            A, C = scrA(nb), scrC(nb)
            P.add("pool", lambda e, nb=nb: e.memset(self.xs_f[0:nb, 0, 0:256], 1.0), writes=K0)
            P.add("pool", lambda e, A=A, B=B, half=half: e.affine_select(out=A, in_=A, pattern=[[1, 128]], compare_op=ALU.is_ge, fill=0.0,
                                                                       base=-half, channel_multiplier=-B), reads=K0, writes=K0)
            P.add("pool", lambda e, A=A, B=B: e.affine_select(out=A, in_=A, pattern=[[-1, 128]], compare_op=ALU.is_ge, fill=0.0,
                                                             base=B - 1, channel_multiplier=B), reads=K0, writes=K0)
            P.add("pool", lambda e, C=C, B=B: e.affine_select(out=C, in_=C, pattern=[[1, 128]], compare_op=ALU.is_ge, fill=0.0,
                                                             base=0, channel_multiplier=-B), reads=K0, writes=K0)
            P.add("pool", lambda e, C=C, B=B, half=half: e.affine_select(out=C, in_=C, pattern=[[-1, 128]], compare_op=ALU.is_ge, fill=0.0,
                                                                       base=half - 1, channel_multiplier=B), reads=K0, writes=K0)
            pf = self.PF[1]
            P.add("pe", lambda e, pf=pf, A=A, C=C: e.matmul(pf[:, 0:128], A, C, start=True, stop=True, skip_group_check=True),
                  reads=K0, writes=[("pf", 1)])
            P.add("pe", lambda e, pf=pf, A=A, C=C: e.matmul(pf[:, 128:256], C, A, start=True, stop=True, skip_group_check=True),
                  reads=K0, writes=[("pf", 1)])
            P.add("act", lambda e, pf=pf, k=k: e.activation(out=NEGM[:, k, :], in_=pf[:, 0:256], func=AF.Copy, scale=-1.0),
                  reads=[("pf", 1)], writes=["NEGM"])

    def setup_gd_consts(self, j):
        P = self.P
        d = self.d
        convw, gdc, identf = self.convw, self.gdc, self.identf
        crow96 = self.xs_f[0:96, 1, 0:128]
        rows = d["gd_conv_w"][j].rearrange("k (c p) -> (k c) p", p=128)
        P.add("sp", lambda e: e.dma_start(out=crow96, in_=rows), writes=[("xs", 1)], dma="gcw%d" % j)
        pf = self.PF[0]
        P.add("pe", lambda e: e.transpose(out=pf[:, 0:96], in_=crow96, identity=identf[0:96, 0:96]),
              reads=[("xs", 1), "identf"], writes=[("pf", 0)])
        P.add("dve", lambda e: e.tensor_copy(out=convw[:, j, :], in_=pf[:, 0:96]), reads=[("pf", 0)], writes=[("convw", j)])
        col = lambda ap: ap.rearrange("(p o) -> p o", o=1)
        P.add("sp", lambda e: e.dma_start(out=gdc[:, j, 0:1], in_=col(d["gd_out_norm"][j])), writes=[("gdc", j)], dma="gon%d" % j)
        gsm = self.gsm
        P.add("sp", lambda e: e.dma_start(out=gsm[:, j, 0:8], in_=d["gd_a_log"][j:j + 1, :].partition_broadcast(128)),
              writes=[("gsm", j)], dma="gal%d" % j)
        P.add("sp", lambda e: e.dma_start(out=gsm[:, j, 8:16], in_=d["gd_dt_bias"][j:j + 1, :].partition_broadcast(128)),
              writes=[("gsm", j)], dma="gdt%d" % j)
        P.add("act", lambda e: e.activation(out=gsm[:, j, 0:8], in_=gsm[:, j, 0:8], func=AF.Exp), reads=[("gsm", j)], writes=[("gsm", j)])
        P.add("dve", lambda e: e.tensor_scalar(out=gsm[:, j, 0:8], in0=gsm[:, j, 0:8], scalar1=-1.0, scalar2=None, op0=ALU.mult),
              reads=[("gsm", j)], writes=[("gsm", j)])

    def gdn(self, l):
        P = self.P
        j = l // 2
        g = self.g
        X, xnT, U, ring = self.X, self.xnT, self.U, self.ring
        identb, identf = self.identb, self.identf
        PF, PB = self.PF, self.PB
        epsc = self.epsc
        DKS = 128.0 ** -0.5
        win = self.d["gd_w_in"][j].rearrange("(kc p) f -> p kc f", p=128)
        wout = self.d["gd_w_out"][j].rearrange("(kc p) f -> p kc f", p=128)
        self.rmsnorm(8 * l)
        XN0 = [("xnT", 0, 0)]
        cnt = {"pf": 0, "rb": 0, "sq": 0}
        BA, BETA, GC, EG, GLB, EGL, EKD = (g[k] for k in ("BA", "BETA", "GC", "EG", "GLB", "EGL", "EKD"))
        flat = lambda v: v.rearrange("p a b -> p (a b)")

        sba = self.ws.get(win[:, :, 4096:4112], 16)
        slot_ba, wkey_ba, _ = sba
        pf_ba = PF[0]
        for t in range(NT):
            for kc in range(8):
                P.add("pe", lambda e, t=t, kc=kc: e.matmul(pf_ba[:, t * 16:(t + 1) * 16], xnT[:, kc, t * 128:(t + 1) * 128],
                                                          ring[:, slot_ba, kc, 0:16], start=(kc == 0), stop=(kc == 7),
                                                          skip_group_check=True),
                      reads=[wkey_ba, ("xnT", t, kc)], writes=[("pf", 0)])
        self.ws.done(sba[2])
        P.add("dve", lambda e: e.tensor_copy(out=flat(BA), in_=pf_ba[:, 0:256]), reads=[("pf", 0)] + XN0, writes=["BA"])
        P.add("act", lambda e: e.activation(out=BETA, in_=BA[:, :, 0:8], func=AF.Sigmoid), reads=["BA"] + XN0, writes=["BETA"])
        gsm = self.gsm
        for t in range(NT):
            P.add("dve", lambda e, t=t: e.tensor_tensor(out=GC[:, t, :], in0=BA[:, t, 8:16], in1=gsm[:, j, 8:16], op=ALU.add),
                  reads=["BA", ("gsm", j)] + XN0, writes=["GC"])
        P.add("act", lambda e: e.activation(out=GC, in_=GC, func=AF.Exp), reads=["GC"], writes=["GC"])
        P.add("act", lambda e: e.activation(out=GC, in_=GC, func=AF.Ln, bias=1.0), reads=["GC"], writes=["GC"])
        for t in range(NT):
            P.add("dve", lambda e, t=t: e.tensor_tensor(out=GLB[:, t, :], in0=GC[:, t, :], in1=gsm[:, j, 0:8], op=ALU.mult),
                  reads=["GC", "BETA", ("gsm", j)] + XN0, writes=["BA"])
        pf1 = PF[1]
        for t in range(NT):
            P.add("pe", lambda e, t=t: e.matmul(pf1[:, t * 8:(t + 1) * 8], self.trif[:, :], GLB[:, t, :], start=True, stop=True,
                                                skip_group_check=True),
                  reads=["BA", "trif"], writes=[("pf", 1)])
        P.add("dve", lambda e: e.tensor_copy(out=flat(GC), in_=pf1[:, 0:128]), reads=[("pf", 1)], writes=["GC"])
        P.add("act", lambda e: e.activation(out=EG, in_=GC, func=AF.Exp), reads=["GC"] + XN0, writes=["EG"])
        for t in range(NT):
            P.add("pe", lambda e, t=t: e.matmul(pf1[:, 128 + t * 8:128 + (t + 1) * 8], self.sellast[:, :], GC[:, t, :], start=True, stop=True,
                                                skip_group_check=True),
                  reads=["GC", "sellast"], writes=[("pf", 1)])
        P.add("dve", lambda e: e.tensor_copy(out=flat(GLB), in_=pf1[:, 128:256]), reads=[("pf", 1)], writes=["BA"])
        P.add("act", lambda e: e.activation(out=EGL, in_=GLB, func=AF.Exp), reads=["BA"] + XN0, writes=["EGL"])
        P.add("dve", lambda e: e.tensor_tensor(out=EKD, in0=GLB, in1=GC, op=ALU.subtract), reads=["BA", "GC"] + XN0, writes=["EKD"])
        P.add("act", lambda e: e.activation(out=EKD, in_=EKD, func=AF.Exp), reads=["EKD"], writes=["EKD"])

        raw, acc, halo, sT, sq, zs, R, SC, Sf, Sb = (g[k] for k in ("raw", "acc", "halo", "sT", "sq", "zs", "R", "SC", "Sf", "Sb"))
        convw = self.convw
        for hp in range(4):
            slabs = {}
            for wi, which in enumerate(("q", "k", "v", "z")):
                slabs[which] = self.ws.get(win[:, :, wi * 1024 + hp * 256: wi * 1024 + (hp + 1) * 256], 256)
            P.add("pool", lambda e: e.memset(flat(Sf), 0.0), reads=XN0, writes=[("Sf", 0), ("Sf", 1)])
            P.add("pool", lambda e: e.memset(flat(Sb), 0.0), reads=XN0, writes=[("Sb", 0), ("Sb", 1)])
            P.add("pool", lambda e: e.memset(flat(halo), 0.0), reads=XN0, writes=[("halo", i) for i in range(6)])
            for gi in range(4):
                for wi, which in enumerate(("k", "q", "v")):
                    slot, wkey, _ = slabs[which]
                    cbase = {"q": 0, "k": 8, "v": 16}[which]
                    for hh in range(2):
                        h = 2 * hp + hh
                        bank = cnt["pf"] % 2
                        cnt["pf"] += 1
                        pf = PF[bank]
                        rb = cnt["rb"] % 2
                        cnt["rb"] += 1
                        hi = wi * 2 + hh
                        for kc in range(8):
                            P.add("pe", lambda e, pf=pf, slot=slot, kc=kc, hh=hh, gi=gi: e.matmul(
                                pf[:, :], ring[:, slot, kc, hh * 128:(hh + 1) * 128], xnT[:, kc, gi * 512:(gi + 1) * 512],
                                start=(kc == 0), stop=(kc == 7)),
                                reads=[wkey] + [("xnT", t, kc) for t in range(4 * gi, 4 * gi + 4)], writes=[("pf", bank)])
                        P.add("dve", lambda e, rb=rb, hi=hi: e.tensor_copy(out=raw[:, rb, 0:3], in_=halo[:, hi, 0:3]),
                              reads=[("halo", hi)], writes=[("raw", rb)])
                        P.add("act", lambda e, pf=pf, rb=rb: e.activation(out=raw[:, rb, 3:515], in_=pf[:, :], func=AF.Copy),
                              reads=[("pf", bank), ("raw", rb)], writes=[("raw", rb)])
                        P.add("dve", lambda e, rb=rb, hi=hi: e.tensor_copy(out=halo[:, hi, 0:3], in_=raw[:, rb, 512:515]),
                              reads=[("raw", rb)], writes=[("halo", hi)])
                        cc = cbase + h
                        P.add("dve", lambda e, rb=rb, cc=cc: e.tensor_scalar(
                            out=acc[:, 0, :], in0=raw[:, rb, 0:512], scalar1=convw[:, j, cc:cc + 1], scalar2=None, op0=ALU.mult),
                            reads=[("raw", rb), ("convw", j)], writes=[("acc", 0)])
                        for tap in range(1, 4):
                            P.add("dve", lambda e, rb=rb, cc=cc, tap=tap: e.scalar_tensor_tensor(
                                out=acc[:, 0, :], in0=raw[:, rb, tap:tap + 512], scalar=convw[:, j, tap * 24 + cc:tap * 24 + cc + 1],
                                in1=acc[:, 0, :], op0=ALU.mult, op1=ALU.add),
                                reads=[("raw", rb), ("acc", 0), ("convw", j)], writes=[("acc", 0)])
                        P.add("act", lambda e, rb=rb, hh=hh, wi=wi: e.activation(
                            out=sT[:, hh, :, wi * 128:(wi + 1) * 128], in_=acc[:, 0, :].rearrange("p (a b) -> p a b", a=4), func=AF.Silu),
                            reads=[("acc", 0)], writes=[("sT", hh, wi)])
                        if which in ("k", "q"):
                            sb_ = cnt["sq"] % 2
                            cnt["sq"] += 1
                            P.add("pool", lambda e, hh=hh, wi=wi, sb_=sb_: e.tensor_tensor(
                                out=sq[:, sb_, :].rearrange("p (a b) -> p a b", a=4), in0=sT[:, hh, :, wi * 128:(wi + 1) * 128],
                                in1=sT[:, hh, :, wi * 128:(wi + 1) * 128], op=ALU.mult),
                                reads=[("sT", hh, wi)], writes=[("sq", sb_)])
                            for tl in range(4):
                                colr = tl * 4 + wi * 2 + hh
                                P.add("pe", lambda e, sb_=sb_, tl=tl, colr=colr: e.matmul(
                                    PF[5][:, 384 + colr:384 + colr + 1], sq[:, sb_, tl * 128:(tl + 1) * 128], self.onecol[:, 0:1],
                                    start=True, stop=True, skip_group_check=True),
                                    reads=[("sq", sb_), "onecol"], writes=[("pf", 5)])
                slot, wkey, _ = slabs["z"]
                for tl in range(4):
                    t = 4 * gi + tl
                    bank = cnt["pf"] % 2
                    cnt["pf"] += 1
                    pf = PF[bank]
                    for kc in range(8):
                        P.add("pe", lambda e, pf=pf, slot=slot, kc=kc, t=t: e.matmul(
                            pf[:, 0:256], xnT[:, kc, t * 128:(t + 1) * 128], ring[:, slot, kc, 0:256], start=(kc == 0), stop=(kc == 7)),
                            reads=[wkey, ("xnT", t, kc)], writes=[("pf", bank)])
                    P.add("act", lambda e, pf=pf, tl=tl: e.activation(out=zs[:, tl, :], in_=pf[:, 0:256], func=AF.Silu),
                          reads=[("pf", bank)], writes=[("zs", tl)])
                P.add("act", lambda e: e.activation(out=flat(R), in_=PF[5][:, 384:400], func=AF.Sqrt, bias=epsc[:, 0:1], scale=1.0),
                      reads=[("pf", 5), "epsc"], writes=["R"])
                P.add("dve", lambda e: e.reciprocal(out=flat(R), in_=flat(R)), reads=["R"], writes=["R"])
                hs = slice(2 * hp, 2 * hp + 2)
                ts = slice(4 * gi, 4 * gi + 4)
                rk, rq = R[:, :, 0:2], R[:, :, 2:4]
                scv = lambda q_: SC[:, q_, :].rearrange("p (a b) -> p a b", a=4)
                T1, CKBG, CKD, CQ, UL, UA, BIAS, LN = (scv(i) for i in range(8))
                bt, egs, ekds, gcs = BETA[:, ts, hs], EG[:, ts, hs], EKD[:, ts, hs], GC[:, ts, hs]
                P.add("dve", lambda e, T1=T1, rk=rk, bt=bt: e.tensor_tensor(out=T1, in0=rk, in1=bt, op=ALU.mult), reads=["R", "BETA"], writes=[("SC", 0)])
                P.add("dve", lambda e, CKBG=CKBG, T1=T1, egs=egs: e.tensor_tensor(out=CKBG, in0=T1, in1=egs, op=ALU.mult), reads=[("SC", 0), "EG"], writes=[("SC", 1)])
                P.add("dve", lambda e, CKD=CKD, rk=rk, ekds=ekds: e.tensor_tensor(out=CKD, in0=rk, in1=ekds, op=ALU.mult), reads=["R", "EKD"], writes=[("SC", 2)])
                P.add("dve", lambda e, CQ=CQ, rq=rq, egs=egs: e.scalar_tensor_tensor(out=CQ, in0=rq, scalar=DKS, in1=egs, op0=ALU.mult, op1=ALU.mult),
                      reads=["R", "EG"], writes=[("SC", 3)])
                P.add("act", lambda e, UL=UL, T1=T1: e.activation(out=UL, in_=T1, func=AF.Ln), reads=[("SC", 0)], writes=[("SC", 4)])
                P.add("dve", lambda e, UL=UL, gcs=gcs: e.tensor_tensor(out=UL, in0=UL, in1=gcs, op=ALU.add), reads=[("SC", 4), "GC"], writes=[("SC", 4)])
                P.add("act", lambda e, UA=UA, rq=rq: e.activation(out=UA, in_=rq, func=AF.Ln, scale=DKS), reads=["R"], writes=[("SC", 5)])
                P.add("dve", lambda e, UA=UA, gcs=gcs: e.tensor_tensor(out=UA, in0=UA, in1=gcs, op=ALU.add), reads=[("SC", 5), "GC"], writes=[("SC", 5)])
                P.add("act", lambda e, BIAS=BIAS, rk=rk: e.activation(out=BIAS, in_=rk, func=AF.Ln), reads=["R"], writes=[("SC", 6)])
                P.add("dve", lambda e, BIAS=BIAS, gcs=gcs: e.tensor_tensor(out=BIAS, in0=BIAS, in1=gcs, op=ALU.subtract), reads=[("SC", 6), "GC"], writes=[("SC", 6)])
                SCK = [("SC", i) for i in range(7)]
                for tl in range(4):
                    t = 4 * gi + tl
                    for c in range(2):
                        hh = c
                        sc1 = lambda q_, tl=tl, hh=hh: SC[:, q_, tl * 2 + hh:tl * 2 + hh + 1]
                        pb, pc = PB[c], PF[2 + c]
                        kd, kbg, vb, E, MA = (g[k, c] for k in ("kd", "kbg", "vb", "E", "MA"))
                        ksT = sT[:, hh, tl, 0:128]
                        vsT = sT[:, hh, tl, 256:384]
                        P.add("pe", lambda e, pb=pb, ksT=ksT: e.transpose(out=pb[:, 0:128], in_=ksT, identity=identb[:]),
                              reads=[("sT", hh, 0), "identb"], writes=[("pb", c)])
                        P.add("pe", lambda e, pb=pb, vsT=vsT: e.transpose(out=pb[:, 128:256], in_=vsT, identity=identb[:]),
                              reads=[("sT", hh, 2), "identb"], writes=[("pb", c)])
                        P.add("act", lambda e, pb=pb, kd=kd, sc1=sc1: e.activation(out=kd, in_=pb[:, 0:128], func=AF.Copy, scale=sc1(2)),
                              reads=[("pb", c)] + SCK, writes=[("kd", c)])
                        P.add("dve", lambda e, pb=pb, kbg=kbg, sc1=sc1: e.tensor_scalar(out=kbg, in0=pb[:, 0:128], scalar1=sc1(1), scalar2=None, op0=ALU.mult),
                              reads=[("pb", c)] + SCK, writes=[("kbg", c)])
                        hcol = 2 * hp + hh
                        P.add("dve", lambda e, pb=pb, vb=vb, t=t, hcol=hcol: e.tensor_scalar(
                            out=vb, in0=pb[:, 128:256], scalar1=BETA[:, t, hcol:hcol + 1], scalar2=None, op0=ALU.mult),
                            reads=[("pb", c), "BETA"], writes=[("vb", c)])
                        P.add("pe", lambda e, pc=pc, ksT=ksT, hh=hh, tl=tl: e.matmul(pc[:, 0:256], ksT, sT[:, hh, tl, 0:256], start=True, stop=True,
                                                                                      skip_group_check=True),
                              reads=[("sT", hh, 0), ("sT", hh, 1)], writes=[("pf", 2 + c)])
                        P.add("pool", lambda e, E=E, sc1=sc1: e.tensor_scalar(out=E[:, 0:128], in0=identf[:, :], scalar1=sc1(4), scalar2=None, op0=ALU.mult),
                              reads=["identf"] + SCK, writes=[("E", c)])
                        P.add("pool", lambda e, E=E, sc1=sc1: e.tensor_scalar(out=E[:, 128:256], in0=identf[:, :], scalar1=sc1(5), scalar2=None, op0=ALU.mult),
                              reads=["identf", ("E", c)] + SCK, writes=[("E", c)])
                        P.add("pe", lambda e, pc=pc, E=E: e.matmul(pc[:, 256:512], self.onesf[:, :], E[:, :], start=True, stop=False, skip_group_check=True),
                              reads=[("E", c), "onesf"], writes=[("pf", 2 + c)])
                        P.add("pe", lambda e, pc=pc: e.matmul(pc[:, 256:512], identf[:, :], self.masks[:, :], start=False, stop=True, skip_group_check=True),
                              reads=["identf", "masks"], writes=[("pf", 2 + c)])
                        P.add("act", lambda e, pc=pc, E=E, sc1=sc1: e.activation(out=E[:, :], in_=pc[:, 256:512], func=AF.Exp, bias=sc1(6)),
                              reads=[("pf", 2 + c)] + SCK, writes=[("E", c)])
                        P.add("dve", lambda e, pc=pc, E=E, MA=MA: e.tensor_tensor(out=MA[:, :], in0=pc[:, 0:256], in1=E[:, :], op=ALU.mult),
                              reads=[("pf", 2 + c), ("E", c)], writes=[("MA", c)])
                        Lb, DD, TM = g["Lb", c], g["DD", c], g["TM", c]
                        P.add("pe", lambda e, pb=pb, MA=MA: e.transpose(out=pb[:, 256:384], in_=MA[:, 0:128], identity=identb[:]),
                              reads=[("MA", c), "identb"], writes=[("pb", c)])
                        P.add("act", lambda e, pb=pb, Lb=Lb: e.activation(out=Lb, in_=pb[:, 256:384], func=AF.Copy),
                              reads=[("pb", c)], writes=[("Lb", c)])
                        NEGM = self.NEGM
                        P.add("pool", lambda e, TM=TM, Lb=Lb: e.tensor_tensor(out=TM[:, 0:128], in0=Lb, in1=NEGM[:, 0, 0:128], op=ALU.mult),
                              reads=[("Lb", c), "NEGM"], writes=[("TM", c)])
                        P.add("pool", lambda e, TM=TM, MA=MA: e.tensor_tensor(out=TM[:, 128:256], in0=MA[:, 0:128], in1=NEGM[:, 0, 128:256], op=ALU.mult),
                              reads=[("MA", c), "NEGM", ("TM", c)], writes=[("TM", c)])
                        P.add("pool", lambda e, TM=TM, DD=DD: e.tensor_tensor(out=DD[:, 0, 0:128], in0=TM[:, 0:128], in1=identb[:, :], op=ALU.add),
                              reads=[("TM", c), "identb"], writes=[("DD", c, 0)])
                        P.add("pool", lambda e, TM=TM, DD=DD: e.tensor_tensor(out=DD[:, 0, 128:256], in0=TM[:, 128:256], in1=identb[:, :], op=ALU.add),
                              reads=[("TM", c), "identb", ("DD", c, 0)], writes=[("DD", c, 0)])
                    for lev in range(1, 7):
                        for c in range(2):
                            pc = PF[2 + c]
                            MA, Lb, DD, QQ, TM = (g[k_, c] for k_ in ("MA", "Lb", "DD", "QQ", "TM"))
                            pi, po = (lev - 1) % 2, lev % 2
                            P.add("pe", lambda e, pc=pc, MA=MA, DD=DD, pi=pi: e.matmul(pc[:, 0:128], MA[:, 0:128], DD[:, pi, 0:128], start=True, stop=True,
                                                                                        skip_group_check=True),
                                  reads=[("MA", c), ("DD", c, pi)], writes=[("pf", 2 + c)])
                            P.add("pe", lambda e, pc=pc, Lb=Lb, DD=DD, pi=pi: e.matmul(pc[:, 128:256], Lb, DD[:, pi, 128:256], start=True, stop=True,
                                                                                        skip_group_check=True),
                                  reads=[("Lb", c), ("DD", c, pi)], writes=[("pf", 2 + c)])
                            P.add("act", lambda e, pc=pc, QQ=QQ: e.activation(out=QQ[:, :], in_=pc[:, 0:256], func=AF.Copy),
                                  reads=[("pf", 2 + c)], writes=[("QQ", c)])
                            P.add("pe", lambda e, pc=pc, QQ=QQ, DD=DD, pi=pi: e.matmul(pc[:, 256:384], DD[:, pi, 128:256], QQ[:, 0:128], start=True, stop=True,
                                                                                        skip_group_check=True),
                                  reads=[("QQ", c), ("DD", c, pi)], writes=[("pf", 2 + c)])
                            P.add("pe", lambda e, pc=pc, QQ=QQ, DD=DD, pi=pi: e.matmul(pc[:, 384:512], DD[:, pi, 0:128], QQ[:, 128:256], start=True, stop=True,
                                                                                        skip_group_check=True),
                                  reads=[("QQ", c), ("DD", c, pi)], writes=[("pf", 2 + c)])
                            P.add("dve", lambda e, pc=pc, TM=TM, lev=lev: e.tensor_tensor(out=TM[:, :], in0=pc[:, 256:512], in1=self.NEGM[:, lev, :], op=ALU.mult),
                                  reads=[("pf", 2 + c), "NEGM"], writes=[("TM", c)])
                            P.add("pool", lambda e, TM=TM, DD=DD, pi=pi, po=po: e.tensor_tensor(out=DD[:, po, :], in0=DD[:, pi, :], in1=TM[:, :], op=ALU.add),
                                  reads=[("TM", c), ("DD", c, pi)], writes=[("DD", c, po)])
                    for c in range(2):
                        hh = c
                        h = 2 * hp + hh
                        pc, ps_, pb = PF[2 + c], PF[4 + c], PB[c]
                        kd, kbg, vb, MA, DD, u, wT, vn, o, og, psm = (g[k, c] for k in ("kd", "kbg", "vb", "MA", "DD", "u", "wT", "vn", "o", "og", "ps"))
                        sc1 = lambda q_, tl=tl, hh=hh: SC[:, q_, tl * 2 + hh:tl * 2 + hh + 1]
                        qsT = sT[:, hh, tl, 128:256]
                        P.add("pe", lambda e, pc=pc, DD=DD, vb=vb: e.matmul(pc[:, 0:128], DD[:, 0, 128:256], vb, start=True, stop=True, skip_group_check=True),
                              reads=[("DD", c, 0), ("vb", c)], writes=[("pf", 2 + c)])
                        P.add("pe", lambda e, pc=pc, DD=DD, kbg=kbg: e.matmul(pc[:, 128:256], kbg, DD[:, 0, 128:256], start=True, stop=True, skip_group_check=True),
                              reads=[("DD", c, 0), ("kbg", c)], writes=[("pf", 2 + c)])
                        P.add("act", lambda e, pc=pc, u=u: e.activation(out=u, in_=pc[:, 0:128], func=AF.Copy), reads=[("pf", 2 + c)], writes=[("u", c)])
                        P.add("dve", lambda e, pc=pc, wT=wT: e.tensor_copy(out=wT, in_=pc[:, 128:256]), reads=[("pf", 2 + c)], writes=[("wT", c)])
                        P.add("pe", lambda e, ps_=ps_, wT=wT, hh=hh: e.matmul(ps_[:, 0:128], wT, Sb[:, hh, :], start=True, stop=True, skip_group_check=True),
                              reads=[("wT", c), ("Sb", hh)], writes=[("pf", 4 + c)])
                        P.add("pe", lambda e, ps_=ps_, qsT=qsT, hh=hh: e.matmul(ps_[:, 128:256], qsT, Sb[:, hh, :], start=True, stop=True, skip_group_check=True),
                              reads=[("sT", hh, 1), ("Sb", hh)], writes=[("pf", 4 + c)])
                        P.add("dve", lambda e, ps_=ps_, u=u, vn=vn: e.tensor_tensor(out=vn, in0=u, in1=ps_[:, 0:128], op=ALU.subtract),
                              reads=[("pf", 4 + c), ("u", c)], writes=[("vn", c)])
                        P.add("pe", lambda e, ps_=ps_, MA=MA, vn=vn: e.matmul(ps_[:, 256:384], MA[:, 128:256], vn, start=True, stop=True, skip_group_check=True),
                              reads=[("MA", c), ("vn", c)], writes=[("pf", 4 + c)])
                        if hh == 0:
                            colsq = 384
                        P.add("pe", lambda e, ps_=ps_, kd=kd, vn=vn, c=c: e.matmul(PF[2 + c][:, 384:512], kd, vn, start=True, stop=True, skip_group_check=True),
                              reads=[("kd", c), ("vn", c)], writes=[("pf", 2 + c)])
                        P.add("act", lambda e, ps_=ps_, o=o: e.activation(out=o, in_=ps_[:, 256:384], func=AF.Copy), reads=[("pf", 4 + c)], writes=[("o", c)])
                        P.add("dve", lambda e, ps_=ps_, o=o, sc1=sc1: e.scalar_tensor_tensor(out=o, in0=ps_[:, 128:256], scalar=sc1(3), in1=o, op0=ALU.mult, op1=ALU.add),
                              reads=[("pf", 4 + c), ("o", c)] + SCK, writes=[("o", c)])
                        P.add("dve", lambda e, c=c, hh=hh, t=t, h=h: e.scalar_tensor_tensor(
                            out=Sf[:, hh, :], in0=Sf[:, hh, :], scalar=EGL[:, t, h:h + 1], in1=PF[2 + c][:, 384:512], op0=ALU.mult, op1=ALU.add),
                            reads=[("pf", 2 + c), ("Sf", hh), "EGL"], writes=[("Sf", hh)])
                        P.add("act", lambda e, hh=hh: e.activation(out=Sb[:, hh, :], in_=Sf[:, hh, :], func=AF.Copy), reads=[("Sf", hh)], writes=[("Sb", hh)])
                        P.add("act", lambda e, o=o, og=og, psm=psm: e.activation(out=og, in_=o, func=AF.Square, accum_out=psm[:, 0:1]),
                              reads=[("o", c)], writes=[("og", c), ("psm", c)])
                        P.add("act", lambda e, psm=psm: e.activation(out=psm[:, 1:2], in_=psm[:, 0:1], func=AF.Sqrt, bias=epsc[:, 0:1], scale=1.0 / 128),
                              reads=[("psm", c), "epsc"], writes=[("psm", c)])
                        P.add("dve", lambda e, psm=psm: e.reciprocal(out=psm[:, 1:2], in_=psm[:, 1:2]), reads=[("psm", c)], writes=[("psm", c)])
                        P.add("dve", lambda e, o=o, og=og, psm=psm, tl=tl, hh=hh: e.scalar_tensor_tensor(
                            out=og, in0=o, scalar=psm[:, 1:2], in1=zs[:, tl, hh * 128:(hh + 1) * 128], op0=ALU.mult, op1=ALU.mult),
                            reads=[("o", c), ("psm", c), ("zs", tl)], writes=[("og", c)])
                        P.add("pe", lambda e, pb=pb, og=og: e.transpose(out=pb[:, 384:512], in_=og, identity=identb[:]),
                              reads=[("og", c), "identb"], writes=[("pb", c)])
                        P.add("act", lambda e, pb=pb, h=h, t=t: e.activation(out=U[:, h, t * 128:(t + 1) * 128], in_=pb[:, 384:512], func=AF.Copy,
                                                                            scale=self.gdc[:, j, 0:1]),
                              reads=[("pb", c), ("gdc", j)], writes=[("U", h, t // 4)])
            for which in ("q", "k", "v", "z"):
                self.ws.done(slabs[which][2])
        slabs = [self.ws.get(wout[:, :, dh * 512:(dh + 1) * 512], 512) for dh in range(2)]
        for t in range(NT):
            for dh in range(2):
                slot, wkey, _ = slabs[dh]
                bank = cnt["pf"] % 2
                cnt["pf"] += 1
                pf = PF[bank]
                for hc in range(8):
                    P.add("pe", lambda e, pf=pf, slot=slot, hc=hc, t=t: e.matmul(
                        pf[:, :], U[:, hc, t * 128:(t + 1) * 128], ring[:, slot, hc, :], start=(hc == 0), stop=(hc == 7)),
                        reads=[wkey, ("U", hc, t // 4)], writes=[("pf", bank)])
                P.add("dve", lambda e, pf=pf, t=t, dh=dh: e.tensor_tensor(
                    out=X[:, t, dh * 512:(dh + 1) * 512], in0=X[:, t, dh * 512:(dh + 1) * 512], in1=pf[:, :], op=ALU.add),
                    reads=[("pf", bank), ("x", t)], writes=[("x", t)])
        for sl in slabs:
            self.ws.done(sl[2])

    def load_x(self, s):
        P = self.P
        X = self.X
        xv = self.d["x"][s].rearrange("(t p) d -> p t d", p=128)
        for q in range(4):
            P.add("sp", lambda e, q=q: e.dma_start(out=X[:, 4 * q:4 * q + 4, :], in_=xv[:, 4 * q:4 * q + 4, :]),
                  writes=[("x", t) for t in range(4 * q, 4 * q + 4)], dma=("xl", q))

    def store_x(self, s):
        P = self.P
        X = self.X
        ov = self.d["out"][s].rearrange("(t p) d -> p t d", p=128)
        ids = []
        for q in range(4):
            ids.append(P.add("sp", lambda e, q=q: e.dma_start(out=ov[:, 4 * q:4 * q + 4, :], in_=X[:, 4 * q:4 * q + 4, :]),
                             reads=[("x", t) for t in range(4 * q, 4 * q + 4)], writes=[("xst", q)], dma=("xs", q)))
        return ids

    def rmsnorm(self, gbase):
        P = self.P
        X, xs, ss, rstd, xnT = self.X, self.xs, self.ss, self.rstd, self.xnT
        identb, gcol = self.identb, self.gcol
        import os
        DBG = int(os.environ.get("K_DBG", "9"))
        for t in range(NT):
            P.add("act", lambda e, t=t: e.activation(out=xs[:, 1, :], in_=X[:, t, :], func=AF.Square,
                                                     accum_out=ss[:, t:t + 1]),
                  reads=[("x", t)], writes=[("xs", 1), ("ss", t)])
        P.add("act", lambda e: e.activation(out=rstd[:], in_=ss[:], func=AF.Sqrt, bias=self.epsc[:, 0:1], scale=1.0 / D),
              reads=[("ss", t) for t in range(NT)] + ["epsc"], writes=["rstd"])
        if DBG < 2:
            return
        P.add("dve", lambda e: e.reciprocal(out=rstd[:], in_=rstd[:]), reads=["rstd"], writes=["rstd"])
        for t in range(NT if DBG >= 6 else 1):
            b = t % 2
            pb = self.PB[b]
            P.add("act", lambda e, t=t, b=b: e.activation(out=xs[:, b, :], in_=X[:, t, :], func=AF.Copy,
                                                          scale=rstd[:, t:t + 1]),
                  reads=[("x", t), "rstd"], writes=[("xs", b)])
            if DBG < 4:
                continue
            for kc in range(8):
                P.add("pe", lambda e, b=b, kc=kc, pb=pb: e.transpose(out=pb[:, kc * 128:(kc + 1) * 128],
                                                                      in_=xs[:, b, kc * 128:(kc + 1) * 128],
                                                                      identity=identb[:]),
                      reads=[("xs", b), "identb"], writes=[("pb", b)])
            if DBG < 5:
                continue
            for kc in range(8):
                eng = "dve" if b == 0 else "act"
                if eng == "dve":
                    fn = lambda e, t=t, kc=kc, pb=pb: e.tensor_scalar(
                        out=xnT[:, kc, t * 128:(t + 1) * 128], in0=pb[:, kc * 128:(kc + 1) * 128],
                        scalar1=gcol[:, gbase + kc:gbase + kc + 1], scalar2=None, op0=ALU.mult)
                else:
                    fn = lambda e, t=t, kc=kc, pb=pb: e.activation(
                        out=xnT[:, kc, t * 128:(t + 1) * 128], in_=pb[:, kc * 128:(kc + 1) * 128],
                        func=AF.Copy, scale=gcol[:, gbase + kc:gbase + kc + 1])
                P.add(eng, fn, reads=[("pb", b), "gcol"], writes=[("xnT", t, kc)])

    def mlp(self, l):
        P = self.P
        X, xnT, U, ring = self.X, self.xnT, self.U, self.ring
        w1 = self.d["mlp_w_in"][l].rearrange("(kc p) f -> p kc f", p=128)
        w2 = self.d["mlp_w_out"][l].rearrange("(fc p) d -> p fc d", p=128)
        self.rmsnorm(32 + 8 * l)
        pfi = 0
        for fg in range(4):
            slabs = [self.ws.get(w1[:, :, fg * 1024 + s2 * 512: fg * 1024 + (s2 + 1) * 512], 512) for s2 in range(2)]
            for fc in range(8):
                slot, wkey, _ = slabs[fc // 4]
                off = (fc % 4) * 128
                for tt in range(4):
                    bank = pfi % 4
                    pfi += 1
                    pf = self.PF[bank]
                    for kc in range(8):
                        P.add("pe", lambda e, pf=pf, slot=slot, kc=kc, off=off, tt=tt: e.matmul(
                            pf[:, :], ring[:, slot, kc, off:off + 128], xnT[:, kc, tt * 512:(tt + 1) * 512],
                            start=(kc == 0), stop=(kc == 7)),
                            reads=[wkey] + [("xnT", t, kc) for t in range(4 * tt, 4 * tt + 4)],
                            writes=[("pf", bank)])
                    rb = pfi % 2
                    P.add("act", lambda e, pf=pf, rb=rb: e.activation(out=self.rtmp[:, rb, :], in_=pf[:, :], func=AF.Relu),
                          reads=[("pf", bank)], writes=[("xs", rb)])
                    P.add("dve", lambda e, fc=fc, tt=tt, rb=rb: e.tensor_tensor(
                        out=U[:, fc, tt * 512:(tt + 1) * 512], in0=self.rtmp[:, rb, :], in1=self.rtmp[:, rb, :],
                        op=ALU.mult),
                        reads=[("xs", rb)], writes=[("U", fc, tt)])
            for sl in slabs:
                self.ws.done(sl[2])
            slabs2 = [self.ws.get(w2[:, fg * 8:(fg + 1) * 8, dh * 512:(dh + 1) * 512], 512) for dh in range(2)]
            for t in range(NT):
                for dh in range(2):
                    slot, wkey, _ = slabs2[dh]
                    bank = pfi % 4
                    pfi += 1
                    pf = self.PF[bank]
                    for fc in range(8):
                        P.add("pe", lambda e, pf=pf, slot=slot, fc=fc, t=t: e.matmul(
                            pf[:, :], U[:, fc, t * 128:(t + 1) * 128], ring[:, slot, fc, :],
                            start=(fc == 0), stop=(fc == 7)),
                            reads=[wkey, ("U", fc, t // 4)], writes=[("pf", bank)])
                    P.add("dve", lambda e, pf=pf, t=t, dh=dh: e.tensor_tensor(
                        out=X[:, t, dh * 512:(dh + 1) * 512], in0=X[:, t, dh * 512:(dh + 1) * 512], in1=pf[:, :],
                        op=ALU.add),
                        reads=[("pf", bank), ("x", t)], writes=[("x", t)])
            for sl in slabs2:
                self.ws.done(sl[2])

    def build(self):
        P = self.P
        self.setup_consts()
        kinds = {k for k, _ in self.layers}
        if "gd" in kinds:
            self.setup_gd_static()
            self.setup_gd_levelmasks()
            for jj in sorted({l // 2 for (k, l) in self.layers if k == "gd"}):
                self.setup_gd_consts(jj)
        if "da" in kinds:
            self.setup_da_static()
            for (k, l) in self.layers:
                if k == "da":
                    self.setup_da_consts(l // 2, l)
        last_stores = []
        for s in range(self.n_seq):
            self.load_x(s)
            for l in self.layers:
                if l[0] == "mlp":
                    self.mlp(l[1])
                elif l[0] == "norm":
                    self.rmsnorm(32 + 8 * l[1])
                elif l[0] == "da":
                    self.diffattn(l[1])
                elif l[0] == "gd":
                    self.gdn(l[1])
            last_stores = self.store_x(s)
        P.add("sp", None, reads=[("xst", q) for q in range(4)])


def layer_plan():
    plan = []
    for i in range(DEPTH):
        plan.append(("da" if i % 2 == 0 else "gd", i))
        plan.append(("mlp", i))
    return plan


def build_program(n_seq=SEQ_PER_CORE, layers=None):
    if layers is None:
        layers = layer_plan()
    nc = bass.Bass("TRN2", target_bir_lowering=False)
    with ExitStack() as es:
        b = Builder(nc, Prog(nc, dry=True), None, n_seq, layers, es)
        b.build()
        future = list(b.ws.requests)
        P = Prog(nc, dry=False)
        b.P = P
        b.ws = WStream(P, b.ring, b.NSLOT, future)
        b.build()
        P.emit(es)
    return nc


WEIGHT_NAMES = ["mix_norm", "mlp_norm", "mlp_w_in", "mlp_w_out", "da_w_in", "da_q_norm", "da_k_norm",
                "da_lambda_q1", "da_lambda_k1", "da_lambda_q2", "da_lambda_k2", "da_sub_norm", "da_w_out",
                "gd_w_in", "gd_conv_w", "gd_a_log", "gd_dt_bias", "gd_out_norm", "gd_w_out"]


def run(inputs, n_seq=SEQ_PER_CORE, layers=None, ncores=NCORES, trace=False):
    nc = build_program(n_seq, layers)
    x = np.ascontiguousarray(np.asarray(inputs["x"], dtype=np.float32))
    weights = {k: np.ascontiguousarray(np.asarray(inputs[k], dtype=np.float32)) for k in WEIGHT_NAMES}
    in_maps = []
    for c in range(ncores):
        m = {"x": x[c * n_seq:(c + 1) * n_seq]}
        m.update(weights)
        in_maps.append(m)
    res = run_bass_kernel_spmd(nc, in_maps, core_ids=list(range(ncores)), trace=trace)
    out = np.concatenate([r["out"] for r in res.results], axis=0)
    return out, res


def kernel(**inputs):
    out, _ = run(inputs)
    return out
```

```python
import math
from contextlib import ExitStack

import numpy as np
import concourse.bass as bass
import concourse.mybir as mybir
from concourse.bass_utils import run_bass_kernel_spmd

F32 = mybir.dt.float32
BF16 = mybir.dt.bfloat16
AF = mybir.ActivationFunctionType
ALU = mybir.AluOpType
AX = mybir.AxisListType

D = 1024
S = 2048
NT = S // 128
DFF = 4096
DEPTH = 4
EPS = 1e-6
NCORES = 8
SEQ_PER_CORE = 4
GD_IN = 4 * 1024 + 16


class Op:
    __slots__ = ("id", "eng", "fn", "deps", "dma", "seq", "signal")


class Prog:
    ENGS = ("pe", "act", "dve", "pool", "sp")

    def __init__(self, nc, dry=False):
        self.nc = nc
        self.dry = dry
        self.ops = []
        self.by_eng = {e: [] for e in self.ENGS}
        self.lw = {}
        self.rd = {}
        self.dma_groups = {}
        self.group_all = set()
        self.psum_last = {}
        self.arena_names = set()
        self.cap = None

    def begin_capture(self):
        self.cap = []

    def end_capture(self):
        c, self.cap = self.cap, None
        return c

    def replay(self, lists):
        idx = [0] * len(lists)
        live = True
        while live:
            live = False
            for i, L in enumerate(lists):
                if idx[i] < len(L):
                    self.add(*L[idx[i]])
                    idx[i] += 1
                    live = True

    def add(self, eng, fn, reads=(), writes=(), dma=None):
        if self.dry:
            return None
        if self.cap is not None:
            self.cap.append((eng, fn, tuple(reads), tuple(writes), dma))
            return None
        op = Op()
        op.id = len(self.ops)
        op.eng = eng
        op.fn = fn
        op.dma = dma
        op.seq = 0
        op.signal = False
        if self.arena_names:
            for k in tuple(reads) + tuple(writes):
                nm = k[0] if isinstance(k, tuple) else k
                if nm in self.arena_names:
                    reads = tuple(reads) + ("ARENA",)
                    break
        deps = set()
        for k in reads:
            w = self.lw.get(k)
            if w is not None:
                deps.add(w)
        for k in writes:
            w = self.lw.get(k)
            if w is not None:
                deps.add(w)
            for r in self.rd.get(k, ()):
                deps.add(r)
        for k in reads:
            self.rd.setdefault(k, []).append(op.id)
        for k in writes:
            self.lw[k] = op.id
            self.rd[k] = []
        for k in tuple(reads) + tuple(writes):
            if isinstance(k, tuple) and k[0] in ("pf", "pb"):
                last = self.psum_last.setdefault(k, {})
                for eng2, oid in last.items():
                    if eng2 != eng:
                        deps.add(oid)
                last[eng] = op.id
        deps.discard(op.id)
        if eng == "pe" and dma is None:
            deps = {d for d in deps if not (self.ops[d].eng == "pe" and self.ops[d].dma is None)}
        op.deps = deps
        self.ops.append(op)
        self.by_eng[eng].append(op)
        if dma is not None:
            self.dma_groups.setdefault(dma, []).append(op.id)
        return op.id

    def emit(self, es):
        nc = self.nc
        ops = self.ops
        for op in ops:
            for d in op.deps:
                ops[d].signal = True
        for e in self.ENGS:
            c = 0
            for op in self.by_eng[e]:
                if op.dma is None and op.signal:
                    c += 1
                    op.seq = c
        for g, ids in self.dma_groups.items():
            for i, oid in enumerate(ids):
                ops[oid].seq = i + 1
        eng_sem = {e: es.enter_context(nc.semaphore("s_" + e)) for e in self.ENGS}
        dma_sem = {g: es.enter_context(nc.semaphore("d_" + str(g))) for g in self.dma_groups}
        block = es.enter_context(nc.Block())

        def emit_engine(ename, e):
            waited = {}
            for op in self.by_eng[ename]:
                need = {}
                for d in op.deps:
                    dop = ops[d]
                    if dop.dma is not None:
                        sem = dma_sem[dop.dma]
                        if dop.dma in self.group_all:
                            val = 16 * len(self.dma_groups[dop.dma])
                        else:
                            val = 16 * dop.seq
                    else:
                        sem = eng_sem[dop.eng]
                        val = dop.seq
                    key = id(sem)
                    if key not in need or need[key][1] < val:
                        need[key] = (sem, val)
                for key, (sem, val) in need.items():
                    if waited.get(key, 0) < val:
                        e.wait_ge(sem, val)
                        waited[key] = val
                if op.fn is None:
                    continue
                inst = op.fn(e)
                if op.dma is not None:
                    inst.then_inc(dma_sem[op.dma], 16)
                elif op.signal:
                    inst.then_inc(eng_sem[ename], 1)

        @block.tensor
        def _(e):
            emit_engine("pe", e)

        @block.scalar
        def _(e):
            emit_engine("act", e)

        @block.vector
        def _(e):
            emit_engine("dve", e)

        @block.gpsimd
        def _(e):
            emit_engine("pool", e)

        @block.sync
        def _(e):
            emit_engine("sp", e)


class WStream:
    def __init__(self, P, ring, nslot, lookahead_list=None):
        self.P = P
        self.ring = ring
        self.nslot = nslot
        self.future = lookahead_list
        self.requests = []
        self.issued = 0
        self.released = set()

    def _pump(self):
        if self.P.dry:
            return
        lim = min(len(self.future), len(self.requests) + self.nslot - 1)
        while self.issued < lim:
            j = self.issued
            if j >= self.nslot and (j - self.nslot) not in self.released:
                break
            src, w = self.future[j]
            slot = j % self.nslot
            ring = self.ring
            dst = self.view(slot, w)
            self.P.add("pool", lambda e, dst=dst, src=src: e.dma_start(out=dst, in_=src),
                       writes=[("w", slot)], dma=("w", slot))
            self.issued += 1

    def view(self, slot, w):
        ring = self.ring
        if isinstance(w, tuple):
            a, b = w
            return ring[:, slot, :, :].rearrange("p a b -> p (a b)")[:, 0:a * b].rearrange("p (a b) -> p a b", a=a)
        return ring[:, slot, :, 0:w]

    def get(self, src, w):
        i = len(self.requests)
        self.requests.append((src, w))
        if self.P.dry:
            return 0, ("w", 0), i
        self._pump()
        assert self.issued > i, "weight ring deadlock: release slabs before requesting more"
        slot = i % self.nslot
        return slot, ("w", slot), i

    def done(self, idx):
        self.released.add(idx)
        self._pump()


class Builder:
    def __init__(self, nc, P, ws_future, n_seq, layers, es):
        self.nc = nc
        self.P = P
        self.n_seq = n_seq
        self.layers = layers
        self.es = es
        self.ws_future = ws_future
        self.alloc()

    def sb(self, name, shape, dt):
        return self.es.enter_context(self.nc.sbuf_tensor(name, shape, dt))

    def carve(self, shape, dt):
        esz = 4 if dt == F32 else 2
        n = 1
        for s_ in shape:
            n *= s_
        nbytes = (n * esz + 3) // 4 * 4
        off = self._aoff
        assert off + nbytes <= self.ARENA_BYTES, "arena overflow"
        self._aoff = off + nbytes
        v = self.arena[:, off // 4:(off + nbytes) // 4]
        if dt != F32:
            v = v.bitcast(dt)
        v = v[:, 0:n]
        if len(shape) == 2:
            v = v.rearrange("p (a b) -> p a b", a=shape[0])
        elif len(shape) == 3:
            v = v.rearrange("p (a b c) -> p a b c", a=shape[0], b=shape[1])
        return v

    def alloc(self):
        nc = self.nc
        n_seq = self.n_seq
        d = {}
        d["x"] = nc.dram_tensor("x", [n_seq, S, D], F32, kind="ExternalInput").ap()
        d["out"] = nc.dram_tensor("out", [n_seq, S, D], F32, kind="ExternalOutput").ap()
        specs = [
            ("mix_norm", [4, D]), ("mlp_norm", [4, D]), ("mlp_w_in", [4, D, DFF]), ("mlp_w_out", [4, DFF, D]),
            ("da_w_in", [2, D, 3072]), ("da_q_norm", [2, 64]), ("da_k_norm", [2, 64]),
            ("da_lambda_q1", [2, 64]), ("da_lambda_k1", [2, 64]), ("da_lambda_q2", [2, 64]), ("da_lambda_k2", [2, 64]),
            ("da_sub_norm", [2, 128]), ("da_w_out", [2, D, D]),
            ("gd_w_in", [2, D, GD_IN]), ("gd_conv_w", [2, 4, 3072]), ("gd_a_log", [2, 8]), ("gd_dt_bias", [2, 8]),
            ("gd_out_norm", [2, 128]), ("gd_w_out", [2, D, D]),
        ]
        for name, shape in specs:
            d[name] = nc.dram_tensor(name, shape, F32, kind="ExternalInput").ap()
        self.d = d
        self.NSLOT = 4
        self.X = self.sb("X", [128, NT, D], F32)
        self.xnT = self.sb("xnT", [128, 8, S], BF16)
        self.U = self.sb("U", [128, 2, S], BF16)
        self.ring = self.sb("ring", [128, self.NSLOT, 8, 512], BF16)
        self.xs_f = self.sb("xs_f", [128, 2, 512], F32)
        self.xs = self.xs_f[:, :, :].rearrange("p a b -> p (a b)").bitcast(BF16).rearrange("p (a b) -> p a b", a=2)
        self.rtmp = self.xs_f
        self.ss = self.sb("ss", [128, NT], F32)
        self.rstd = self.sb("rstd", [128, NT], F32)
        self.epsc = self.sb("epsc", [128, 4], F32)
        self.identb = self.sb("identb", [128, 128], BF16)
        self.identf = self.sb("identf", [128, 128], F32)
        self.crow = self.xs_f[0:64, 1, 0:128]
        self.gcol = self.sb("gcol", [128, 64], F32)
        self.ARENA_BYTES = 35840 + 24576
        self.arena = self.sb("arena", [128, self.ARENA_BYTES // 4], F32)
        self._aoff = 0
        self.hT = self.carve([8, S], BF16)
        self._aoff = 0
        self.qT = self.carve([2, S], BF16)
        self.kT = self.carve([2, S], BF16)
        self.vaug = self.carve([NT, 2, 130], BF16)
        self.pT = self.carve([3, 512], BF16)
        self.qraw = self.carve([2, 512], BF16)
        self.qsq = self.carve([2, 512], BF16)
        self.qrs = self.carve([1, 512], F32)
        self.osb = self.carve([2, 128], F32)
        self.obf = self.carve([2, 128], BF16)
        self.fsm = self.carve([2, 8], F32)
        self.da_arena_end = self._aoff
        self._aoff = 0
        g = {}
        g["BA"] = self.carve([NT, 16], F32)
        for nm in ("BETA", "GC", "EG", "EGL", "EKD"):
            g[nm] = self.carve([NT, 8], F32)
        g["GLB"] = g["BA"][:, :, :].rearrange("p a b -> p (a b)")[:, 0:NT * 8].rearrange("p (a b) -> p a b", a=NT)
        g["raw"] = self.carve([2, 516], F32)
        g["acc"] = self.carve([1, 512], F32)
        g["halo"] = self.carve([6, 4], F32)
        g["sT"] = [self.carve([2, 4, 3 * 128], BF16) for _ in range(2)]
        g["sq"] = self.carve([2, 512], BF16)
        g["zs"] = [self.carve([4, 256], BF16) for _ in range(2)]
        g["R"] = [self.carve([4, 4], F32) for _ in range(2)]
        g["SC"] = [self.carve([8, 8], F32) for _ in range(2)]
        g["Sf"] = self.carve([2, 128], F32)
        g["Sb"] = self.carve([2, 128], BF16)
        self.NCH = 4
        for c in range(self.NCH):
            g["kd", c] = self.carve([128], BF16)
            g["kbg", c] = self.carve([128], BF16)
            g["vb", c] = self.carve([128], BF16)
            g["E", c] = self.carve([256], F32)
            g["MA", c] = self.carve([256], BF16)
            g["Lb", c] = self.carve([128], BF16)
            g["QQ", c] = self.carve([256], BF16)
            g["TM", c] = self.carve([256], BF16)
            g["DD", c] = self.carve([2, 256], BF16)
            g["u", c] = self.carve([128], F32)
            g["wT", c] = self.carve([128], BF16)
            g["vn", c] = self.carve([128], BF16)
            g["o", c] = self.carve([128], F32)
            g["og", c] = self.carve([128], BF16)
            g["ps", c] = self.carve([8], F32)
        self.g = g
        self.gd_arena_end = self._aoff
        assert max(self.da_arena_end, self.gd_arena_end) <= self.ARENA_BYTES
        self.convw = self.sb("convw", [128, 2, 96], F32)
        self.gdc = self.sb("gdc", [128, 2, 4], F32)
        self.gsm = self.sb("gsm", [128, 2, 16], F32)
        self.NEGM = self.sb("NEGM", [128, 7, 256], BF16)
        self.trif = self.sb("trif", [128, 128], F32)
        self.sellast = self.sb("sellast", [128, 128], F32)
        self.onesf = self.sb("onesf", [128, 128], F32)
        self.masks = self.sb("masks", [128, 256], F32)
        self.onecol = self.sb("onecol", [128, 2], BF16)
        self.onesbd = self.sb("onesbd", [128, 128], BF16)
        self.negmask = self.sb("negmask", [128, 128], BF16)
        self.cda = self.sb("cda", [128, 16], F32)
        self.lamt = self.xs_f[:, 0, 256:512].rearrange("p (a b) -> p a b", a=4)
        self.lamp = self.sb("lamp", [128, 4], F32)
        self.scr_f = self.xs_f[:, 0, 0:128]
        self.scr_f2 = self.xs_f[:, 0, 128:256]
        self.PF = [self.es.enter_context(nc.psum_tensor("pf%d" % i, [128, 512], F32)) for i in range(6)]
        self.PB = [self.es.enter_context(nc.psum_tensor("pb%d" % i, [128, 1024], BF16)) for i in range(2)]
        self.ws = WStream(self.P, self.ring, self.NSLOT, self.ws_future)
        self.dummy = self.sb("abar", [128, 2], F32)
        self.ARENA_NAMES = {"hT", "qT", "kT", "va", "pT", "qraw", "qsq", "qrs", "osb", "obf", "fsm", "vones",
                            "BA", "BETA", "GC", "EG", "EGL", "EKD", "raw", "acc", "halo", "sT", "sq", "zs", "R", "SC", "Sf", "Sb",
                            "kd", "kbg", "vb", "E", "MA", "Lb", "QQ", "TM", "DD", "u", "wT", "vn", "o", "og", "psm"}

    def arena_barrier(self):
        self.P.arena_names = self.ARENA_NAMES
        dummy = self.dummy
        self.P.add("pool", lambda e: e.memset(dummy[:], 0.0), writes=["ARENA"])

    def setup_consts(self):
        P = self.P
        nc = self.nc
        identf, identb = self.identf, self.identb
        P.add("pool", lambda e: e.memset(identf[:], 0.0), writes=["identf"])
        P.add("pool", lambda e: e.memset(self.epsc[:], EPS), writes=["epsc"])
        P.add("pool", lambda e: e.affine_select(out=identf[:], in_=identf[:], pattern=[[-1, 128]],
                                                compare_op=ALU.not_equal, fill=1.0, base=0,
                                                channel_multiplier=1),
              reads=["identf"], writes=["identf"])
        P.add("dve", lambda e: e.tensor_copy(out=identb[:], in_=identf[:]), reads=["identf"], writes=["identb"])
        crow, gcol = self.crow, self.gcol
        mixr = self.d["mix_norm"].rearrange("l (kc p) -> (l kc) p", p=128)
        mlpr = self.d["mlp_norm"].rearrange("l (kc p) -> (l kc) p", p=128)
        P.add("sp", lambda e: e.dma_start(out=crow[0:32, :], in_=mixr), writes=[("xs", 1)], dma="c0")
        P.add("sp", lambda e: e.dma_start(out=crow[32:64, :], in_=mlpr), writes=[("xs", 1)], dma="c1")
        pf = self.PF[0]
        P.add("pe", lambda e: e.transpose(out=pf[:, 0:64], in_=crow[0:64, :], identity=identf[0:64, 0:64]),
              reads=[("xs", 1), "identf"], writes=[("pf", 0)])
        P.add("dve", lambda e: e.tensor_copy(out=gcol[:, :], in_=pf[:, 0:64]), reads=[("pf", 0)], writes=["gcol"])

    def setup_da_consts(self, j, l):
        P = self.P
        d = self.d
        cda, lamt, lamp = self.cda, self.lamt, self.lamp
        c0 = 5 * j
        col = lambda ap: ap.rearrange("(p o) -> p o", o=1)
        for half in range(2):
            P.add("sp", lambda e, half=half: e.dma_start(out=cda[64 * half:64 * half + 64, c0:c0 + 1], in_=col(d["da_q_norm"][j])),
                  writes=[("cda", j)], dma="cq%d%d" % (j, half))
            P.add("sp", lambda e, half=half: e.dma_start(out=cda[64 * half:64 * half + 64, c0 + 1:c0 + 2], in_=col(d["da_k_norm"][j])),
                  writes=[("cda", j)], dma="ck%d%d" % (j, half))
        P.add("sp", lambda e: e.dma_start(out=cda[:, c0 + 4:c0 + 5], in_=col(d["da_sub_norm"][j])),
              writes=[("cda", j)], dma="cs%d" % j)
        for i, nm in enumerate(["da_lambda_q1", "da_lambda_k1", "da_lambda_q2", "da_lambda_k2"]):
            P.add("sp", lambda e, i=i, nm=nm: e.dma_start(out=lamt[:, i, :], in_=d[nm][j:j + 1, :].partition_broadcast(128)),
                  writes=[("xs", 0)], dma="cl%d%d" % (j, i))
        lam_init = 0.8 - 0.6 * math.exp(-0.3 * l)
        for i in range(2):
            P.add("dve", lambda e, i=i: e.tensor_tensor(out=lamt[:, 2 * i, :], in0=lamt[:, 2 * i, :], in1=lamt[:, 2 * i + 1, :], op=ALU.mult),
                  reads=[("xs", 0)], writes=[("xs", 0)])
            P.add("dve", lambda e, i=i: e.reduce_sum(out=lamp[:, i:i + 1], in_=lamt[:, 2 * i, :], axis=AX.X),
                  reads=[("xs", 0)], writes=["lamp"])
        P.add("act", lambda e: e.activation(out=lamp[:, 0:2], in_=lamp[:, 0:2], func=AF.Exp), reads=["lamp"], writes=["lamp"])
        P.add("dve", lambda e: e.tensor_tensor(out=cda[:, c0 + 2:c0 + 3], in0=lamp[:, 0:1], in1=lamp[:, 1:2], op=ALU.subtract),
              reads=["lamp"], writes=[("cda", j)])
        P.add("dve", lambda e: e.tensor_scalar(out=cda[:, c0 + 2:c0 + 3], in0=cda[:, c0 + 2:c0 + 3], scalar1=lam_init, scalar2=None, op0=ALU.add),
              reads=[("cda", j)], writes=[("cda", j)])
        P.add("dve", lambda e: e.tensor_scalar(out=cda[:, c0 + 3:c0 + 4], in0=cda[:, c0 + 2:c0 + 3], scalar1=-1.0, scalar2=None, op0=ALU.mult),
              reads=[("cda", j)], writes=[("cda", j)])
        P.add("dve", lambda e: e.tensor_scalar(out=cda[:, c0:c0 + 1], in0=cda[:, c0:c0 + 1], scalar1=0.125, scalar2=None, op0=ALU.mult),
              reads=[("cda", j)], writes=[("cda", j)])
        P.add("dve", lambda e: e.tensor_scalar(out=cda[:, c0 + 4:c0 + 5], in0=cda[:, c0 + 4:c0 + 5], scalar1=1.0 - lam_init, scalar2=None, op0=ALU.mult),
              reads=[("cda", j)], writes=[("cda", j)])

    def setup_da_static(self):
        P = self.P
        onesbd, negmask, vaug = self.onesbd, self.negmask, self.vaug
        scr = self.scr_f
        P.add("pool", lambda e: e.memset(scr[:], 0.0), writes=[("xs", 0)])
        P.add("pool", lambda e: e.memset(scr[0:64, 0:64], 1.0 / 64), reads=[("xs", 0)], writes=[("xs", 0)])
        P.add("pool", lambda e: e.memset(scr[64:128, 64:128], 1.0 / 64), reads=[("xs", 0)], writes=[("xs", 0)])
        P.add("dve", lambda e: e.tensor_copy(out=onesbd[:], in_=scr[:]), reads=[("xs", 0)], writes=["onesbd"])
        scr2 = self.scr_f2
        P.add("pool", lambda e: e.memset(scr2[:], 0.0), writes=[("xs", 0)])
        P.add("pool", lambda e: e.affine_select(out=scr2[:], in_=scr2[:], pattern=[[1, 128]], compare_op=ALU.is_ge,
                                                fill=-30000.0, base=0, channel_multiplier=-1),
              reads=[("xs", 0)], writes=[("xs", 0)])
        P.add("dve", lambda e: e.tensor_copy(out=negmask[:], in_=scr2[:]), reads=[("xs", 0)], writes=["negmask"])

    def diffattn(self, l):
        P = self.P
        j = l // 2
        X, xnT, U, ring = self.X, self.xnT, self.U, self.ring
        qT, kT, vaug, pT = self.qT, self.kT, self.vaug, self.pT
        qraw, qsq, qrs = self.qraw, self.qsq, self.qrs
        cda, onesbd, negmask, identb = self.cda, self.onesbd, self.negmask, self.identb
        osb, obf, fsm = self.osb, self.obf, self.fsm
        PF, PB = self.PF, self.PB
        c0 = 5 * j
        win = self.d["da_w_in"][j].rearrange("(kc p) f -> p kc f", p=128)
        self.rmsnorm(8 * l)
        self.arena_barrier()
        P.add("pool", lambda e: e.memset(vaug[:, :, :, 128:130], 1.0), reads=[("xnT", 0, 0)], writes=["vones"])
        cnt = {"pf": 0, "nb": 0, "pt": 0, "fin": 0}
        for hp in range(4):
            sq = self.ws.get(win[:, :, hp * 256:(hp + 1) * 256], 256)
            sk = self.ws.get(win[:, :, 1024 + hp * 256:1024 + (hp + 1) * 256], 256)
            sv = self.ws.get(win[:, :, 2048 + hp * 256:2048 + (hp + 1) * 256], 256)
            for which, slab, dst, gcolq in (("q", sq, qT, c0), ("k", sk, kT, c0 + 1)):
                slot, wkey, _ = slab
                for hh in range(2):
                    for tt in range(4):
                        bank = cnt["pf"] % 2
                        cnt["pf"] += 1
                        pf = PF[bank]
                        nb = cnt["nb"] % 2
                        cnt["nb"] += 1
                        for kc in range(8):
                            P.add("pe", lambda e, pf=pf, slot=slot, kc=kc, hh=hh, tt=tt: e.matmul(
                                pf[:, :], ring[:, slot, kc, hh * 128:(hh + 1) * 128], xnT[:, kc, tt * 512:(tt + 1) * 512],
                                start=(kc == 0), stop=(kc == 7)),
                                reads=[wkey] + [("xnT", t, kc) for t in range(4 * tt, 4 * tt + 4)], writes=[("pf", bank)])
                        P.add("act", lambda e, pf=pf, nb=nb: e.activation(out=qraw[:, nb, :], in_=pf[:, :], func=AF.Copy),
                              reads=[("pf", bank)], writes=[("qraw", nb)])
                        P.add("dve", lambda e, nb=nb: e.tensor_tensor(out=qsq[:, nb, :], in0=qraw[:, nb, :], in1=qraw[:, nb, :], op=ALU.mult),
                              reads=[("qraw", nb)], writes=[("qsq", nb)])
                        mbank = 2 + nb
                        pm = PF[mbank]
                        P.add("pe", lambda e, pm=pm, nb=nb: e.matmul(pm[:, :], onesbd[:, :], qsq[:, nb, :], start=True, stop=True),
                              reads=[("qsq", nb), "onesbd"], writes=[("pf", mbank)])
                        P.add("act", lambda e, pm=pm, nb=nb: e.activation(out=qrs[:, 0, :], in_=pm[:, :], func=AF.Ln, bias=self.epsc[:, 0:1], scale=1.0),
                              reads=[("pf", mbank), "epsc"], writes=[("qrs", 0)])
                        P.add("act", lambda e, nb=nb: e.activation(out=qrs[:, 0, :], in_=qrs[:, 0, :], func=AF.Exp, scale=-0.5),
                              reads=[("qrs", 0)], writes=[("qrs", 0)])
                        P.add("dve", lambda e, nb=nb, dst=dst, hh=hh, tt=tt, gcolq=gcolq: e.scalar_tensor_tensor(
                            out=dst[:, hh, tt * 512:(tt + 1) * 512], in0=qraw[:, nb, :], scalar=cda[:, gcolq:gcolq + 1],
                            in1=qrs[:, 0, :], op0=ALU.mult, op1=ALU.mult),
                            reads=[("qraw", nb), ("qrs", 0), ("cda", j)], writes=[(which + "T", hh, tt)])
            slot, wkey, _ = sv
            for t in range(NT):
                bank = cnt["pf"] % 2
                cnt["pf"] += 1
                pf = PF[bank]
                for kc in range(8):
                    P.add("pe", lambda e, pf=pf, slot=slot, kc=kc, t=t: e.matmul(
                        pf[:, 0:256], xnT[:, kc, t * 128:(t + 1) * 128], ring[:, slot, kc, 0:256],
                        start=(kc == 0), stop=(kc == 7)),
                        reads=[wkey, ("xnT", t, kc)], writes=[("pf", bank)])
                P.add("act", lambda e, pf=pf, t=t: e.activation(
                    out=vaug[:, t, :, 0:128], in_=pf[:, 0:256].rearrange("p (h d) -> p h d", h=2), func=AF.Copy),
                    reads=[("pf", bank), "vones"], writes=[("va", t)])
            for sl in (sq, sk, sv):
                self.ws.done(sl[2])
            for hh in range(2):
                h = 2 * hp + hh
                for qt in range(4):
                    first_in_bank = {}
                    for c in range(2):
                        pl, ph = 64 * c, 64 * c + 64
                        for kb in range(4 * qt + 4):
                            r = kb - 4 * qt
                            col0 = max(r, 0) * 128
                            bank = cnt["pf"] % 2
                            cnt["pf"] += 1
                            ps = PF[bank]
                            krd = [("kT", hh, kb // 4)]
                            qrd = [("qT", hh, qt)]
                            if r >= 0:
                                P.add("pe", lambda e, ps=ps, pl=pl, ph=ph, hh=hh, kb=kb, qt=qt, col0=col0: e.matmul(
                                    ps[:, col0:col0 + 128], kT[pl:ph, hh, kb * 128:(kb + 1) * 128],
                                    qT[pl:ph, hh, qt * 512 + col0:qt * 512 + col0 + 128], start=True, stop=False,
                                    skip_group_check=True),
                                    reads=krd + qrd, writes=[("pf", bank)])
                                P.add("pe", lambda e, ps=ps, col0=col0: e.matmul(
                                    ps[:, col0:col0 + 128], identb[:, :], negmask[:, :], start=False, stop=True,
                                    skip_group_check=True),
                                    reads=["identb", "negmask"], writes=[("pf", bank)])
                                if col0 + 128 < 512:
                                    P.add("pe", lambda e, ps=ps, pl=pl, ph=ph, hh=hh, kb=kb, qt=qt, col0=col0: e.matmul(
                                        ps[:, col0 + 128:512], kT[pl:ph, hh, kb * 128:(kb + 1) * 128],
                                        qT[pl:ph, hh, qt * 512 + col0 + 128:qt * 512 + 512], start=True, stop=True,
                                        skip_group_check=True),
                                        reads=krd + qrd, writes=[("pf", bank)])
                            else:
                                P.add("pe", lambda e, ps=ps, pl=pl, ph=ph, hh=hh, kb=kb, qt=qt: e.matmul(
                                    ps[:, :], kT[pl:ph, hh, kb * 128:(kb + 1) * 128],
                                    qT[pl:ph, hh, qt * 512:qt * 512 + 512], start=True, stop=True, skip_group_check=True),
                                    reads=krd + qrd, writes=[("pf", bank)])
                            pi = cnt["pt"] % 3
                            cnt["pt"] += 1
                            P.add("act", lambda e, ps=ps, pi=pi, col0=col0: e.activation(
                                out=pT[:, pi, col0:512], in_=ps[:, col0:512], func=AF.Exp),
                                reads=[("pf", bank)], writes=[("pT", pi)])
                            for rr in range(max(r, 0), 4):
                                abank = 2 + 2 * c + rr // 2
                                off = (rr % 2) * 256
                                st = abank not in first_in_bank
                                first_in_bank[abank] = True
                                P.add("pe", lambda e, abank=abank, off=off, pi=pi, rr=rr, kb=kb, hh=hh, st=st, qt=qt: e.matmul(
                                    PF[abank][:, off:off + 129], pT[:, pi, rr * 128:(rr + 1) * 128], vaug[:, kb, hh, 0:129],
                                    start=st, stop=(kb == 4 * qt + rr), skip_group_check=True),
                                    reads=[("pT", pi), ("va", kb), "vones"], writes=[("pf", abank)])
                    for rr in range(4):
                        fi = cnt["fin"] % 2
                        cnt["fin"] += 1
                        b0, b1 = 2 + rr // 2, 4 + rr // 2
                        off = (rr % 2) * 256
                        t = 4 * qt + rr
                        a0, a1 = PF[b0], PF[b1]
                        P.add("dve", lambda e, a0=a0, off=off, fi=fi: e.reciprocal(out=fsm[:, fi, 0:1], in_=a0[:, off + 128:off + 129]),
                              reads=[("pf", b0)], writes=[("fsm", fi)])
                        P.add("dve", lambda e, a1=a1, off=off, fi=fi: e.reciprocal(out=fsm[:, fi, 1:2], in_=a1[:, off + 128:off + 129]),
                              reads=[("pf", b1), ("fsm", fi)], writes=[("fsm", fi)])
                        P.add("dve", lambda e, fi=fi: e.tensor_tensor(out=fsm[:, fi, 2:3], in0=fsm[:, fi, 1:2], in1=cda[:, c0 + 3:c0 + 4], op=ALU.mult),
                              reads=[("fsm", fi), ("cda", j)], writes=[("fsm", fi)])
                        P.add("act", lambda e, a0=a0, off=off, fi=fi: e.activation(out=osb[:, fi, :], in_=a0[:, off:off + 128], func=AF.Copy,
                                                                                  scale=fsm[:, fi, 0:1]),
                              reads=[("pf", b0), ("fsm", fi)], writes=[("osb", fi)])
                        P.add("dve", lambda e, a1=a1, off=off, fi=fi: e.scalar_tensor_tensor(
                            out=osb[:, fi, :], in0=a1[:, off:off + 128], scalar=fsm[:, fi, 2:3], in1=osb[:, fi, :],
                            op0=ALU.mult, op1=ALU.add),
                            reads=[("pf", b1), ("fsm", fi), ("osb", fi)], writes=[("osb", fi)])
                        P.add("act", lambda e, fi=fi: e.activation(out=obf[:, fi, :], in_=osb[:, fi, :], func=AF.Square,
                                                                   accum_out=fsm[:, fi, 3:4]),
                              reads=[("osb", fi), ("fsm", fi)], writes=[("obf", fi), ("fsm", fi)])
                        P.add("act", lambda e, fi=fi: e.activation(out=fsm[:, fi, 4:5], in_=fsm[:, fi, 3:4], func=AF.Ln,
                                                                   bias=self.epsc[:, 0:1], scale=1.0 / 128),
                              reads=[("fsm", fi), "epsc"], writes=[("fsm", fi)])
                        P.add("act", lambda e, fi=fi: e.activation(out=fsm[:, fi, 4:5], in_=fsm[:, fi, 4:5], func=AF.Exp, scale=-0.5),
                              reads=[("fsm", fi)], writes=[("fsm", fi)])
                        P.add("act", lambda e, fi=fi: e.activation(out=obf[:, fi, :], in_=osb[:, fi, :], func=AF.Copy, scale=fsm[:, fi, 4:5]),
                              reads=[("osb", fi), ("fsm", fi)], writes=[("obf", fi)])
                        pb = PB[fi]
                        P.add("pe", lambda e, pb=pb, fi=fi: e.transpose(out=pb[:, 0:128], in_=obf[:, fi, :], identity=identb[:]),
                              reads=[("obf", fi), "identb"], writes=[("pb", fi)])
                        P.add("dve", lambda e, pb=pb, hh=hh, t=t: e.tensor_scalar(
                            out=U[:, hh, t * 128:(t + 1) * 128], in0=pb[:, 0:128], scalar1=cda[:, c0 + 4:c0 + 5], scalar2=None, op0=ALU.mult),
                            reads=[("pb", fi), ("cda", j)], writes=[("U", hh, t // 4)])
            wsrc = self.d["da_w_out"][j][hp * 256:(hp + 1) * 256, :].rearrange("(kc p) f -> p kc f", p=128)
            so = self.ws.get(wsrc, (2, 1024))
            wv = self.ws.view(so[0], (2, 1024))
            for t in range(NT):
                for dh in range(2):
                    bank = cnt["pf"] % 2
                    cnt["pf"] += 1
                    pf = PF[bank]
                    for hc in range(2):
                        P.add("pe", lambda e, pf=pf, wv=wv, hc=hc, t=t, dh=dh: e.matmul(
                            pf[:, :], U[:, hc, t * 128:(t + 1) * 128], wv[:, hc, dh * 512:(dh + 1) * 512], start=(hc == 0), stop=(hc == 1)),
                            reads=[so[1], ("U", hc, t // 4)], writes=[("pf", bank)])
                    P.add("dve", lambda e, pf=pf, t=t, dh=dh: e.tensor_tensor(
                        out=X[:, t, dh * 512:(dh + 1) * 512], in0=X[:, t, dh * 512:(dh + 1) * 512], in1=pf[:, :], op=ALU.add),
                        reads=[("pf", bank), ("x", t)], writes=[("x", t)])
            self.ws.done(so[2])

    def setup_gd_static(self):
        P = self.P
        trif, sellast, onesf, masks, onecol = self.trif, self.sellast, self.onesf, self.masks, self.onecol
        P.add("pool", lambda e: e.memset(onesf[:], 1.0), writes=["onesf"])
        P.add("pool", lambda e: e.memset(onecol[:], 1.0), writes=["onecol"])
        P.add("pool", lambda e: e.memset(trif[:], 1.0), writes=["trif"])
        P.add("pool", lambda e: e.affine_select(out=trif[:], in_=trif[:], pattern=[[1, 128]], compare_op=ALU.is_ge,
                                                fill=0.0, base=0, channel_multiplier=-1), reads=["trif"], writes=["trif"])
        P.add("pool", lambda e: e.memset(sellast[:], 1.0), writes=["sellast"])
        P.add("pool", lambda e: e.affine_select(out=sellast[:], in_=sellast[:], pattern=[[0, 128]], compare_op=ALU.is_ge,
                                                fill=0.0, base=-127, channel_multiplier=1), reads=["sellast"], writes=["sellast"])
        P.add("pool", lambda e: e.memset(masks[:], 0.0), writes=["masks"])
        P.add("pool", lambda e: e.affine_select(out=masks[:, 0:128], in_=masks[:, 0:128], pattern=[[1, 128]], compare_op=ALU.is_ge,
                                                fill=-30000.0, base=-1, channel_multiplier=-1), reads=["masks"], writes=["masks"])
        P.add("pool", lambda e: e.affine_select(out=masks[:, 128:256], in_=masks[:, 128:256], pattern=[[1, 128]], compare_op=ALU.is_ge,
                                                fill=-30000.0, base=0, channel_multiplier=-1), reads=["masks"], writes=["masks"])

    def setup_gd_levelmasks(self):
        P = self.P
        NEGM = self.NEGM
        scrA = lambda nb: self.xs_f[0:nb, 0, 0:128]
        scrC = lambda nb: self.xs_f[0:nb, 0, 128:256]
        K0 = [("xs", 0)]
        for k in range(7):
            B, half = 2 ** (k + 1), 2 ** k
            nb = 128 // B
            A, C = scrA(nb), scrC(nb)
            P.add("pool", lambda e, nb=nb: e.memset(self.xs_f[0:nb, 0, 0:256], 1.0), writes=K0)
            P.add("pool", lambda e, A=A, B=B, half=half: e.affine_select(out=A, in_=A, pattern=[[1, 128]], compare_op=ALU.is_ge, fill=0.0,
                                                                       base=-half, channel_multiplier=-B), reads=K0, writes=K0)
            P.add("pool", lambda e, A=A, B=B: e.affine_select(out=A, in_=A, pattern=[[-1, 128]], compare_op=ALU.is_ge, fill=0.0,
                                                             base=B - 1, channel_multiplier=B), reads=K0, writes=K0)
            P.add("pool", lambda e, C=C, B=B: e.affine_select(out=C, in_=C, pattern=[[1, 128]], compare_op=ALU.is_ge, fill=0.0,
                                                             base=0, channel_multiplier=-B), reads=K0, writes=K0)
            P.add("pool", lambda e, C=C, B=B, half=half: e.affine_select(out=C, in_=C, pattern=[[-1, 128]], compare_op=ALU.is_ge, fill=0.0,
                                                                       base=half - 1, channel_multiplier=B), reads=K0, writes=K0)
            pf = self.PF[1]
            P.add("pe", lambda e, pf=pf, A=A, C=C: e.matmul(pf[:, 0:128], A, C, start=True, stop=True, skip_group_check=True),
                  reads=K0, writes=[("pf", 1)])
            P.add("pe", lambda e, pf=pf, A=A, C=C: e.matmul(pf[:, 128:256], C, A, start=True, stop=True, skip_group_check=True),
                  reads=K0, writes=[("pf", 1)])
            P.add("act", lambda e, pf=pf, k=k: e.activation(out=NEGM[:, k, :], in_=pf[:, 0:256], func=AF.Copy),
                  reads=[("pf", 1)], writes=["NEGM"])

    def setup_gd_consts(self, j):
        P = self.P
        d = self.d
        convw, gdc, identf = self.convw, self.gdc, self.identf
        crow96 = self.xs_f[0:96, 1, 0:128]
        rows = d["gd_conv_w"][j].rearrange("k (c p) -> (k c) p", p=128)
        P.add("sp", lambda e: e.dma_start(out=crow96, in_=rows), writes=[("xs", 1)], dma="gcw%d" % j)
        pf = self.PF[0]
        P.add("pe", lambda e: e.transpose(out=pf[:, 0:96], in_=crow96, identity=identf[0:96, 0:96]),
              reads=[("xs", 1), "identf"], writes=[("pf", 0)])
        P.add("dve", lambda e: e.tensor_copy(out=convw[:, j, :], in_=pf[:, 0:96]), reads=[("pf", 0)], writes=[("convw", j)])
        col = lambda ap: ap.rearrange("(p o) -> p o", o=1)
        P.add("sp", lambda e: e.dma_start(out=gdc[:, j, 0:1], in_=col(d["gd_out_norm"][j])), writes=[("gdc", j)], dma="gon%d" % j)
        gsm = self.gsm
        P.add("sp", lambda e: e.dma_start(out=gsm[:, j, 0:8], in_=d["gd_a_log"][j:j + 1, :].partition_broadcast(128)),
              writes=[("gsm", j)], dma="gal%d" % j)
        P.add("sp", lambda e: e.dma_start(out=gsm[:, j, 8:16], in_=d["gd_dt_bias"][j:j + 1, :].partition_broadcast(128)),
              writes=[("gsm", j)], dma="gdt%d" % j)
        P.add("act", lambda e: e.activation(out=gsm[:, j, 0:8], in_=gsm[:, j, 0:8], func=AF.Exp), reads=[("gsm", j)], writes=[("gsm", j)])
        P.add("dve", lambda e: e.tensor_scalar(out=gsm[:, j, 0:8], in0=gsm[:, j, 0:8], scalar1=-1.0, scalar2=None, op0=ALU.mult),
              reads=[("gsm", j)], writes=[("gsm", j)])

    def gdn(self, l):
        P = self.P
        j = l // 2
        g = self.g
        X, xnT, U, ring = self.X, self.xnT, self.U, self.ring
        identb, identf = self.identb, self.identf
        PF, PB = self.PF, self.PB
        epsc = self.epsc
        DKS = 128.0 ** -0.5
        win = self.d["gd_w_in"][j].rearrange("(kc p) f -> p kc f", p=128)
        self.rmsnorm(8 * l)
        self.arena_barrier()
        XN0 = [("xnT", 0, 0)]
        cnt = {"pf": 0, "rb": 0, "sq": 0}
        BA, BETA, GC, EG, GLB, EGL, EKD = (g[k] for k in ("BA", "BETA", "GC", "EG", "GLB", "EGL", "EKD"))
        flat = lambda v: v.rearrange("p a b -> p (a b)")

        sba = self.ws.get(win[:, :, 4096:4112], 16)
        slot_ba, wkey_ba, _ = sba
        pf_ba = PF[0]
        for t in range(NT):
            for kc in range(8):
                P.add("pe", lambda e, t=t, kc=kc: e.matmul(pf_ba[:, t * 16:(t + 1) * 16], xnT[:, kc, t * 128:(t + 1) * 128],
                                                          ring[:, slot_ba, kc, 0:16], start=(kc == 0), stop=(kc == 7),
                                                          skip_group_check=True),
                      reads=[wkey_ba, ("xnT", t, kc)], writes=[("pf", 0)])
        self.ws.done(sba[2])
        P.add("dve", lambda e: e.tensor_copy(out=flat(BA), in_=pf_ba[:, 0:256]), reads=[("pf", 0)] + XN0, writes=["BA"])
        P.add("act", lambda e: e.activation(out=BETA, in_=BA[:, :, 0:8], func=AF.Exp, scale=-1.0), reads=["BA"] + XN0, writes=["BETA"])
        P.add("act", lambda e: e.activation(out=BETA, in_=BETA, func=AF.Ln, bias=1.0), reads=["BETA"], writes=["BETA"])
        P.add("act", lambda e: e.activation(out=BETA, in_=BETA, func=AF.Exp, scale=-1.0), reads=["BETA"], writes=["BETA"])
        gsm = self.gsm
        for t in range(NT):
            P.add("dve", lambda e, t=t: e.tensor_tensor(out=GC[:, t, :], in0=BA[:, t, 8:16], in1=gsm[:, j, 8:16], op=ALU.add),
                  reads=["BA", ("gsm", j)] + XN0, writes=["GC"])
        P.add("act", lambda e: e.activation(out=GC, in_=GC, func=AF.Exp), reads=["GC"], writes=["GC"])
        P.add("act", lambda e: e.activation(out=GC, in_=GC, func=AF.Ln, bias=1.0), reads=["GC"], writes=["GC"])
        for t in range(NT):
            P.add("dve", lambda e, t=t: e.tensor_tensor(out=GLB[:, t, :], in0=GC[:, t, :], in1=gsm[:, j, 0:8], op=ALU.mult),
                  reads=["GC", "BETA", ("gsm", j)] + XN0, writes=["BA"])
        pf1 = PF[1]
        for t in range(NT):
            P.add("pe", lambda e, t=t: e.matmul(pf1[:, t * 8:(t + 1) * 8], self.trif[:, :], GLB[:, t, :], start=True, stop=True,
                                                skip_group_check=True),
                  reads=["BA", "trif"], writes=[("pf", 1)])
        P.add("dve", lambda e: e.tensor_copy(out=flat(GC), in_=pf1[:, 0:128]), reads=[("pf", 1)], writes=["GC"])
        P.add("act", lambda e: e.activation(out=EG, in_=GC, func=AF.Exp), reads=["GC"] + XN0, writes=["EG"])
        for t in range(NT):
            P.add("pe", lambda e, t=t: e.matmul(pf1[:, 128 + t * 8:128 + (t + 1) * 8], self.sellast[:, :], GC[:, t, :], start=True, stop=True,
                                                skip_group_check=True),
                  reads=["GC", "sellast"], writes=[("pf", 1)])
        P.add("dve", lambda e: e.tensor_copy(out=flat(GLB), in_=pf1[:, 128:256]), reads=[("pf", 1)], writes=["BA"])
        P.add("act", lambda e: e.activation(out=EGL, in_=GLB, func=AF.Exp), reads=["BA"] + XN0, writes=["EGL"])
        P.add("dve", lambda e: e.tensor_tensor(out=EKD, in0=GLB, in1=GC, op=ALU.subtract), reads=["BA", "GC"] + XN0, writes=["EKD"])
        P.add("act", lambda e: e.activation(out=EKD, in_=EKD, func=AF.Exp), reads=["EKD"], writes=["EKD"])

        raw, acc, halo, sq, Sf, Sb = (g[k] for k in ("raw", "acc", "halo", "sq", "Sf", "Sb"))
        convw = self.convw
        NEGM = self.NEGM
        for hp in range(4):
            slabs = {}
            for wi, which in enumerate(("q", "k", "v", "z")):
                slabs[which] = self.ws.get(win[:, :, wi * 1024 + hp * 256: wi * 1024 + (hp + 1) * 256], 256)
            P.add("pool", lambda e: e.memset(flat(Sf), 0.0), reads=XN0, writes=[("Sf", 0), ("Sf", 1)])
            P.add("pool", lambda e: e.memset(flat(Sb), 0.0), reads=XN0, writes=[("Sb", 0), ("Sb", 1)])
            P.add("pool", lambda e: e.memset(flat(halo), 0.0), reads=XN0, writes=[("halo", i) for i in range(6)])
            FE, PREP, SCAN = {}, {}, {}
            for gi in range(4):
                gp = gi % 2
                sT, zs, R, SC = g["sT"][gp], g["zs"][gp], g["R"][gp], g["SC"][gp]
                P.begin_capture()
                for wi, which in enumerate(("k", "q", "v")):
                    slot, wkey, _ = slabs[which]
                    cbase = {"q": 0, "k": 8, "v": 16}[which]
                    for hh in range(2):
                        h = 2 * hp + hh
                        pf = PF[0]
                        rb = cnt["rb"] % 2
                        cnt["rb"] += 1
                        hi = wi * 2 + hh
                        for kc in range(8):
                            P.add("pe", lambda e, pf=pf, slot=slot, kc=kc, hh=hh, gi=gi: e.matmul(
                                pf[:, :], ring[:, slot, kc, hh * 128:(hh + 1) * 128], xnT[:, kc, gi * 512:(gi + 1) * 512],
                                start=(kc == 0), stop=(kc == 7)),
                                reads=[wkey] + [("xnT", t, kc) for t in range(4 * gi, 4 * gi + 4)], writes=[("pf", 0)])
                        P.add("dve", lambda e, rb=rb, hi=hi: e.tensor_copy(out=raw[:, rb, 0:3], in_=halo[:, hi, 0:3]),
                              reads=[("halo", hi)], writes=[("raw", rb)])
                        P.add("act", lambda e, pf=pf, rb=rb: e.activation(out=raw[:, rb, 3:515], in_=pf[:, :], func=AF.Copy),
                              reads=[("pf", 0), ("raw", rb)], writes=[("raw", rb)])
                        if gi < 3:
                            P.add("dve", lambda e, rb=rb, hi=hi: e.tensor_copy(out=halo[:, hi, 0:3], in_=raw[:, rb, 512:515]),
                                  reads=[("raw", rb)], writes=[("halo", hi)])
                        cc = cbase + h
                        P.add("dve", lambda e, rb=rb, cc=cc: e.tensor_scalar(
                            out=acc[:, 0, :], in0=raw[:, rb, 0:512], scalar1=convw[:, j, cc:cc + 1], scalar2=None, op0=ALU.mult),
                            reads=[("raw", rb), ("convw", j)], writes=[("acc", 0)])
                        for tap in range(1, 4):
                            P.add("dve", lambda e, rb=rb, cc=cc, tap=tap: e.scalar_tensor_tensor(
                                out=acc[:, 0, :], in0=raw[:, rb, tap:tap + 512], scalar=convw[:, j, tap * 24 + cc:tap * 24 + cc + 1],
                                in1=acc[:, 0, :], op0=ALU.mult, op1=ALU.add),
                                reads=[("raw", rb), ("acc", 0), ("convw", j)], writes=[("acc", 0)])
                        sgb = raw[:, rb, 0:512]
                        P.add("act", lambda e, sgb=sgb: e.activation(out=sgb, in_=acc[:, 0, :], func=AF.Exp, scale=-1.0),
                              reads=[("acc", 0), ("raw", rb), ("halo", hi)], writes=[("raw", rb)])
                        P.add("act", lambda e, sgb=sgb: e.activation(out=sgb, in_=sgb, func=AF.Ln, bias=1.0), reads=[("raw", rb)], writes=[("raw", rb)])
                        P.add("act", lambda e, sgb=sgb: e.activation(out=sgb, in_=sgb, func=AF.Exp, scale=-1.0), reads=[("raw", rb)], writes=[("raw", rb)])
                        P.add("dve", lambda e, sgb=sgb, hh=hh, wi=wi, sT=sT: e.tensor_tensor(
                            out=sT[:, hh, :, wi * 128:(wi + 1) * 128], in0=acc[:, 0, :].rearrange("p (a b) -> p a b", a=4),
                            in1=sgb.rearrange("p (a b) -> p a b", a=4), op=ALU.mult),
                            reads=[("acc", 0), ("raw", rb)], writes=[("sT", gp, hh, wi)])
                        if which in ("k", "q"):
                            sb_ = cnt["sq"] % 2
                            cnt["sq"] += 1
                            P.add("pool", lambda e, hh=hh, wi=wi, sb_=sb_, sT=sT: e.tensor_tensor(
                                out=sq[:, sb_, :].rearrange("p (a b) -> p a b", a=4), in0=sT[:, hh, :, wi * 128:(wi + 1) * 128],
                                in1=sT[:, hh, :, wi * 128:(wi + 1) * 128], op=ALU.mult),
                                reads=[("sT", gp, hh, wi)], writes=[("sq", sb_)])
                            for tl in range(4):
                                colr = tl * 4 + wi * 2 + hh
                                P.add("pe", lambda e, sb_=sb_, tl=tl, colr=colr: e.matmul(
                                    PF[1][:, 384 + colr:384 + colr + 1], sq[:, sb_, tl * 128:(tl + 1) * 128], self.onecol[:, 0:1],
                                    start=True, stop=True, skip_group_check=True),
                                    reads=[("sq", sb_), "onecol"], writes=[("pf", 1)])
                slot, wkey, _ = slabs["z"]
                for tl in range(4):
                    t = 4 * gi + tl
                    pf = PF[1]
                    for kc in range(8):
                        P.add("pe", lambda e, pf=pf, slot=slot, kc=kc, t=t: e.matmul(
                            pf[:, 0:256], xnT[:, kc, t * 128:(t + 1) * 128], ring[:, slot, kc, 0:256], start=(kc == 0), stop=(kc == 7),
                            skip_group_check=True),
                            reads=[wkey, ("xnT", t, kc)], writes=[("pf", 1)])
                    zt = acc[:, 0, 0:256]
                    zr = acc[:, 0, 256:512]
                    P.add("act", lambda e, pf=pf, zt=zt: e.activation(out=zt, in_=pf[:, 0:256], func=AF.Exp, scale=-1.0),
                          reads=[("pf", 1)], writes=[("acc", 0)])
                    P.add("act", lambda e, pf=pf, zr=zr: e.activation(out=zr, in_=pf[:, 0:256], func=AF.Copy),
                          reads=[("pf", 1), ("acc", 0)], writes=[("acc", 0)])
                    P.add("act", lambda e, zt=zt: e.activation(out=zt, in_=zt, func=AF.Ln, bias=1.0), reads=[("acc", 0)], writes=[("acc", 0)])
                    P.add("act", lambda e, zt=zt: e.activation(out=zt, in_=zt, func=AF.Exp, scale=-1.0), reads=[("acc", 0)], writes=[("acc", 0)])
                    P.add("dve", lambda e, zt=zt, zr=zr, tl=tl, zs=zs: e.tensor_tensor(out=zs[:, tl, :], in0=zr, in1=zt, op=ALU.mult),
                          reads=[("acc", 0)], writes=[("zs", gp, tl)])
                Rk = ("R", gp)
                P.add("act", lambda e, R=R: e.activation(out=flat(R), in_=PF[1][:, 384:400], func=AF.Ln, bias=epsc[:, 0:1], scale=1.0),
                      reads=[("pf", 1), "epsc"], writes=[Rk])
                P.add("act", lambda e, R=R: e.activation(out=flat(R), in_=flat(R), func=AF.Exp, scale=-0.5), reads=[Rk], writes=[Rk])
                hs = slice(2 * hp, 2 * hp + 2)
                ts = slice(4 * gi, 4 * gi + 4)
                rk, rq = R[:, :, 0:2], R[:, :, 2:4]
                scv = lambda q_, SC=SC: SC[:, q_, :].rearrange("p (a b) -> p a b", a=4)
                T1, CKBG, CKD, CQ, UL, UA, BIAS, LN = (scv(i) for i in range(8))
                bt, egs, ekds, gcs = BETA[:, ts, hs], EG[:, ts, hs], EKD[:, ts, hs], GC[:, ts, hs]
                sk = lambda i: ("SC", gp, i)
                P.add("dve", lambda e, T1=T1, rk=rk, bt=bt: e.tensor_tensor(out=T1, in0=rk, in1=bt, op=ALU.mult), reads=[Rk, "BETA"], writes=[sk(0)])
                P.add("dve", lambda e, CKBG=CKBG, T1=T1, egs=egs: e.tensor_tensor(out=CKBG, in0=T1, in1=egs, op=ALU.mult), reads=[sk(0), "EG"], writes=[sk(1)])
                P.add("dve", lambda e, CKD=CKD, rk=rk, ekds=ekds: e.tensor_tensor(out=CKD, in0=rk, in1=ekds, op=ALU.mult), reads=[Rk, "EKD"], writes=[sk(2)])
                P.add("dve", lambda e, CQ=CQ, rq=rq, egs=egs: e.scalar_tensor_tensor(out=CQ, in0=rq, scalar=DKS, in1=egs, op0=ALU.mult, op1=ALU.mult),
                      reads=[Rk, "EG"], writes=[sk(3)])
                P.add("act", lambda e, UL=UL, T1=T1: e.activation(out=UL, in_=T1, func=AF.Ln), reads=[sk(0)], writes=[sk(4)])
                P.add("dve", lambda e, UL=UL, gcs=gcs: e.tensor_tensor(out=UL, in0=UL, in1=gcs, op=ALU.add), reads=[sk(4), "GC"], writes=[sk(4)])
                P.add("act", lambda e, UA=UA, rq=rq: e.activation(out=UA, in_=rq, func=AF.Ln, scale=DKS), reads=[Rk], writes=[sk(5)])
                P.add("dve", lambda e, UA=UA, gcs=gcs: e.tensor_tensor(out=UA, in0=UA, in1=gcs, op=ALU.add), reads=[sk(5), "GC"], writes=[sk(5)])
                P.add("act", lambda e, BIAS=BIAS, rk=rk: e.activation(out=BIAS, in_=rk, func=AF.Ln), reads=[Rk], writes=[sk(6)])
                P.add("dve", lambda e, BIAS=BIAS, gcs=gcs: e.tensor_tensor(out=BIAS, in0=BIAS, in1=gcs, op=ALU.subtract), reads=[sk(6), "GC"], writes=[sk(6)])
                FE[gi] = P.end_capture()
                SCK = [sk(i) for i in range(7)]
                for tl in range(4):
                    t = 4 * gi + tl
                    P.begin_capture()
                    for c in range(2):
                        hh = c
                        cs = 2 * (t % 2) + c
                        pbk, pbo = cs // 2, (cs % 2) * 512
                        sc1 = lambda q_, tl=tl, hh=hh, SC=SC: SC[:, q_, tl * 2 + hh:tl * 2 + hh + 1]
                        pb, pc = PB[pbk], PF[2 + cs]
                        kd, kbg, vb, E, MA = (g[k, cs] for k in ("kd", "kbg", "vb", "E", "MA"))
                        ksT = sT[:, hh, tl, 0:128]
                        vsT = sT[:, hh, tl, 256:384]
                        P.add("pe", lambda e, pb=pb, ksT=ksT, pbo=pbo: e.transpose(out=pb[:, pbo:pbo + 128], in_=ksT, identity=identb[:]),
                              reads=[("sT", gp, hh, 0), "identb"], writes=[("pb", pbk)])
                        P.add("pe", lambda e, pb=pb, vsT=vsT, pbo=pbo: e.transpose(out=pb[:, pbo + 128:pbo + 256], in_=vsT, identity=identb[:]),
                              reads=[("sT", gp, hh, 2), "identb"], writes=[("pb", pbk)])
                        P.add("act", lambda e, pb=pb, kd=kd, sc1=sc1, pbo=pbo: e.activation(out=kd, in_=pb[:, pbo:pbo + 128], func=AF.Copy, scale=sc1(2)),
                              reads=[("pb", pbk)] + SCK, writes=[("kd", cs)])
                        P.add("dve", lambda e, pb=pb, kbg=kbg, sc1=sc1, pbo=pbo: e.tensor_scalar(out=kbg, in0=pb[:, pbo:pbo + 128], scalar1=sc1(1), scalar2=None, op0=ALU.mult),
                              reads=[("pb", pbk)] + SCK, writes=[("kbg", cs)])
                        hcol = 2 * hp + hh
                        P.add("dve", lambda e, pb=pb, vb=vb, t=t, hcol=hcol, pbo=pbo: e.tensor_scalar(
                            out=vb, in0=pb[:, pbo + 128:pbo + 256], scalar1=BETA[:, t, hcol:hcol + 1], scalar2=None, op0=ALU.mult),
                            reads=[("pb", pbk), "BETA"], writes=[("vb", cs)])
                        P.add("pe", lambda e, pc=pc, ksT=ksT, hh=hh, tl=tl, sT=sT: e.matmul(pc[:, 0:256], ksT, sT[:, hh, tl, 0:256], start=True, stop=True,
                                                                                              skip_group_check=True),
                              reads=[("sT", gp, hh, 0), ("sT", gp, hh, 1)], writes=[("pf", 2 + cs)])
                        P.add("act", lambda e, E=E, sc1=sc1: e.activation(out=E[:, 0:128], in_=identf[:, :], func=AF.Copy, scale=sc1(4)),
                              reads=["identf"] + SCK, writes=[("E", cs)])
                        P.add("act", lambda e, E=E, sc1=sc1: e.activation(out=E[:, 128:256], in_=identf[:, :], func=AF.Copy, scale=sc1(5)),
                              reads=["identf", ("E", cs)] + SCK, writes=[("E", cs)])
                        P.add("pe", lambda e, pc=pc, E=E: e.matmul(pc[:, 256:512], self.onesf[:, :], E[:, :], start=True, stop=False, skip_group_check=True),
                              reads=[("E", cs), "onesf"], writes=[("pf", 2 + cs)])
                        P.add("pe", lambda e, pc=pc: e.matmul(pc[:, 256:512], identf[:, :], self.masks[:, :], start=False, stop=True, skip_group_check=True),
                              reads=["identf", "masks"], writes=[("pf", 2 + cs)])
                        P.add("act", lambda e, pc=pc, E=E, sc1=sc1: e.activation(out=E[:, :], in_=pc[:, 256:512], func=AF.Exp, bias=sc1(6)),
                              reads=[("pf", 2 + cs)] + SCK, writes=[("E", cs)])
                        P.add("dve", lambda e, pc=pc, E=E, MA=MA: e.tensor_tensor(out=MA[:, :], in0=pc[:, 0:256], in1=E[:, :], op=ALU.mult),
                              reads=[("pf", 2 + cs), ("E", cs)], writes=[("MA", cs)])
                        Lb, DD, TM = g["Lb", cs], g["DD", cs], g["TM", cs]
                        P.add("pe", lambda e, pb=pb, MA=MA, pbo=pbo: e.transpose(out=pb[:, pbo + 256:pbo + 384], in_=MA[:, 0:128], identity=identb[:]),
                              reads=[("MA", cs), "identb"], writes=[("pb", pbk)])
                        P.add("act", lambda e, pb=pb, Lb=Lb, pbo=pbo: e.activation(out=Lb, in_=pb[:, pbo + 256:pbo + 384], func=AF.Copy),
                              reads=[("pb", pbk)], writes=[("Lb", cs)])
                        P.add("dve", lambda e, TM=TM, Lb=Lb: e.tensor_tensor(out=TM[:, 0:128], in0=Lb, in1=NEGM[:, 0, 0:128], op=ALU.mult),
                              reads=[("Lb", cs), "NEGM"], writes=[("TM", cs)])
                        P.add("dve", lambda e, TM=TM, MA=MA: e.tensor_tensor(out=TM[:, 128:256], in0=MA[:, 0:128], in1=NEGM[:, 0, 128:256], op=ALU.mult),
                              reads=[("MA", cs), "NEGM", ("TM", cs)], writes=[("TM", cs)])
                        P.add("dve", lambda e, TM=TM, DD=DD: e.tensor_tensor(out=DD[:, 0, 0:128], in0=identb[:, :], in1=TM[:, 0:128], op=ALU.subtract),
                              reads=[("TM", cs), "identb"], writes=[("DD", cs, 0)])
                        P.add("dve", lambda e, TM=TM, DD=DD: e.tensor_tensor(out=DD[:, 0, 128:256], in0=identb[:, :], in1=TM[:, 128:256], op=ALU.subtract),
                              reads=[("TM", cs), "identb", ("DD", cs, 0)], writes=[("DD", cs, 0)])
                    for lev in range(1, 7):
                        for c in range(2):
                            cs = 2 * (t % 2) + c
                            pc = PF[2 + cs]
                            MA, Lb, DD, QQ, TM = (g[k_, cs] for k_ in ("MA", "Lb", "DD", "QQ", "TM"))
                            pi, po = (lev - 1) % 2, lev % 2
                            P.add("pe", lambda e, pc=pc, MA=MA, DD=DD, pi=pi: e.matmul(pc[:, 0:128], MA[:, 0:128], DD[:, pi, 0:128], start=True, stop=True,
                                                                                        skip_group_check=True),
                                  reads=[("MA", cs), ("DD", cs, pi)], writes=[("pf", 2 + cs)])
                            P.add("pe", lambda e, pc=pc, Lb=Lb, DD=DD, pi=pi: e.matmul(pc[:, 128:256], Lb, DD[:, pi, 128:256], start=True, stop=True,
                                                                                        skip_group_check=True),
                                  reads=[("Lb", cs), ("DD", cs, pi)], writes=[("pf", 2 + cs)])
                            P.add("dve", lambda e, pc=pc, QQ=QQ, lev=lev: e.tensor_tensor(out=QQ[:, :], in0=pc[:, 0:256], in1=NEGM[:, lev, :], op=ALU.mult),
                                  reads=[("pf", 2 + cs), "NEGM"], writes=[("QQ", cs)])
                            P.add("pe", lambda e, pc=pc, QQ=QQ, DD=DD, pi=pi: e.matmul(pc[:, 256:384], DD[:, pi, 128:256], QQ[:, 0:128], start=True, stop=True,
                                                                                        skip_group_check=True),
                                  reads=[("QQ", cs), ("DD", cs, pi)], writes=[("pf", 2 + cs)])
                            P.add("pe", lambda e, pc=pc, QQ=QQ, DD=DD, pi=pi: e.matmul(pc[:, 384:512], DD[:, pi, 0:128], QQ[:, 128:256], start=True, stop=True,
                                                                                        skip_group_check=True),
                                  reads=[("QQ", cs), ("DD", cs, pi)], writes=[("pf", 2 + cs)])
                            P.add("dve", lambda e, pc=pc, DD=DD, pi=pi, po=po: e.tensor_tensor(out=DD[:, po, :], in0=DD[:, pi, :], in1=pc[:, 256:512], op=ALU.subtract),
                                  reads=[("pf", 2 + cs), ("DD", cs, pi)], writes=[("DD", cs, po)])
                    for c in range(2):
                        cs = 2 * (t % 2) + c
                        pc = PF[2 + cs]
                        kbg, vb, DD, u, wT = (g[k, cs] for k in ("kbg", "vb", "DD", "u", "wT"))
                        P.add("pe", lambda e, pc=pc, DD=DD, vb=vb: e.matmul(pc[:, 0:128], DD[:, 0, 128:256], vb, start=True, stop=True, skip_group_check=True),
                              reads=[("DD", cs, 0), ("vb", cs)], writes=[("pf", 2 + cs)])
                        P.add("pe", lambda e, pc=pc, DD=DD, kbg=kbg: e.matmul(pc[:, 128:256], kbg, DD[:, 0, 128:256], start=True, stop=True, skip_group_check=True),
                              reads=[("DD", cs, 0), ("kbg", cs)], writes=[("pf", 2 + cs)])
                        P.add("act", lambda e, pc=pc, u=u: e.activation(out=u, in_=pc[:, 0:128], func=AF.Copy), reads=[("pf", 2 + cs)], writes=[("u", cs)])
                        P.add("dve", lambda e, pc=pc, wT=wT: e.tensor_copy(out=wT, in_=pc[:, 128:256]), reads=[("pf", 2 + cs)], writes=[("wT", cs)])
                    PREP[t] = P.end_capture()
                    P.begin_capture()
                    for c in range(2):
                        hh = c
                        h = 2 * hp + hh
                        cs = 2 * (t % 2) + c
                        pbk, pbo = cs // 2, (cs % 2) * 512
                        ps_, pb = PF[2 + cs], PB[pbk]
                        kd, MA, u, wT, vn, o, og, psm = (g[k, cs] for k in ("kd", "MA", "u", "wT", "vn", "o", "og", "ps"))
                        sc1 = lambda q_, tl=tl, hh=hh, SC=SC: SC[:, q_, tl * 2 + hh:tl * 2 + hh + 1]
                        qsT = sT[:, hh, tl, 128:256]
                        PK = ("pf", 2 + cs)
                        P.add("pe", lambda e, ps_=ps_, wT=wT, hh=hh: e.matmul(ps_[:, 0:128], wT, Sb[:, hh, :], start=True, stop=True, skip_group_check=True),
                              reads=[("wT", cs), ("Sb", hh)], writes=[PK])
                        P.add("pe", lambda e, ps_=ps_, qsT=qsT, hh=hh: e.matmul(ps_[:, 128:256], qsT, Sb[:, hh, :], start=True, stop=True, skip_group_check=True),
                              reads=[("sT", gp, hh, 1), ("Sb", hh)], writes=[PK])
                        P.add("dve", lambda e, ps_=ps_, u=u, vn=vn: e.tensor_tensor(out=vn, in0=u, in1=ps_[:, 0:128], op=ALU.subtract),
                              reads=[PK, ("u", cs)], writes=[("vn", cs)])
                        P.add("pe", lambda e, ps_=ps_, MA=MA, vn=vn: e.matmul(ps_[:, 256:384], MA[:, 128:256], vn, start=True, stop=True, skip_group_check=True),
                              reads=[("MA", cs), ("vn", cs)], writes=[PK])
                        P.add("pe", lambda e, ps_=ps_, kd=kd, vn=vn: e.matmul(ps_[:, 384:512], kd, vn, start=True, stop=True, skip_group_check=True),
                              reads=[("kd", cs), ("vn", cs)], writes=[PK])
                        P.add("act", lambda e, ps_=ps_, o=o: e.activation(out=o, in_=ps_[:, 256:384], func=AF.Copy), reads=[PK], writes=[("o", cs)])
                        P.add("dve", lambda e, ps_=ps_, o=o, sc1=sc1: e.scalar_tensor_tensor(out=o, in0=ps_[:, 128:256], scalar=sc1(3), in1=o, op0=ALU.mult, op1=ALU.add),
                              reads=[PK, ("o", cs)] + SCK, writes=[("o", cs)])
                        P.add("dve", lambda e, ps_=ps_, hh=hh, t=t, h=h: e.scalar_tensor_tensor(
                            out=Sf[:, hh, :], in0=Sf[:, hh, :], scalar=EGL[:, t, h:h + 1], in1=ps_[:, 384:512], op0=ALU.mult, op1=ALU.add),
                            reads=[PK, ("Sf", hh), "EGL"], writes=[("Sf", hh)])
                        P.add("act", lambda e, hh=hh: e.activation(out=Sb[:, hh, :], in_=Sf[:, hh, :], func=AF.Copy), reads=[("Sf", hh)], writes=[("Sb", hh)])
                        P.add("act", lambda e, o=o, og=og, psm=psm: e.activation(out=og, in_=o, func=AF.Square, accum_out=psm[:, 0:1]),
                              reads=[("o", cs)], writes=[("og", cs), ("psm", cs)])
                        P.add("act", lambda e, psm=psm: e.activation(out=psm[:, 1:2], in_=psm[:, 0:1], func=AF.Ln, bias=epsc[:, 0:1], scale=1.0 / 128),
                              reads=[("psm", cs), "epsc"], writes=[("psm", cs)])
                        P.add("act", lambda e, psm=psm: e.activation(out=psm[:, 1:2], in_=psm[:, 1:2], func=AF.Exp, scale=-0.5), reads=[("psm", cs)], writes=[("psm", cs)])
                        P.add("dve", lambda e, o=o, og=og, psm=psm, tl=tl, hh=hh, zs=zs: e.scalar_tensor_tensor(
                            out=og, in0=o, scalar=psm[:, 1:2], in1=zs[:, tl, hh * 128:(hh + 1) * 128], op0=ALU.mult, op1=ALU.mult),
                            reads=[("o", cs), ("psm", cs), ("zs", gp, tl)], writes=[("og", cs)])
                        P.add("pe", lambda e, pb=pb, og=og, pbo=pbo: e.transpose(out=pb[:, pbo + 384:pbo + 512], in_=og, identity=identb[:]),
                              reads=[("og", cs), "identb"], writes=[("pb", pbk)])
                        P.add("act", lambda e, pb=pb, hh=hh, t=t, pbo=pbo: e.activation(out=U[:, hh, t * 128:(t + 1) * 128], in_=pb[:, pbo + 384:pbo + 512], func=AF.Copy,
                                                                                        scale=self.gdc[:, j, 0:1]),
                              reads=[("pb", pbk), ("gdc", j)], writes=[("U", hh, t // 4)])
                    SCAN[t] = P.end_capture()
            P.replay([FE[0]])
            fe_parts = {}
            for gi in range(1, 4):
                L = FE[gi]
                n = (len(L) + 2) // 3
                for k in range(3):
                    fe_parts[4 * (gi - 1) + 1 + k] = L[k * n:(k + 1) * n]
            for s_ in range(NT + 1):
                lists = []
                if s_ < NT:
                    lists.append(PREP[s_])
                if s_ >= 1:
                    lists.append(SCAN[s_ - 1])
                if s_ in fe_parts:
                    lists.append(fe_parts[s_])
                P.replay(lists)
            for which in ("q", "k", "v", "z"):
                self.ws.done(slabs[which][2])
            wsrc = self.d["gd_w_out"][j][hp * 256:(hp + 1) * 256, :].rearrange("(kc p) f -> p kc f", p=128)
            so = self.ws.get(wsrc, (2, 1024))
            wv = self.ws.view(so[0], (2, 1024))
            for t in range(NT):
                for dh in range(2):
                    bank = cnt["pf"] % 2
                    cnt["pf"] += 1
                    pf = PF[bank]
                    for hc in range(2):
                        P.add("pe", lambda e, pf=pf, wv=wv, hc=hc, t=t, dh=dh: e.matmul(
                            pf[:, :], U[:, hc, t * 128:(t + 1) * 128], wv[:, hc, dh * 512:(dh + 1) * 512], start=(hc == 0), stop=(hc == 1)),
                            reads=[so[1], ("U", hc, t // 4)], writes=[("pf", bank)])
                    P.add("dve", lambda e, pf=pf, t=t, dh=dh: e.tensor_tensor(
                        out=X[:, t, dh * 512:(dh + 1) * 512], in0=X[:, t, dh * 512:(dh + 1) * 512], in1=pf[:, :], op=ALU.add),
                        reads=[("pf", bank), ("x", t)], writes=[("x", t)])
            self.ws.done(so[2])

    def load_x(self, s):
        P = self.P
        X = self.X
        xv = self.d["x"][s].rearrange("(t p) d -> p t d", p=128)
        for q in range(4):
            P.add("sp", lambda e, q=q: e.dma_start(out=X[:, 4 * q:4 * q + 4, :], in_=xv[:, 4 * q:4 * q + 4, :]),
                  writes=[("x", t) for t in range(4 * q, 4 * q + 4)], dma=("xl", q))

    def store_x(self, s):
        P = self.P
        X = self.X
        ov = self.d["out"][s].rearrange("(t p) d -> p t d", p=128)
        ids = []
        for q in range(4):
            ids.append(P.add("sp", lambda e, q=q: e.dma_start(out=ov[:, 4 * q:4 * q + 4, :], in_=X[:, 4 * q:4 * q + 4, :]),
                             reads=[("x", t) for t in range(4 * q, 4 * q + 4)], writes=[("xst", q)], dma=("xs", q)))
        return ids

    def rmsnorm(self, gbase):
        P = self.P
        X, xs, ss, rstd, xnT = self.X, self.xs, self.ss, self.rstd, self.xnT
        identb, gcol = self.identb, self.gcol
        import os
        DBG = int(os.environ.get("K_DBG", "9"))
        for t in range(NT):
            P.add("act", lambda e, t=t: e.activation(out=xs[:, 1, :], in_=X[:, t, :], func=AF.Square,
                                                     accum_out=ss[:, t:t + 1]),
                  reads=[("x", t)], writes=[("xs", 1), ("ss", t)])
        P.add("act", lambda e: e.activation(out=rstd[:], in_=ss[:], func=AF.Ln, bias=self.epsc[:, 0:1], scale=1.0 / D),
              reads=[("ss", t) for t in range(NT)] + ["epsc"], writes=["rstd"])
        P.add("act", lambda e: e.activation(out=rstd[:], in_=rstd[:], func=AF.Exp, scale=-0.5), reads=["rstd"], writes=["rstd"])
        if DBG < 2:
            return
        for t in range(NT if DBG >= 6 else 1):
            b = t % 2
            pb = self.PB[b]
            P.add("act", lambda e, t=t, b=b: e.activation(out=xs[:, b, :], in_=X[:, t, :], func=AF.Copy,
                                                          scale=rstd[:, t:t + 1]),
                  reads=[("x", t), "rstd"], writes=[("xs", b)])
            if DBG < 4:
                continue
            for kc in range(8):
                P.add("pe", lambda e, b=b, kc=kc, pb=pb: e.transpose(out=pb[:, kc * 128:(kc + 1) * 128],
                                                                      in_=xs[:, b, kc * 128:(kc + 1) * 128],
                                                                      identity=identb[:]),
                      reads=[("xs", b), "identb"], writes=[("pb", b)])
            if DBG < 5:
                continue
            for kc in range(8):
                eng = "dve" if b == 0 else "act"
                if eng == "dve":
                    fn = lambda e, t=t, kc=kc, pb=pb: e.tensor_scalar(
                        out=xnT[:, kc, t * 128:(t + 1) * 128], in0=pb[:, kc * 128:(kc + 1) * 128],
                        scalar1=gcol[:, gbase + kc:gbase + kc + 1], scalar2=None, op0=ALU.mult)
                else:
                    fn = lambda e, t=t, kc=kc, pb=pb: e.activation(
                        out=xnT[:, kc, t * 128:(t + 1) * 128], in_=pb[:, kc * 128:(kc + 1) * 128],
                        func=AF.Copy, scale=gcol[:, gbase + kc:gbase + kc + 1])
                P.add(eng, fn, reads=[("pb", b), "gcol"], writes=[("xnT", t, kc)])

    def mlp(self, l):
        P = self.P
        X, xnT, U, ring = self.X, self.xnT, self.hT, self.ring
        w1 = self.d["mlp_w_in"][l].rearrange("(kc p) f -> p kc f", p=128)
        w2 = self.d["mlp_w_out"][l].rearrange("(fc p) d -> p fc d", p=128)
        self.rmsnorm(32 + 8 * l)
        self.arena_barrier()
        pfi = 0
        for fg in range(4):
            slabs = [self.ws.get(w1[:, :, fg * 1024 + s2 * 512: fg * 1024 + (s2 + 1) * 512], 512) for s2 in range(2)]
            for fc in range(8):
                slot, wkey, _ = slabs[fc // 4]
                off = (fc % 4) * 128
                for tt in range(4):
                    bank = pfi % 4
                    pfi += 1
                    pf = self.PF[bank]
                    for kc in range(8):
                        P.add("pe", lambda e, pf=pf, slot=slot, kc=kc, off=off, tt=tt: e.matmul(
                            pf[:, :], ring[:, slot, kc, off:off + 128], xnT[:, kc, tt * 512:(tt + 1) * 512],
                            start=(kc == 0), stop=(kc == 7)),
                            reads=[wkey] + [("xnT", t, kc) for t in range(4 * tt, 4 * tt + 4)],
                            writes=[("pf", bank)])
                    rb = pfi % 2
                    P.add("act", lambda e, pf=pf, rb=rb: e.activation(out=self.rtmp[:, rb, :], in_=pf[:, :], func=AF.Relu),
                          reads=[("pf", bank)], writes=[("xs", rb)])
                    P.add("dve", lambda e, fc=fc, tt=tt, rb=rb: e.tensor_tensor(
                        out=U[:, fc, tt * 512:(tt + 1) * 512], in0=self.rtmp[:, rb, :], in1=self.rtmp[:, rb, :],
                        op=ALU.mult),
                        reads=[("xs", rb)], writes=[("hT", fc, tt)])
            for sl in slabs:
                self.ws.done(sl[2])
            slabs2 = [self.ws.get(w2[:, fg * 8:(fg + 1) * 8, dh * 512:(dh + 1) * 512], 512) for dh in range(2)]
            for t in range(NT):
                for dh in range(2):
                    slot, wkey, _ = slabs2[dh]
                    bank = pfi % 4
                    pfi += 1
                    pf = self.PF[bank]
                    for fc in range(8):
                        P.add("pe", lambda e, pf=pf, slot=slot, fc=fc, t=t: e.matmul(
                            pf[:, :], U[:, fc, t * 128:(t + 1) * 128], ring[:, slot, fc, :],
                            start=(fc == 0), stop=(fc == 7)),
                            reads=[wkey, ("hT", fc, t // 4)], writes=[("pf", bank)])
                    P.add("dve", lambda e, pf=pf, t=t, dh=dh: e.tensor_tensor(
                        out=X[:, t, dh * 512:(dh + 1) * 512], in0=X[:, t, dh * 512:(dh + 1) * 512], in1=pf[:, :],
                        op=ALU.add),
                        reads=[("pf", bank), ("x", t)], writes=[("x", t)])
            for sl in slabs2:
                self.ws.done(sl[2])

    def build(self):
        P = self.P
        self.setup_consts()
        kinds = {k for k, _ in self.layers}
        if "gd" in kinds:
            self.setup_gd_static()
            self.setup_gd_levelmasks()
            for jj in sorted({l // 2 for (k, l) in self.layers if k == "gd"}):
                self.setup_gd_consts(jj)
        if "da" in kinds:
            self.setup_da_static()
            for (k, l) in self.layers:
                if k == "da":
                    self.setup_da_consts(l // 2, l)
        last_stores = []
        for s in range(self.n_seq):
            self.load_x(s)
            for l in self.layers:
                if l[0] == "mlp":
                    self.mlp(l[1])
                elif l[0] == "norm":
                    self.rmsnorm(32 + 8 * l[1])
                elif l[0] == "da":
                    self.diffattn(l[1])
                elif l[0] == "gd":
                    self.gdn(l[1])
            last_stores = self.store_x(s)
        P.add("sp", None, reads=[("xst", q) for q in range(4)])


def layer_plan():
    plan = []
    for i in range(DEPTH):
        plan.append(("da" if i % 2 == 0 else "gd", i))
        plan.append(("mlp", i))
    return plan


def build_program(n_seq=SEQ_PER_CORE, layers=None):
    if layers is None:
        layers = layer_plan()
    nc = bass.Bass("TRN2", target_bir_lowering=False)
    with ExitStack() as es:
        b = Builder(nc, Prog(nc, dry=True), None, n_seq, layers, es)
        b.build()
        future = list(b.ws.requests)
        P = Prog(nc, dry=False)
        b.P = P
        b.ws = WStream(P, b.ring, b.NSLOT, future)
        b.build()
        P.emit(es)
    return nc


WEIGHT_NAMES = ["mix_norm", "mlp_norm", "mlp_w_in", "mlp_w_out", "da_w_in", "da_q_norm", "da_k_norm",
                "da_lambda_q1", "da_lambda_k1", "da_lambda_q2", "da_lambda_k2", "da_sub_norm", "da_w_out",
                "gd_w_in", "gd_conv_w", "gd_a_log", "gd_dt_bias", "gd_out_norm", "gd_w_out"]


def run(inputs, n_seq=SEQ_PER_CORE, layers=None, ncores=NCORES, trace=False):
    nc = build_program(n_seq, layers)
    x = np.ascontiguousarray(np.asarray(inputs["x"], dtype=np.float32))
    weights = {k: np.ascontiguousarray(np.asarray(inputs[k], dtype=np.float32)) for k in WEIGHT_NAMES}
    in_maps = []
    for c in range(ncores):
        m = {"x": x[c * n_seq:(c + 1) * n_seq]}
        m.update(weights)
        in_maps.append(m)
    res = run_bass_kernel_spmd(nc, in_maps, core_ids=list(range(ncores)), trace=trace)
    out = np.concatenate([r["out"] for r in res.results], axis=0)
    return out, res


def kernel(**inputs):
    out, _ = run(inputs)
    return out
```

```python
import math
from contextlib import ExitStack

import numpy as np
import concourse.bass as bass
import concourse.mybir as mybir
from concourse.bass_utils import run_bass_kernel_spmd

F32 = mybir.dt.float32
BF16 = mybir.dt.bfloat16
AF = mybir.ActivationFunctionType
ALU = mybir.AluOpType
AX = mybir.AxisListType

D = 1024
S = 2048
NT = S // 128
DFF = 4096
DEPTH = 4
EPS = 1e-6
NCORES = 8
SEQ_PER_CORE = 4
GD_IN = 4 * 1024 + 16


class Op:
    __slots__ = ("id", "eng", "fn", "deps", "dma", "seq", "signal")


class Prog:
    ENGS = ("pe", "act", "dve", "pool", "sp")

    def __init__(self, nc, dry=False):
        self.nc = nc
        self.dry = dry
        self.ops = []
        self.by_eng = {e: [] for e in self.ENGS}
        self.lw = {}
        self.rd = {}
        self.dma_groups = {}
        self.group_all = set()
        self.psum_last = {}
        self.arena_names = set()
        self.cap = None

    def begin_capture(self):
        self.cap = []

    def end_capture(self):
        c, self.cap = self.cap, None
        return c

    def replay(self, lists):
        idx = [0] * len(lists)
        live = True
        while live:
            live = False
            for i, L in enumerate(lists):
                if idx[i] < len(L):
                    self.add(*L[idx[i]])
                    idx[i] += 1
                    live = True

    def add(self, eng, fn, reads=(), writes=(), dma=None):
        if self.dry:
            return None
        if self.cap is not None:
            self.cap.append((eng, fn, tuple(reads), tuple(writes), dma))
            return None
        op = Op()
        op.id = len(self.ops)
        op.eng = eng
        op.fn = fn
        op.dma = dma
        op.seq = 0
        op.signal = False
        if self.arena_names:
            for k in tuple(reads) + tuple(writes):
                nm = k[0] if isinstance(k, tuple) else k
                if nm in self.arena_names:
                    reads = tuple(reads) + ("ARENA",)
                    break
        deps = set()
        for k in reads:
            w = self.lw.get(k)
            if w is not None:
                deps.add(w)
        for k in writes:
            w = self.lw.get(k)
            if w is not None:
                deps.add(w)
            for r in self.rd.get(k, ()):
                deps.add(r)
        for k in reads:
            self.rd.setdefault(k, []).append(op.id)
        for k in writes:
            self.lw[k] = op.id
            self.rd[k] = []
        for k in tuple(reads) + tuple(writes):
            if isinstance(k, tuple) and k[0] in ("pf", "pb"):
                last = self.psum_last.setdefault(k, {})
                for eng2, oid in last.items():
                    if eng2 != eng:
                        deps.add(oid)
                last[eng] = op.id
        deps.discard(op.id)
        if eng == "pe" and dma is None:
            deps = {d for d in deps if not (self.ops[d].eng == "pe" and self.ops[d].dma is None)}
        op.deps = deps
        self.ops.append(op)
        self.by_eng[eng].append(op)
        if dma is not None:
            self.dma_groups.setdefault(dma, []).append(op.id)
        return op.id

    def emit(self, es):
        nc = self.nc
        ops = self.ops
        for op in ops:
            for d in op.deps:
                ops[d].signal = True
        for e in self.ENGS:
            c = 0
            for op in self.by_eng[e]:
                if op.dma is None and op.signal:
                    c += 1
                    op.seq = c
        for g, ids in self.dma_groups.items():
            for i, oid in enumerate(ids):
                ops[oid].seq = i + 1
        eng_sem = {e: es.enter_context(nc.semaphore("s_" + e)) for e in self.ENGS}
        dma_sem = {g: es.enter_context(nc.semaphore("d_" + str(g))) for g in self.dma_groups}
        block = es.enter_context(nc.Block())

        def emit_engine(ename, e):
            waited = {}
            for op in self.by_eng[ename]:
                need = {}
                for d in op.deps:
                    dop = ops[d]
                    if dop.dma is not None:
                        sem = dma_sem[dop.dma]
                        if dop.dma in self.group_all:
                            val = 16 * len(self.dma_groups[dop.dma])
                        else:
                            val = 16 * dop.seq
                    else:
                        sem = eng_sem[dop.eng]
                        val = dop.seq
                    key = id(sem)
                    if key not in need or need[key][1] < val:
                        need[key] = (sem, val)
                for key, (sem, val) in need.items():
                    if waited.get(key, 0) < val:
                        e.wait_ge(sem, val)
                        waited[key] = val
                if op.fn is None:
                    continue
                inst = op.fn(e)
                if op.dma is not None:
                    inst.then_inc(dma_sem[op.dma], 16)
                elif op.signal:
                    inst.then_inc(eng_sem[ename], 1)

        @block.tensor
        def _(e):
            emit_engine("pe", e)

        @block.scalar
        def _(e):
            emit_engine("act", e)

        @block.vector
        def _(e):
            emit_engine("dve", e)

        @block.gpsimd
        def _(e):
            emit_engine("pool", e)

        @block.sync
        def _(e):
            emit_engine("sp", e)


class WStream:
    def __init__(self, P, ring, nslot, lookahead_list=None):
        self.P = P
        self.ring = ring
        self.nslot = nslot
        self.future = lookahead_list
        self.requests = []
        self.issued = 0
        self.released = set()

    def _pump(self):
        if self.P.dry:
            return
        lim = min(len(self.future), len(self.requests) + self.nslot - 1)
        while self.issued < lim:
            j = self.issued
            if j >= self.nslot and (j - self.nslot) not in self.released:
                break
            src, w = self.future[j]
            slot = j % self.nslot
            ring = self.ring
            dst = self.view(slot, w)
            self.P.add("pool", lambda e, dst=dst, src=src: e.dma_start(out=dst, in_=src),
                       writes=[("w", slot)], dma=("w", slot))
            self.issued += 1

    def view(self, slot, w):
        ring = self.ring
        if isinstance(w, tuple):
            a, b = w
            return ring[:, slot, :, :].rearrange("p a b -> p (a b)")[:, 0:a * b].rearrange("p (a b) -> p a b", a=a)
        return ring[:, slot, :, 0:w]

    def get(self, src, w):
        i = len(self.requests)
        self.requests.append((src, w))
        if self.P.dry:
            return 0, ("w", 0), i
        self._pump()
        assert self.issued > i, "weight ring deadlock: release slabs before requesting more"
        slot = i % self.nslot
        return slot, ("w", slot), i

    def done(self, idx):
        self.released.add(idx)
        self._pump()


class Builder:
    def __init__(self, nc, P, ws_future, n_seq, layers, es):
        self.nc = nc
        self.P = P
        self.n_seq = n_seq
        self.layers = layers
        self.es = es
        self.ws_future = ws_future
        self.alloc()

    def sb(self, name, shape, dt):
        return self.es.enter_context(self.nc.sbuf_tensor(name, shape, dt))

    def carve(self, shape, dt):
        esz = 4 if dt == F32 else 2
        n = 1
        for s_ in shape:
            n *= s_
        nbytes = (n * esz + 3) // 4 * 4
        off = self._aoff
        assert off + nbytes <= self.ARENA_BYTES, "arena overflow"
        self._aoff = off + nbytes
        v = self.arena[:, off // 4:(off + nbytes) // 4]
        if dt != F32:
            v = v.bitcast(dt)
        v = v[:, 0:n]
        if len(shape) == 2:
            v = v.rearrange("p (a b) -> p a b", a=shape[0])
        elif len(shape) == 3:
            v = v.rearrange("p (a b c) -> p a b c", a=shape[0], b=shape[1])
        return v

    def alloc(self):
        nc = self.nc
        n_seq = self.n_seq
        d = {}
        d["x"] = nc.dram_tensor("x", [n_seq, S, D], F32, kind="ExternalInput").ap()
        d["out"] = nc.dram_tensor("out", [n_seq, S, D], F32, kind="ExternalOutput").ap()
        specs = [
            ("mix_norm", [4, D]), ("mlp_norm", [4, D]), ("mlp_w_in", [4, D, DFF]), ("mlp_w_out", [4, DFF, D]),
            ("da_w_in", [2, D, 3072]), ("da_q_norm", [2, 64]), ("da_k_norm", [2, 64]),
            ("da_lambda_q1", [2, 64]), ("da_lambda_k1", [2, 64]), ("da_lambda_q2", [2, 64]), ("da_lambda_k2", [2, 64]),
            ("da_sub_norm", [2, 128]), ("da_w_out", [2, D, D]),
            ("gd_w_in", [2, D, GD_IN]), ("gd_conv_w", [2, 4, 3072]), ("gd_a_log", [2, 8]), ("gd_dt_bias", [2, 8]),
            ("gd_out_norm", [2, 128]), ("gd_w_out", [2, D, D]),
        ]
        for name, shape in specs:
            d[name] = nc.dram_tensor(name, shape, F32, kind="ExternalInput").ap()
        self.d = d
        self.NSLOT = 4
        self.X = self.sb("X", [128, NT, D], F32)
        self.xnT = self.sb("xnT", [128, 8, S], BF16)
        self.U = self.sb("U", [128, 2, S], BF16)
        self.ring = self.sb("ring", [128, self.NSLOT, 8, 512], BF16)
        self.xs_f = self.sb("xs_f", [128, 2, 512], F32)
        self.xs = self.xs_f[:, :, :].rearrange("p a b -> p (a b)").bitcast(BF16).rearrange("p (a b) -> p a b", a=2)
        self.rtmp = self.xs_f
        self.ss = self.sb("ss", [128, NT], F32)
        self.rstd = self.sb("rstd", [128, NT], F32)
        self.epsc = self.sb("epsc", [128, 4], F32)
        self.identb = self.sb("identb", [128, 128], BF16)
        self.identf = self.sb("identf", [128, 128], F32)
        self.crow = self.xs_f[0:64, 1, 0:128]
        self.gcol = self.sb("gcol", [128, 64], F32)
        self.ARENA_BYTES = 35840 + 24576
        self.arena = self.sb("arena", [128, self.ARENA_BYTES // 4], F32)
        self._aoff = 0
        self.hT = self.carve([8, S], BF16)
        self._aoff = 0
        self.qT = self.carve([2, S], BF16)
        self.kTz = self.carve([2, 2, S], BF16)
        self.vaug = self.carve([NT, 2, 130], BF16)
        self.pT = self.carve([3, 512], BF16)
        self.qraw = self.carve([2, 512], BF16)
        self.qsq = self.carve([2, 512], BF16)
        self.qrs = self.carve([1, 512], F32)
        self.accS = self.carve([4, 512], F32)
        self.osb = self.carve([2, 4, 128], F32)
        self.obf = self.carve([2, 4, 128], BF16)
        self.fsm = self.carve([2, 24], F32)
        self.da_arena_end = self._aoff
        self._aoff = 0
        g = {}
        g["BA"] = self.carve([NT, 16], F32)
        for nm in ("BETA", "GC", "EG", "EGL", "EKD"):
            g[nm] = self.carve([NT, 8], F32)
        g["GLB"] = g["BA"][:, :, :].rearrange("p a b -> p (a b)")[:, 0:NT * 8].rearrange("p (a b) -> p a b", a=NT)
        g["raw"] = self.carve([2, 516], F32)
        g["acc"] = self.carve([1, 512], F32)
        g["halo"] = self.carve([6, 4], F32)
        g["sT"] = [self.carve([2, 4, 3 * 128], BF16) for _ in range(2)]
        g["sq"] = self.carve([2, 512], BF16)
        g["zs"] = [self.carve([4, 256], BF16) for _ in range(2)]
        g["R"] = [self.carve([4, 4], F32) for _ in range(2)]
        g["SC"] = [self.carve([8, 8], F32) for _ in range(2)]
        g["Sf"] = self.carve([2, 128], F32)
        g["Sb"] = self.carve([2, 128], BF16)
        self.NCH = 4
        for c in range(self.NCH):
            g["kd", c] = self.carve([128], BF16)
            g["kbg", c] = self.carve([128], BF16)
            g["vb", c] = self.carve([128], BF16)
            g["E", c] = self.carve([256], F32)
            g["MA", c] = self.carve([256], BF16)
            g["Lb", c] = self.carve([128], BF16)
            g["QQ", c] = self.carve([256], BF16)
            g["TM", c] = self.carve([256], BF16)
            g["DD", c] = self.carve([2, 256], BF16)
            g["u", c] = self.carve([128], F32)
            g["wT", c] = self.carve([128], BF16)
            g["vn", c] = self.carve([128], BF16)
            g["o", c] = self.carve([128], F32)
            g["og", c] = self.carve([128], BF16)
            g["ps", c] = self.carve([8], F32)
        self.g = g
        self.gd_arena_end = self._aoff
        assert max(self.da_arena_end, self.gd_arena_end) <= self.ARENA_BYTES
        self.convw = self.sb("convw", [128, 2, 96], F32)
        self.gdc = self.sb("gdc", [128, 2, 4], F32)
        self.gsm = self.sb("gsm", [128, 2, 16], F32)
        self.NEGM = self.sb("NEGM", [128, 7, 256], BF16)
        self.trif = self.sb("trif", [128, 128], F32)
        self.sellast = self.sb("sellast", [128, 128], F32)
        self.onesf = self.sb("onesf", [128, 128], F32)
        self.masks = self.sb("masks", [128, 256], F32)
        self.onecol = self.sb("onecol", [128, 2], BF16)
        self.onesbd = self.sb("onesbd", [128, 128], BF16)
        self.negmask = self.sb("negmask", [128, 128], BF16)
        self.cda = self.sb("cda", [128, 16], F32)
        self.lamt = self.xs_f[:, 0, 256:512].rearrange("p (a b) -> p a b", a=4)
        self.lamp = self.sb("lamp", [128, 4], F32)
        self.scr_f = self.xs_f[:, 0, 0:128]
        self.scr_f2 = self.xs_f[:, 0, 128:256]
        self.PF = [self.es.enter_context(nc.psum_tensor("pf%d" % i, [128, 512], F32)) for i in range(6)]
        self.PB = [self.es.enter_context(nc.psum_tensor("pb%d" % i, [128, 1024], BF16)) for i in range(2)]
        self.ws = WStream(self.P, self.ring, self.NSLOT, self.ws_future)
        self.dummy = self.sb("abar", [128, 2], F32)
        self.ARENA_NAMES = {"hT", "qT", "kT", "kTz", "accS", "va", "pT", "qraw", "qsq", "qrs", "osb", "obf", "fsm", "fss", "frs", "vones",
                            "BA", "BETA", "GC", "EG", "EGL", "EKD", "raw", "acc", "halo", "sT", "sq", "zs", "R", "SC", "Sf", "Sb",
                            "kd", "kbg", "vb", "E", "MA", "Lb", "QQ", "TM", "DD", "u", "wT", "vn", "o", "og", "psm"}

    def arena_barrier(self):
        self.P.arena_names = self.ARENA_NAMES
        dummy = self.dummy
        self.P.add("pool", lambda e: e.memset(dummy[:], 0.0), writes=["ARENA"])

    def setup_consts(self):
        P = self.P
        nc = self.nc
        identf, identb = self.identf, self.identb
        P.add("pool", lambda e: e.memset(identf[:], 0.0), writes=["identf"])
        P.add("pool", lambda e: e.memset(self.epsc[:], EPS), writes=["epsc"])
        P.add("pool", lambda e: e.affine_select(out=identf[:], in_=identf[:], pattern=[[-1, 128]],
                                                compare_op=ALU.not_equal, fill=1.0, base=0,
                                                channel_multiplier=1),
              reads=["identf"], writes=["identf"])
        P.add("dve", lambda e: e.tensor_copy(out=identb[:], in_=identf[:]), reads=["identf"], writes=["identb"])
        crow, gcol = self.crow, self.gcol
        mixr = self.d["mix_norm"].rearrange("l (kc p) -> (l kc) p", p=128)
        mlpr = self.d["mlp_norm"].rearrange("l (kc p) -> (l kc) p", p=128)
        P.add("sp", lambda e: e.dma_start(out=crow[0:32, :], in_=mixr), writes=[("xs", 1)], dma="c0")
        P.add("sp", lambda e: e.dma_start(out=crow[32:64, :], in_=mlpr), writes=[("xs", 1)], dma="c1")
        pf = self.PF[0]
        P.add("pe", lambda e: e.transpose(out=pf[:, 0:64], in_=crow[0:64, :], identity=identf[0:64, 0:64]),
              reads=[("xs", 1), "identf"], writes=[("pf", 0)])
        P.add("dve", lambda e: e.tensor_copy(out=gcol[:, :], in_=pf[:, 0:64]), reads=[("pf", 0)], writes=["gcol"])

    def setup_da_consts(self, j, l):
        P = self.P
        d = self.d
        cda, lamt, lamp = self.cda, self.lamt, self.lamp
        c0 = 5 * j
        col = lambda ap: ap.rearrange("(p o) -> p o", o=1)
        for half in range(2):
            P.add("sp", lambda e, half=half: e.dma_start(out=cda[64 * half:64 * half + 64, c0:c0 + 1], in_=col(d["da_q_norm"][j])),
                  writes=[("cda", j)], dma="cq%d%d" % (j, half))
            P.add("sp", lambda e, half=half: e.dma_start(out=cda[64 * half:64 * half + 64, c0 + 1:c0 + 2], in_=col(d["da_k_norm"][j])),
                  writes=[("cda", j)], dma="ck%d%d" % (j, half))
        P.add("sp", lambda e: e.dma_start(out=cda[:, c0 + 4:c0 + 5], in_=col(d["da_sub_norm"][j])),
              writes=[("cda", j)], dma="cs%d" % j)
        for i, nm in enumerate(["da_lambda_q1", "da_lambda_k1", "da_lambda_q2", "da_lambda_k2"]):
            P.add("sp", lambda e, i=i, nm=nm: e.dma_start(out=lamt[:, i, :], in_=d[nm][j:j + 1, :].partition_broadcast(128)),
                  writes=[("xs", 0)], dma="cl%d%d" % (j, i))
        lam_init = 0.8 - 0.6 * math.exp(-0.3 * l)
        for i in range(2):
            P.add("dve", lambda e, i=i: e.tensor_tensor(out=lamt[:, 2 * i, :], in0=lamt[:, 2 * i, :], in1=lamt[:, 2 * i + 1, :], op=ALU.mult),
                  reads=[("xs", 0)], writes=[("xs", 0)])
            P.add("dve", lambda e, i=i: e.reduce_sum(out=lamp[:, i:i + 1], in_=lamt[:, 2 * i, :], axis=AX.X),
                  reads=[("xs", 0)], writes=["lamp"])
        P.add("act", lambda e: e.activation(out=lamp[:, 0:2], in_=lamp[:, 0:2], func=AF.Exp), reads=["lamp"], writes=["lamp"])
        P.add("dve", lambda e: e.tensor_tensor(out=cda[:, c0 + 2:c0 + 3], in0=lamp[:, 0:1], in1=lamp[:, 1:2], op=ALU.subtract),
              reads=["lamp"], writes=[("cda", j)])
        P.add("dve", lambda e: e.tensor_scalar(out=cda[:, c0 + 2:c0 + 3], in0=cda[:, c0 + 2:c0 + 3], scalar1=lam_init, scalar2=None, op0=ALU.add),
              reads=[("cda", j)], writes=[("cda", j)])
        P.add("dve", lambda e: e.tensor_scalar(out=cda[:, c0 + 3:c0 + 4], in0=cda[:, c0 + 2:c0 + 3], scalar1=-1.0, scalar2=None, op0=ALU.mult),
              reads=[("cda", j)], writes=[("cda", j)])
        P.add("dve", lambda e: e.tensor_scalar(out=cda[:, c0:c0 + 1], in0=cda[:, c0:c0 + 1], scalar1=0.125, scalar2=None, op0=ALU.mult),
              reads=[("cda", j)], writes=[("cda", j)])
        P.add("dve", lambda e: e.tensor_scalar(out=cda[:, c0 + 4:c0 + 5], in0=cda[:, c0 + 4:c0 + 5], scalar1=1.0 - lam_init, scalar2=None, op0=ALU.mult),
              reads=[("cda", j)], writes=[("cda", j)])

    def setup_da_static(self):
        P = self.P
        onesbd, negmask, vaug = self.onesbd, self.negmask, self.vaug
        scr = self.scr_f
        P.add("pool", lambda e: e.memset(scr[:], 0.0), writes=[("xs", 0)])
        P.add("pool", lambda e: e.memset(scr[0:64, 0:64], 1.0 / 64), reads=[("xs", 0)], writes=[("xs", 0)])
        P.add("pool", lambda e: e.memset(scr[64:128, 64:128], 1.0 / 64), reads=[("xs", 0)], writes=[("xs", 0)])
        P.add("dve", lambda e: e.tensor_copy(out=onesbd[:], in_=scr[:]), reads=[("xs", 0)], writes=["onesbd"])
        scr2 = self.scr_f2
        P.add("pool", lambda e: e.memset(scr2[:], 0.0), writes=[("xs", 0)])
        P.add("pool", lambda e: e.affine_select(out=scr2[:], in_=scr2[:], pattern=[[1, 128]], compare_op=ALU.is_ge,
                                                fill=-30000.0, base=0, channel_multiplier=-1),
              reads=[("xs", 0)], writes=[("xs", 0)])
        P.add("dve", lambda e: e.tensor_copy(out=negmask[:], in_=scr2[:]), reads=[("xs", 0)], writes=["negmask"])

    def diffattn(self, l):
        P = self.P
        j = l // 2
        X, xnT, U, ring = self.X, self.xnT, self.U, self.ring
        qT, kTz, vaug, pT, accS = self.qT, self.kTz, self.vaug, self.pT, self.accS
        qraw, qsq, qrs = self.qraw, self.qsq, self.qrs
        cda, onesbd, negmask, identb = self.cda, self.onesbd, self.negmask, self.identb
        osb, obf, fsm = self.osb, self.obf, self.fsm
        PF, PB = self.PF, self.PB
        c0 = 5 * j
        win = self.d["da_w_in"][j].rearrange("(kc p) f -> p kc f", p=128)
        self.rmsnorm(8 * l)
        self.arena_barrier()
        P.add("pool", lambda e: e.memset(vaug[:, :, :, 128:130], 1.0), reads=[("xnT", 0, 0)], writes=["vones"])
        P.add("pool", lambda e: e.memset(kTz[64:128, 0, :, :], 0.0), reads=[("xnT", 0, 0)], writes=["kTz"])
        P.add("pool", lambda e: e.memset(kTz[0:64, 1, :, :], 0.0), reads=[("xnT", 0, 0)], writes=["kTz"])
        cnt = {"pf": 0, "nb": 0, "pt": 0, "fin": 0}
        for hp in range(4):
            sq = self.ws.get(win[:, :, hp * 256:(hp + 1) * 256], 256)
            sk = self.ws.get(win[:, :, 1024 + hp * 256:1024 + (hp + 1) * 256], 256)
            sv = self.ws.get(win[:, :, 2048 + hp * 256:2048 + (hp + 1) * 256], 256)
            jobs = [(which, slab, gcolq, hh, tt) for which, slab, gcolq in (("q", sq, c0), ("k", sk, c0 + 1))
                    for hh in range(2) for tt in range(4)]
            jstate = {}

            def emit_proj(i):
                which, slab, gcolq, hh, tt = jobs[i]
                slot, wkey, _ = slab
                bank = cnt["pf"] % 2
                cnt["pf"] += 1
                nb = cnt["nb"] % 2
                cnt["nb"] += 1
                jstate[i] = (bank, nb)
                pf = PF[bank]
                for kc in range(8):
                    P.add("pe", lambda e, pf=pf, slot=slot, kc=kc, hh=hh, tt=tt: e.matmul(
                        pf[:, :], ring[:, slot, kc, hh * 128:(hh + 1) * 128], xnT[:, kc, tt * 512:(tt + 1) * 512],
                        start=(kc == 0), stop=(kc == 7)),
                        reads=[wkey] + [("xnT", t, kc) for t in range(4 * tt, 4 * tt + 4)], writes=[("pf", bank)])

            def emit_norm(i):
                which, slab, gcolq, hh, tt = jobs[i]
                bank, nb = jstate[i]
                pf = PF[bank]
                P.add("act", lambda e, pf=pf, nb=nb: e.activation(out=qraw[:, nb, :], in_=pf[:, :], func=AF.Copy),
                      reads=[("pf", bank)], writes=[("qraw", nb)])
                P.add("dve", lambda e, nb=nb: e.tensor_tensor(out=qsq[:, nb, :], in0=qraw[:, nb, :], in1=qraw[:, nb, :], op=ALU.mult),
                      reads=[("qraw", nb)], writes=[("qsq", nb)])
                mbank = 2 + nb
                pm = PF[mbank]
                P.add("pe", lambda e, pm=pm, nb=nb: e.matmul(pm[:, :], onesbd[:, :], qsq[:, nb, :], start=True, stop=True),
                      reads=[("qsq", nb), "onesbd"], writes=[("pf", mbank)])
                P.add("act", lambda e, pm=pm, nb=nb: e.activation(out=qrs[:, 0, :], in_=pm[:, :], func=AF.Ln, bias=self.epsc[:, 0:1], scale=1.0),
                      reads=[("pf", mbank), "epsc"], writes=[("qrs", 0)])
                P.add("act", lambda e, nb=nb: e.activation(out=qrs[:, 0, :], in_=qrs[:, 0, :], func=AF.Exp, scale=-0.5),
                      reads=[("qrs", 0)], writes=[("qrs", 0)])
                if which == "q":
                    P.add("dve", lambda e, nb=nb, hh=hh, tt=tt, gcolq=gcolq: e.scalar_tensor_tensor(
                        out=qT[:, hh, tt * 512:(tt + 1) * 512], in0=qraw[:, nb, :], scalar=cda[:, gcolq:gcolq + 1],
                        in1=qrs[:, 0, :], op0=ALU.mult, op1=ALU.mult),
                        reads=[("qraw", nb), ("qrs", 0), ("cda", j)], writes=[("qT", hh, tt)])
                else:
                    for c in range(2):
                        pl, ph = 64 * c, 64 * c + 64
                        P.add("dve", lambda e, nb=nb, hh=hh, tt=tt, gcolq=gcolq, c=c, pl=pl, ph=ph: e.scalar_tensor_tensor(
                            out=kTz[pl:ph, c, hh, tt * 512:(tt + 1) * 512], in0=qraw[pl:ph, nb, :], scalar=cda[pl:ph, gcolq:gcolq + 1],
                            in1=qrs[pl:ph, 0, :], op0=ALU.mult, op1=ALU.mult),
                            reads=[("qraw", nb), ("qrs", 0), ("cda", j), "kTz"], writes=[("kT", hh, tt, c)])

            emit_proj(0)
            for i in range(len(jobs)):
                if i + 1 < len(jobs):
                    emit_proj(i + 1)
                emit_norm(i)
            slot, wkey, _ = sv
            for t in range(NT):
                bank = cnt["pf"] % 2
                cnt["pf"] += 1
                pf = PF[bank]
                for kc in range(8):
                    P.add("pe", lambda e, pf=pf, slot=slot, kc=kc, t=t: e.matmul(
                        pf[:, 0:256], xnT[:, kc, t * 128:(t + 1) * 128], ring[:, slot, kc, 0:256],
                        start=(kc == 0), stop=(kc == 7)),
                        reads=[wkey, ("xnT", t, kc)], writes=[("pf", bank)])
                P.add("act", lambda e, pf=pf, t=t: e.activation(
                    out=vaug[:, t, :, 0:128], in_=pf[:, 0:256].rearrange("p (h d) -> p h d", h=2), func=AF.Copy),
                    reads=[("pf", bank), "vones"], writes=[("va", t)])
            for sl in (sq, sk, sv):
                self.ws.done(sl[2])
            ATT, FIN = [], []
            for hh in range(2):
                h = 2 * hp + hh
                for qt in range(4):
                    P.begin_capture()
                    first_in_bank = {}
                    units = [(c, kb) for c in range(2) for kb in range(4 * qt + 4)]
                    ubank = {}

                    def emit_scores(u):
                        c, kb = units[u]
                        r = kb - 4 * qt
                        col0 = max(r, 0) * 128
                        bank = cnt["pf"] % 2
                        cnt["pf"] += 1
                        ubank[u] = bank
                        ps = PF[bank]
                        krd = [("kT", hh, kb // 4, c)]
                        qrd = [("qT", hh, qt)]
                        if r >= 0:
                            P.add("pe", lambda e, ps=ps, c=c, hh=hh, kb=kb, qt=qt, col0=col0: e.matmul(
                                ps[:, col0:col0 + 128], kTz[:, c, hh, kb * 128:(kb + 1) * 128],
                                qT[:, hh, qt * 512 + col0:qt * 512 + col0 + 128], start=True, stop=False,
                                skip_group_check=True),
                                reads=krd + qrd, writes=[("pf", bank)])
                            P.add("pe", lambda e, ps=ps, col0=col0: e.matmul(
                                ps[:, col0:col0 + 128], identb[:, :], negmask[:, :], start=False, stop=True,
                                skip_group_check=True),
                                reads=["identb", "negmask"], writes=[("pf", bank)])
                            if col0 + 128 < 512:
                                P.add("pe", lambda e, ps=ps, c=c, hh=hh, kb=kb, qt=qt, col0=col0: e.matmul(
                                    ps[:, col0 + 128:512], kTz[:, c, hh, kb * 128:(kb + 1) * 128],
                                    qT[:, hh, qt * 512 + col0 + 128:qt * 512 + 512], start=True, stop=True,
                                    skip_group_check=True),
                                    reads=krd + qrd, writes=[("pf", bank)])
                        else:
                            P.add("pe", lambda e, ps=ps, c=c, hh=hh, kb=kb, qt=qt: e.matmul(
                                ps[:, :], kTz[:, c, hh, kb * 128:(kb + 1) * 128],
                                qT[:, hh, qt * 512:qt * 512 + 512], start=True, stop=True, skip_group_check=True),
                                reads=krd + qrd, writes=[("pf", bank)])

                    def emit_exp_pv(u):
                        c, kb = units[u]
                        r = kb - 4 * qt
                        col0 = max(r, 0) * 128
                        bank = ubank[u]
                        ps = PF[bank]
                        pi = cnt["pt"] % 3
                        cnt["pt"] += 1
                        P.add("act", lambda e, ps=ps, pi=pi, col0=col0: e.activation(
                            out=pT[:, pi, col0:512], in_=ps[:, col0:512], func=AF.Exp),
                            reads=[("pf", bank)], writes=[("pT", pi)])
                        for rr in range(max(r, 0), 4):
                            abank = 2 + 2 * c + rr // 2
                            off = (rr % 2) * 256
                            st = abank not in first_in_bank
                            first_in_bank[abank] = True
                            P.add("pe", lambda e, abank=abank, off=off, pi=pi, rr=rr, kb=kb, st=st, hh=hh, qt=qt: e.matmul(
                                PF[abank][:, off:off + 129], pT[:, pi, rr * 128:(rr + 1) * 128], vaug[:, kb, hh, 0:129],
                                start=st, stop=(kb == 4 * qt + rr), skip_group_check=True),
                                reads=[("pT", pi), ("va", kb), "vones"], writes=[("pf", abank)])

                    emit_scores(0)
                    for u in range(len(units)):
                        if u + 1 < len(units):
                            emit_scores(u + 1)
                        emit_exp_pv(u)
                    for b in range(4):
                        srcv = PF[2 + b][:, :].rearrange("p (s w) -> p s w", s=2)[:, :, 0:129]
                        dstv = accS[:, b, :].rearrange("p (s w) -> p s w", s=2)[:, :, 0:129]
                        if b % 2 == 0:
                            P.add("act", lambda e, srcv=srcv, dstv=dstv: e.activation(out=dstv, in_=srcv, func=AF.Copy),
                                  reads=[("pf", 2 + b)], writes=[("accS", b)])
                        else:
                            P.add("dve", lambda e, srcv=srcv, dstv=dstv: e.tensor_copy(out=dstv, in_=srcv),
                                  reads=[("pf", 2 + b)], writes=[("accS", b)])
                    ATT.append(P.end_capture())
                    P.begin_capture()
                    fi = cnt["fin"] % 2
                    cnt["fin"] += 1
                    AK = [("accS", b) for b in range(4)]
                    FK = ("fsm", fi)
                    slots = accS[:, :, :].rearrange("p b (s w) -> p (b s) w", s=2)
                    P.add("dve", lambda e, fi=fi, slots=slots: e.reciprocal(out=fsm[:, fi, 0:8], in_=slots[:, :, 128]),
                          reads=AK, writes=[FK])
                    P.add("dve", lambda e, fi=fi: e.tensor_scalar(out=fsm[:, fi, 8:12], in0=fsm[:, fi, 4:8], scalar1=cda[:, c0 + 3:c0 + 4], scalar2=None, op0=ALU.mult),
                          reads=[FK, ("cda", j)], writes=[FK])
                    for rr in range(4):
                        P.add("act", lambda e, fi=fi, rr=rr, slots=slots: e.activation(out=osb[:, fi, rr, :], in_=slots[:, rr, 0:128], func=AF.Copy,
                                                                                        scale=fsm[:, fi, rr:rr + 1]),
                              reads=AK + [FK], writes=[("osb", fi, rr)])
                    for rr in range(4):
                        P.add("dve", lambda e, fi=fi, rr=rr, slots=slots: e.scalar_tensor_tensor(
                            out=osb[:, fi, rr, :], in0=slots[:, 4 + rr, 0:128], scalar=fsm[:, fi, 8 + rr:9 + rr], in1=osb[:, fi, rr, :],
                            op0=ALU.mult, op1=ALU.add),
                            reads=AK + [FK, ("osb", fi, rr)], writes=[("osb", fi, rr)])
                    for rr in range(4):
                        P.add("act", lambda e, fi=fi, rr=rr: e.activation(out=obf[:, fi, rr, :], in_=osb[:, fi, rr, :], func=AF.Square,
                                                                          accum_out=fsm[:, fi, 12 + rr:13 + rr]),
                              reads=[("osb", fi, rr)], writes=[("obf", fi, rr), ("fss", fi, rr)])
                    P.add("act", lambda e, fi=fi: e.activation(out=fsm[:, fi, 16:20], in_=fsm[:, fi, 12:16], func=AF.Ln,
                                                               bias=self.epsc[:, 0:1], scale=1.0 / 128),
                          reads=[("fss", fi, rr) for rr in range(4)] + ["epsc"], writes=[("frs", fi)])
                    P.add("act", lambda e, fi=fi: e.activation(out=fsm[:, fi, 16:20], in_=fsm[:, fi, 16:20], func=AF.Exp, scale=-0.5),
                          reads=[("frs", fi)], writes=[("frs", fi)])
                    for rr in range(4):
                        P.add("dve", lambda e, fi=fi, rr=rr: e.tensor_scalar(out=obf[:, fi, rr, :], in0=osb[:, fi, rr, :], scalar1=fsm[:, fi, 16 + rr:17 + rr],
                                                                              scalar2=None, op0=ALU.mult),
                              reads=[("osb", fi, rr), ("frs", fi), ("obf", fi, rr)], writes=[("obf", fi, rr)])
                    pb = PB[fi]
                    for rr in range(4):
                        P.add("pe", lambda e, pb=pb, fi=fi, rr=rr: e.transpose(out=pb[:, rr * 128:(rr + 1) * 128], in_=obf[:, fi, rr, :], identity=identb[:]),
                              reads=[("obf", fi, rr), "identb"], writes=[("pb", fi)])
                    P.add("dve", lambda e, pb=pb, hh=hh, qt=qt: e.tensor_scalar(
                        out=U[:, hh, qt * 512:(qt + 1) * 512], in0=pb[:, 0:512], scalar1=cda[:, c0 + 4:c0 + 5], scalar2=None, op0=ALU.mult),
                        reads=[("pb", fi), ("cda", j)], writes=[("U", hh, qt)])
                    FIN.append(P.end_capture())
            P.replay([ATT[0]])
            for i in range(1, 8):
                P.replay([ATT[i], FIN[i - 1]])
            P.replay([FIN[7]])
            wsrc = self.d["da_w_out"][j][hp * 256:(hp + 1) * 256, :].rearrange("(kc p) f -> p kc f", p=128)
            so = self.ws.get(wsrc, (2, 1024))
            wv = self.ws.view(so[0], (2, 1024))
            for t in range(NT):
                for dh in range(2):
                    bank = cnt["pf"] % 2
                    cnt["pf"] += 1
                    pf = PF[bank]
                    for hc in range(2):
                        P.add("pe", lambda e, pf=pf, wv=wv, hc=hc, t=t, dh=dh: e.matmul(
                            pf[:, :], U[:, hc, t * 128:(t + 1) * 128], wv[:, hc, dh * 512:(dh + 1) * 512], start=(hc == 0), stop=(hc == 1)),
                            reads=[so[1], ("U", hc, t // 4)], writes=[("pf", bank)])
                    P.add("dve", lambda e, pf=pf, t=t, dh=dh: e.tensor_tensor(
                        out=X[:, t, dh * 512:(dh + 1) * 512], in0=X[:, t, dh * 512:(dh + 1) * 512], in1=pf[:, :], op=ALU.add),
                        reads=[("pf", bank), ("x", t)], writes=[("x", t)])
            self.ws.done(so[2])

    def setup_gd_static(self):
        P = self.P
        trif, sellast, onesf, masks, onecol = self.trif, self.sellast, self.onesf, self.masks, self.onecol
        P.add("pool", lambda e: e.memset(onesf[:], 1.0), writes=["onesf"])
        P.add("pool", lambda e: e.memset(onecol[:], 1.0), writes=["onecol"])
        P.add("pool", lambda e: e.memset(trif[:], 1.0), writes=["trif"])
        P.add("pool", lambda e: e.affine_select(out=trif[:], in_=trif[:], pattern=[[1, 128]], compare_op=ALU.is_ge,
                                                fill=0.0, base=0, channel_multiplier=-1), reads=["trif"], writes=["trif"])
        P.add("pool", lambda e: e.memset(sellast[:], 1.0), writes=["sellast"])
        P.add("pool", lambda e: e.affine_select(out=sellast[:], in_=sellast[:], pattern=[[0, 128]], compare_op=ALU.is_ge,
                                                fill=0.0, base=-127, channel_multiplier=1), reads=["sellast"], writes=["sellast"])
        P.add("pool", lambda e: e.memset(masks[:], 0.0), writes=["masks"])
        P.add("pool", lambda e: e.affine_select(out=masks[:, 0:128], in_=masks[:, 0:128], pattern=[[1, 128]], compare_op=ALU.is_ge,
                                                fill=-30000.0, base=-1, channel_multiplier=-1), reads=["masks"], writes=["masks"])
        P.add("pool", lambda e: e.affine_select(out=masks[:, 128:256], in_=masks[:, 128:256], pattern=[[1, 128]], compare_op=ALU.is_ge,
                                                fill=-30000.0, base=0, channel_multiplier=-1), reads=["masks"], writes=["masks"])

    def setup_gd_levelmasks(self):
        P = self.P
        NEGM = self.NEGM
        scrA = lambda nb: self.xs_f[0:nb, 0, 0:128]
        scrC = lambda nb: self.xs_f[0:nb, 0, 128:256]
        K0 = [("xs", 0)]
        for k in range(7):
            B, half = 2 ** (k + 1), 2 ** k
            nb = 128 // B
            A, C = scrA(nb), scrC(nb)
            P.add("pool", lambda e, nb=nb: e.memset(self.xs_f[0:nb, 0, 0:256], 1.0), writes=K0)
            P.add("pool", lambda e, A=A, B=B, half=half: e.affine_select(out=A, in_=A, pattern=[[1, 128]], compare_op=ALU.is_ge, fill=0.0,
                                                                       base=-half, channel_multiplier=-B), reads=K0, writes=K0)
            P.add("pool", lambda e, A=A, B=B: e.affine_select(out=A, in_=A, pattern=[[-1, 128]], compare_op=ALU.is_ge, fill=0.0,
                                                             base=B - 1, channel_multiplier=B), reads=K0, writes=K0)
            P.add("pool", lambda e, C=C, B=B: e.affine_select(out=C, in_=C, pattern=[[1, 128]], compare_op=ALU.is_ge, fill=0.0,
                                                             base=0, channel_multiplier=-B), reads=K0, writes=K0)
            P.add("pool", lambda e, C=C, B=B, half=half: e.affine_select(out=C, in_=C, pattern=[[-1, 128]], compare_op=ALU.is_ge, fill=0.0,
                                                                       base=half - 1, channel_multiplier=B), reads=K0, writes=K0)
            pf = self.PF[1]
            P.add("pe", lambda e, pf=pf, A=A, C=C: e.matmul(pf[:, 0:128], A, C, start=True, stop=True, skip_group_check=True),
                  reads=K0, writes=[("pf", 1)])
            P.add("pe", lambda e, pf=pf, A=A, C=C: e.matmul(pf[:, 128:256], C, A, start=True, stop=True, skip_group_check=True),
                  reads=K0, writes=[("pf", 1)])
            P.add("act", lambda e, pf=pf, k=k: e.activation(out=NEGM[:, k, :], in_=pf[:, 0:256], func=AF.Copy),
                  reads=[("pf", 1)], writes=["NEGM"])

    def setup_gd_consts(self, j):
        P = self.P
        d = self.d
        convw, gdc, identf = self.convw, self.gdc, self.identf
        crow96 = self.xs_f[0:96, 1, 0:128]
        rows = d["gd_conv_w"][j].rearrange("k (c p) -> (k c) p", p=128)
        P.add("sp", lambda e: e.dma_start(out=crow96, in_=rows), writes=[("xs", 1)], dma="gcw%d" % j)
        pf = self.PF[0]
        P.add("pe", lambda e: e.transpose(out=pf[:, 0:96], in_=crow96, identity=identf[0:96, 0:96]),
              reads=[("xs", 1), "identf"], writes=[("pf", 0)])
        P.add("dve", lambda e: e.tensor_copy(out=convw[:, j, :], in_=pf[:, 0:96]), reads=[("pf", 0)], writes=[("convw", j)])
        col = lambda ap: ap.rearrange("(p o) -> p o", o=1)
        P.add("sp", lambda e: e.dma_start(out=gdc[:, j, 0:1], in_=col(d["gd_out_norm"][j])), writes=[("gdc", j)], dma="gon%d" % j)
        gsm = self.gsm
        P.add("sp", lambda e: e.dma_start(out=gsm[:, j, 0:8], in_=d["gd_a_log"][j:j + 1, :].partition_broadcast(128)),
              writes=[("gsm", j)], dma="gal%d" % j)
        P.add("sp", lambda e: e.dma_start(out=gsm[:, j, 8:16], in_=d["gd_dt_bias"][j:j + 1, :].partition_broadcast(128)),
              writes=[("gsm", j)], dma="gdt%d" % j)
        P.add("act", lambda e: e.activation(out=gsm[:, j, 0:8], in_=gsm[:, j, 0:8], func=AF.Exp), reads=[("gsm", j)], writes=[("gsm", j)])
        P.add("dve", lambda e: e.tensor_scalar(out=gsm[:, j, 0:8], in0=gsm[:, j, 0:8], scalar1=-1.0, scalar2=None, op0=ALU.mult),
              reads=[("gsm", j)], writes=[("gsm", j)])

    def gdn(self, l):
        P = self.P
        j = l // 2
        g = self.g
        X, xnT, U, ring = self.X, self.xnT, self.U, self.ring
        identb, identf = self.identb, self.identf
        PF, PB = self.PF, self.PB
        epsc = self.epsc
        DKS = 128.0 ** -0.5
        win = self.d["gd_w_in"][j].rearrange("(kc p) f -> p kc f", p=128)
        self.rmsnorm(8 * l)
        self.arena_barrier()
        XN0 = [("xnT", 0, 0)]
        cnt = {"pf": 0, "rb": 0, "sq": 0}
        BA, BETA, GC, EG, GLB, EGL, EKD = (g[k] for k in ("BA", "BETA", "GC", "EG", "GLB", "EGL", "EKD"))
        flat = lambda v: v.rearrange("p a b -> p (a b)")

        sba = self.ws.get(win[:, :, 4096:4112], 16)
        slot_ba, wkey_ba, _ = sba
        pf_ba = PF[0]
        for t in range(NT):
            for kc in range(8):
                P.add("pe", lambda e, t=t, kc=kc: e.matmul(pf_ba[:, t * 16:(t + 1) * 16], xnT[:, kc, t * 128:(t + 1) * 128],
                                                          ring[:, slot_ba, kc, 0:16], start=(kc == 0), stop=(kc == 7),
                                                          skip_group_check=True),
                      reads=[wkey_ba, ("xnT", t, kc)], writes=[("pf", 0)])
        self.ws.done(sba[2])
        P.add("dve", lambda e: e.tensor_copy(out=flat(BA), in_=pf_ba[:, 0:256]), reads=[("pf", 0)] + XN0, writes=["BA"])
        P.add("act", lambda e: e.activation(out=BETA, in_=BA[:, :, 0:8], func=AF.Exp, scale=-1.0), reads=["BA"] + XN0, writes=["BETA"])
        P.add("act", lambda e: e.activation(out=BETA, in_=BETA, func=AF.Ln, bias=1.0), reads=["BETA"], writes=["BETA"])
        P.add("act", lambda e: e.activation(out=BETA, in_=BETA, func=AF.Exp, scale=-1.0), reads=["BETA"], writes=["BETA"])
        gsm = self.gsm
        for t in range(NT):
            P.add("dve", lambda e, t=t: e.tensor_tensor(out=GC[:, t, :], in0=BA[:, t, 8:16], in1=gsm[:, j, 8:16], op=ALU.add),
                  reads=["BA", ("gsm", j)] + XN0, writes=["GC"])
        P.add("act", lambda e: e.activation(out=GC, in_=GC, func=AF.Exp), reads=["GC"], writes=["GC"])
        P.add("act", lambda e: e.activation(out=GC, in_=GC, func=AF.Ln, bias=1.0), reads=["GC"], writes=["GC"])
        for t in range(NT):
            P.add("dve", lambda e, t=t: e.tensor_tensor(out=GLB[:, t, :], in0=GC[:, t, :], in1=gsm[:, j, 0:8], op=ALU.mult),
                  reads=["GC", "BETA", ("gsm", j)] + XN0, writes=["BA"])
        pf1 = PF[1]
        for t in range(NT):
            P.add("pe", lambda e, t=t: e.matmul(pf1[:, t * 8:(t + 1) * 8], self.trif[:, :], GLB[:, t, :], start=True, stop=True,
                                                skip_group_check=True),
                  reads=["BA", "trif"], writes=[("pf", 1)])
        P.add("dve", lambda e: e.tensor_copy(out=flat(GC), in_=pf1[:, 0:128]), reads=[("pf", 1)], writes=["GC"])
        P.add("act", lambda e: e.activation(out=EG, in_=GC, func=AF.Exp), reads=["GC"] + XN0, writes=["EG"])
        for t in range(NT):
            P.add("pe", lambda e, t=t: e.matmul(pf1[:, 128 + t * 8:128 + (t + 1) * 8], self.sellast[:, :], GC[:, t, :], start=True, stop=True,
                                                skip_group_check=True),
                  reads=["GC", "sellast"], writes=[("pf", 1)])
        P.add("dve", lambda e: e.tensor_copy(out=flat(GLB), in_=pf1[:, 128:256]), reads=[("pf", 1)], writes=["BA"])
        P.add("act", lambda e: e.activation(out=EGL, in_=GLB, func=AF.Exp), reads=["BA"] + XN0, writes=["EGL"])
        P.add("dve", lambda e: e.tensor_tensor(out=EKD, in0=GLB, in1=GC, op=ALU.subtract), reads=["BA", "GC"] + XN0, writes=["EKD"])
        P.add("act", lambda e: e.activation(out=EKD, in_=EKD, func=AF.Exp), reads=["EKD"], writes=["EKD"])

        raw, acc, halo, sq, Sf, Sb = (g[k] for k in ("raw", "acc", "halo", "sq", "Sf", "Sb"))
        convw = self.convw
        NEGM = self.NEGM
        for hp in range(4):
            slabs = {}
            for wi, which in enumerate(("q", "k", "v", "z")):
                slabs[which] = self.ws.get(win[:, :, wi * 1024 + hp * 256: wi * 1024 + (hp + 1) * 256], 256)
            P.add("pool", lambda e: e.memset(flat(Sf), 0.0), reads=XN0, writes=[("Sf", 0), ("Sf", 1)])
            P.add("pool", lambda e: e.memset(flat(Sb), 0.0), reads=XN0, writes=[("Sb", 0), ("Sb", 1)])
            P.add("pool", lambda e: e.memset(flat(halo), 0.0), reads=XN0, writes=[("halo", i) for i in range(6)])
            FE, PREP, SCAN = {}, {}, {}
            for gi in range(4):
                gp = gi % 2
                sT, zs, R, SC = g["sT"][gp], g["zs"][gp], g["R"][gp], g["SC"][gp]
                P.begin_capture()
                for wi, which in enumerate(("k", "q", "v")):
                    slot, wkey, _ = slabs[which]
                    cbase = {"q": 0, "k": 8, "v": 16}[which]
                    for hh in range(2):
                        h = 2 * hp + hh
                        pf = PF[0]
                        rb = cnt["rb"] % 2
                        cnt["rb"] += 1
                        hi = wi * 2 + hh
                        for kc in range(8):
                            P.add("pe", lambda e, pf=pf, slot=slot, kc=kc, hh=hh, gi=gi: e.matmul(
                                pf[:, :], ring[:, slot, kc, hh * 128:(hh + 1) * 128], xnT[:, kc, gi * 512:(gi + 1) * 512],
                                start=(kc == 0), stop=(kc == 7)),
                                reads=[wkey] + [("xnT", t, kc) for t in range(4 * gi, 4 * gi + 4)], writes=[("pf", 0)])
                        P.add("dve", lambda e, rb=rb, hi=hi: e.tensor_copy(out=raw[:, rb, 0:3], in_=halo[:, hi, 0:3]),
                              reads=[("halo", hi)], writes=[("raw", rb)])
                        P.add("act", lambda e, pf=pf, rb=rb: e.activation(out=raw[:, rb, 3:515], in_=pf[:, :], func=AF.Copy),
                              reads=[("pf", 0), ("raw", rb)], writes=[("raw", rb)])
                        if gi < 3:
                            P.add("dve", lambda e, rb=rb, hi=hi: e.tensor_copy(out=halo[:, hi, 0:3], in_=raw[:, rb, 512:515]),
                                  reads=[("raw", rb)], writes=[("halo", hi)])
                        cc = cbase + h
                        P.add("dve", lambda e, rb=rb, cc=cc: e.tensor_scalar(
                            out=acc[:, 0, :], in0=raw[:, rb, 0:512], scalar1=convw[:, j, cc:cc + 1], scalar2=None, op0=ALU.mult),
                            reads=[("raw", rb), ("convw", j)], writes=[("acc", 0)])
                        for tap in range(1, 4):
                            P.add("dve", lambda e, rb=rb, cc=cc, tap=tap: e.scalar_tensor_tensor(
                                out=acc[:, 0, :], in0=raw[:, rb, tap:tap + 512], scalar=convw[:, j, tap * 24 + cc:tap * 24 + cc + 1],
                                in1=acc[:, 0, :], op0=ALU.mult, op1=ALU.add),
                                reads=[("raw", rb), ("acc", 0), ("convw", j)], writes=[("acc", 0)])
                        sgb = raw[:, rb, 0:512]
                        P.add("act", lambda e, sgb=sgb: e.activation(out=sgb, in_=acc[:, 0, :], func=AF.Exp, scale=-1.0),
                              reads=[("acc", 0), ("raw", rb), ("halo", hi)], writes=[("raw", rb)])
                        P.add("act", lambda e, sgb=sgb: e.activation(out=sgb, in_=sgb, func=AF.Ln, bias=1.0), reads=[("raw", rb)], writes=[("raw", rb)])
                        P.add("act", lambda e, sgb=sgb: e.activation(out=sgb, in_=sgb, func=AF.Exp, scale=-1.0), reads=[("raw", rb)], writes=[("raw", rb)])
                        P.add("dve", lambda e, sgb=sgb, hh=hh, wi=wi, sT=sT: e.tensor_tensor(
                            out=sT[:, hh, :, wi * 128:(wi + 1) * 128], in0=acc[:, 0, :].rearrange("p (a b) -> p a b", a=4),
                            in1=sgb.rearrange("p (a b) -> p a b", a=4), op=ALU.mult),
                            reads=[("acc", 0), ("raw", rb)], writes=[("sT", gp, hh, wi)])
                        if which in ("k", "q"):
                            sb_ = cnt["sq"] % 2
                            cnt["sq"] += 1
                            P.add("pool", lambda e, hh=hh, wi=wi, sb_=sb_, sT=sT: e.tensor_tensor(
                                out=sq[:, sb_, :].rearrange("p (a b) -> p a b", a=4), in0=sT[:, hh, :, wi * 128:(wi + 1) * 128],
                                in1=sT[:, hh, :, wi * 128:(wi + 1) * 128], op=ALU.mult),
                                reads=[("sT", gp, hh, wi)], writes=[("sq", sb_)])
                            for tl in range(4):
                                colr = tl * 4 + wi * 2 + hh
                                P.add("pe", lambda e, sb_=sb_, tl=tl, colr=colr: e.matmul(
                                    PF[1][:, 384 + colr:384 + colr + 1], sq[:, sb_, tl * 128:(tl + 1) * 128], self.onecol[:, 0:1],
                                    start=True, stop=True, skip_group_check=True),
                                    reads=[("sq", sb_), "onecol"], writes=[("pf", 1)])
                slot, wkey, _ = slabs["z"]
                for tl in range(4):
                    t = 4 * gi + tl
                    pf = PF[1]
                    for kc in range(8):
                        P.add("pe", lambda e, pf=pf, slot=slot, kc=kc, t=t: e.matmul(
                            pf[:, 0:256], xnT[:, kc, t * 128:(t + 1) * 128], ring[:, slot, kc, 0:256], start=(kc == 0), stop=(kc == 7),
                            skip_group_check=True),
                            reads=[wkey, ("xnT", t, kc)], writes=[("pf", 1)])
                    zt = acc[:, 0, 0:256]
                    zr = acc[:, 0, 256:512]
                    P.add("act", lambda e, pf=pf, zt=zt: e.activation(out=zt, in_=pf[:, 0:256], func=AF.Exp, scale=-1.0),
                          reads=[("pf", 1)], writes=[("acc", 0)])
                    P.add("act", lambda e, pf=pf, zr=zr: e.activation(out=zr, in_=pf[:, 0:256], func=AF.Copy),
                          reads=[("pf", 1), ("acc", 0)], writes=[("acc", 0)])
                    P.add("act", lambda e, zt=zt: e.activation(out=zt, in_=zt, func=AF.Ln, bias=1.0), reads=[("acc", 0)], writes=[("acc", 0)])
                    P.add("act", lambda e, zt=zt: e.activation(out=zt, in_=zt, func=AF.Exp, scale=-1.0), reads=[("acc", 0)], writes=[("acc", 0)])
                    P.add("dve", lambda e, zt=zt, zr=zr, tl=tl, zs=zs: e.tensor_tensor(out=zs[:, tl, :], in0=zr, in1=zt, op=ALU.mult),
                          reads=[("acc", 0)], writes=[("zs", gp, tl)])
                Rk = ("R", gp)
                P.add("act", lambda e, R=R: e.activation(out=flat(R), in_=PF[1][:, 384:400], func=AF.Ln, bias=epsc[:, 0:1], scale=1.0),
                      reads=[("pf", 1), "epsc"], writes=[Rk])
                P.add("act", lambda e, R=R: e.activation(out=flat(R), in_=flat(R), func=AF.Exp, scale=-0.5), reads=[Rk], writes=[Rk])
                hs = slice(2 * hp, 2 * hp + 2)
                ts = slice(4 * gi, 4 * gi + 4)
                rk, rq = R[:, :, 0:2], R[:, :, 2:4]
                scv = lambda q_, SC=SC: SC[:, q_, :].rearrange("p (a b) -> p a b", a=4)
                T1, CKBG, CKD, CQ, UL, UA, BIAS, LN = (scv(i) for i in range(8))
                bt, egs, ekds, gcs = BETA[:, ts, hs], EG[:, ts, hs], EKD[:, ts, hs], GC[:, ts, hs]
                sk = lambda i: ("SC", gp, i)
                P.add("dve", lambda e, T1=T1, rk=rk, bt=bt: e.tensor_tensor(out=T1, in0=rk, in1=bt, op=ALU.mult), reads=[Rk, "BETA"], writes=[sk(0)])
                P.add("dve", lambda e, CKBG=CKBG, T1=T1, egs=egs: e.tensor_tensor(out=CKBG, in0=T1, in1=egs, op=ALU.mult), reads=[sk(0), "EG"], writes=[sk(1)])
                P.add("dve", lambda e, CKD=CKD, rk=rk, ekds=ekds: e.tensor_tensor(out=CKD, in0=rk, in1=ekds, op=ALU.mult), reads=[Rk, "EKD"], writes=[sk(2)])
                P.add("dve", lambda e, CQ=CQ, rq=rq, egs=egs: e.scalar_tensor_tensor(out=CQ, in0=rq, scalar=DKS, in1=egs, op0=ALU.mult, op1=ALU.mult),
                      reads=[Rk, "EG"], writes=[sk(3)])
                P.add("act", lambda e, UL=UL, T1=T1: e.activation(out=UL, in_=T1, func=AF.Ln), reads=[sk(0)], writes=[sk(4)])
                P.add("dve", lambda e, UL=UL, gcs=gcs: e.tensor_tensor(out=UL, in0=UL, in1=gcs, op=ALU.add), reads=[sk(4), "GC"], writes=[sk(4)])
                P.add("act", lambda e, UA=UA, rq=rq: e.activation(out=UA, in_=rq, func=AF.Ln, scale=DKS), reads=[Rk], writes=[sk(5)])
                P.add("dve", lambda e, UA=UA, gcs=gcs: e.tensor_tensor(out=UA, in0=UA, in1=gcs, op=ALU.add), reads=[sk(5), "GC"], writes=[sk(5)])
                P.add("act", lambda e, BIAS=BIAS, rk=rk: e.activation(out=BIAS, in_=rk, func=AF.Ln), reads=[Rk], writes=[sk(6)])
                P.add("dve", lambda e, BIAS=BIAS, gcs=gcs: e.tensor_tensor(out=BIAS, in0=BIAS, in1=gcs, op=ALU.subtract), reads=[sk(6), "GC"], writes=[sk(6)])
                FE[gi] = P.end_capture()
                SCK = [sk(i) for i in range(7)]
                for tl in range(4):
                    t = 4 * gi + tl
                    P.begin_capture()
                    for c in range(2):
                        hh = c
                        cs = 2 * (t % 2) + c
                        pbk, pbo = cs // 2, (cs % 2) * 512
                        sc1 = lambda q_, tl=tl, hh=hh, SC=SC: SC[:, q_, tl * 2 + hh:tl * 2 + hh + 1]
                        pb, pc = PB[pbk], PF[2 + cs]
                        kd, kbg, vb, E, MA = (g[k, cs] for k in ("kd", "kbg", "vb", "E", "MA"))
                        ksT = sT[:, hh, tl, 0:128]
                        vsT = sT[:, hh, tl, 256:384]
                        P.add("pe", lambda e, pb=pb, ksT=ksT, pbo=pbo: e.transpose(out=pb[:, pbo:pbo + 128], in_=ksT, identity=identb[:]),
                              reads=[("sT", gp, hh, 0), "identb"], writes=[("pb", pbk)])
                        P.add("pe", lambda e, pb=pb, vsT=vsT, pbo=pbo: e.transpose(out=pb[:, pbo + 128:pbo + 256], in_=vsT, identity=identb[:]),
                              reads=[("sT", gp, hh, 2), "identb"], writes=[("pb", pbk)])
                        P.add("act", lambda e, pb=pb, kd=kd, sc1=sc1, pbo=pbo: e.activation(out=kd, in_=pb[:, pbo:pbo + 128], func=AF.Copy, scale=sc1(2)),
                              reads=[("pb", pbk)] + SCK, writes=[("kd", cs)])
                        P.add("dve", lambda e, pb=pb, kbg=kbg, sc1=sc1, pbo=pbo: e.tensor_scalar(out=kbg, in0=pb[:, pbo:pbo + 128], scalar1=sc1(1), scalar2=None, op0=ALU.mult),
                              reads=[("pb", pbk)] + SCK, writes=[("kbg", cs)])
                        hcol = 2 * hp + hh
                        P.add("dve", lambda e, pb=pb, vb=vb, t=t, hcol=hcol, pbo=pbo: e.tensor_scalar(
                            out=vb, in0=pb[:, pbo + 128:pbo + 256], scalar1=BETA[:, t, hcol:hcol + 1], scalar2=None, op0=ALU.mult),
                            reads=[("pb", pbk), "BETA"], writes=[("vb", cs)])
                        P.add("pe", lambda e, pc=pc, ksT=ksT, hh=hh, tl=tl, sT=sT: e.matmul(pc[:, 0:256], ksT, sT[:, hh, tl, 0:256], start=True, stop=True,
                                                                                              skip_group_check=True),
                              reads=[("sT", gp, hh, 0), ("sT", gp, hh, 1)], writes=[("pf", 2 + cs)])
                        P.add("act", lambda e, E=E, sc1=sc1: e.activation(out=E[:, 0:128], in_=identf[:, :], func=AF.Copy, scale=sc1(4)),
                              reads=["identf"] + SCK, writes=[("E", cs)])
                        P.add("act", lambda e, E=E, sc1=sc1: e.activation(out=E[:, 128:256], in_=identf[:, :], func=AF.Copy, scale=sc1(5)),
                              reads=["identf", ("E", cs)] + SCK, writes=[("E", cs)])
                        P.add("pe", lambda e, pc=pc, E=E: e.matmul(pc[:, 256:512], self.onesf[:, :], E[:, :], start=True, stop=False, skip_group_check=True),
                              reads=[("E", cs), "onesf"], writes=[("pf", 2 + cs)])
                        P.add("pe", lambda e, pc=pc: e.matmul(pc[:, 256:512], identf[:, :], self.masks[:, :], start=False, stop=True, skip_group_check=True),
                              reads=["identf", "masks"], writes=[("pf", 2 + cs)])
                        P.add("act", lambda e, pc=pc, E=E, sc1=sc1: e.activation(out=E[:, :], in_=pc[:, 256:512], func=AF.Exp, bias=sc1(6)),
                              reads=[("pf", 2 + cs)] + SCK, writes=[("E", cs)])
                        P.add("dve", lambda e, pc=pc, E=E, MA=MA: e.tensor_tensor(out=MA[:, :], in0=pc[:, 0:256], in1=E[:, :], op=ALU.mult),
                              reads=[("pf", 2 + cs), ("E", cs)], writes=[("MA", cs)])
                        Lb, DD, TM = g["Lb", cs], g["DD", cs], g["TM", cs]
                        P.add("pe", lambda e, pb=pb, MA=MA, pbo=pbo: e.transpose(out=pb[:, pbo + 256:pbo + 384], in_=MA[:, 0:128], identity=identb[:]),
                              reads=[("MA", cs), "identb"], writes=[("pb", pbk)])
                        P.add("act", lambda e, pb=pb, Lb=Lb, pbo=pbo: e.activation(out=Lb, in_=pb[:, pbo + 256:pbo + 384], func=AF.Copy),
                              reads=[("pb", pbk)], writes=[("Lb", cs)])
                        P.add("dve", lambda e, TM=TM, Lb=Lb: e.tensor_tensor(out=TM[:, 0:128], in0=Lb, in1=NEGM[:, 0, 0:128], op=ALU.mult),
                              reads=[("Lb", cs), "NEGM"], writes=[("TM", cs)])
                        P.add("dve", lambda e, TM=TM, MA=MA: e.tensor_tensor(out=TM[:, 128:256], in0=MA[:, 0:128], in1=NEGM[:, 0, 128:256], op=ALU.mult),
                              reads=[("MA", cs), "NEGM", ("TM", cs)], writes=[("TM", cs)])
                        P.add("dve", lambda e, TM=TM, DD=DD: e.tensor_tensor(out=DD[:, 0, 0:128], in0=identb[:, :], in1=TM[:, 0:128], op=ALU.subtract),
                              reads=[("TM", cs), "identb"], writes=[("DD", cs, 0)])
                        P.add("dve", lambda e, TM=TM, DD=DD: e.tensor_tensor(out=DD[:, 0, 128:256], in0=identb[:, :], in1=TM[:, 128:256], op=ALU.subtract),
                              reads=[("TM", cs), "identb", ("DD", cs, 0)], writes=[("DD", cs, 0)])
                    for lev in range(1, 7):
                        for c in range(2):
                            cs = 2 * (t % 2) + c
                            pc = PF[2 + cs]
                            MA, Lb, DD, QQ, TM = (g[k_, cs] for k_ in ("MA", "Lb", "DD", "QQ", "TM"))
                            pi, po = (lev - 1) % 2, lev % 2
                            P.add("pe", lambda e, pc=pc, MA=MA, DD=DD, pi=pi: e.matmul(pc[:, 0:128], MA[:, 0:128], DD[:, pi, 0:128], start=True, stop=True,
                                                                                        skip_group_check=True),
                                  reads=[("MA", cs), ("DD", cs, pi)], writes=[("pf", 2 + cs)])
                            P.add("pe", lambda e, pc=pc, Lb=Lb, DD=DD, pi=pi: e.matmul(pc[:, 128:256], Lb, DD[:, pi, 128:256], start=True, stop=True,
                                                                                        skip_group_check=True),
                                  reads=[("Lb", cs), ("DD", cs, pi)], writes=[("pf", 2 + cs)])
                            P.add("dve", lambda e, pc=pc, QQ=QQ, lev=lev: e.tensor_tensor(out=QQ[:, :], in0=pc[:, 0:256], in1=NEGM[:, lev, :], op=ALU.mult),
                                  reads=[("pf", 2 + cs), "NEGM"], writes=[("QQ", cs)])
                            P.add("pe", lambda e, pc=pc, QQ=QQ, DD=DD, pi=pi: e.matmul(pc[:, 256:384], DD[:, pi, 128:256], QQ[:, 0:128], start=True, stop=True,
                                                                                        skip_group_check=True),
                                  reads=[("QQ", cs), ("DD", cs, pi)], writes=[("pf", 2 + cs)])
                            P.add("pe", lambda e, pc=pc, QQ=QQ, DD=DD, pi=pi: e.matmul(pc[:, 384:512], DD[:, pi, 0:128], QQ[:, 128:256], start=True, stop=True,
                                                                                        skip_group_check=True),
                                  reads=[("QQ", cs), ("DD", cs, pi)], writes=[("pf", 2 + cs)])
                            P.add("dve", lambda e, pc=pc, DD=DD, pi=pi, po=po: e.tensor_tensor(out=DD[:, po, :], in0=DD[:, pi, :], in1=pc[:, 256:512], op=ALU.subtract),
                                  reads=[("pf", 2 + cs), ("DD", cs, pi)], writes=[("DD", cs, po)])
                    for c in range(2):
                        cs = 2 * (t % 2) + c
                        pc = PF[2 + cs]
                        kbg, vb, DD, u, wT = (g[k, cs] for k in ("kbg", "vb", "DD", "u", "wT"))
                        P.add("pe", lambda e, pc=pc, DD=DD, vb=vb: e.matmul(pc[:, 0:128], DD[:, 0, 128:256], vb, start=True, stop=True, skip_group_check=True),
                              reads=[("DD", cs, 0), ("vb", cs)], writes=[("pf", 2 + cs)])
                        P.add("pe", lambda e, pc=pc, DD=DD, kbg=kbg: e.matmul(pc[:, 128:256], kbg, DD[:, 0, 128:256], start=True, stop=True, skip_group_check=True),
                              reads=[("DD", cs, 0), ("kbg", cs)], writes=[("pf", 2 + cs)])
                        P.add("act", lambda e, pc=pc, u=u: e.activation(out=u, in_=pc[:, 0:128], func=AF.Copy), reads=[("pf", 2 + cs)], writes=[("u", cs)])
                        P.add("dve", lambda e, pc=pc, wT=wT: e.tensor_copy(out=wT, in_=pc[:, 128:256]), reads=[("pf", 2 + cs)], writes=[("wT", cs)])
                    PREP[t] = P.end_capture()
                    P.begin_capture()
                    for c in range(2):
                        hh = c
                        h = 2 * hp + hh
                        cs = 2 * (t % 2) + c
                        pbk, pbo = cs // 2, (cs % 2) * 512
                        ps_, pb = PF[2 + cs], PB[pbk]
                        kd, MA, u, wT, vn, o, og, psm = (g[k, cs] for k in ("kd", "MA", "u", "wT", "vn", "o", "og", "ps"))
                        sc1 = lambda q_, tl=tl, hh=hh, SC=SC: SC[:, q_, tl * 2 + hh:tl * 2 + hh + 1]
                        qsT = sT[:, hh, tl, 128:256]
                        PK = ("pf", 2 + cs)
                        P.add("pe", lambda e, ps_=ps_, wT=wT, hh=hh: e.matmul(ps_[:, 0:128], wT, Sb[:, hh, :], start=True, stop=True, skip_group_check=True),
                              reads=[("wT", cs), ("Sb", hh)], writes=[PK])
                        P.add("pe", lambda e, ps_=ps_, qsT=qsT, hh=hh: e.matmul(ps_[:, 128:256], qsT, Sb[:, hh, :], start=True, stop=True, skip_group_check=True),
                              reads=[("sT", gp, hh, 1), ("Sb", hh)], writes=[PK])
                        P.add("dve", lambda e, ps_=ps_, u=u, vn=vn: e.tensor_tensor(out=vn, in0=u, in1=ps_[:, 0:128], op=ALU.subtract),
                              reads=[PK, ("u", cs)], writes=[("vn", cs)])
                        P.add("pe", lambda e, ps_=ps_, MA=MA, vn=vn: e.matmul(ps_[:, 256:384], MA[:, 128:256], vn, start=True, stop=True, skip_group_check=True),
                              reads=[("MA", cs), ("vn", cs)], writes=[PK])
                        P.add("pe", lambda e, ps_=ps_, kd=kd, vn=vn: e.matmul(ps_[:, 384:512], kd, vn, start=True, stop=True, skip_group_check=True),
                              reads=[("kd", cs), ("vn", cs)], writes=[PK])
                        P.add("act", lambda e, ps_=ps_, o=o: e.activation(out=o, in_=ps_[:, 256:384], func=AF.Copy), reads=[PK], writes=[("o", cs)])
                        P.add("dve", lambda e, ps_=ps_, o=o, sc1=sc1: e.scalar_tensor_tensor(out=o, in0=ps_[:, 128:256], scalar=sc1(3), in1=o, op0=ALU.mult, op1=ALU.add),
                              reads=[PK, ("o", cs)] + SCK, writes=[("o", cs)])
                        P.add("dve", lambda e, ps_=ps_, hh=hh, t=t, h=h: e.scalar_tensor_tensor(
                            out=Sf[:, hh, :], in0=Sf[:, hh, :], scalar=EGL[:, t, h:h + 1], in1=ps_[:, 384:512], op0=ALU.mult, op1=ALU.add),
                            reads=[PK, ("Sf", hh), "EGL"], writes=[("Sf", hh)])
                        P.add("act", lambda e, hh=hh: e.activation(out=Sb[:, hh, :], in_=Sf[:, hh, :], func=AF.Copy), reads=[("Sf", hh)], writes=[("Sb", hh)])
                        P.add("act", lambda e, o=o, og=og, psm=psm: e.activation(out=og, in_=o, func=AF.Square, accum_out=psm[:, 0:1]),
                              reads=[("o", cs)], writes=[("og", cs), ("psm", cs)])
                        P.add("act", lambda e, psm=psm: e.activation(out=psm[:, 1:2], in_=psm[:, 0:1], func=AF.Ln, bias=epsc[:, 0:1], scale=1.0 / 128),
                              reads=[("psm", cs), "epsc"], writes=[("psm", cs)])
                        P.add("act", lambda e, psm=psm: e.activation(out=psm[:, 1:2], in_=psm[:, 1:2], func=AF.Exp, scale=-0.5), reads=[("psm", cs)], writes=[("psm", cs)])
                        P.add("dve", lambda e, o=o, og=og, psm=psm, tl=tl, hh=hh, zs=zs: e.scalar_tensor_tensor(
                            out=og, in0=o, scalar=psm[:, 1:2], in1=zs[:, tl, hh * 128:(hh + 1) * 128], op0=ALU.mult, op1=ALU.mult),
                            reads=[("o", cs), ("psm", cs), ("zs", gp, tl)], writes=[("og", cs)])
                        P.add("pe", lambda e, pb=pb, og=og, pbo=pbo: e.transpose(out=pb[:, pbo + 384:pbo + 512], in_=og, identity=identb[:]),
                              reads=[("og", cs), "identb"], writes=[("pb", pbk)])
                        P.add("act", lambda e, pb=pb, hh=hh, t=t, pbo=pbo: e.activation(out=U[:, hh, t * 128:(t + 1) * 128], in_=pb[:, pbo + 384:pbo + 512], func=AF.Copy,
                                                                                        scale=self.gdc[:, j, 0:1]),
                              reads=[("pb", pbk), ("gdc", j)], writes=[("U", hh, t // 4)])
                    SCAN[t] = P.end_capture()
            P.replay([FE[0]])
            fe_parts = {}
            for gi in range(1, 4):
                L = FE[gi]
                n = (len(L) + 2) // 3
                for k in range(3):
                    fe_parts[4 * (gi - 1) + 1 + k] = L[k * n:(k + 1) * n]
            for s_ in range(NT + 1):
                lists = []
                if s_ < NT:
                    lists.append(PREP[s_])
                if s_ >= 1:
                    lists.append(SCAN[s_ - 1])
                if s_ in fe_parts:
                    lists.append(fe_parts[s_])
                P.replay(lists)
            for which in ("q", "k", "v", "z"):
                self.ws.done(slabs[which][2])
            wsrc = self.d["gd_w_out"][j][hp * 256:(hp + 1) * 256, :].rearrange("(kc p) f -> p kc f", p=128)
            so = self.ws.get(wsrc, (2, 1024))
            wv = self.ws.view(so[0], (2, 1024))
            for t in range(NT):
                for dh in range(2):
                    bank = cnt["pf"] % 2
                    cnt["pf"] += 1
                    pf = PF[bank]
                    for hc in range(2):
                        P.add("pe", lambda e, pf=pf, wv=wv, hc=hc, t=t, dh=dh: e.matmul(
                            pf[:, :], U[:, hc, t * 128:(t + 1) * 128], wv[:, hc, dh * 512:(dh + 1) * 512], start=(hc == 0), stop=(hc == 1)),
                            reads=[so[1], ("U", hc, t // 4)], writes=[("pf", bank)])
                    P.add("dve", lambda e, pf=pf, t=t, dh=dh: e.tensor_tensor(
                        out=X[:, t, dh * 512:(dh + 1) * 512], in0=X[:, t, dh * 512:(dh + 1) * 512], in1=pf[:, :], op=ALU.add),
                        reads=[("pf", bank), ("x", t)], writes=[("x", t)])
            self.ws.done(so[2])

    def load_x(self, s):
        P = self.P
        X = self.X
        xv = self.d["x"][s].rearrange("(t p) d -> p t d", p=128)
        for q in range(4):
            P.add("sp", lambda e, q=q: e.dma_start(out=X[:, 4 * q:4 * q + 4, :], in_=xv[:, 4 * q:4 * q + 4, :]),
                  writes=[("x", t) for t in range(4 * q, 4 * q + 4)], dma=("xl", q))

    def store_x(self, s):
        P = self.P
        X = self.X
        ov = self.d["out"][s].rearrange("(t p) d -> p t d", p=128)
        ids = []
        for q in range(4):
            ids.append(P.add("sp", lambda e, q=q: e.dma_start(out=ov[:, 4 * q:4 * q + 4, :], in_=X[:, 4 * q:4 * q + 4, :]),
                             reads=[("x", t) for t in range(4 * q, 4 * q + 4)], writes=[("xst", q)], dma=("xs", q)))
        return ids

    def rmsnorm(self, gbase):
        P = self.P
        X, xs, ss, rstd, xnT = self.X, self.xs, self.ss, self.rstd, self.xnT
        identb, gcol = self.identb, self.gcol
        import os
        DBG = int(os.environ.get("K_DBG", "9"))
        for t in range(NT):
            P.add("act", lambda e, t=t: e.activation(out=xs[:, 1, :], in_=X[:, t, :], func=AF.Square,
                                                     accum_out=ss[:, t:t + 1]),
                  reads=[("x", t)], writes=[("xs", 1), ("ss", t)])
        P.add("act", lambda e: e.activation(out=rstd[:], in_=ss[:], func=AF.Ln, bias=self.epsc[:, 0:1], scale=1.0 / D),
              reads=[("ss", t) for t in range(NT)] + ["epsc"], writes=["rstd"])
        P.add("act", lambda e: e.activation(out=rstd[:], in_=rstd[:], func=AF.Exp, scale=-0.5), reads=["rstd"], writes=["rstd"])
        if DBG < 2:
            return
        for t in range(NT if DBG >= 6 else 1):
            b = t % 2
            pb = self.PB[b]
            P.add("act", lambda e, t=t, b=b: e.activation(out=xs[:, b, :], in_=X[:, t, :], func=AF.Copy,
                                                          scale=rstd[:, t:t + 1]),
                  reads=[("x", t), "rstd"], writes=[("xs", b)])
            if DBG < 4:
                continue
            for kc in range(8):
                P.add("pe", lambda e, b=b, kc=kc, pb=pb: e.transpose(out=pb[:, kc * 128:(kc + 1) * 128],
                                                                      in_=xs[:, b, kc * 128:(kc + 1) * 128],
                                                                      identity=identb[:]),
                      reads=[("xs", b), "identb"], writes=[("pb", b)])
            if DBG < 5:
                continue
            for kc in range(8):
                eng = "dve" if b == 0 else "act"
                if eng == "dve":
                    fn = lambda e, t=t, kc=kc, pb=pb: e.tensor_scalar(
                        out=xnT[:, kc, t * 128:(t + 1) * 128], in0=pb[:, kc * 128:(kc + 1) * 128],
                        scalar1=gcol[:, gbase + kc:gbase + kc + 1], scalar2=None, op0=ALU.mult)
                else:
                    fn = lambda e, t=t, kc=kc, pb=pb: e.activation(
                        out=xnT[:, kc, t * 128:(t + 1) * 128], in_=pb[:, kc * 128:(kc + 1) * 128],
                        func=AF.Copy, scale=gcol[:, gbase + kc:gbase + kc + 1])
                P.add(eng, fn, reads=[("pb", b), "gcol"], writes=[("xnT", t, kc)])

    def mlp(self, l):
        P = self.P
        X, xnT, U, ring = self.X, self.xnT, self.hT, self.ring
        w1 = self.d["mlp_w_in"][l].rearrange("(kc p) f -> p kc f", p=128)
        w2 = self.d["mlp_w_out"][l].rearrange("(fc p) d -> p fc d", p=128)
        self.rmsnorm(32 + 8 * l)
        self.arena_barrier()
        pfi = 0
        for fg in range(4):
            slabs = [self.ws.get(w1[:, :, fg * 1024 + s2 * 512: fg * 1024 + (s2 + 1) * 512], 512) for s2 in range(2)]
            for fc in range(8):
                slot, wkey, _ = slabs[fc // 4]
                off = (fc % 4) * 128
                for tt in range(4):
                    bank = pfi % 4
                    pfi += 1
                    pf = self.PF[bank]
                    for kc in range(8):
                        P.add("pe", lambda e, pf=pf, slot=slot, kc=kc, off=off, tt=tt: e.matmul(
                            pf[:, :], ring[:, slot, kc, off:off + 128], xnT[:, kc, tt * 512:(tt + 1) * 512],
                            start=(kc == 0), stop=(kc == 7)),
                            reads=[wkey] + [("xnT", t, kc) for t in range(4 * tt, 4 * tt + 4)],
                            writes=[("pf", bank)])
                    rb = pfi % 2
                    P.add("act", lambda e, pf=pf, rb=rb: e.activation(out=self.rtmp[:, rb, :], in_=pf[:, :], func=AF.Relu),
                          reads=[("pf", bank)], writes=[("xs", rb)])
                    P.add("dve", lambda e, fc=fc, tt=tt, rb=rb: e.tensor_tensor(
                        out=U[:, fc, tt * 512:(tt + 1) * 512], in0=self.rtmp[:, rb, :], in1=self.rtmp[:, rb, :],
                        op=ALU.mult),
                        reads=[("xs", rb)], writes=[("hT", fc, tt)])
            for sl in slabs:
                self.ws.done(sl[2])
            slabs2 = [self.ws.get(w2[:, fg * 8:(fg + 1) * 8, dh * 512:(dh + 1) * 512], 512) for dh in range(2)]
            for t in range(NT):
                for dh in range(2):
                    slot, wkey, _ = slabs2[dh]
                    bank = pfi % 4
                    pfi += 1
                    pf = self.PF[bank]
                    for fc in range(8):
                        P.add("pe", lambda e, pf=pf, slot=slot, fc=fc, t=t: e.matmul(
                            pf[:, :], U[:, fc, t * 128:(t + 1) * 128], ring[:, slot, fc, :],
                            start=(fc == 0), stop=(fc == 7)),
                            reads=[wkey, ("hT", fc, t // 4)], writes=[("pf", bank)])
                    P.add("dve", lambda e, pf=pf, t=t, dh=dh: e.tensor_tensor(
                        out=X[:, t, dh * 512:(dh + 1) * 512], in0=X[:, t, dh * 512:(dh + 1) * 512], in1=pf[:, :],
                        op=ALU.add),
                        reads=[("pf", bank), ("x", t)], writes=[("x", t)])
            for sl in slabs2:
                self.ws.done(sl[2])

    def build(self):
        P = self.P
        self.setup_consts()
        kinds = {k for k, _ in self.layers}
        if "gd" in kinds:
            self.setup_gd_static()
            self.setup_gd_levelmasks()
            for jj in sorted({l // 2 for (k, l) in self.layers if k == "gd"}):
                self.setup_gd_consts(jj)
        if "da" in kinds:
            self.setup_da_static()
            for (k, l) in self.layers:
                if k == "da":
                    self.setup_da_consts(l // 2, l)
        last_stores = []
        for s in range(self.n_seq):
            self.load_x(s)
            for l in self.layers:
                if l[0] == "mlp":
                    self.mlp(l[1])
                elif l[0] == "norm":
                    self.rmsnorm(32 + 8 * l[1])
                elif l[0] == "da":
                    self.diffattn(l[1])
                elif l[0] == "gd":
                    self.gdn(l[1])
            last_stores = self.store_x(s)
        P.add("sp", None, reads=[("xst", q) for q in range(4)])


def layer_plan():
    plan = []
    for i in range(DEPTH):
        plan.append(("da" if i % 2 == 0 else "gd", i))
        plan.append(("mlp", i))
    return plan


def build_program(n_seq=SEQ_PER_CORE, layers=None):
    if layers is None:
        layers = layer_plan()
    nc = bass.Bass("TRN2", target_bir_lowering=False)
    with ExitStack() as es:
        b = Builder(nc, Prog(nc, dry=True), None, n_seq, layers, es)
        b.build()
        future = list(b.ws.requests)
        P = Prog(nc, dry=False)
        b.P = P
        b.ws = WStream(P, b.ring, b.NSLOT, future)
        b.build()
        P.emit(es)
    return nc


WEIGHT_NAMES = ["mix_norm", "mlp_norm", "mlp_w_in", "mlp_w_out", "da_w_in", "da_q_norm", "da_k_norm",
                "da_lambda_q1", "da_lambda_k1", "da_lambda_q2", "da_lambda_k2", "da_sub_norm", "da_w_out",
                "gd_w_in", "gd_conv_w", "gd_a_log", "gd_dt_bias", "gd_out_norm", "gd_w_out"]


def run(inputs, n_seq=SEQ_PER_CORE, layers=None, ncores=NCORES, trace=False):
    nc = build_program(n_seq, layers)
    x = np.ascontiguousarray(np.asarray(inputs["x"], dtype=np.float32))
    weights = {k: np.ascontiguousarray(np.asarray(inputs[k], dtype=np.float32)) for k in WEIGHT_NAMES}
    in_maps = []
    for c in range(ncores):
        m = {"x": x[c * n_seq:(c + 1) * n_seq]}
        m.update(weights)
        in_maps.append(m)
    res = run_bass_kernel_spmd(nc, in_maps, core_ids=list(range(ncores)), trace=trace)
    out = np.concatenate([r["out"] for r in res.results], axis=0)
    return out, res


def kernel(**inputs):
    out, _ = run(inputs)
    return out
```

```python
import math
from contextlib import ExitStack

import numpy as np
import concourse.bass as bass
import concourse.mybir as mybir
from concourse.bass_utils import run_bass_kernel_spmd

F32 = mybir.dt.float32
BF16 = mybir.dt.bfloat16
AF = mybir.ActivationFunctionType
ALU = mybir.AluOpType
AX = mybir.AxisListType

D = 1024
S = 2048
NT = S // 128
DFF = 4096
DEPTH = 4
EPS = 1e-6
NCORES = 8
SEQ_PER_CORE = 4
GD_IN = 4 * 1024 + 16


class Op:
    __slots__ = ("id", "eng", "fn", "deps", "dma", "seq", "signal")


class Prog:
    ENGS = ("pe", "act", "dve", "pool", "sp")

    def __init__(self, nc, dry=False):
        self.nc = nc
        self.dry = dry
        self.ops = []
        self.by_eng = {e: [] for e in self.ENGS}
        self.lw = {}
        self.rd = {}
        self.dma_groups = {}
        self.group_all = set()
        self.psum_last = {}
        self.arena_names = set()
        self.cap = None

    def begin_capture(self):
        self.cap = []

    def end_capture(self):
        c, self.cap = self.cap, None
        return c

    @staticmethod
    def merge(lists):
        lists = [L for L in lists if L]
        idx = [0] * len(lists)
        out = []
        total = sum(len(L) for L in lists)
        while len(out) < total:
            best, bf = None, None
            for i, L in enumerate(lists):
                if idx[i] < len(L):
                    f = (idx[i] + 0.5) / len(L)
                    if bf is None or f < bf:
                        best, bf = i, f
            out.append(lists[best][idx[best]])
            idx[best] += 1
        return out

    def replay(self, lists):
        for rec in self.merge(lists):
            self.add(*rec)

    def add(self, eng, fn, reads=(), writes=(), dma=None):
        if self.dry:
            return None
        if self.cap is not None:
            self.cap.append((eng, fn, tuple(reads), tuple(writes), dma))
            return None
        op = Op()
        op.id = len(self.ops)
        op.eng = eng
        op.fn = fn
        op.dma = dma
        op.seq = 0
        op.signal = False
        if self.arena_names:
            for k in tuple(reads) + tuple(writes):
                nm = k[0] if isinstance(k, tuple) else k
                if nm in self.arena_names:
                    reads = tuple(reads) + ("ARENA",)
                    break
        deps = set()
        for k in reads:
            w = self.lw.get(k)
            if w is not None:
                deps.add(w)
        for k in writes:
            w = self.lw.get(k)
            if w is not None:
                deps.add(w)
            for r in self.rd.get(k, ()):
                deps.add(r)
        for k in reads:
            self.rd.setdefault(k, []).append(op.id)
        for k in writes:
            self.lw[k] = op.id
            self.rd[k] = []
        for k in tuple(reads) + tuple(writes):
            if isinstance(k, tuple) and k[0] in ("pf", "pb"):
                last = self.psum_last.setdefault(k, {})
                for eng2, oid in last.items():
                    if eng2 != eng:
                        deps.add(oid)
                last[eng] = op.id
        deps.discard(op.id)
        if eng == "pe" and dma is None:
            deps = {d for d in deps if not (self.ops[d].eng == "pe" and self.ops[d].dma is None)}
        op.deps = deps
        self.ops.append(op)
        self.by_eng[eng].append(op)
        if dma is not None:
            self.dma_groups.setdefault(dma, []).append(op.id)
        return op.id

    def emit(self, es):
        nc = self.nc
        ops = self.ops
        for op in ops:
            for d in op.deps:
                ops[d].signal = True
        for e in self.ENGS:
            c = 0
            for op in self.by_eng[e]:
                if op.dma is None and op.signal:
                    c += 1
                    op.seq = c
        for g, ids in self.dma_groups.items():
            for i, oid in enumerate(ids):
                ops[oid].seq = i + 1
        eng_sem = {e: es.enter_context(nc.semaphore("s_" + e)) for e in self.ENGS}
        dma_sem = {g: es.enter_context(nc.semaphore("d_" + str(g))) for g in self.dma_groups}
        block = es.enter_context(nc.Block())

        def emit_engine(ename, e):
            waited = {}
            for op in self.by_eng[ename]:
                need = {}
                for d in op.deps:
                    dop = ops[d]
                    if dop.dma is not None:
                        sem = dma_sem[dop.dma]
                        if dop.dma in self.group_all:
                            val = 16 * len(self.dma_groups[dop.dma])
                        else:
                            val = 16 * dop.seq
                    else:
                        sem = eng_sem[dop.eng]
                        val = dop.seq
                    key = id(sem)
                    if key not in need or need[key][1] < val:
                        need[key] = (sem, val)
                for key, (sem, val) in need.items():
                    if waited.get(key, 0) < val:
                        e.wait_ge(sem, val)
                        waited[key] = val
                if op.fn is None:
                    continue
                inst = op.fn(e)
                if op.dma is not None:
                    inst.then_inc(dma_sem[op.dma], 16)
                elif op.signal:
                    inst.then_inc(eng_sem[ename], 1)

        @block.tensor
        def _(e):
            emit_engine("pe", e)

        @block.scalar
        def _(e):
            emit_engine("act", e)

        @block.vector
        def _(e):
            emit_engine("dve", e)

        @block.gpsimd
        def _(e):
            emit_engine("pool", e)

        @block.sync
        def _(e):
            emit_engine("sp", e)


class WStream:
    def __init__(self, P, ring, nslot, lookahead_list=None):
        self.P = P
        self.ring = ring
        self.nslot = nslot
        self.future = lookahead_list
        self.requests = []
        self.issued = 0
        self.released = set()

    def _pump(self):
        if self.P.dry:
            return
        lim = min(len(self.future), len(self.requests) + self.nslot - 1)
        while self.issued < lim:
            j = self.issued
            if j >= self.nslot and (j - self.nslot) not in self.released:
                break
            src, w = self.future[j]
            slot = j % self.nslot
            ring = self.ring
            dst = self.view(slot, w)
            self.P.add("pool", lambda e, dst=dst, src=src: e.dma_start(out=dst, in_=src),
                       writes=[("w", slot)], dma=("w", slot))
            self.issued += 1

    def view(self, slot, w):
        ring = self.ring
        if isinstance(w, tuple):
            a, b = w
            return ring[:, slot, :, :].rearrange("p a b -> p (a b)")[:, 0:a * b].rearrange("p (a b) -> p a b", a=a)
        return ring[:, slot, :, 0:w]

    def get(self, src, w):
        i = len(self.requests)
        self.requests.append((src, w))
        if self.P.dry:
            return 0, ("w", 0), i
        self._pump()
        assert self.issued > i, "weight ring deadlock: release slabs before requesting more"
        slot = i % self.nslot
        return slot, ("w", slot), i

    def done(self, idx):
        self.released.add(idx)
        self._pump()


class Builder:
    def __init__(self, nc, P, ws_future, n_seq, layers, es):
        self.nc = nc
        self.P = P
        self.n_seq = n_seq
        self.layers = layers
        self.es = es
        self.ws_future = ws_future
        self.alloc()

    def sb(self, name, shape, dt):
        return self.es.enter_context(self.nc.sbuf_tensor(name, shape, dt))

    def carve(self, shape, dt):
        esz = 4 if dt == F32 else 2
        n = 1
        for s_ in shape:
            n *= s_
        nbytes = (n * esz + 3) // 4 * 4
        off = self._aoff
        assert off + nbytes <= self.ARENA_BYTES, "arena overflow"
        self._aoff = off + nbytes
        v = self.arena[:, off // 4:(off + nbytes) // 4]
        if dt != F32:
            v = v.bitcast(dt)
        v = v[:, 0:n]
        if len(shape) == 2:
            v = v.rearrange("p (a b) -> p a b", a=shape[0])
        elif len(shape) == 3:
            v = v.rearrange("p (a b c) -> p a b c", a=shape[0], b=shape[1])
        return v

    def alloc(self):
        nc = self.nc
        n_seq = self.n_seq
        d = {}
        d["x"] = nc.dram_tensor("x", [n_seq, S, D], F32, kind="ExternalInput").ap()
        d["out"] = nc.dram_tensor("out", [n_seq, S, D], F32, kind="ExternalOutput").ap()
        specs = [
            ("mix_norm", [4, D]), ("mlp_norm", [4, D]), ("mlp_w_in", [4, D, DFF]), ("mlp_w_out", [4, DFF, D]),
            ("da_w_in", [2, D, 3072]), ("da_q_norm", [2, 64]), ("da_k_norm", [2, 64]),
            ("da_lambda_q1", [2, 64]), ("da_lambda_k1", [2, 64]), ("da_lambda_q2", [2, 64]), ("da_lambda_k2", [2, 64]),
            ("da_sub_norm", [2, 128]), ("da_w_out", [2, D, D]),
            ("gd_w_in", [2, D, GD_IN]), ("gd_conv_w", [2, 4, 3072]), ("gd_a_log", [2, 8]), ("gd_dt_bias", [2, 8]),
            ("gd_out_norm", [2, 128]), ("gd_w_out", [2, D, D]),
        ]
        for name, shape in specs:
            d[name] = nc.dram_tensor(name, shape, F32, kind="ExternalInput").ap()
        self.d = d
        self.NSLOT = 4
        self.X = self.sb("X", [128, NT, D], F32)
        self.xnT = self.sb("xnT", [128, 8, S], BF16)
        self.U = self.sb("U", [128, 2, S], BF16)
        self.ring = self.sb("ring", [128, self.NSLOT, 8, 512], BF16)
        self.xs_f = self.sb("xs_f", [128, 2, 512], F32)
        self.xs = self.xs_f[:, :, :].rearrange("p a b -> p (a b)").bitcast(BF16).rearrange("p (a b) -> p a b", a=2)
        self.rtmp = self.xs_f
        self.ss = self.sb("ss", [128, NT], F32)
        self.rstd = self.sb("rstd", [128, NT], F32)
        self.epsc = self.sb("epsc", [128, 4], F32)
        self.identb = self.sb("identb", [128, 128], BF16)
        self.identf = self.sb("identf", [128, 128], F32)
        self.crow = self.xs_f[0:64, 1, 0:128]
        self.gcol = self.sb("gcol", [128, 64], F32)
        self.ARENA_BYTES = 35840 + 24576
        self.arena = self.sb("arena", [128, self.ARENA_BYTES // 4], F32)
        self._aoff = 0
        self.hT = self.carve([8, S], BF16)
        self._aoff = 0
        self.qT = self.carve([2, S], BF16)
        self.kTz = self.carve([2, 2, S], BF16)
        self.vaug = self.carve([NT, 2, 130], BF16)
        self.pT = self.carve([3, 512], BF16)
        self.qraw = self.carve([2, 512], BF16)
        self.qsq = self.carve([2, 512], BF16)
        self.qrs = self.carve([1, 512], F32)
        self.accS = self.carve([4, 512], F32)
        self.osb = self.carve([2, 4, 128], F32)
        self.obf = self.carve([2, 4, 128], BF16)
        self.fsm = self.carve([2, 24], F32)
        self.da_arena_end = self._aoff
        self._aoff = 0
        g = {}
        g["BA"] = self.carve([NT, 16], F32)
        for nm in ("BETA", "GC", "EG", "EGL", "EKD"):
            g[nm] = self.carve([NT, 8], F32)
        g["GLB"] = g["BA"][:, :, :].rearrange("p a b -> p (a b)")[:, 0:NT * 8].rearrange("p (a b) -> p a b", a=NT)
        g["raw"] = self.carve([2, 516], F32)
        g["acc"] = self.carve([1, 512], F32)
        g["halo"] = self.carve([6, 4], F32)
        g["sT"] = [self.carve([2, 4, 3 * 128], BF16) for _ in range(2)]
        g["sq"] = self.carve([2, 512], BF16)
        g["zs"] = [self.carve([4, 256], BF16) for _ in range(2)]
        g["R"] = [self.carve([4, 4], F32) for _ in range(2)]
        g["SC"] = [self.carve([8, 8], F32) for _ in range(2)]
        g["Sf"] = self.carve([2, 128], F32)
        g["Sb"] = self.carve([2, 128], BF16)
        self.NCH = 4
        for c in range(self.NCH):
            g["kd", c] = self.carve([128], BF16)
            g["kbg", c] = self.carve([128], BF16)
            g["vb", c] = self.carve([128], BF16)
            g["E", c] = self.carve([256], F32)
            g["MA", c] = self.carve([256], BF16)
            g["Lb", c] = self.carve([128], BF16)
            g["QQ", c] = self.carve([256], BF16)
            g["TM", c] = self.carve([256], BF16)
            g["DD", c] = self.carve([2, 256], BF16)
            g["u", c] = self.carve([128], F32)
            g["wT", c] = self.carve([128], BF16)
            g["vn", c] = self.carve([128], BF16)
            g["o", c] = self.carve([128], F32)
            g["og", c] = self.carve([128], BF16)
            g["ps", c] = self.carve([8], F32)
        self.g = g
        self.gd_arena_end = self._aoff
        assert max(self.da_arena_end, self.gd_arena_end) <= self.ARENA_BYTES
        self.convw = self.sb("convw", [128, 2, 96], F32)
        self.gdc = self.sb("gdc", [128, 2, 4], F32)
        self.gsm = self.sb("gsm", [128, 2, 16], F32)
        self.NEGM = self.sb("NEGM", [128, 7, 256], BF16)
        self.trif = self.sb("trif", [128, 128], F32)
        self.sellast = self.sb("sellast", [128, 128], F32)
        self.onesf = self.sb("onesf", [128, 128], F32)
        self.masks = self.sb("masks", [128, 256], F32)
        self.onecol = self.sb("onecol", [128, 2], BF16)
        self.onesbd = self.sb("onesbd", [128, 128], BF16)
        self.negmask = self.sb("negmask", [128, 128], BF16)
        self.cda = self.sb("cda", [128, 16], F32)
        self.lamt = self.xs_f[:, 0, 256:512].rearrange("p (a b) -> p a b", a=4)
        self.lamp = self.sb("lamp", [128, 4], F32)
        self.scr_f = self.xs_f[:, 0, 0:128]
        self.scr_f2 = self.xs_f[:, 0, 128:256]
        self.PF = [self.es.enter_context(nc.psum_tensor("pf%d" % i, [128, 512], F32)) for i in range(6)]
        self.PB = [self.es.enter_context(nc.psum_tensor("pb%d" % i, [128, 1024], BF16)) for i in range(2)]
        self.ws = WStream(self.P, self.ring, self.NSLOT, self.ws_future)
        self.dummy = self.sb("abar", [128, 2], F32)
        self.ARENA_NAMES = {"hT", "qT", "kT", "kTz", "accS", "va", "pT", "qraw", "qsq", "qrs", "osb", "obf", "fsm", "fss", "frs", "vones",
                            "BA", "BETA", "GC", "EG", "EGL", "EKD", "raw", "acc", "halo", "sT", "sq", "zs", "R", "SC", "Sf", "Sb",
                            "kd", "kbg", "vb", "E", "MA", "Lb", "QQ", "TM", "DD", "u", "wT", "vn", "o", "og", "psm"}

    def arena_barrier(self):
        self.P.arena_names = self.ARENA_NAMES
        dummy = self.dummy
        self.P.add("pool", lambda e: e.memset(dummy[:], 0.0), writes=["ARENA"])

    def setup_consts(self):
        P = self.P
        nc = self.nc
        identf, identb = self.identf, self.identb
        P.add("pool", lambda e: e.memset(identf[:], 0.0), writes=["identf"])
        P.add("pool", lambda e: e.memset(self.epsc[:], EPS), writes=["epsc"])
        P.add("pool", lambda e: e.affine_select(out=identf[:], in_=identf[:], pattern=[[-1, 128]],
                                                compare_op=ALU.not_equal, fill=1.0, base=0,
                                                channel_multiplier=1),
              reads=["identf"], writes=["identf"])
        P.add("dve", lambda e: e.tensor_copy(out=identb[:], in_=identf[:]), reads=["identf"], writes=["identb"])
        crow, gcol = self.crow, self.gcol
        mixr = self.d["mix_norm"].rearrange("l (kc p) -> (l kc) p", p=128)
        mlpr = self.d["mlp_norm"].rearrange("l (kc p) -> (l kc) p", p=128)
        P.add("sp", lambda e: e.dma_start(out=crow[0:32, :], in_=mixr), writes=[("xs", 1)], dma="c0")
        P.add("sp", lambda e: e.dma_start(out=crow[32:64, :], in_=mlpr), writes=[("xs", 1)], dma="c1")
        pf = self.PF[0]
        P.add("pe", lambda e: e.transpose(out=pf[:, 0:64], in_=crow[0:64, :], identity=identf[0:64, 0:64]),
              reads=[("xs", 1), "identf"], writes=[("pf", 0)])
        P.add("dve", lambda e: e.tensor_copy(out=gcol[:, :], in_=pf[:, 0:64]), reads=[("pf", 0)], writes=["gcol"])

    def setup_da_consts(self, j, l):
        P = self.P
        d = self.d
        cda, lamt, lamp = self.cda, self.lamt, self.lamp
        c0 = 5 * j
        col = lambda ap: ap.rearrange("(p o) -> p o", o=1)
        for half in range(2):
            P.add("sp", lambda e, half=half: e.dma_start(out=cda[64 * half:64 * half + 64, c0:c0 + 1], in_=col(d["da_q_norm"][j])),
                  writes=[("cda", j)], dma="cq%d%d" % (j, half))
            P.add("sp", lambda e, half=half: e.dma_start(out=cda[64 * half:64 * half + 64, c0 + 1:c0 + 2], in_=col(d["da_k_norm"][j])),
                  writes=[("cda", j)], dma="ck%d%d" % (j, half))
        P.add("sp", lambda e: e.dma_start(out=cda[:, c0 + 4:c0 + 5], in_=col(d["da_sub_norm"][j])),
              writes=[("cda", j)], dma="cs%d" % j)
        for i, nm in enumerate(["da_lambda_q1", "da_lambda_k1", "da_lambda_q2", "da_lambda_k2"]):
            P.add("sp", lambda e, i=i, nm=nm: e.dma_start(out=lamt[:, i, :], in_=d[nm][j:j + 1, :].partition_broadcast(128)),
                  writes=[("xs", 0)], dma="cl%d%d" % (j, i))
        lam_init = 0.8 - 0.6 * math.exp(-0.3 * l)
        for i in range(2):
            P.add("dve", lambda e, i=i: e.tensor_tensor(out=lamt[:, 2 * i, :], in0=lamt[:, 2 * i, :], in1=lamt[:, 2 * i + 1, :], op=ALU.mult),
                  reads=[("xs", 0)], writes=[("xs", 0)])
            P.add("dve", lambda e, i=i: e.reduce_sum(out=lamp[:, i:i + 1], in_=lamt[:, 2 * i, :], axis=AX.X),
                  reads=[("xs", 0)], writes=["lamp"])
        P.add("act", lambda e: e.activation(out=lamp[:, 0:2], in_=lamp[:, 0:2], func=AF.Exp), reads=["lamp"], writes=["lamp"])
        P.add("dve", lambda e: e.tensor_tensor(out=cda[:, c0 + 2:c0 + 3], in0=lamp[:, 0:1], in1=lamp[:, 1:2], op=ALU.subtract),
              reads=["lamp"], writes=[("cda", j)])
        P.add("dve", lambda e: e.tensor_scalar(out=cda[:, c0 + 2:c0 + 3], in0=cda[:, c0 + 2:c0 + 3], scalar1=lam_init, scalar2=None, op0=ALU.add),
              reads=[("cda", j)], writes=[("cda", j)])
        P.add("dve", lambda e: e.tensor_scalar(out=cda[:, c0 + 3:c0 + 4], in0=cda[:, c0 + 2:c0 + 3], scalar1=-1.0, scalar2=None, op0=ALU.mult),
              reads=[("cda", j)], writes=[("cda", j)])
        P.add("dve", lambda e: e.tensor_scalar(out=cda[:, c0:c0 + 1], in0=cda[:, c0:c0 + 1], scalar1=0.125, scalar2=None, op0=ALU.mult),
              reads=[("cda", j)], writes=[("cda", j)])
        P.add("dve", lambda e: e.tensor_scalar(out=cda[:, c0 + 4:c0 + 5], in0=cda[:, c0 + 4:c0 + 5], scalar1=1.0 - lam_init, scalar2=None, op0=ALU.mult),
              reads=[("cda", j)], writes=[("cda", j)])

    def setup_da_static(self):
        P = self.P
        onesbd, negmask, vaug = self.onesbd, self.negmask, self.vaug
        scr = self.scr_f
        P.add("pool", lambda e: e.memset(scr[:], 0.0), writes=[("xs", 0)])
        P.add("pool", lambda e: e.memset(scr[0:64, 0:64], 1.0 / 64), reads=[("xs", 0)], writes=[("xs", 0)])
        P.add("pool", lambda e: e.memset(scr[64:128, 64:128], 1.0 / 64), reads=[("xs", 0)], writes=[("xs", 0)])
        P.add("dve", lambda e: e.tensor_copy(out=onesbd[:], in_=scr[:]), reads=[("xs", 0)], writes=["onesbd"])
        scr2 = self.scr_f2
        P.add("pool", lambda e: e.memset(scr2[:], 0.0), writes=[("xs", 0)])
        P.add("pool", lambda e: e.affine_select(out=scr2[:], in_=scr2[:], pattern=[[1, 128]], compare_op=ALU.is_ge,
                                                fill=-30000.0, base=0, channel_multiplier=-1),
              reads=[("xs", 0)], writes=[("xs", 0)])
        P.add("dve", lambda e: e.tensor_copy(out=negmask[:], in_=scr2[:]), reads=[("xs", 0)], writes=["negmask"])

    def diffattn(self, l):
        P = self.P
        j = l // 2
        X, xnT, U, ring = self.X, self.xnT, self.U, self.ring
        qT, kTz, vaug, pT, accS = self.qT, self.kTz, self.vaug, self.pT, self.accS
        qraw, qsq, qrs = self.qraw, self.qsq, self.qrs
        cda, onesbd, negmask, identb = self.cda, self.onesbd, self.negmask, self.identb
        osb, obf, fsm = self.osb, self.obf, self.fsm
        PF, PB = self.PF, self.PB
        c0 = 5 * j
        win = self.d["da_w_in"][j].rearrange("(kc p) f -> p kc f", p=128)
        self.rmsnorm(8 * l)
        self.arena_barrier()
        P.add("pool", lambda e: e.memset(vaug[:, :, :, 128:130], 1.0), reads=[("xnT", 0, 0)], writes=["vones"])
        P.add("pool", lambda e: e.memset(kTz[64:128, 0, :, :], 0.0), reads=[("xnT", 0, 0)], writes=["kTz"])
        P.add("pool", lambda e: e.memset(kTz[0:64, 1, :, :], 0.0), reads=[("xnT", 0, 0)], writes=["kTz"])
        cnt = {"pf": 0, "nb": 0, "pt": 0, "fin": 0}
        for hp in range(4):
            sq = self.ws.get(win[:, :, hp * 256:(hp + 1) * 256], 256)
            sk = self.ws.get(win[:, :, 1024 + hp * 256:1024 + (hp + 1) * 256], 256)
            sv = self.ws.get(win[:, :, 2048 + hp * 256:2048 + (hp + 1) * 256], 256)
            jobs = [(which, slab, gcolq, hh, tt) for which, slab, gcolq in (("q", sq, c0), ("k", sk, c0 + 1))
                    for hh in range(2) for tt in range(4)]
            jstate = {}

            def emit_proj(i):
                which, slab, gcolq, hh, tt = jobs[i]
                slot, wkey, _ = slab
                bank = cnt["pf"] % 2
                cnt["pf"] += 1
                nb = cnt["nb"] % 2
                cnt["nb"] += 1
                jstate[i] = (bank, nb)
                pf = PF[bank]
                for kc in range(8):
                    P.add("pe", lambda e, pf=pf, slot=slot, kc=kc, hh=hh, tt=tt: e.matmul(
                        pf[:, :], ring[:, slot, kc, hh * 128:(hh + 1) * 128], xnT[:, kc, tt * 512:(tt + 1) * 512],
                        start=(kc == 0), stop=(kc == 7)),
                        reads=[wkey] + [("xnT", t, kc) for t in range(4 * tt, 4 * tt + 4)], writes=[("pf", bank)])

            def emit_norm(i):
                which, slab, gcolq, hh, tt = jobs[i]
                bank, nb = jstate[i]
                pf = PF[bank]
                P.add("act", lambda e, pf=pf, nb=nb: e.activation(out=qraw[:, nb, :], in_=pf[:, :], func=AF.Copy),
                      reads=[("pf", bank)], writes=[("qraw", nb)])
                P.add("dve", lambda e, nb=nb: e.tensor_tensor(out=qsq[:, nb, :], in0=qraw[:, nb, :], in1=qraw[:, nb, :], op=ALU.mult),
                      reads=[("qraw", nb)], writes=[("qsq", nb)])
                mbank = 2 + nb
                pm = PF[mbank]
                P.add("pe", lambda e, pm=pm, nb=nb: e.matmul(pm[:, :], onesbd[:, :], qsq[:, nb, :], start=True, stop=True),
                      reads=[("qsq", nb), "onesbd"], writes=[("pf", mbank)])
                P.add("act", lambda e, pm=pm, nb=nb: e.activation(out=qrs[:, 0, :], in_=pm[:, :], func=AF.Ln, bias=self.epsc[:, 0:1], scale=1.0),
                      reads=[("pf", mbank), "epsc"], writes=[("qrs", 0)])
                P.add("act", lambda e, nb=nb: e.activation(out=qrs[:, 0, :], in_=qrs[:, 0, :], func=AF.Exp, scale=-0.5),
                      reads=[("qrs", 0)], writes=[("qrs", 0)])
                if which == "q":
                    P.add("dve", lambda e, nb=nb, hh=hh, tt=tt, gcolq=gcolq: e.scalar_tensor_tensor(
                        out=qT[:, hh, tt * 512:(tt + 1) * 512], in0=qraw[:, nb, :], scalar=cda[:, gcolq:gcolq + 1],
                        in1=qrs[:, 0, :], op0=ALU.mult, op1=ALU.mult),
                        reads=[("qraw", nb), ("qrs", 0), ("cda", j)], writes=[("qT", hh, tt)])
                else:
                    for c in range(2):
                        pl, ph = 64 * c, 64 * c + 64
                        P.add("dve", lambda e, nb=nb, hh=hh, tt=tt, gcolq=gcolq, c=c, pl=pl, ph=ph: e.scalar_tensor_tensor(
                            out=kTz[pl:ph, c, hh, tt * 512:(tt + 1) * 512], in0=qraw[pl:ph, nb, :], scalar=cda[pl:ph, gcolq:gcolq + 1],
                            in1=qrs[pl:ph, 0, :], op0=ALU.mult, op1=ALU.mult),
                            reads=[("qraw", nb), ("qrs", 0), ("cda", j), "kTz"], writes=[("kT", hh, tt, c)])

            emit_proj(0)
            for i in range(len(jobs)):
                if i + 1 < len(jobs):
                    emit_proj(i + 1)
                emit_norm(i)
            slot, wkey, _ = sv
            for t in range(NT):
                bank = cnt["pf"] % 2
                cnt["pf"] += 1
                pf = PF[bank]
                for kc in range(8):
                    P.add("pe", lambda e, pf=pf, slot=slot, kc=kc, t=t: e.matmul(
                        pf[:, 0:256], xnT[:, kc, t * 128:(t + 1) * 128], ring[:, slot, kc, 0:256],
                        start=(kc == 0), stop=(kc == 7)),
                        reads=[wkey, ("xnT", t, kc)], writes=[("pf", bank)])
                P.add("act", lambda e, pf=pf, t=t: e.activation(
                    out=vaug[:, t, :, 0:128], in_=pf[:, 0:256].rearrange("p (h d) -> p h d", h=2), func=AF.Copy),
                    reads=[("pf", bank), "vones"], writes=[("va", t)])
            for sl in (sq, sk, sv):
                self.ws.done(sl[2])
            ATT, FIN = [], []
            for hh in range(2):
                h = 2 * hp + hh
                for qt in range(4):
                    P.begin_capture()
                    first_in_bank = {}
                    units = [(c, kb) for c in range(2) for kb in range(4 * qt + 4)]
                    ubank = {}

                    def emit_scores(u):
                        c, kb = units[u]
                        r = kb - 4 * qt
                        col0 = max(r, 0) * 128
                        bank = cnt["pf"] % 2
                        cnt["pf"] += 1
                        ubank[u] = bank
                        ps = PF[bank]
                        krd = [("kT", hh, kb // 4, c)]
                        qrd = [("qT", hh, qt)]
                        if r >= 0:
                            P.add("pe", lambda e, ps=ps, c=c, hh=hh, kb=kb, qt=qt, col0=col0: e.matmul(
                                ps[:, col0:col0 + 128], kTz[:, c, hh, kb * 128:(kb + 1) * 128],
                                qT[:, hh, qt * 512 + col0:qt * 512 + col0 + 128], start=True, stop=False,
                                skip_group_check=True),
                                reads=krd + qrd, writes=[("pf", bank)])
                            P.add("pe", lambda e, ps=ps, col0=col0: e.matmul(
                                ps[:, col0:col0 + 128], identb[:, :], negmask[:, :], start=False, stop=True,
                                skip_group_check=True),
                                reads=["identb", "negmask"], writes=[("pf", bank)])
                            if col0 + 128 < 512:
                                P.add("pe", lambda e, ps=ps, c=c, hh=hh, kb=kb, qt=qt, col0=col0: e.matmul(
                                    ps[:, col0 + 128:512], kTz[:, c, hh, kb * 128:(kb + 1) * 128],
                                    qT[:, hh, qt * 512 + col0 + 128:qt * 512 + 512], start=True, stop=True,
                                    skip_group_check=True),
                                    reads=krd + qrd, writes=[("pf", bank)])
                        else:
                            P.add("pe", lambda e, ps=ps, c=c, hh=hh, kb=kb, qt=qt: e.matmul(
                                ps[:, :], kTz[:, c, hh, kb * 128:(kb + 1) * 128],
                                qT[:, hh, qt * 512:qt * 512 + 512], start=True, stop=True, skip_group_check=True),
                                reads=krd + qrd, writes=[("pf", bank)])

                    def emit_exp_pv(u):
                        c, kb = units[u]
                        r = kb - 4 * qt
                        col0 = max(r, 0) * 128
                        bank = ubank[u]
                        ps = PF[bank]
                        pi = cnt["pt"] % 3
                        cnt["pt"] += 1
                        P.add("act", lambda e, ps=ps, pi=pi, col0=col0: e.activation(
                            out=pT[:, pi, col0:512], in_=ps[:, col0:512], func=AF.Exp),
                            reads=[("pf", bank)], writes=[("pT", pi)])
                        for rr in range(max(r, 0), 4):
                            abank = 2 + 2 * c + rr // 2
                            off = (rr % 2) * 256
                            st = abank not in first_in_bank
                            first_in_bank[abank] = True
                            P.add("pe", lambda e, abank=abank, off=off, pi=pi, rr=rr, kb=kb, st=st, hh=hh, qt=qt: e.matmul(
                                PF[abank][:, off:off + 129], pT[:, pi, rr * 128:(rr + 1) * 128], vaug[:, kb, hh, 0:129],
                                start=st, stop=(kb == 4 * qt + rr), skip_group_check=True),
                                reads=[("pT", pi), ("va", kb), "vones"], writes=[("pf", abank)])

                    emit_scores(0)
                    for u in range(len(units)):
                        if u + 1 < len(units):
                            emit_scores(u + 1)
                        emit_exp_pv(u)
                    for b in range(4):
                        srcv = PF[2 + b][:, :].rearrange("p (s w) -> p s w", s=2)[:, :, 0:129]
                        dstv = accS[:, b, :].rearrange("p (s w) -> p s w", s=2)[:, :, 0:129]
                        if b % 2 == 0:
                            P.add("act", lambda e, srcv=srcv, dstv=dstv: e.activation(out=dstv, in_=srcv, func=AF.Copy),
                                  reads=[("pf", 2 + b)], writes=[("accS", b)])
                        else:
                            P.add("dve", lambda e, srcv=srcv, dstv=dstv: e.tensor_copy(out=dstv, in_=srcv),
                                  reads=[("pf", 2 + b)], writes=[("accS", b)])
                    ATT.append(P.end_capture())
                    P.begin_capture()
                    fi = cnt["fin"] % 2
                    cnt["fin"] += 1
                    AK = [("accS", b) for b in range(4)]
                    FK = ("fsm", fi)
                    slots = accS[:, :, :].rearrange("p b (s w) -> p (b s) w", s=2)
                    P.add("dve", lambda e, fi=fi, slots=slots: e.reciprocal(out=fsm[:, fi, 0:8], in_=slots[:, :, 128]),
                          reads=AK, writes=[FK])
                    P.add("dve", lambda e, fi=fi: e.tensor_scalar(out=fsm[:, fi, 8:12], in0=fsm[:, fi, 4:8], scalar1=cda[:, c0 + 3:c0 + 4], scalar2=None, op0=ALU.mult),
                          reads=[FK, ("cda", j)], writes=[FK])
                    for rr in range(4):
                        P.add("act", lambda e, fi=fi, rr=rr, slots=slots: e.activation(out=osb[:, fi, rr, :], in_=slots[:, rr, 0:128], func=AF.Copy,
                                                                                        scale=fsm[:, fi, rr:rr + 1]),
                              reads=AK + [FK], writes=[("osb", fi, rr)])
                    for rr in range(4):
                        P.add("dve", lambda e, fi=fi, rr=rr, slots=slots: e.scalar_tensor_tensor(
                            out=osb[:, fi, rr, :], in0=slots[:, 4 + rr, 0:128], scalar=fsm[:, fi, 8 + rr:9 + rr], in1=osb[:, fi, rr, :],
                            op0=ALU.mult, op1=ALU.add),
                            reads=AK + [FK, ("osb", fi, rr)], writes=[("osb", fi, rr)])
                    for rr in range(4):
                        P.add("act", lambda e, fi=fi, rr=rr: e.activation(out=obf[:, fi, rr, :], in_=osb[:, fi, rr, :], func=AF.Square,
                                                                          accum_out=fsm[:, fi, 12 + rr:13 + rr]),
                              reads=[("osb", fi, rr)], writes=[("obf", fi, rr), ("fss", fi, rr)])
                    P.add("act", lambda e, fi=fi: e.activation(out=fsm[:, fi, 16:20], in_=fsm[:, fi, 12:16], func=AF.Ln,
                                                               bias=self.epsc[:, 0:1], scale=1.0 / 128),
                          reads=[("fss", fi, rr) for rr in range(4)] + ["epsc"], writes=[("frs", fi)])
                    P.add("act", lambda e, fi=fi: e.activation(out=fsm[:, fi, 16:20], in_=fsm[:, fi, 16:20], func=AF.Exp, scale=-0.5),
                          reads=[("frs", fi)], writes=[("frs", fi)])
                    for rr in range(4):
                        P.add("dve", lambda e, fi=fi, rr=rr: e.tensor_scalar(out=obf[:, fi, rr, :], in0=osb[:, fi, rr, :], scalar1=fsm[:, fi, 16 + rr:17 + rr],
                                                                              scalar2=None, op0=ALU.mult),
                              reads=[("osb", fi, rr), ("frs", fi), ("obf", fi, rr)], writes=[("obf", fi, rr)])
                    pb = PB[fi]
                    for rr in range(4):
                        P.add("pe", lambda e, pb=pb, fi=fi, rr=rr: e.transpose(out=pb[:, rr * 128:(rr + 1) * 128], in_=obf[:, fi, rr, :], identity=identb[:]),
                              reads=[("obf", fi, rr), "identb"], writes=[("pb", fi)])
                    P.add("dve", lambda e, pb=pb, hh=hh, qt=qt: e.tensor_scalar(
                        out=U[:, hh, qt * 512:(qt + 1) * 512], in0=pb[:, 0:512], scalar1=cda[:, c0 + 4:c0 + 5], scalar2=None, op0=ALU.mult),
                        reads=[("pb", fi), ("cda", j)], writes=[("U", hh, qt)])
                    FIN.append(P.end_capture())
            P.replay([ATT[0]])
            for i in range(1, 8):
                P.replay([ATT[i], FIN[i - 1]])
            P.replay([FIN[7]])
            wsrc = self.d["da_w_out"][j][hp * 256:(hp + 1) * 256, :].rearrange("(kc p) f -> p kc f", p=128)
            so = self.ws.get(wsrc, (2, 1024))
            wv = self.ws.view(so[0], (2, 1024))
            for t in range(NT):
                for dh in range(2):
                    bank = cnt["pf"] % 2
                    cnt["pf"] += 1
                    pf = PF[bank]
                    for hc in range(2):
                        P.add("pe", lambda e, pf=pf, wv=wv, hc=hc, t=t, dh=dh: e.matmul(
                            pf[:, :], U[:, hc, t * 128:(t + 1) * 128], wv[:, hc, dh * 512:(dh + 1) * 512], start=(hc == 0), stop=(hc == 1)),
                            reads=[so[1], ("U", hc, t // 4)], writes=[("pf", bank)])
                    P.add("dve", lambda e, pf=pf, t=t, dh=dh: e.tensor_tensor(
                        out=X[:, t, dh * 512:(dh + 1) * 512], in0=X[:, t, dh * 512:(dh + 1) * 512], in1=pf[:, :], op=ALU.add),
                        reads=[("pf", bank), ("x", t)], writes=[("x", t)])
            self.ws.done(so[2])

    def setup_gd_static(self):
        P = self.P
        trif, sellast, onesf, masks, onecol = self.trif, self.sellast, self.onesf, self.masks, self.onecol
        P.add("pool", lambda e: e.memset(onesf[:], 1.0), writes=["onesf"])
        P.add("pool", lambda e: e.memset(onecol[:], 1.0), writes=["onecol"])
        P.add("pool", lambda e: e.memset(trif[:], 1.0), writes=["trif"])
        P.add("pool", lambda e: e.affine_select(out=trif[:], in_=trif[:], pattern=[[1, 128]], compare_op=ALU.is_ge,
                                                fill=0.0, base=0, channel_multiplier=-1), reads=["trif"], writes=["trif"])
        P.add("pool", lambda e: e.memset(sellast[:], 1.0), writes=["sellast"])
        P.add("pool", lambda e: e.affine_select(out=sellast[:], in_=sellast[:], pattern=[[0, 128]], compare_op=ALU.is_ge,
                                                fill=0.0, base=-127, channel_multiplier=1), reads=["sellast"], writes=["sellast"])
        P.add("pool", lambda e: e.memset(masks[:], 0.0), writes=["masks"])
        P.add("pool", lambda e: e.affine_select(out=masks[:, 0:128], in_=masks[:, 0:128], pattern=[[1, 128]], compare_op=ALU.is_ge,
                                                fill=-30000.0, base=-1, channel_multiplier=-1), reads=["masks"], writes=["masks"])
        P.add("pool", lambda e: e.affine_select(out=masks[:, 128:256], in_=masks[:, 128:256], pattern=[[1, 128]], compare_op=ALU.is_ge,
                                                fill=-30000.0, base=0, channel_multiplier=-1), reads=["masks"], writes=["masks"])

    def setup_gd_levelmasks(self):
        P = self.P
        NEGM = self.NEGM
        scrA = lambda nb: self.xs_f[0:nb, 0, 0:128]
        scrC = lambda nb: self.xs_f[0:nb, 0, 128:256]
        K0 = [("xs", 0)]
        for k in range(7):
            B, half = 2 ** (k + 1), 2 ** k
            nb = 128 // B
            A, C = scrA(nb), scrC(nb)
            P.add("pool", lambda e, nb=nb: e.memset(self.xs_f[0:nb, 0, 0:256], 1.0), writes=K0)
            P.add("pool", lambda e, A=A, B=B, half=half: e.affine_select(out=A, in_=A, pattern=[[1, 128]], compare_op=ALU.is_ge, fill=0.0,
                                                                       base=-half, channel_multiplier=-B), reads=K0, writes=K0)
            P.add("pool", lambda e, A=A, B=B: e.affine_select(out=A, in_=A, pattern=[[-1, 128]], compare_op=ALU.is_ge, fill=0.0,
                                                             base=B - 1, channel_multiplier=B), reads=K0, writes=K0)
            P.add("pool", lambda e, C=C, B=B: e.affine_select(out=C, in_=C, pattern=[[1, 128]], compare_op=ALU.is_ge, fill=0.0,
                                                             base=0, channel_multiplier=-B), reads=K0, writes=K0)
            P.add("pool", lambda e, C=C, B=B, half=half: e.affine_select(out=C, in_=C, pattern=[[-1, 128]], compare_op=ALU.is_ge, fill=0.0,
                                                                       base=half - 1, channel_multiplier=B), reads=K0, writes=K0)
            pf = self.PF[1]
            P.add("pe", lambda e, pf=pf, A=A, C=C: e.matmul(pf[:, 0:128], A, C, start=True, stop=True, skip_group_check=True),
                  reads=K0, writes=[("pf", 1)])
            P.add("pe", lambda e, pf=pf, A=A, C=C: e.matmul(pf[:, 128:256], C, A, start=True, stop=True, skip_group_check=True),
                  reads=K0, writes=[("pf", 1)])
            P.add("act", lambda e, pf=pf, k=k: e.activation(out=NEGM[:, k, :], in_=pf[:, 0:256], func=AF.Copy),
                  reads=[("pf", 1)], writes=["NEGM"])

    def setup_gd_consts(self, j):
        P = self.P
        d = self.d
        convw, gdc, identf = self.convw, self.gdc, self.identf
        crow96 = self.xs_f[0:96, 1, 0:128]
        rows = d["gd_conv_w"][j].rearrange("k (c p) -> (k c) p", p=128)
        P.add("sp", lambda e: e.dma_start(out=crow96, in_=rows), writes=[("xs", 1)], dma="gcw%d" % j)
        pf = self.PF[0]
        P.add("pe", lambda e: e.transpose(out=pf[:, 0:96], in_=crow96, identity=identf[0:96, 0:96]),
              reads=[("xs", 1), "identf"], writes=[("pf", 0)])
        P.add("dve", lambda e: e.tensor_copy(out=convw[:, j, :], in_=pf[:, 0:96]), reads=[("pf", 0)], writes=[("convw", j)])
        col = lambda ap: ap.rearrange("(p o) -> p o", o=1)
        P.add("sp", lambda e: e.dma_start(out=gdc[:, j, 0:1], in_=col(d["gd_out_norm"][j])), writes=[("gdc", j)], dma="gon%d" % j)
        gsm = self.gsm
        P.add("sp", lambda e: e.dma_start(out=gsm[:, j, 0:8], in_=d["gd_a_log"][j:j + 1, :].partition_broadcast(128)),
              writes=[("gsm", j)], dma="gal%d" % j)
        P.add("sp", lambda e: e.dma_start(out=gsm[:, j, 8:16], in_=d["gd_dt_bias"][j:j + 1, :].partition_broadcast(128)),
              writes=[("gsm", j)], dma="gdt%d" % j)
        P.add("act", lambda e: e.activation(out=gsm[:, j, 0:8], in_=gsm[:, j, 0:8], func=AF.Exp), reads=[("gsm", j)], writes=[("gsm", j)])
        P.add("dve", lambda e: e.tensor_scalar(out=gsm[:, j, 0:8], in0=gsm[:, j, 0:8], scalar1=-1.0, scalar2=None, op0=ALU.mult),
              reads=[("gsm", j)], writes=[("gsm", j)])

    def gdn(self, l):
        P = self.P
        j = l // 2
        g = self.g
        X, xnT, U, ring = self.X, self.xnT, self.U, self.ring
        identb, identf = self.identb, self.identf
        PF, PB = self.PF, self.PB
        epsc = self.epsc
        DKS = 128.0 ** -0.5
        win = self.d["gd_w_in"][j].rearrange("(kc p) f -> p kc f", p=128)
        self.rmsnorm(8 * l)
        self.arena_barrier()
        XN0 = [("xnT", 0, 0)]
        cnt = {"pf": 0, "rb": 0, "sq": 0}
        BA, BETA, GC, EG, GLB, EGL, EKD = (g[k] for k in ("BA", "BETA", "GC", "EG", "GLB", "EGL", "EKD"))
        flat = lambda v: v.rearrange("p a b -> p (a b)")

        sba = self.ws.get(win[:, :, 4096:4112], 16)
        slot_ba, wkey_ba, _ = sba
        pf_ba = PF[0]
        for t in range(NT):
            for kc in range(8):
                P.add("pe", lambda e, t=t, kc=kc: e.matmul(pf_ba[:, t * 16:(t + 1) * 16], xnT[:, kc, t * 128:(t + 1) * 128],
                                                          ring[:, slot_ba, kc, 0:16], start=(kc == 0), stop=(kc == 7),
                                                          skip_group_check=True),
                      reads=[wkey_ba, ("xnT", t, kc)], writes=[("pf", 0)])
        self.ws.done(sba[2])
        P.add("dve", lambda e: e.tensor_copy(out=flat(BA), in_=pf_ba[:, 0:256]), reads=[("pf", 0)] + XN0, writes=["BA"])
        P.add("act", lambda e: e.activation(out=BETA, in_=BA[:, :, 0:8], func=AF.Exp, scale=-1.0), reads=["BA"] + XN0, writes=["BETA"])
        P.add("act", lambda e: e.activation(out=BETA, in_=BETA, func=AF.Ln, bias=1.0), reads=["BETA"], writes=["BETA"])
        P.add("act", lambda e: e.activation(out=BETA, in_=BETA, func=AF.Exp, scale=-1.0), reads=["BETA"], writes=["BETA"])
        gsm = self.gsm
        for t in range(NT):
            P.add("dve", lambda e, t=t: e.tensor_tensor(out=GC[:, t, :], in0=BA[:, t, 8:16], in1=gsm[:, j, 8:16], op=ALU.add),
                  reads=["BA", ("gsm", j)] + XN0, writes=["GC"])
        P.add("act", lambda e: e.activation(out=GC, in_=GC, func=AF.Exp), reads=["GC"], writes=["GC"])
        P.add("act", lambda e: e.activation(out=GC, in_=GC, func=AF.Ln, bias=1.0), reads=["GC"], writes=["GC"])
        for t in range(NT):
            P.add("dve", lambda e, t=t: e.tensor_tensor(out=GLB[:, t, :], in0=GC[:, t, :], in1=gsm[:, j, 0:8], op=ALU.mult),
                  reads=["GC", "BETA", ("gsm", j)] + XN0, writes=["BA"])
        pf1 = PF[1]
        for t in range(NT):
            P.add("pe", lambda e, t=t: e.matmul(pf1[:, t * 8:(t + 1) * 8], self.trif[:, :], GLB[:, t, :], start=True, stop=True,
                                                skip_group_check=True),
                  reads=["BA", "trif"], writes=[("pf", 1)])
        P.add("dve", lambda e: e.tensor_copy(out=flat(GC), in_=pf1[:, 0:128]), reads=[("pf", 1)], writes=["GC"])
        P.add("act", lambda e: e.activation(out=EG, in_=GC, func=AF.Exp), reads=["GC"] + XN0, writes=["EG"])
        for t in range(NT):
            P.add("pe", lambda e, t=t: e.matmul(pf1[:, 128 + t * 8:128 + (t + 1) * 8], self.sellast[:, :], GC[:, t, :], start=True, stop=True,
                                                skip_group_check=True),
                  reads=["GC", "sellast"], writes=[("pf", 1)])
        P.add("dve", lambda e: e.tensor_copy(out=flat(GLB), in_=pf1[:, 128:256]), reads=[("pf", 1)], writes=["BA"])
        P.add("act", lambda e: e.activation(out=EGL, in_=GLB, func=AF.Exp), reads=["BA"] + XN0, writes=["EGL"])
        P.add("dve", lambda e: e.tensor_tensor(out=EKD, in0=GLB, in1=GC, op=ALU.subtract), reads=["BA", "GC"] + XN0, writes=["EKD"])
        P.add("act", lambda e: e.activation(out=EKD, in_=EKD, func=AF.Exp), reads=["EKD"], writes=["EKD"])

        raw, acc, halo, sq, Sf, Sb = (g[k] for k in ("raw", "acc", "halo", "sq", "Sf", "Sb"))
        convw = self.convw
        NEGM = self.NEGM
        for hp in range(4):
            slabs = {}
            for wi, which in enumerate(("q", "k", "v", "z")):
                slabs[which] = self.ws.get(win[:, :, wi * 1024 + hp * 256: wi * 1024 + (hp + 1) * 256], 256)
            P.add("pool", lambda e: e.memset(flat(Sf), 0.0), reads=XN0, writes=[("Sf", 0), ("Sf", 1)])
            P.add("pool", lambda e: e.memset(flat(Sb), 0.0), reads=XN0, writes=[("Sb", 0), ("Sb", 1)])
            P.add("pool", lambda e: e.memset(flat(halo), 0.0), reads=XN0, writes=[("halo", i) for i in range(6)])
            FE, PREP, SCAN = {}, {}, {}
            for gi in range(4):
                gp = gi % 2
                sT, zs, R, SC = g["sT"][gp], g["zs"][gp], g["R"][gp], g["SC"][gp]
                P.begin_capture()
                for wi, which in enumerate(("k", "q", "v")):
                    slot, wkey, _ = slabs[which]
                    cbase = {"q": 0, "k": 8, "v": 16}[which]
                    for hh in range(2):
                        h = 2 * hp + hh
                        pf = PF[0]
                        rb = cnt["rb"] % 2
                        cnt["rb"] += 1
                        hi = wi * 2 + hh
                        for kc in range(8):
                            P.add("pe", lambda e, pf=pf, slot=slot, kc=kc, hh=hh, gi=gi: e.matmul(
                                pf[:, :], ring[:, slot, kc, hh * 128:(hh + 1) * 128], xnT[:, kc, gi * 512:(gi + 1) * 512],
                                start=(kc == 0), stop=(kc == 7)),
                                reads=[wkey] + [("xnT", t, kc) for t in range(4 * gi, 4 * gi + 4)], writes=[("pf", 0)])
                        P.add("dve", lambda e, rb=rb, hi=hi: e.tensor_copy(out=raw[:, rb, 0:3], in_=halo[:, hi, 0:3]),
                              reads=[("halo", hi)], writes=[("raw", rb)])
                        P.add("act", lambda e, pf=pf, rb=rb: e.activation(out=raw[:, rb, 3:515], in_=pf[:, :], func=AF.Copy),
                              reads=[("pf", 0), ("raw", rb)], writes=[("raw", rb)])
                        if gi < 3:
                            P.add("dve", lambda e, rb=rb, hi=hi: e.tensor_copy(out=halo[:, hi, 0:3], in_=raw[:, rb, 512:515]),
                                  reads=[("raw", rb)], writes=[("halo", hi)])
                        cc = cbase + h
                        P.add("dve", lambda e, rb=rb, cc=cc: e.tensor_scalar(
                            out=acc[:, 0, :], in0=raw[:, rb, 0:512], scalar1=convw[:, j, cc:cc + 1], scalar2=None, op0=ALU.mult),
                            reads=[("raw", rb), ("convw", j)], writes=[("acc", 0)])
                        for tap in range(1, 4):
                            P.add("dve", lambda e, rb=rb, cc=cc, tap=tap: e.scalar_tensor_tensor(
                                out=acc[:, 0, :], in0=raw[:, rb, tap:tap + 512], scalar=convw[:, j, tap * 24 + cc:tap * 24 + cc + 1],
                                in1=acc[:, 0, :], op0=ALU.mult, op1=ALU.add),
                                reads=[("raw", rb), ("acc", 0), ("convw", j)], writes=[("acc", 0)])
                        sgb = raw[:, rb, 0:512]
                        P.add("act", lambda e, sgb=sgb: e.activation(out=sgb, in_=acc[:, 0, :], func=AF.Exp, scale=-1.0),
                              reads=[("acc", 0), ("raw", rb), ("halo", hi)], writes=[("raw", rb)])
                        P.add("act", lambda e, sgb=sgb: e.activation(out=sgb, in_=sgb, func=AF.Ln, bias=1.0), reads=[("raw", rb)], writes=[("raw", rb)])
                        P.add("act", lambda e, sgb=sgb: e.activation(out=sgb, in_=sgb, func=AF.Exp, scale=-1.0), reads=[("raw", rb)], writes=[("raw", rb)])
                        P.add("dve", lambda e, sgb=sgb, hh=hh, wi=wi, sT=sT: e.tensor_tensor(
                            out=sT[:, hh, :, wi * 128:(wi + 1) * 128], in0=acc[:, 0, :].rearrange("p (a b) -> p a b", a=4),
                            in1=sgb.rearrange("p (a b) -> p a b", a=4), op=ALU.mult),
                            reads=[("acc", 0), ("raw", rb)], writes=[("sT", gp, hh, wi)])
                        if which in ("k", "q"):
                            sb_ = cnt["sq"] % 2
                            cnt["sq"] += 1
                            P.add("pool", lambda e, hh=hh, wi=wi, sb_=sb_, sT=sT: e.tensor_tensor(
                                out=sq[:, sb_, :].rearrange("p (a b) -> p a b", a=4), in0=sT[:, hh, :, wi * 128:(wi + 1) * 128],
                                in1=sT[:, hh, :, wi * 128:(wi + 1) * 128], op=ALU.mult),
                                reads=[("sT", gp, hh, wi)], writes=[("sq", sb_)])
                            for tl in range(4):
                                colr = tl * 4 + wi * 2 + hh
                                P.add("pe", lambda e, sb_=sb_, tl=tl, colr=colr: e.matmul(
                                    PF[1][:, 384 + colr:384 + colr + 1], sq[:, sb_, tl * 128:(tl + 1) * 128], self.onecol[:, 0:1],
                                    start=True, stop=True, skip_group_check=True),
                                    reads=[("sq", sb_), "onecol"], writes=[("pf", 1)])
                slot, wkey, _ = slabs["z"]
                for tl in range(4):
                    t = 4 * gi + tl
                    pf = PF[1]
                    for kc in range(8):
                        P.add("pe", lambda e, pf=pf, slot=slot, kc=kc, t=t: e.matmul(
                            pf[:, 0:256], xnT[:, kc, t * 128:(t + 1) * 128], ring[:, slot, kc, 0:256], start=(kc == 0), stop=(kc == 7),
                            skip_group_check=True),
                            reads=[wkey, ("xnT", t, kc)], writes=[("pf", 1)])
                    zt = acc[:, 0, 0:256]
                    zr = acc[:, 0, 256:512]
                    P.add("act", lambda e, pf=pf, zt=zt: e.activation(out=zt, in_=pf[:, 0:256], func=AF.Exp, scale=-1.0),
                          reads=[("pf", 1)], writes=[("acc", 0)])
                    P.add("act", lambda e, pf=pf, zr=zr: e.activation(out=zr, in_=pf[:, 0:256], func=AF.Copy),
                          reads=[("pf", 1), ("acc", 0)], writes=[("acc", 0)])
                    P.add("act", lambda e, zt=zt: e.activation(out=zt, in_=zt, func=AF.Ln, bias=1.0), reads=[("acc", 0)], writes=[("acc", 0)])
                    P.add("act", lambda e, zt=zt: e.activation(out=zt, in_=zt, func=AF.Exp, scale=-1.0), reads=[("acc", 0)], writes=[("acc", 0)])
                    P.add("dve", lambda e, zt=zt, zr=zr, tl=tl, zs=zs: e.tensor_tensor(out=zs[:, tl, :], in0=zr, in1=zt, op=ALU.mult),
                          reads=[("acc", 0)], writes=[("zs", gp, tl)])
                Rk = ("R", gp)
                P.add("act", lambda e, R=R: e.activation(out=flat(R), in_=PF[1][:, 384:400], func=AF.Ln, bias=epsc[:, 0:1], scale=1.0),
                      reads=[("pf", 1), "epsc"], writes=[Rk])
                P.add("act", lambda e, R=R: e.activation(out=flat(R), in_=flat(R), func=AF.Exp, scale=-0.5), reads=[Rk], writes=[Rk])
                hs = slice(2 * hp, 2 * hp + 2)
                ts = slice(4 * gi, 4 * gi + 4)
                rk, rq = R[:, :, 0:2], R[:, :, 2:4]
                scv = lambda q_, SC=SC: SC[:, q_, :].rearrange("p (a b) -> p a b", a=4)
                T1, CKBG, CKD, CQ, UL, UA, BIAS, LN = (scv(i) for i in range(8))
                bt, egs, ekds, gcs = BETA[:, ts, hs], EG[:, ts, hs], EKD[:, ts, hs], GC[:, ts, hs]
                sk = lambda i: ("SC", gp, i)
                P.add("dve", lambda e, T1=T1, rk=rk, bt=bt: e.tensor_tensor(out=T1, in0=rk, in1=bt, op=ALU.mult), reads=[Rk, "BETA"], writes=[sk(0)])
                P.add("dve", lambda e, CKBG=CKBG, T1=T1, egs=egs: e.tensor_tensor(out=CKBG, in0=T1, in1=egs, op=ALU.mult), reads=[sk(0), "EG"], writes=[sk(1)])
                P.add("dve", lambda e, CKD=CKD, rk=rk, ekds=ekds: e.tensor_tensor(out=CKD, in0=rk, in1=ekds, op=ALU.mult), reads=[Rk, "EKD"], writes=[sk(2)])
                P.add("dve", lambda e, CQ=CQ, rq=rq, egs=egs: e.scalar_tensor_tensor(out=CQ, in0=rq, scalar=DKS, in1=egs, op0=ALU.mult, op1=ALU.mult),
                      reads=[Rk, "EG"], writes=[sk(3)])
                P.add("act", lambda e, UL=UL, T1=T1: e.activation(out=UL, in_=T1, func=AF.Ln), reads=[sk(0)], writes=[sk(4)])
                P.add("dve", lambda e, UL=UL, gcs=gcs: e.tensor_tensor(out=UL, in0=UL, in1=gcs, op=ALU.add), reads=[sk(4), "GC"], writes=[sk(4)])
                P.add("act", lambda e, UA=UA, rq=rq: e.activation(out=UA, in_=rq, func=AF.Ln, scale=DKS), reads=[Rk], writes=[sk(5)])
                P.add("dve", lambda e, UA=UA, gcs=gcs: e.tensor_tensor(out=UA, in0=UA, in1=gcs, op=ALU.add), reads=[sk(5), "GC"], writes=[sk(5)])
                P.add("act", lambda e, BIAS=BIAS, rk=rk: e.activation(out=BIAS, in_=rk, func=AF.Ln), reads=[Rk], writes=[sk(6)])
                P.add("dve", lambda e, BIAS=BIAS, gcs=gcs: e.tensor_tensor(out=BIAS, in0=BIAS, in1=gcs, op=ALU.subtract), reads=[sk(6), "GC"], writes=[sk(6)])
                FE[gi] = P.end_capture()
                SCK = [sk(i) for i in range(7)]
                for tl in range(4):
                    t = 4 * gi + tl
                    stageA = []
                    for c in range(2):
                        P.begin_capture()
                        hh = c
                        cs = 2 * (t % 2) + c
                        pbk, pbo = cs // 2, (cs % 2) * 512
                        sc1 = lambda q_, tl=tl, hh=hh, SC=SC: SC[:, q_, tl * 2 + hh:tl * 2 + hh + 1]
                        pb, pc = PB[pbk], PF[2 + cs]
                        kd, kbg, vb, E, MA = (g[k, cs] for k in ("kd", "kbg", "vb", "E", "MA"))
                        ksT = sT[:, hh, tl, 0:128]
                        vsT = sT[:, hh, tl, 256:384]
                        P.add("pe", lambda e, pb=pb, ksT=ksT, pbo=pbo: e.transpose(out=pb[:, pbo:pbo + 128], in_=ksT, identity=identb[:]),
                              reads=[("sT", gp, hh, 0), "identb"], writes=[("pb", pbk)])
                        P.add("pe", lambda e, pb=pb, vsT=vsT, pbo=pbo: e.transpose(out=pb[:, pbo + 128:pbo + 256], in_=vsT, identity=identb[:]),
                              reads=[("sT", gp, hh, 2), "identb"], writes=[("pb", pbk)])
                        P.add("act", lambda e, pb=pb, kd=kd, sc1=sc1, pbo=pbo: e.activation(out=kd, in_=pb[:, pbo:pbo + 128], func=AF.Copy, scale=sc1(2)),
                              reads=[("pb", pbk)] + SCK, writes=[("kd", cs)])
                        P.add("dve", lambda e, pb=pb, kbg=kbg, sc1=sc1, pbo=pbo: e.tensor_scalar(out=kbg, in0=pb[:, pbo:pbo + 128], scalar1=sc1(1), scalar2=None, op0=ALU.mult),
                              reads=[("pb", pbk)] + SCK, writes=[("kbg", cs)])
                        hcol = 2 * hp + hh
                        P.add("dve", lambda e, pb=pb, vb=vb, t=t, hcol=hcol, pbo=pbo: e.tensor_scalar(
                            out=vb, in0=pb[:, pbo + 128:pbo + 256], scalar1=BETA[:, t, hcol:hcol + 1], scalar2=None, op0=ALU.mult),
                            reads=[("pb", pbk), "BETA"], writes=[("vb", cs)])
                        P.add("pe", lambda e, pc=pc, ksT=ksT, hh=hh, tl=tl, sT=sT: e.matmul(pc[:, 0:256], ksT, sT[:, hh, tl, 0:256], start=True, stop=True,
                                                                                              skip_group_check=True),
                              reads=[("sT", gp, hh, 0), ("sT", gp, hh, 1)], writes=[("pf", 2 + cs)])
                        P.add("act", lambda e, E=E, sc1=sc1: e.activation(out=E[:, 0:128], in_=identf[:, :], func=AF.Copy, scale=sc1(4)),
                              reads=["identf"] + SCK, writes=[("E", cs)])
                        P.add("act", lambda e, E=E, sc1=sc1: e.activation(out=E[:, 128:256], in_=identf[:, :], func=AF.Copy, scale=sc1(5)),
                              reads=["identf", ("E", cs)] + SCK, writes=[("E", cs)])
                        P.add("pe", lambda e, pc=pc, E=E: e.matmul(pc[:, 256:512], self.onesf[:, :], E[:, :], start=True, stop=False, skip_group_check=True),
                              reads=[("E", cs), "onesf"], writes=[("pf", 2 + cs)])
                        P.add("pe", lambda e, pc=pc: e.matmul(pc[:, 256:512], identf[:, :], self.masks[:, :], start=False, stop=True, skip_group_check=True),
                              reads=["identf", "masks"], writes=[("pf", 2 + cs)])
                        P.add("act", lambda e, pc=pc, E=E, sc1=sc1: e.activation(out=E[:, :], in_=pc[:, 256:512], func=AF.Exp, bias=sc1(6)),
                              reads=[("pf", 2 + cs)] + SCK, writes=[("E", cs)])
                        P.add("dve", lambda e, pc=pc, E=E, MA=MA: e.tensor_tensor(out=MA[:, :], in0=pc[:, 0:256], in1=E[:, :], op=ALU.mult),
                              reads=[("pf", 2 + cs), ("E", cs)], writes=[("MA", cs)])
                        Lb, DD, TM = g["Lb", cs], g["DD", cs], g["TM", cs]
                        P.add("pe", lambda e, pb=pb, MA=MA, pbo=pbo: e.transpose(out=pb[:, pbo + 256:pbo + 384], in_=MA[:, 0:128], identity=identb[:]),
                              reads=[("MA", cs), "identb"], writes=[("pb", pbk)])
                        P.add("act", lambda e, pb=pb, Lb=Lb, pbo=pbo: e.activation(out=Lb, in_=pb[:, pbo + 256:pbo + 384], func=AF.Copy),
                              reads=[("pb", pbk)], writes=[("Lb", cs)])
                        P.add("dve", lambda e, TM=TM, Lb=Lb: e.tensor_tensor(out=TM[:, 0:128], in0=Lb, in1=NEGM[:, 0, 0:128], op=ALU.mult),
                              reads=[("Lb", cs), "NEGM"], writes=[("TM", cs)])
                        P.add("dve", lambda e, TM=TM, MA=MA: e.tensor_tensor(out=TM[:, 128:256], in0=MA[:, 0:128], in1=NEGM[:, 0, 128:256], op=ALU.mult),
                              reads=[("MA", cs), "NEGM", ("TM", cs)], writes=[("TM", cs)])
                        P.add("dve", lambda e, TM=TM, DD=DD: e.tensor_tensor(out=DD[:, 0, 0:128], in0=identb[:, :], in1=TM[:, 0:128], op=ALU.subtract),
                              reads=[("TM", cs), "identb"], writes=[("DD", cs, 0)])
                        P.add("dve", lambda e, TM=TM, DD=DD: e.tensor_tensor(out=DD[:, 0, 128:256], in0=identb[:, :], in1=TM[:, 128:256], op=ALU.subtract),
                              reads=[("TM", cs), "identb", ("DD", cs, 0)], writes=[("DD", cs, 0)])
                        stageA.append(P.end_capture())
                    P.begin_capture()
                    for lev in range(1, 7):
                        pi, po = (lev - 1) % 2, lev % 2
                        for c in range(2):
                            cs = 2 * (t % 2) + c
                            pc = PF[2 + cs]
                            MA, Lb, DD, QQ = (g[k_, cs] for k_ in ("MA", "Lb", "DD", "QQ"))
                            P.add("pe", lambda e, pc=pc, MA=MA, DD=DD, pi=pi: e.matmul(pc[:, 0:128], MA[:, 0:128], DD[:, pi, 0:128], start=True, stop=True,
                                                                                        skip_group_check=True),
                                  reads=[("MA", cs), ("DD", cs, pi)], writes=[("pf", 2 + cs)])
                            P.add("pe", lambda e, pc=pc, Lb=Lb, DD=DD, pi=pi: e.matmul(pc[:, 128:256], Lb, DD[:, pi, 128:256], start=True, stop=True,
                                                                                        skip_group_check=True),
                                  reads=[("Lb", cs), ("DD", cs, pi)], writes=[("pf", 2 + cs)])
                            P.add("dve", lambda e, pc=pc, QQ=QQ, lev=lev: e.tensor_tensor(out=QQ[:, :], in0=pc[:, 0:256], in1=NEGM[:, lev, :], op=ALU.mult),
                                  reads=[("pf", 2 + cs), "NEGM"], writes=[("QQ", cs)])
                        for c in range(2):
                            cs = 2 * (t % 2) + c
                            pc = PF[2 + cs]
                            DD, QQ = (g[k_, cs] for k_ in ("DD", "QQ"))
                            P.add("pe", lambda e, pc=pc, QQ=QQ, DD=DD, pi=pi: e.matmul(pc[:, 256:384], DD[:, pi, 128:256], QQ[:, 0:128], start=True, stop=True,
                                                                                        skip_group_check=True),
                                  reads=[("QQ", cs), ("DD", cs, pi)], writes=[("pf", 2 + cs)])
                            P.add("pe", lambda e, pc=pc, QQ=QQ, DD=DD, pi=pi: e.matmul(pc[:, 384:512], DD[:, pi, 0:128], QQ[:, 128:256], start=True, stop=True,
                                                                                        skip_group_check=True),
                                  reads=[("QQ", cs), ("DD", cs, pi)], writes=[("pf", 2 + cs)])
                            P.add("dve", lambda e, pc=pc, DD=DD, pi=pi, po=po: e.tensor_tensor(out=DD[:, po, :], in0=DD[:, pi, :], in1=pc[:, 256:512], op=ALU.subtract),
                                  reads=[("pf", 2 + cs), ("DD", cs, pi)], writes=[("DD", cs, po)])
                    for c in range(2):
                        cs = 2 * (t % 2) + c
                        pc = PF[2 + cs]
                        kbg, vb, DD, u, wT = (g[k, cs] for k in ("kbg", "vb", "DD", "u", "wT"))
                        P.add("pe", lambda e, pc=pc, DD=DD, vb=vb: e.matmul(pc[:, 0:128], DD[:, 0, 128:256], vb, start=True, stop=True, skip_group_check=True),
                              reads=[("DD", cs, 0), ("vb", cs)], writes=[("pf", 2 + cs)])
                        P.add("pe", lambda e, pc=pc, DD=DD, kbg=kbg: e.matmul(pc[:, 128:256], kbg, DD[:, 0, 128:256], start=True, stop=True, skip_group_check=True),
                              reads=[("DD", cs, 0), ("kbg", cs)], writes=[("pf", 2 + cs)])
                        P.add("act", lambda e, pc=pc, u=u: e.activation(out=u, in_=pc[:, 0:128], func=AF.Copy), reads=[("pf", 2 + cs)], writes=[("u", cs)])
                        P.add("dve", lambda e, pc=pc, wT=wT: e.tensor_copy(out=wT, in_=pc[:, 128:256]), reads=[("pf", 2 + cs)], writes=[("wT", cs)])
                    PREP[t] = Prog.merge(stageA) + P.end_capture()
                    scans = []
                    for c in range(2):
                        P.begin_capture()
                        hh = c
                        h = 2 * hp + hh
                        cs = 2 * (t % 2) + c
                        pbk, pbo = cs // 2, (cs % 2) * 512
                        ps_, pb = PF[2 + cs], PB[pbk]
                        kd, MA, u, wT, vn, o, og, psm = (g[k, cs] for k in ("kd", "MA", "u", "wT", "vn", "o", "og", "ps"))
                        sc1 = lambda q_, tl=tl, hh=hh, SC=SC: SC[:, q_, tl * 2 + hh:tl * 2 + hh + 1]
                        qsT = sT[:, hh, tl, 128:256]
                        PK = ("pf", 2 + cs)
                        P.add("pe", lambda e, ps_=ps_, wT=wT, hh=hh: e.matmul(ps_[:, 0:128], wT, Sb[:, hh, :], start=True, stop=True, skip_group_check=True),
                              reads=[("wT", cs), ("Sb", hh)], writes=[PK])
                        P.add("pe", lambda e, ps_=ps_, qsT=qsT, hh=hh: e.matmul(ps_[:, 128:256], qsT, Sb[:, hh, :], start=True, stop=True, skip_group_check=True),
                              reads=[("sT", gp, hh, 1), ("Sb", hh)], writes=[PK])
                        P.add("dve", lambda e, ps_=ps_, u=u, vn=vn: e.tensor_tensor(out=vn, in0=u, in1=ps_[:, 0:128], op=ALU.subtract),
                              reads=[PK, ("u", cs)], writes=[("vn", cs)])
                        P.add("pe", lambda e, ps_=ps_, MA=MA, vn=vn: e.matmul(ps_[:, 256:384], MA[:, 128:256], vn, start=True, stop=True, skip_group_check=True),
                              reads=[("MA", cs), ("vn", cs)], writes=[PK])
                        P.add("pe", lambda e, ps_=ps_, kd=kd, vn=vn: e.matmul(ps_[:, 384:512], kd, vn, start=True, stop=True, skip_group_check=True),
                              reads=[("kd", cs), ("vn", cs)], writes=[PK])
                        P.add("act", lambda e, ps_=ps_, o=o: e.activation(out=o, in_=ps_[:, 256:384], func=AF.Copy), reads=[PK], writes=[("o", cs)])
                        P.add("dve", lambda e, ps_=ps_, o=o, sc1=sc1: e.scalar_tensor_tensor(out=o, in0=ps_[:, 128:256], scalar=sc1(3), in1=o, op0=ALU.mult, op1=ALU.add),
                              reads=[PK, ("o", cs)] + SCK, writes=[("o", cs)])
                        P.add("dve", lambda e, ps_=ps_, hh=hh, t=t, h=h: e.scalar_tensor_tensor(
                            out=Sf[:, hh, :], in0=Sf[:, hh, :], scalar=EGL[:, t, h:h + 1], in1=ps_[:, 384:512], op0=ALU.mult, op1=ALU.add),
                            reads=[PK, ("Sf", hh), "EGL"], writes=[("Sf", hh)])
                        P.add("act", lambda e, hh=hh: e.activation(out=Sb[:, hh, :], in_=Sf[:, hh, :], func=AF.Copy), reads=[("Sf", hh)], writes=[("Sb", hh)])
                        P.add("act", lambda e, o=o, og=og, psm=psm: e.activation(out=og, in_=o, func=AF.Square, accum_out=psm[:, 0:1]),
                              reads=[("o", cs)], writes=[("og", cs), ("psm", cs)])
                        P.add("act", lambda e, psm=psm: e.activation(out=psm[:, 1:2], in_=psm[:, 0:1], func=AF.Ln, bias=epsc[:, 0:1], scale=1.0 / 128),
                              reads=[("psm", cs), "epsc"], writes=[("psm", cs)])
                        P.add("act", lambda e, psm=psm: e.activation(out=psm[:, 1:2], in_=psm[:, 1:2], func=AF.Exp, scale=-0.5), reads=[("psm", cs)], writes=[("psm", cs)])
                        P.add("dve", lambda e, o=o, og=og, psm=psm, tl=tl, hh=hh, zs=zs: e.scalar_tensor_tensor(
                            out=og, in0=o, scalar=psm[:, 1:2], in1=zs[:, tl, hh * 128:(hh + 1) * 128], op0=ALU.mult, op1=ALU.mult),
                            reads=[("o", cs), ("psm", cs), ("zs", gp, tl)], writes=[("og", cs)])
                        P.add("pe", lambda e, pb=pb, og=og, pbo=pbo: e.transpose(out=pb[:, pbo + 384:pbo + 512], in_=og, identity=identb[:]),
                              reads=[("og", cs), "identb"], writes=[("pb", pbk)])
                        P.add("act", lambda e, pb=pb, hh=hh, t=t, pbo=pbo: e.activation(out=U[:, hh, t * 128:(t + 1) * 128], in_=pb[:, pbo + 384:pbo + 512], func=AF.Copy,
                                                                                        scale=self.gdc[:, j, 0:1]),
                              reads=[("pb", pbk), ("gdc", j)], writes=[("U", hh, t // 4)])
                        scans.append(P.end_capture())
                    SCAN[t] = Prog.merge(scans)
            P.replay([FE[0]])
            fe_parts = {}
            for gi in range(1, 4):
                L = FE[gi]
                n = (len(L) + 2) // 3
                for k in range(3):
                    fe_parts[4 * (gi - 1) + 1 + k] = L[k * n:(k + 1) * n]
            for s_ in range(NT + 1):
                lists = []
                if s_ < NT:
                    lists.append(PREP[s_])
                if s_ >= 1:
                    lists.append(SCAN[s_ - 1])
                if s_ in fe_parts:
                    lists.append(fe_parts[s_])
                P.replay(lists)
            for which in ("q", "k", "v", "z"):
                self.ws.done(slabs[which][2])
            wsrc = self.d["gd_w_out"][j][hp * 256:(hp + 1) * 256, :].rearrange("(kc p) f -> p kc f", p=128)
            so = self.ws.get(wsrc, (2, 1024))
            wv = self.ws.view(so[0], (2, 1024))
            for t in range(NT):
                for dh in range(2):
                    bank = cnt["pf"] % 2
                    cnt["pf"] += 1
                    pf = PF[bank]
                    for hc in range(2):
                        P.add("pe", lambda e, pf=pf, wv=wv, hc=hc, t=t, dh=dh: e.matmul(
                            pf[:, :], U[:, hc, t * 128:(t + 1) * 128], wv[:, hc, dh * 512:(dh + 1) * 512], start=(hc == 0), stop=(hc == 1)),
                            reads=[so[1], ("U", hc, t // 4)], writes=[("pf", bank)])
                    P.add("dve", lambda e, pf=pf, t=t, dh=dh: e.tensor_tensor(
                        out=X[:, t, dh * 512:(dh + 1) * 512], in0=X[:, t, dh * 512:(dh + 1) * 512], in1=pf[:, :], op=ALU.add),
                        reads=[("pf", bank), ("x", t)], writes=[("x", t)])
            self.ws.done(so[2])

    def load_x(self, s):
        P = self.P
        X = self.X
        xv = self.d["x"][s].rearrange("(t p) d -> p t d", p=128)
        for q in range(4):
            P.add("sp", lambda e, q=q: e.dma_start(out=X[:, 4 * q:4 * q + 4, :], in_=xv[:, 4 * q:4 * q + 4, :]),
                  writes=[("x", t) for t in range(4 * q, 4 * q + 4)], dma=("xl", q))

    def store_x(self, s):
        P = self.P
        X = self.X
        ov = self.d["out"][s].rearrange("(t p) d -> p t d", p=128)
        ids = []
        for q in range(4):
            ids.append(P.add("sp", lambda e, q=q: e.dma_start(out=ov[:, 4 * q:4 * q + 4, :], in_=X[:, 4 * q:4 * q + 4, :]),
                             reads=[("x", t) for t in range(4 * q, 4 * q + 4)], writes=[("xst", q)], dma=("xs", q)))
        return ids

    def rmsnorm(self, gbase):
        P = self.P
        X, xs, ss, rstd, xnT = self.X, self.xs, self.ss, self.rstd, self.xnT
        identb, gcol = self.identb, self.gcol
        import os
        DBG = int(os.environ.get("K_DBG", "9"))
        for t in range(NT):
            P.add("act", lambda e, t=t: e.activation(out=xs[:, 1, :], in_=X[:, t, :], func=AF.Square,
                                                     accum_out=ss[:, t:t + 1]),
                  reads=[("x", t)], writes=[("xs", 1), ("ss", t)])
        P.add("act", lambda e: e.activation(out=rstd[:], in_=ss[:], func=AF.Ln, bias=self.epsc[:, 0:1], scale=1.0 / D),
              reads=[("ss", t) for t in range(NT)] + ["epsc"], writes=["rstd"])
        P.add("act", lambda e: e.activation(out=rstd[:], in_=rstd[:], func=AF.Exp, scale=-0.5), reads=["rstd"], writes=["rstd"])
        if DBG < 2:
            return
        for t in range(NT if DBG >= 6 else 1):
            b = t % 2
            pb = self.PB[b]
            P.add("act", lambda e, t=t, b=b: e.activation(out=xs[:, b, :], in_=X[:, t, :], func=AF.Copy,
                                                          scale=rstd[:, t:t + 1]),
                  reads=[("x", t), "rstd"], writes=[("xs", b)])
            if DBG < 4:
                continue
            for kc in range(8):
                P.add("pe", lambda e, b=b, kc=kc, pb=pb: e.transpose(out=pb[:, kc * 128:(kc + 1) * 128],
                                                                      in_=xs[:, b, kc * 128:(kc + 1) * 128],
                                                                      identity=identb[:]),
                      reads=[("xs", b), "identb"], writes=[("pb", b)])
            if DBG < 5:
                continue
            for kc in range(8):
                eng = "dve" if b == 0 else "act"
                if eng == "dve":
                    fn = lambda e, t=t, kc=kc, pb=pb: e.tensor_scalar(
                        out=xnT[:, kc, t * 128:(t + 1) * 128], in0=pb[:, kc * 128:(kc + 1) * 128],
                        scalar1=gcol[:, gbase + kc:gbase + kc + 1], scalar2=None, op0=ALU.mult)
                else:
                    fn = lambda e, t=t, kc=kc, pb=pb: e.activation(
                        out=xnT[:, kc, t * 128:(t + 1) * 128], in_=pb[:, kc * 128:(kc + 1) * 128],
                        func=AF.Copy, scale=gcol[:, gbase + kc:gbase + kc + 1])
                P.add(eng, fn, reads=[("pb", b), "gcol"], writes=[("xnT", t, kc)])

    def mlp(self, l):
        P = self.P
        X, xnT, U, ring = self.X, self.xnT, self.hT, self.ring
        w1 = self.d["mlp_w_in"][l].rearrange("(kc p) f -> p kc f", p=128)
        w2 = self.d["mlp_w_out"][l].rearrange("(fc p) d -> p fc d", p=128)
        self.rmsnorm(32 + 8 * l)
        self.arena_barrier()
        pfi = 0
        for fg in range(4):
            slabs = [self.ws.get(w1[:, :, fg * 1024 + s2 * 512: fg * 1024 + (s2 + 1) * 512], 512) for s2 in range(2)]
            for fc in range(8):
                slot, wkey, _ = slabs[fc // 4]
                off = (fc % 4) * 128
                for tt in range(4):
                    bank = pfi % 4
                    pfi += 1
                    pf = self.PF[bank]
                    for kc in range(8):
                        P.add("pe", lambda e, pf=pf, slot=slot, kc=kc, off=off, tt=tt: e.matmul(
                            pf[:, :], ring[:, slot, kc, off:off + 128], xnT[:, kc, tt * 512:(tt + 1) * 512],
                            start=(kc == 0), stop=(kc == 7)),
                            reads=[wkey] + [("xnT", t, kc) for t in range(4 * tt, 4 * tt + 4)],
                            writes=[("pf", bank)])
                    rb = pfi % 2
                    P.add("act", lambda e, pf=pf, rb=rb: e.activation(out=self.rtmp[:, rb, :], in_=pf[:, :], func=AF.Relu),
                          reads=[("pf", bank)], writes=[("xs", rb)])
                    P.add("dve", lambda e, fc=fc, tt=tt, rb=rb: e.tensor_tensor(
                        out=U[:, fc, tt * 512:(tt + 1) * 512], in0=self.rtmp[:, rb, :], in1=self.rtmp[:, rb, :],
                        op=ALU.mult),
                        reads=[("xs", rb)], writes=[("hT", fc, tt)])
            for sl in slabs:
                self.ws.done(sl[2])
            slabs2 = [self.ws.get(w2[:, fg * 8:(fg + 1) * 8, dh * 512:(dh + 1) * 512], 512) for dh in range(2)]
            for t in range(NT):
                for dh in range(2):
                    slot, wkey, _ = slabs2[dh]
                    bank = pfi % 4
                    pfi += 1
                    pf = self.PF[bank]
                    for fc in range(8):
                        P.add("pe", lambda e, pf=pf, slot=slot, fc=fc, t=t: e.matmul(
                            pf[:, :], U[:, fc, t * 128:(t + 1) * 128], ring[:, slot, fc, :],
                            start=(fc == 0), stop=(fc == 7)),
                            reads=[wkey, ("hT", fc, t // 4)], writes=[("pf", bank)])
                    P.add("dve", lambda e, pf=pf, t=t, dh=dh: e.tensor_tensor(
                        out=X[:, t, dh * 512:(dh + 1) * 512], in0=X[:, t, dh * 512:(dh + 1) * 512], in1=pf[:, :],
                        op=ALU.add),
                        reads=[("pf", bank), ("x", t)], writes=[("x", t)])
            for sl in slabs2:
                self.ws.done(sl[2])

    def build(self):
        P = self.P
        self.setup_consts()
        kinds = {k for k, _ in self.layers}
        if "gd" in kinds:
            self.setup_gd_static()
            self.setup_gd_levelmasks()
            for jj in sorted({l // 2 for (k, l) in self.layers if k == "gd"}):
                self.setup_gd_consts(jj)
        if "da" in kinds:
            self.setup_da_static()
            for (k, l) in self.layers:
                if k == "da":
                    self.setup_da_consts(l // 2, l)
        last_stores = []
        for s in range(self.n_seq):
            self.load_x(s)
            for l in self.layers:
                if l[0] == "mlp":
                    self.mlp(l[1])
                elif l[0] == "norm":
                    self.rmsnorm(32 + 8 * l[1])
                elif l[0] == "da":
                    self.diffattn(l[1])
                elif l[0] == "gd":
                    self.gdn(l[1])
            last_stores = self.store_x(s)
        P.add("sp", None, reads=[("xst", q) for q in range(4)])


def layer_plan():
    plan = []
    for i in range(DEPTH):
        plan.append(("da" if i % 2 == 0 else "gd", i))
        plan.append(("mlp", i))
    return plan


def build_program(n_seq=SEQ_PER_CORE, layers=None):
    if layers is None:
        layers = layer_plan()
    nc = bass.Bass("TRN2", target_bir_lowering=False)
    with ExitStack() as es:
        b = Builder(nc, Prog(nc, dry=True), None, n_seq, layers, es)
        b.build()
        future = list(b.ws.requests)
        P = Prog(nc, dry=False)
        b.P = P
        b.ws = WStream(P, b.ring, b.NSLOT, future)
        b.build()
        P.emit(es)
    return nc


WEIGHT_NAMES = ["mix_norm", "mlp_norm", "mlp_w_in", "mlp_w_out", "da_w_in", "da_q_norm", "da_k_norm",
                "da_lambda_q1", "da_lambda_k1", "da_lambda_q2", "da_lambda_k2", "da_sub_norm", "da_w_out",
                "gd_w_in", "gd_conv_w", "gd_a_log", "gd_dt_bias", "gd_out_norm", "gd_w_out"]


def run(inputs, n_seq=SEQ_PER_CORE, layers=None, ncores=NCORES, trace=False):
    nc = build_program(n_seq, layers)
    x = np.ascontiguousarray(np.asarray(inputs["x"], dtype=np.float32))
    weights = {k: np.ascontiguousarray(np.asarray(inputs[k], dtype=np.float32)) for k in WEIGHT_NAMES}
    in_maps = []
    for c in range(ncores):
        m = {"x": x[c * n_seq:(c + 1) * n_seq]}
        m.update(weights)
        in_maps.append(m)
    res = run_bass_kernel_spmd(nc, in_maps, core_ids=list(range(ncores)), trace=trace)
    out = np.concatenate([r["out"] for r in res.results], axis=0)
    return out, res


def kernel(**inputs):
    out, _ = run(inputs)
    return out
```

```python
import math
from contextlib import ExitStack

import numpy as np
import concourse.bass as bass
import concourse.mybir as mybir
from concourse.bass_utils import run_bass_kernel_spmd

F32 = mybir.dt.float32
BF16 = mybir.dt.bfloat16
AF = mybir.ActivationFunctionType
ALU = mybir.AluOpType
AX = mybir.AxisListType

D = 1024
S = 2048
NT = S // 128
DFF = 4096
DEPTH = 4
EPS = 1e-6
NCORES = 8
SEQ_PER_CORE = 4
GD_IN = 4 * 1024 + 16


class Op:
    __slots__ = ("id", "eng", "fn", "deps", "dma", "seq", "signal")


class Prog:
    ENGS = ("pe", "act", "dve", "pool", "sp")

    def __init__(self, nc, dry=False):
        self.nc = nc
        self.dry = dry
        self.ops = []
        self.by_eng = {e: [] for e in self.ENGS}
        self.lw = {}
        self.rd = {}
        self.dma_groups = {}
        self.group_all = set()
        self.psum_last = {}
        self.arena_names = set()
        self.cap = None
        self.atom = None

    def begin_capture(self):
        self.cap = []

    def begin_atomic(self):
        if self.cap is not None:
            self.atom = []

    def end_atomic(self):
        if self.cap is not None:
            self.cap.append(self.atom)
            self.atom = None

    def end_capture(self):
        c, self.cap = self.cap, None
        return c

    @staticmethod
    def merge(lists):
        lists = [L for L in lists if L]
        idx = [0] * len(lists)
        out = []
        total = sum(len(L) for L in lists)
        nel = total
        done_el = 0
        while done_el < nel:
            done_el += 1
            best, bf = None, None
            for i, L in enumerate(lists):
                if idx[i] < len(L):
                    f = (idx[i] + 0.5) / len(L)
                    if bf is None or f < bf:
                        best, bf = i, f
            el = lists[best][idx[best]]
            idx[best] += 1
            total -= 1
            if isinstance(el, list):
                out.extend(el)
                total += len(el)
            else:
                out.append(el)
                total += 1
        return out

    def replay(self, lists):
        for rec in self.merge(lists):
            self.add(*rec)

    def add(self, eng, fn, reads=(), writes=(), dma=None):
        if self.dry:
            return None
        if self.cap is not None:
            rec = (eng, fn, tuple(reads), tuple(writes), dma)
            if self.atom is not None:
                self.atom.append(rec)
            else:
                self.cap.append(rec)
            return None
        op = Op()
        op.id = len(self.ops)
        op.eng = eng
        op.fn = fn
        op.dma = dma
        op.seq = 0
        op.signal = False
        if self.arena_names:
            for k in tuple(reads) + tuple(writes):
                nm = k[0] if isinstance(k, tuple) else k
                if nm in self.arena_names:
                    reads = tuple(reads) + ("ARENA",)
                    break
        deps = set()
        for k in reads:
            w = self.lw.get(k)
            if w is not None:
                deps.add(w)
        for k in writes:
            w = self.lw.get(k)
            if w is not None:
                deps.add(w)
            for r in self.rd.get(k, ()):
                deps.add(r)
        for k in reads:
            self.rd.setdefault(k, []).append(op.id)
        for k in writes:
            self.lw[k] = op.id
            self.rd[k] = []
        for k in tuple(reads) + tuple(writes):
            if isinstance(k, tuple) and k[0] in ("pf", "pb"):
                last = self.psum_last.setdefault(k, {})
                for eng2, oid in last.items():
                    if eng2 != eng:
                        deps.add(oid)
                last[eng] = op.id
        deps.discard(op.id)
        if eng == "pe" and dma is None:
            deps = {d for d in deps if not (self.ops[d].eng == "pe" and self.ops[d].dma is None)}
        op.deps = deps
        self.ops.append(op)
        self.by_eng[eng].append(op)
        if dma is not None:
            self.dma_groups.setdefault(dma, []).append(op.id)
        return op.id

    def emit(self, es):
        nc = self.nc
        ops = self.ops
        for op in ops:
            for d in op.deps:
                ops[d].signal = True
        for e in self.ENGS:
            c = 0
            for op in self.by_eng[e]:
                if op.dma is None and op.signal:
                    c += 1
                    op.seq = c
        for g, ids in self.dma_groups.items():
            for i, oid in enumerate(ids):
                ops[oid].seq = i + 1
        eng_sem = {e: es.enter_context(nc.semaphore("s_" + e)) for e in self.ENGS}
        dma_sem = {g: es.enter_context(nc.semaphore("d_" + str(g))) for g in self.dma_groups}
        block = es.enter_context(nc.Block())

        def emit_engine(ename, e):
            waited = {}
            for op in self.by_eng[ename]:
                need = {}
                for d in op.deps:
                    dop = ops[d]
                    if dop.dma is not None:
                        sem = dma_sem[dop.dma]
                        if dop.dma in self.group_all:
                            val = 16 * len(self.dma_groups[dop.dma])
                        else:
                            val = 16 * dop.seq
                    else:
                        sem = eng_sem[dop.eng]
                        val = dop.seq
                    key = id(sem)
                    if key not in need or need[key][1] < val:
                        need[key] = (sem, val)
                for key, (sem, val) in need.items():
                    if waited.get(key, 0) < val:
                        e.wait_ge(sem, val)
                        waited[key] = val
                if op.fn is None:
                    continue
                inst = op.fn(e)
                if op.dma is not None:
                    inst.then_inc(dma_sem[op.dma], 16)
                elif op.signal:
                    inst.then_inc(eng_sem[ename], 1)

        @block.tensor
        def _(e):
            emit_engine("pe", e)

        @block.scalar
        def _(e):
            emit_engine("act", e)

        @block.vector
        def _(e):
            emit_engine("dve", e)

        @block.gpsimd
        def _(e):
            emit_engine("pool", e)

        @block.sync
        def _(e):
            emit_engine("sp", e)


class WStream:
    UNIT = 2048

    def __init__(self, P, ring, nunits, lookahead_list=None):
        self.P = P
        self.ring = ring
        self.nu = nunits
        self.future = lookahead_list
        self.requests = []
        self.issued = 0
        self.released = set()
        self.head = 0
        self.occ = [None] * nunits
        self.units = []
        self.prev = []
        if lookahead_list is not None:
            for (src, w) in lookahead_list:
                self._place(w)

    @staticmethod
    def _nelem(w):
        return w[0] * w[1] if isinstance(w, tuple) else 8 * w

    def _place(self, w):
        n = (self._nelem(w) + self.UNIT - 1) // self.UNIT
        if self.head + n > self.nu:
            self.head = 0
        j = len(self.units)
        us = list(range(self.head, self.head + n))
        self.prev.append({self.occ[u] for u in us if self.occ[u] is not None})
        for u in us:
            self.occ[u] = j
        self.units.append((self.head, n))
        self.head = (self.head + n) % self.nu

    def view(self, idx, w):
        u0, n = self.units[idx]
        ne = self._nelem(w)
        v = self.ring[:, u0 * self.UNIT:u0 * self.UNIT + ne]
        if isinstance(w, tuple):
            return v.rearrange("p (a b) -> p a b", a=w[0])
        return v.rearrange("p (a b) -> p a b", a=8)

    def _pump(self):
        if self.P.dry:
            return
        while self.issued < len(self.future):
            j = self.issued
            if not all(pj in self.released for pj in self.prev[j]):
                break
            src, w = self.future[j]
            dst = self.view(j, w)
            self.P.add("pool", lambda e, dst=dst, src=src: e.dma_start(out=dst, in_=src),
                       writes=[("w", j)] + [("w", pj) for pj in self.prev[j]], dma=("wu", self.units[j][0]))
            self.issued += 1

    def get(self, src, w):
        i = len(self.requests)
        self.requests.append((src, w))
        if self.future is None:
            self._place(w)
        if self.P.dry:
            return self.view(i, w), ("w", i), i
        self._pump()
        assert self.issued > i, "weight ring deadlock: release slabs before requesting more"
        return self.view(i, w), ("w", i), i

    def done(self, idx):
        self.released.add(idx)
        self._pump()


class Builder:
    def __init__(self, nc, P, ws_future, n_seq, layers, es):
        self.nc = nc
        self.P = P
        self.n_seq = n_seq
        self.layers = layers
        self.es = es
        self.ws_future = ws_future
        self.alloc()

    def sb(self, name, shape, dt):
        return self.es.enter_context(self.nc.sbuf_tensor(name, shape, dt))

    def carve(self, shape, dt):
        esz = 4 if dt == F32 else 2
        n = 1
        for s_ in shape:
            n *= s_
        nbytes = (n * esz + 3) // 4 * 4
        off = self._aoff
        assert off + nbytes <= self.ARENA_BYTES, "arena overflow"
        self._aoff = off + nbytes
        v = self.arena[:, off // 4:(off + nbytes) // 4]
        if dt != F32:
            v = v.bitcast(dt)
        v = v[:, 0:n]
        if len(shape) == 2:
            v = v.rearrange("p (a b) -> p a b", a=shape[0])
        elif len(shape) == 3:
            v = v.rearrange("p (a b c) -> p a b c", a=shape[0], b=shape[1])
        return v

    def alloc(self):
        nc = self.nc
        n_seq = self.n_seq
        d = {}
        d["x"] = nc.dram_tensor("x", [n_seq, S, D], F32, kind="ExternalInput").ap()
        d["out"] = nc.dram_tensor("out", [n_seq, S, D], F32, kind="ExternalOutput").ap()
        specs = [
            ("mix_norm", [4, D]), ("mlp_norm", [4, D]), ("mlp_w_in", [4, D, DFF]), ("mlp_w_out", [4, DFF, D]),
            ("da_w_in", [2, D, 3072]), ("da_q_norm", [2, 64]), ("da_k_norm", [2, 64]),
            ("da_lambda_q1", [2, 64]), ("da_lambda_k1", [2, 64]), ("da_lambda_q2", [2, 64]), ("da_lambda_k2", [2, 64]),
            ("da_sub_norm", [2, 128]), ("da_w_out", [2, D, D]),
            ("gd_w_in", [2, D, GD_IN]), ("gd_conv_w", [2, 4, 3072]), ("gd_a_log", [2, 8]), ("gd_dt_bias", [2, 8]),
            ("gd_out_norm", [2, 128]), ("gd_w_out", [2, D, D]),
        ]
        for name, shape in specs:
            d[name] = nc.dram_tensor(name, shape, F32, kind="ExternalInput").ap()
        self.d = d
        self.NSLOT = 4
        self.X = self.sb("X", [128, NT, D], F32)
        self.xnT = self.sb("xnT", [128, 8, S], BF16)
        self.U = self.sb("U", [128, 2, S], BF16)
        self.ring = self.sb("ring", [128, self.NSLOT * 4096], BF16)
        self.xs_f = self.sb("xs_f", [128, 2, 512], F32)
        self.xs = self.xs_f[:, :, :].rearrange("p a b -> p (a b)").bitcast(BF16).rearrange("p (a b) -> p a b", a=2)
        self.rtmp = self.xs_f
        self.ss = self.sb("ss", [128, NT], F32)
        self.rstd = self.sb("rstd", [128, NT], F32)
        self.epsc = self.sb("epsc", [128, 4], F32)
        self.identb = self.sb("identb", [128, 128], BF16)
        self.identf = self.sb("identf", [128, 128], F32)
        self.crow = self.xs_f[0:64, 1, 0:128]
        self.gcol = self.sb("gcol", [128, 64], F32)
        self.ARENA_BYTES = 35840 + 24576
        self.arena = self.sb("arena", [128, self.ARENA_BYTES // 4], F32)
        self._aoff = 0
        self.hT = self.carve([8, S], BF16)
        self._aoff = 0
        self.qT = self.carve([2, S], BF16)
        self.kTz = self.carve([2, 2, S], BF16)
        self.vaug = self.carve([NT, 2, 130], BF16)
        self.pT = self.carve([3, 512], BF16)
        self.qraw = self.carve([2, 512], BF16)
        self.qsq = self.carve([2, 512], BF16)
        self.qrs = self.carve([1, 512], F32)
        self.accS = self.carve([4, 512], F32)
        self.osb = self.carve([2, 4, 128], F32)
        self.obf = self.carve([2, 4, 128], BF16)
        self.fsm = self.carve([2, 24], F32)
        self.da_arena_end = self._aoff
        self._aoff = 0
        g = {}
        g["BA"] = self.carve([NT, 16], F32)
        for nm in ("BETA", "GC", "EG", "EGL", "EKD"):
            g[nm] = self.carve([NT, 8], F32)
        g["GLB"] = g["BA"][:, :, :].rearrange("p a b -> p (a b)")[:, 0:NT * 8].rearrange("p (a b) -> p a b", a=NT)
        g["raw"] = self.carve([2, 516], F32)
        g["acc"] = self.carve([1, 512], F32)
        g["halo"] = self.carve([6, 4], F32)
        g["sT"] = [self.carve([2, 4, 3 * 128], BF16) for _ in range(2)]
        g["sq"] = self.carve([2, 512], BF16)
        g["zs"] = [self.carve([4, 256], BF16) for _ in range(2)]
        g["R"] = [self.carve([4, 4], F32) for _ in range(2)]
        g["SC"] = [self.carve([8, 8], F32) for _ in range(2)]
        g["Sf"] = self.carve([2, 128], F32)
        g["Sb"] = self.carve([2, 128], BF16)
        self.NCH = 4
        for c in range(self.NCH):
            g["kd", c] = self.carve([128], BF16)
            g["kbg", c] = self.carve([128], BF16)
            g["vb", c] = self.carve([128], BF16)
            g["E", c] = self.carve([256], F32)
            g["MA", c] = self.carve([256], BF16)
            g["Lb", c] = self.carve([128], BF16)
            g["QQ", c] = self.carve([256], BF16)
            g["TM", c] = self.carve([256], BF16)
            g["DD", c] = self.carve([2, 256], BF16)
            g["u", c] = self.carve([128], F32)
            g["wT", c] = self.carve([128], BF16)
            g["vn", c] = self.carve([128], BF16)
            g["o", c] = self.carve([128], F32)
            g["og", c] = self.carve([128], BF16)
            g["ps", c] = self.carve([8], F32)
        self.g = g
        self.gd_arena_end = self._aoff
        assert max(self.da_arena_end, self.gd_arena_end) <= self.ARENA_BYTES
        self.convw = self.sb("convw", [128, 2, 96], F32)
        self.gdc = self.sb("gdc", [128, 2, 4], F32)
        self.gsm = self.sb("gsm", [128, 2, 16], F32)
        self.NEGM = self.sb("NEGM", [128, 7, 256], BF16)
        self.trif = self.sb("trif", [128, 128], F32)
        self.sellast = self.sb("sellast", [128, 128], F32)
        self.onesf = self.sb("onesf", [128, 128], F32)
        self.masks = self.sb("masks", [128, 256], F32)
        self.onecol = self.sb("onecol", [128, 2], BF16)
        self.onesbd = self.sb("onesbd", [128, 128], BF16)
        self.negmask = self.sb("negmask", [128, 128], BF16)
        self.cda = self.sb("cda", [128, 16], F32)
        self.lamt = self.xs_f[:, 0, 256:512].rearrange("p (a b) -> p a b", a=4)
        self.lamp = self.sb("lamp", [128, 4], F32)
        self.scr_f = self.xs_f[:, 0, 0:128]
        self.scr_f2 = self.xs_f[:, 0, 128:256]
        self.PF = [self.es.enter_context(nc.psum_tensor("pf%d" % i, [128, 512], F32)) for i in range(6)]
        self.PB = [self.es.enter_context(nc.psum_tensor("pb%d" % i, [128, 1024], BF16)) for i in range(2)]
        self.ws = WStream(self.P, self.ring, self.NSLOT * 2, self.ws_future)
        self.dummy = self.sb("abar", [128, 2], F32)
        self.ARENA_NAMES = {"hT", "qT", "kT", "kTz", "accS", "va", "pT", "qraw", "qsq", "qrs", "osb", "obf", "fsm", "fss", "frs", "vones",
                            "BA", "BETA", "GC", "EG", "EGL", "EKD", "raw", "acc", "halo", "sT", "sq", "zs", "R", "SC", "Sf", "Sb",
                            "kd", "kbg", "vb", "E", "MA", "Lb", "QQ", "TM", "DD", "u", "wT", "vn", "o", "og", "psm"}

    def arena_barrier(self):
        self.P.arena_names = self.ARENA_NAMES
        dummy = self.dummy
        self.P.add("pool", lambda e: e.memset(dummy[:], 0.0), writes=["ARENA"])

    def setup_consts(self):
        P = self.P
        nc = self.nc
        identf, identb = self.identf, self.identb
        P.add("pool", lambda e: e.memset(identf[:], 0.0), writes=["identf"])
        P.add("pool", lambda e: e.memset(self.epsc[:], EPS), writes=["epsc"])
        P.add("pool", lambda e: e.affine_select(out=identf[:], in_=identf[:], pattern=[[-1, 128]],
                                                compare_op=ALU.not_equal, fill=1.0, base=0,
                                                channel_multiplier=1),
              reads=["identf"], writes=["identf"])
        P.add("dve", lambda e: e.tensor_copy(out=identb[:], in_=identf[:]), reads=["identf"], writes=["identb"])
        crow, gcol = self.crow, self.gcol
        mixr = self.d["mix_norm"].rearrange("l (kc p) -> (l kc) p", p=128)
        mlpr = self.d["mlp_norm"].rearrange("l (kc p) -> (l kc) p", p=128)
        P.add("sp", lambda e: e.dma_start(out=crow[0:32, :], in_=mixr), writes=[("xs", 1)], dma="c0")
        P.add("sp", lambda e: e.dma_start(out=crow[32:64, :], in_=mlpr), writes=[("xs", 1)], dma="c1")
        pf = self.PF[0]
        P.add("pe", lambda e: e.transpose(out=pf[:, 0:64], in_=crow[0:64, :], identity=identf[0:64, 0:64]),
              reads=[("xs", 1), "identf"], writes=[("pf", 0)])
        P.add("dve", lambda e: e.tensor_copy(out=gcol[:, :], in_=pf[:, 0:64]), reads=[("pf", 0)], writes=["gcol"])

    def setup_da_consts(self, j, l):
        P = self.P
        d = self.d
        cda, lamt, lamp = self.cda, self.lamt, self.lamp
        c0 = 5 * j
        col = lambda ap: ap.rearrange("(p o) -> p o", o=1)
        for half in range(2):
            P.add("sp", lambda e, half=half: e.dma_start(out=cda[64 * half:64 * half + 64, c0:c0 + 1], in_=col(d["da_q_norm"][j])),
                  writes=[("cda", j)], dma="cq%d%d" % (j, half))
            P.add("sp", lambda e, half=half: e.dma_start(out=cda[64 * half:64 * half + 64, c0 + 1:c0 + 2], in_=col(d["da_k_norm"][j])),
                  writes=[("cda", j)], dma="ck%d%d" % (j, half))
        P.add("sp", lambda e: e.dma_start(out=cda[:, c0 + 4:c0 + 5], in_=col(d["da_sub_norm"][j])),
              writes=[("cda", j)], dma="cs%d" % j)
        for i, nm in enumerate(["da_lambda_q1", "da_lambda_k1", "da_lambda_q2", "da_lambda_k2"]):
            P.add("sp", lambda e, i=i, nm=nm: e.dma_start(out=lamt[:, i, :], in_=d[nm][j:j + 1, :].partition_broadcast(128)),
                  writes=[("xs", 0)], dma="cl%d%d" % (j, i))
        lam_init = 0.8 - 0.6 * math.exp(-0.3 * l)
        for i in range(2):
            P.add("dve", lambda e, i=i: e.tensor_tensor(out=lamt[:, 2 * i, :], in0=lamt[:, 2 * i, :], in1=lamt[:, 2 * i + 1, :], op=ALU.mult),
                  reads=[("xs", 0)], writes=[("xs", 0)])
            P.add("dve", lambda e, i=i: e.reduce_sum(out=lamp[:, i:i + 1], in_=lamt[:, 2 * i, :], axis=AX.X),
                  reads=[("xs", 0)], writes=["lamp"])
        P.add("act", lambda e: e.activation(out=lamp[:, 0:2], in_=lamp[:, 0:2], func=AF.Exp), reads=["lamp"], writes=["lamp"])
        P.add("dve", lambda e: e.tensor_tensor(out=cda[:, c0 + 2:c0 + 3], in0=lamp[:, 0:1], in1=lamp[:, 1:2], op=ALU.subtract),
              reads=["lamp"], writes=[("cda", j)])
        P.add("dve", lambda e: e.tensor_scalar(out=cda[:, c0 + 2:c0 + 3], in0=cda[:, c0 + 2:c0 + 3], scalar1=lam_init, scalar2=None, op0=ALU.add),
              reads=[("cda", j)], writes=[("cda", j)])
        P.add("dve", lambda e: e.tensor_scalar(out=cda[:, c0 + 3:c0 + 4], in0=cda[:, c0 + 2:c0 + 3], scalar1=-1.0, scalar2=None, op0=ALU.mult),
              reads=[("cda", j)], writes=[("cda", j)])
        P.add("dve", lambda e: e.tensor_scalar(out=cda[:, c0:c0 + 1], in0=cda[:, c0:c0 + 1], scalar1=0.125, scalar2=None, op0=ALU.mult),
              reads=[("cda", j)], writes=[("cda", j)])
        P.add("dve", lambda e: e.tensor_scalar(out=cda[:, c0 + 4:c0 + 5], in0=cda[:, c0 + 4:c0 + 5], scalar1=1.0 - lam_init, scalar2=None, op0=ALU.mult),
              reads=[("cda", j)], writes=[("cda", j)])

    def setup_da_static(self):
        P = self.P
        onesbd, negmask, vaug = self.onesbd, self.negmask, self.vaug
        scr = self.scr_f
        P.add("pool", lambda e: e.memset(scr[:], 0.0), writes=[("xs", 0)])
        P.add("pool", lambda e: e.memset(scr[0:64, 0:64], 1.0 / 64), reads=[("xs", 0)], writes=[("xs", 0)])
        P.add("pool", lambda e: e.memset(scr[64:128, 64:128], 1.0 / 64), reads=[("xs", 0)], writes=[("xs", 0)])
        P.add("dve", lambda e: e.tensor_copy(out=onesbd[:], in_=scr[:]), reads=[("xs", 0)], writes=["onesbd"])
        scr2 = self.scr_f2
        P.add("pool", lambda e: e.memset(scr2[:], 0.0), writes=[("xs", 0)])
        P.add("pool", lambda e: e.affine_select(out=scr2[:], in_=scr2[:], pattern=[[1, 128]], compare_op=ALU.is_ge,
                                                fill=-30000.0, base=0, channel_multiplier=-1),
              reads=[("xs", 0)], writes=[("xs", 0)])
        P.add("dve", lambda e: e.tensor_copy(out=negmask[:], in_=scr2[:]), reads=[("xs", 0)], writes=["negmask"])

    def diffattn(self, l):
        P = self.P
        j = l // 2
        X, xnT, U, ring = self.X, self.xnT, self.U, self.ring
        qT, kTz, vaug, pT, accS = self.qT, self.kTz, self.vaug, self.pT, self.accS
        qraw, qsq, qrs = self.qraw, self.qsq, self.qrs
        cda, onesbd, negmask, identb = self.cda, self.onesbd, self.negmask, self.identb
        osb, obf, fsm = self.osb, self.obf, self.fsm
        PF, PB = self.PF, self.PB
        c0 = 5 * j
        win = self.d["da_w_in"][j].rearrange("(kc p) f -> p kc f", p=128)
        self.rmsnorm(8 * l)
        self.arena_barrier()
        P.add("pool", lambda e: e.memset(vaug[:, :, :, 128:130], 1.0), reads=[("xnT", 0, 0)], writes=["vones"])
        P.add("pool", lambda e: e.memset(kTz[64:128, 0, :, :], 0.0), reads=[("xnT", 0, 0)], writes=["kTz"])
        P.add("pool", lambda e: e.memset(kTz[0:64, 1, :, :], 0.0), reads=[("xnT", 0, 0)], writes=["kTz"])
        cnt = {"pf": 0, "nb": 0, "pt": 0, "fin": 0}
        for hp in range(4):
            sq = self.ws.get(win[:, :, hp * 256:(hp + 1) * 256], 256)
            sk = self.ws.get(win[:, :, 1024 + hp * 256:1024 + (hp + 1) * 256], 256)
            sv = self.ws.get(win[:, :, 2048 + hp * 256:2048 + (hp + 1) * 256], 256)
            jobs = [(which, slab, gcolq, hh, tt) for which, slab, gcolq in (("q", sq, c0), ("k", sk, c0 + 1))
                    for hh in range(2) for tt in range(4)]
            jstate = {}

            def emit_proj(i):
                which, slab, gcolq, hh, tt = jobs[i]
                slot, wkey, _ = slab
                bank = cnt["pf"] % 2
                cnt["pf"] += 1
                nb = cnt["nb"] % 2
                cnt["nb"] += 1
                jstate[i] = (bank, nb)
                pf = PF[bank]
                for kc in range(8):
                    P.add("pe", lambda e, pf=pf, slot=slot, kc=kc, hh=hh, tt=tt: e.matmul(
                        pf[:, :], slot[:, kc, hh * 128:(hh + 1) * 128], xnT[:, kc, tt * 512:(tt + 1) * 512],
                        start=(kc == 0), stop=(kc == 7)),
                        reads=[wkey] + [("xnT", t, kc) for t in range(4 * tt, 4 * tt + 4)], writes=[("pf", bank)])

            def emit_norm(i):
                which, slab, gcolq, hh, tt = jobs[i]
                bank, nb = jstate[i]
                pf = PF[bank]
                P.add("act", lambda e, pf=pf, nb=nb: e.activation(out=qraw[:, nb, :], in_=pf[:, :], func=AF.Copy),
                      reads=[("pf", bank)], writes=[("qraw", nb)])
                P.add("dve", lambda e, nb=nb: e.tensor_tensor(out=qsq[:, nb, :], in0=qraw[:, nb, :], in1=qraw[:, nb, :], op=ALU.mult),
                      reads=[("qraw", nb)], writes=[("qsq", nb)])
                mbank = 2 + nb
                pm = PF[mbank]
                P.add("pe", lambda e, pm=pm, nb=nb: e.matmul(pm[:, :], onesbd[:, :], qsq[:, nb, :], start=True, stop=True),
                      reads=[("qsq", nb), "onesbd"], writes=[("pf", mbank)])
                P.add("act", lambda e, pm=pm, nb=nb: e.activation(out=qrs[:, 0, :], in_=pm[:, :], func=AF.Ln, bias=self.epsc[:, 0:1], scale=1.0),
                      reads=[("pf", mbank), "epsc"], writes=[("qrs", 0)])
                P.add("act", lambda e, nb=nb: e.activation(out=qrs[:, 0, :], in_=qrs[:, 0, :], func=AF.Exp, scale=-0.5),
                      reads=[("qrs", 0)], writes=[("qrs", 0)])
                if which == "q":
                    P.add("dve", lambda e, nb=nb, hh=hh, tt=tt, gcolq=gcolq: e.scalar_tensor_tensor(
                        out=qT[:, hh, tt * 512:(tt + 1) * 512], in0=qraw[:, nb, :], scalar=cda[:, gcolq:gcolq + 1],
                        in1=qrs[:, 0, :], op0=ALU.mult, op1=ALU.mult),
                        reads=[("qraw", nb), ("qrs", 0), ("cda", j)], writes=[("qT", hh, tt)])
                else:
                    for c in range(2):
                        pl, ph = 64 * c, 64 * c + 64
                        P.add("dve", lambda e, nb=nb, hh=hh, tt=tt, gcolq=gcolq, c=c, pl=pl, ph=ph: e.scalar_tensor_tensor(
                            out=kTz[pl:ph, c, hh, tt * 512:(tt + 1) * 512], in0=qraw[pl:ph, nb, :], scalar=cda[pl:ph, gcolq:gcolq + 1],
                            in1=qrs[pl:ph, 0, :], op0=ALU.mult, op1=ALU.mult),
                            reads=[("qraw", nb), ("qrs", 0), ("cda", j), "kTz"], writes=[("kT", hh, tt, c)])

            emit_proj(0)
            for i in range(len(jobs)):
                if i + 1 < len(jobs):
                    emit_proj(i + 1)
                emit_norm(i)
            slot, wkey, _ = sv
            for t in range(NT):
                bank = cnt["pf"] % 2
                cnt["pf"] += 1
                pf = PF[bank]
                for kc in range(8):
                    P.add("pe", lambda e, pf=pf, slot=slot, kc=kc, t=t: e.matmul(
                        pf[:, 0:256], xnT[:, kc, t * 128:(t + 1) * 128], slot[:, kc, 0:256],
                        start=(kc == 0), stop=(kc == 7)),
                        reads=[wkey, ("xnT", t, kc)], writes=[("pf", bank)])
                P.add("act", lambda e, pf=pf, t=t: e.activation(
                    out=vaug[:, t, :, 0:128], in_=pf[:, 0:256].rearrange("p (h d) -> p h d", h=2), func=AF.Copy),
                    reads=[("pf", bank), "vones"], writes=[("va", t)])
            for sl in (sq, sk, sv):
                self.ws.done(sl[2])
            ATT, FIN = [], []
            for hh in range(2):
                h = 2 * hp + hh
                for qt in range(4):
                    P.begin_capture()
                    first_in_bank = {}
                    units = [(c, kb) for c in range(2) for kb in range(4 * qt + 4)]
                    ubank = {}

                    def emit_scores(u):
                        c, kb = units[u]
                        r = kb - 4 * qt
                        col0 = max(r, 0) * 128
                        bank = cnt["pf"] % 2
                        cnt["pf"] += 1
                        ubank[u] = bank
                        ps = PF[bank]
                        krd = [("kT", hh, kb // 4, c)]
                        qrd = [("qT", hh, qt)]
                        if r >= 0:
                            P.add("pe", lambda e, ps=ps, c=c, hh=hh, kb=kb, qt=qt, col0=col0: e.matmul(
                                ps[:, col0:col0 + 128], kTz[:, c, hh, kb * 128:(kb + 1) * 128],
                                qT[:, hh, qt * 512 + col0:qt * 512 + col0 + 128], start=True, stop=False,
                                skip_group_check=True),
                                reads=krd + qrd, writes=[("pf", bank)])
                            P.add("pe", lambda e, ps=ps, col0=col0: e.matmul(
                                ps[:, col0:col0 + 128], identb[:, :], negmask[:, :], start=False, stop=True,
                                skip_group_check=True),
                                reads=["identb", "negmask"], writes=[("pf", bank)])
                            if col0 + 128 < 512:
                                P.add("pe", lambda e, ps=ps, c=c, hh=hh, kb=kb, qt=qt, col0=col0: e.matmul(
                                    ps[:, col0 + 128:512], kTz[:, c, hh, kb * 128:(kb + 1) * 128],
                                    qT[:, hh, qt * 512 + col0 + 128:qt * 512 + 512], start=True, stop=True,
                                    skip_group_check=True),
                                    reads=krd + qrd, writes=[("pf", bank)])
                        else:
                            P.add("pe", lambda e, ps=ps, c=c, hh=hh, kb=kb, qt=qt: e.matmul(
                                ps[:, :], kTz[:, c, hh, kb * 128:(kb + 1) * 128],
                                qT[:, hh, qt * 512:qt * 512 + 512], start=True, stop=True, skip_group_check=True),
                                reads=krd + qrd, writes=[("pf", bank)])

                    def emit_exp_pv(u):
                        c, kb = units[u]
                        r = kb - 4 * qt
                        col0 = max(r, 0) * 128
                        bank = ubank[u]
                        ps = PF[bank]
                        pi = cnt["pt"] % 3
                        cnt["pt"] += 1
                        P.add("act", lambda e, ps=ps, pi=pi, col0=col0: e.activation(
                            out=pT[:, pi, col0:512], in_=ps[:, col0:512], func=AF.Exp),
                            reads=[("pf", bank)], writes=[("pT", pi)])
                        for rr in range(max(r, 0), 4):
                            abank = 2 + 2 * c + rr // 2
                            off = (rr % 2) * 256
                            st = abank not in first_in_bank
                            first_in_bank[abank] = True
                            P.add("pe", lambda e, abank=abank, off=off, pi=pi, rr=rr, kb=kb, st=st, hh=hh, qt=qt: e.matmul(
                                PF[abank][:, off:off + 129], pT[:, pi, rr * 128:(rr + 1) * 128], vaug[:, kb, hh, 0:129],
                                start=st, stop=(kb == 4 * qt + rr), skip_group_check=True),
                                reads=[("pT", pi), ("va", kb), "vones"], writes=[("pf", abank)])

                    emit_scores(0)
                    for u in range(len(units)):
                        if u + 1 < len(units):
                            emit_scores(u + 1)
                        emit_exp_pv(u)
                    for b in range(4):
                        srcv = PF[2 + b][:, :].rearrange("p (s w) -> p s w", s=2)[:, :, 0:129]
                        dstv = accS[:, b, :].rearrange("p (s w) -> p s w", s=2)[:, :, 0:129]
                        if b % 2 == 0:
                            P.add("act", lambda e, srcv=srcv, dstv=dstv: e.activation(out=dstv, in_=srcv, func=AF.Copy),
                                  reads=[("pf", 2 + b)], writes=[("accS", b)])
                        else:
                            P.add("dve", lambda e, srcv=srcv, dstv=dstv: e.tensor_copy(out=dstv, in_=srcv),
                                  reads=[("pf", 2 + b)], writes=[("accS", b)])
                    ATT.append(P.end_capture())
                    P.begin_capture()
                    fi = cnt["fin"] % 2
                    cnt["fin"] += 1
                    AK = [("accS", b) for b in range(4)]
                    FK = ("fsm", fi)
                    slots = accS[:, :, :].rearrange("p b (s w) -> p (b s) w", s=2)
                    P.add("dve", lambda e, fi=fi, slots=slots: e.reciprocal(out=fsm[:, fi, 0:8], in_=slots[:, :, 128]),
                          reads=AK, writes=[FK])
                    P.add("dve", lambda e, fi=fi: e.tensor_scalar(out=fsm[:, fi, 8:12], in0=fsm[:, fi, 4:8], scalar1=cda[:, c0 + 3:c0 + 4], scalar2=None, op0=ALU.mult),
                          reads=[FK, ("cda", j)], writes=[FK])
                    for rr in range(4):
                        P.add("act", lambda e, fi=fi, rr=rr, slots=slots: e.activation(out=osb[:, fi, rr, :], in_=slots[:, rr, 0:128], func=AF.Copy,
                                                                                        scale=fsm[:, fi, rr:rr + 1]),
                              reads=AK + [FK], writes=[("osb", fi, rr)])
                    for rr in range(4):
                        P.add("dve", lambda e, fi=fi, rr=rr, slots=slots: e.scalar_tensor_tensor(
                            out=osb[:, fi, rr, :], in0=slots[:, 4 + rr, 0:128], scalar=fsm[:, fi, 8 + rr:9 + rr], in1=osb[:, fi, rr, :],
                            op0=ALU.mult, op1=ALU.add),
                            reads=AK + [FK, ("osb", fi, rr)], writes=[("osb", fi, rr)])
                    for rr in range(4):
                        P.add("act", lambda e, fi=fi, rr=rr: e.activation(out=obf[:, fi, rr, :], in_=osb[:, fi, rr, :], func=AF.Square,
                                                                          accum_out=fsm[:, fi, 12 + rr:13 + rr]),
                              reads=[("osb", fi, rr)], writes=[("obf", fi, rr), ("fss", fi, rr)])
                    P.add("act", lambda e, fi=fi: e.activation(out=fsm[:, fi, 16:20], in_=fsm[:, fi, 12:16], func=AF.Ln,
                                                               bias=self.epsc[:, 0:1], scale=1.0 / 128),
                          reads=[("fss", fi, rr) for rr in range(4)] + ["epsc"], writes=[("frs", fi)])
                    P.add("act", lambda e, fi=fi: e.activation(out=fsm[:, fi, 16:20], in_=fsm[:, fi, 16:20], func=AF.Exp, scale=-0.5),
                          reads=[("frs", fi)], writes=[("frs", fi)])
                    for rr in range(4):
                        P.add("dve", lambda e, fi=fi, rr=rr: e.tensor_scalar(out=obf[:, fi, rr, :], in0=osb[:, fi, rr, :], scalar1=fsm[:, fi, 16 + rr:17 + rr],
                                                                              scalar2=None, op0=ALU.mult),
                              reads=[("osb", fi, rr), ("frs", fi), ("obf", fi, rr)], writes=[("obf", fi, rr)])
                    pb = PB[fi]
                    for rr in range(4):
                        P.add("pe", lambda e, pb=pb, fi=fi, rr=rr: e.transpose(out=pb[:, rr * 128:(rr + 1) * 128], in_=obf[:, fi, rr, :], identity=identb[:]),
                              reads=[("obf", fi, rr), "identb"], writes=[("pb", fi)])
                    P.add("dve", lambda e, pb=pb, hh=hh, qt=qt: e.tensor_scalar(
                        out=U[:, hh, qt * 512:(qt + 1) * 512], in0=pb[:, 0:512], scalar1=cda[:, c0 + 4:c0 + 5], scalar2=None, op0=ALU.mult),
                        reads=[("pb", fi), ("cda", j)], writes=[("U", hh, qt)])
                    FIN.append(P.end_capture())
            P.replay([ATT[0]])
            for i in range(1, 8):
                P.replay([ATT[i], FIN[i - 1]])
            P.replay([FIN[7]])
            wsrc = self.d["da_w_out"][j][hp * 256:(hp + 1) * 256, :].rearrange("(kc p) f -> p kc f", p=128)
            so = self.ws.get(wsrc, (2, 1024))
            wv = so[0]
            for t in range(NT):
                for dh in range(2):
                    bank = cnt["pf"] % 2
                    cnt["pf"] += 1
                    pf = PF[bank]
                    for hc in range(2):
                        P.add("pe", lambda e, pf=pf, wv=wv, hc=hc, t=t, dh=dh: e.matmul(
                            pf[:, :], U[:, hc, t * 128:(t + 1) * 128], wv[:, hc, dh * 512:(dh + 1) * 512], start=(hc == 0), stop=(hc == 1)),
                            reads=[so[1], ("U", hc, t // 4)], writes=[("pf", bank)])
                    P.add("dve", lambda e, pf=pf, t=t, dh=dh: e.tensor_tensor(
                        out=X[:, t, dh * 512:(dh + 1) * 512], in0=X[:, t, dh * 512:(dh + 1) * 512], in1=pf[:, :], op=ALU.add),
                        reads=[("pf", bank), ("x", t)], writes=[("x", t)])
            self.ws.done(so[2])

    def setup_gd_static(self):
        P = self.P
        trif, sellast, onesf, masks, onecol = self.trif, self.sellast, self.onesf, self.masks, self.onecol
        P.add("pool", lambda e: e.memset(onesf[:], 1.0), writes=["onesf"])
        P.add("pool", lambda e: e.memset(onecol[:], 1.0), writes=["onecol"])
        P.add("pool", lambda e: e.memset(trif[:], 1.0), writes=["trif"])
        P.add("pool", lambda e: e.affine_select(out=trif[:], in_=trif[:], pattern=[[1, 128]], compare_op=ALU.is_ge,
                                                fill=0.0, base=0, channel_multiplier=-1), reads=["trif"], writes=["trif"])
        P.add("pool", lambda e: e.memset(sellast[:], 1.0), writes=["sellast"])
        P.add("pool", lambda e: e.affine_select(out=sellast[:], in_=sellast[:], pattern=[[0, 128]], compare_op=ALU.is_ge,
                                                fill=0.0, base=-127, channel_multiplier=1), reads=["sellast"], writes=["sellast"])
        P.add("pool", lambda e: e.memset(masks[:], 0.0), writes=["masks"])
        P.add("pool", lambda e: e.affine_select(out=masks[:, 0:128], in_=masks[:, 0:128], pattern=[[1, 128]], compare_op=ALU.is_ge,
                                                fill=-30000.0, base=-1, channel_multiplier=-1), reads=["masks"], writes=["masks"])
        P.add("pool", lambda e: e.affine_select(out=masks[:, 128:256], in_=masks[:, 128:256], pattern=[[1, 128]], compare_op=ALU.is_ge,
                                                fill=-30000.0, base=0, channel_multiplier=-1), reads=["masks"], writes=["masks"])

    def setup_gd_levelmasks(self):
        P = self.P
        NEGM = self.NEGM
        scrA = lambda nb: self.xs_f[0:nb, 0, 0:128]
        scrC = lambda nb: self.xs_f[0:nb, 0, 128:256]
        K0 = [("xs", 0)]
        for k in range(7):
            B, half = 2 ** (k + 1), 2 ** k
            nb = 128 // B
            A, C = scrA(nb), scrC(nb)
            P.add("pool", lambda e, nb=nb: e.memset(self.xs_f[0:nb, 0, 0:256], 1.0), writes=K0)
            P.add("pool", lambda e, A=A, B=B, half=half: e.affine_select(out=A, in_=A, pattern=[[1, 128]], compare_op=ALU.is_ge, fill=0.0,
                                                                       base=-half, channel_multiplier=-B), reads=K0, writes=K0)
            P.add("pool", lambda e, A=A, B=B: e.affine_select(out=A, in_=A, pattern=[[-1, 128]], compare_op=ALU.is_ge, fill=0.0,
                                                             base=B - 1, channel_multiplier=B), reads=K0, writes=K0)
            P.add("pool", lambda e, C=C, B=B: e.affine_select(out=C, in_=C, pattern=[[1, 128]], compare_op=ALU.is_ge, fill=0.0,
                                                             base=0, channel_multiplier=-B), reads=K0, writes=K0)
            P.add("pool", lambda e, C=C, B=B, half=half: e.affine_select(out=C, in_=C, pattern=[[-1, 128]], compare_op=ALU.is_ge, fill=0.0,
                                                                       base=half - 1, channel_multiplier=B), reads=K0, writes=K0)
            pf = self.PF[1]
            P.add("pe", lambda e, pf=pf, A=A, C=C: e.matmul(pf[:, 0:128], A, C, start=True, stop=True, skip_group_check=True),
                  reads=K0, writes=[("pf", 1)])
            P.add("pe", lambda e, pf=pf, A=A, C=C: e.matmul(pf[:, 128:256], C, A, start=True, stop=True, skip_group_check=True),
                  reads=K0, writes=[("pf", 1)])
            P.add("act", lambda e, pf=pf, k=k: e.activation(out=NEGM[:, k, :], in_=pf[:, 0:256], func=AF.Copy),
                  reads=[("pf", 1)], writes=["NEGM"])

    def setup_gd_consts(self, j):
        P = self.P
        d = self.d
        convw, gdc, identf = self.convw, self.gdc, self.identf
        crow96 = self.xs_f[0:96, 1, 0:128]
        rows = d["gd_conv_w"][j].rearrange("k (c p) -> (k c) p", p=128)
        P.add("sp", lambda e: e.dma_start(out=crow96, in_=rows), writes=[("xs", 1)], dma="gcw%d" % j)
        pf = self.PF[0]
        P.add("pe", lambda e: e.transpose(out=pf[:, 0:96], in_=crow96, identity=identf[0:96, 0:96]),
              reads=[("xs", 1), "identf"], writes=[("pf", 0)])
        P.add("dve", lambda e: e.tensor_copy(out=convw[:, j, :], in_=pf[:, 0:96]), reads=[("pf", 0)], writes=[("convw", j)])
        col = lambda ap: ap.rearrange("(p o) -> p o", o=1)
        P.add("sp", lambda e: e.dma_start(out=gdc[:, j, 0:1], in_=col(d["gd_out_norm"][j])), writes=[("gdc", j)], dma="gon%d" % j)
        gsm = self.gsm
        P.add("sp", lambda e: e.dma_start(out=gsm[:, j, 0:8], in_=d["gd_a_log"][j:j + 1, :].partition_broadcast(128)),
              writes=[("gsm", j)], dma="gal%d" % j)
        P.add("sp", lambda e: e.dma_start(out=gsm[:, j, 8:16], in_=d["gd_dt_bias"][j:j + 1, :].partition_broadcast(128)),
              writes=[("gsm", j)], dma="gdt%d" % j)
        P.add("act", lambda e: e.activation(out=gsm[:, j, 0:8], in_=gsm[:, j, 0:8], func=AF.Exp), reads=[("gsm", j)], writes=[("gsm", j)])
        P.add("dve", lambda e: e.tensor_scalar(out=gsm[:, j, 0:8], in0=gsm[:, j, 0:8], scalar1=-1.0, scalar2=None, op0=ALU.mult),
              reads=[("gsm", j)], writes=[("gsm", j)])

    def gdn(self, l):
        P = self.P
        j = l // 2
        g = self.g
        X, xnT, U, ring = self.X, self.xnT, self.U, self.ring
        identb, identf = self.identb, self.identf
        PF, PB = self.PF, self.PB
        epsc = self.epsc
        DKS = 128.0 ** -0.5
        win = self.d["gd_w_in"][j].rearrange("(kc p) f -> p kc f", p=128)
        self.rmsnorm(8 * l)
        self.arena_barrier()
        XN0 = [("xnT", 0, 0)]
        cnt = {"pf": 0, "rb": 0, "sq": 0}
        BA, BETA, GC, EG, GLB, EGL, EKD = (g[k] for k in ("BA", "BETA", "GC", "EG", "GLB", "EGL", "EKD"))
        flat = lambda v: v.rearrange("p a b -> p (a b)")

        sba = self.ws.get(win[:, :, 4096:4112], 16)
        slot_ba, wkey_ba, _ = sba
        pf_ba = PF[0]
        for t in range(NT):
            for kc in range(8):
                P.add("pe", lambda e, t=t, kc=kc: e.matmul(pf_ba[:, t * 16:(t + 1) * 16], xnT[:, kc, t * 128:(t + 1) * 128],
                                                          slot_ba[:, kc, 0:16], start=(kc == 0), stop=(kc == 7),
                                                          skip_group_check=True),
                      reads=[wkey_ba, ("xnT", t, kc)], writes=[("pf", 0)])
        self.ws.done(sba[2])
        P.add("dve", lambda e: e.tensor_copy(out=flat(BA), in_=pf_ba[:, 0:256]), reads=[("pf", 0)] + XN0, writes=["BA"])
        P.add("act", lambda e: e.activation(out=BETA, in_=BA[:, :, 0:8], func=AF.Exp, scale=-1.0), reads=["BA"] + XN0, writes=["BETA"])
        P.add("act", lambda e: e.activation(out=BETA, in_=BETA, func=AF.Ln, bias=1.0), reads=["BETA"], writes=["BETA"])
        P.add("act", lambda e: e.activation(out=BETA, in_=BETA, func=AF.Exp, scale=-1.0), reads=["BETA"], writes=["BETA"])
        gsm = self.gsm
        for t in range(NT):
            P.add("dve", lambda e, t=t: e.tensor_tensor(out=GC[:, t, :], in0=BA[:, t, 8:16], in1=gsm[:, j, 8:16], op=ALU.add),
                  reads=["BA", ("gsm", j)] + XN0, writes=["GC"])
        P.add("act", lambda e: e.activation(out=GC, in_=GC, func=AF.Exp), reads=["GC"], writes=["GC"])
        P.add("act", lambda e: e.activation(out=GC, in_=GC, func=AF.Ln, bias=1.0), reads=["GC"], writes=["GC"])
        for t in range(NT):
            P.add("dve", lambda e, t=t: e.tensor_tensor(out=GLB[:, t, :], in0=GC[:, t, :], in1=gsm[:, j, 0:8], op=ALU.mult),
                  reads=["GC", "BETA", ("gsm", j)] + XN0, writes=["BA"])
        pf1 = PF[1]
        for t in range(NT):
            P.add("pe", lambda e, t=t: e.matmul(pf1[:, t * 8:(t + 1) * 8], self.trif[:, :], GLB[:, t, :], start=True, stop=True,
                                                skip_group_check=True),
                  reads=["BA", "trif"], writes=[("pf", 1)])
        P.add("dve", lambda e: e.tensor_copy(out=flat(GC), in_=pf1[:, 0:128]), reads=[("pf", 1)], writes=["GC"])
        P.add("act", lambda e: e.activation(out=EG, in_=GC, func=AF.Exp), reads=["GC"] + XN0, writes=["EG"])
        for t in range(NT):
            P.add("pe", lambda e, t=t: e.matmul(pf1[:, 128 + t * 8:128 + (t + 1) * 8], self.sellast[:, :], GC[:, t, :], start=True, stop=True,
                                                skip_group_check=True),
                  reads=["GC", "sellast"], writes=[("pf", 1)])
        P.add("dve", lambda e: e.tensor_copy(out=flat(GLB), in_=pf1[:, 128:256]), reads=[("pf", 1)], writes=["BA"])
        P.add("act", lambda e: e.activation(out=EGL, in_=GLB, func=AF.Exp), reads=["BA"] + XN0, writes=["EGL"])
        P.add("dve", lambda e: e.tensor_tensor(out=EKD, in0=GLB, in1=GC, op=ALU.subtract), reads=["BA", "GC"] + XN0, writes=["EKD"])
        P.add("act", lambda e: e.activation(out=EKD, in_=EKD, func=AF.Exp), reads=["EKD"], writes=["EKD"])

        raw, acc, halo, sq, Sf, Sb = (g[k] for k in ("raw", "acc", "halo", "sq", "Sf", "Sb"))
        convw = self.convw
        NEGM = self.NEGM
        def capture_pair(hp):
            slabs = {}
            for wi, which in enumerate(("q", "k", "v", "z")):
                slabs[which] = self.ws.get(win[:, :, wi * 1024 + hp * 256: wi * 1024 + (hp + 1) * 256], 256)
            wsrc = self.d["gd_w_out"][j][hp * 256:(hp + 1) * 256, :].rearrange("(kc p) f -> p kc f", p=128)
            so = self.ws.get(wsrc, (2, 1024))
            FE, PREP, SCAN, OUT = {}, {}, {}, {}
            for gi in range(4):
                gp = gi % 2
                sT, zs, R, SC = g["sT"][gp], g["zs"][gp], g["R"][gp], g["SC"][gp]
                P.begin_capture()
                if gi == 0:
                    P.add("pool", lambda e: e.memset(flat(halo), 0.0), reads=XN0, writes=[("halo", i) for i in range(6)])
                for wi, which in enumerate(("k", "q", "v")):
                    slot, wkey, _ = slabs[which]
                    cbase = {"q": 0, "k": 8, "v": 16}[which]
                    for hh in range(2):
                        h = 2 * hp + hh
                        pf = PF[0]
                        rb = cnt["rb"] % 2
                        cnt["rb"] += 1
                        hi = wi * 2 + hh
                        P.add("dve", lambda e, rb=rb, hi=hi: e.tensor_copy(out=raw[:, rb, 0:3], in_=halo[:, hi, 0:3]),
                              reads=[("halo", hi)], writes=[("raw", rb)])
                        P.begin_atomic()
                        for kc in range(8):
                            P.add("pe", lambda e, pf=pf, slot=slot, kc=kc, hh=hh, gi=gi: e.matmul(
                                pf[:, :], slot[:, kc, hh * 128:(hh + 1) * 128], xnT[:, kc, gi * 512:(gi + 1) * 512],
                                start=(kc == 0), stop=(kc == 7)),
                                reads=[wkey] + [("xnT", t, kc) for t in range(4 * gi, 4 * gi + 4)], writes=[("pf", 0)])
                        P.add("act", lambda e, pf=pf, rb=rb: e.activation(out=raw[:, rb, 3:515], in_=pf[:, :], func=AF.Copy),
                              reads=[("pf", 0), ("raw", rb)], writes=[("raw", rb)])
                        P.end_atomic()
                        if gi < 3:
                            P.add("dve", lambda e, rb=rb, hi=hi: e.tensor_copy(out=halo[:, hi, 0:3], in_=raw[:, rb, 512:515]),
                                  reads=[("raw", rb)], writes=[("halo", hi)])
                        cc = cbase + h
                        P.add("dve", lambda e, rb=rb, cc=cc: e.tensor_scalar(
                            out=acc[:, 0, :], in0=raw[:, rb, 0:512], scalar1=convw[:, j, cc:cc + 1], scalar2=None, op0=ALU.mult),
                            reads=[("raw", rb), ("convw", j)], writes=[("acc", 0)])
                        for tap in range(1, 4):
                            P.add("dve", lambda e, rb=rb, cc=cc, tap=tap: e.scalar_tensor_tensor(
                                out=acc[:, 0, :], in0=raw[:, rb, tap:tap + 512], scalar=convw[:, j, tap * 24 + cc:tap * 24 + cc + 1],
                                in1=acc[:, 0, :], op0=ALU.mult, op1=ALU.add),
                                reads=[("raw", rb), ("acc", 0), ("convw", j)], writes=[("acc", 0)])
                        sgb = raw[:, rb, 0:512]
                        P.add("act", lambda e, sgb=sgb: e.activation(out=sgb, in_=acc[:, 0, :], func=AF.Exp, scale=-1.0),
                              reads=[("acc", 0), ("raw", rb), ("halo", hi)], writes=[("raw", rb)])
                        P.add("act", lambda e, sgb=sgb: e.activation(out=sgb, in_=sgb, func=AF.Ln, bias=1.0), reads=[("raw", rb)], writes=[("raw", rb)])
                        P.add("act", lambda e, sgb=sgb: e.activation(out=sgb, in_=sgb, func=AF.Exp, scale=-1.0), reads=[("raw", rb)], writes=[("raw", rb)])
                        P.add("dve", lambda e, sgb=sgb, hh=hh, wi=wi, sT=sT: e.tensor_tensor(
                            out=sT[:, hh, :, wi * 128:(wi + 1) * 128], in0=acc[:, 0, :].rearrange("p (a b) -> p a b", a=4),
                            in1=sgb.rearrange("p (a b) -> p a b", a=4), op=ALU.mult),
                            reads=[("acc", 0), ("raw", rb)], writes=[("sT", gp, hh, wi)])
                        if which in ("k", "q"):
                            sb_ = cnt["sq"] % 2
                            cnt["sq"] += 1
                            P.add("pool", lambda e, hh=hh, wi=wi, sb_=sb_, sT=sT: e.tensor_tensor(
                                out=sq[:, sb_, :].rearrange("p (a b) -> p a b", a=4), in0=sT[:, hh, :, wi * 128:(wi + 1) * 128],
                                in1=sT[:, hh, :, wi * 128:(wi + 1) * 128], op=ALU.mult),
                                reads=[("sT", gp, hh, wi)], writes=[("sq", sb_)])
                            for tl in range(4):
                                colr = tl * 4 + wi * 2 + hh
                                P.add("pe", lambda e, sb_=sb_, tl=tl, colr=colr: e.matmul(
                                    PF[1][:, 384 + colr:384 + colr + 1], sq[:, sb_, tl * 128:(tl + 1) * 128], self.onecol[:, 0:1],
                                    start=True, stop=True, skip_group_check=True),
                                    reads=[("sq", sb_), "onecol"], writes=[("pf", 1)])
                slot, wkey, _ = slabs["z"]
                for tl in range(4):
                    t = 4 * gi + tl
                    pf = PF[1]
                    for kc in range(8):
                        P.add("pe", lambda e, pf=pf, slot=slot, kc=kc, t=t: e.matmul(
                            pf[:, 0:256], xnT[:, kc, t * 128:(t + 1) * 128], slot[:, kc, 0:256], start=(kc == 0), stop=(kc == 7),
                            skip_group_check=True),
                            reads=[wkey, ("xnT", t, kc)], writes=[("pf", 1)])
                    zt = acc[:, 0, 0:256]
                    zr = acc[:, 0, 256:512]
                    P.add("act", lambda e, pf=pf, zt=zt: e.activation(out=zt, in_=pf[:, 0:256], func=AF.Exp, scale=-1.0),
                          reads=[("pf", 1)], writes=[("acc", 0)])
                    P.add("act", lambda e, pf=pf, zr=zr: e.activation(out=zr, in_=pf[:, 0:256], func=AF.Copy),
                          reads=[("pf", 1), ("acc", 0)], writes=[("acc", 0)])
                    P.add("act", lambda e, zt=zt: e.activation(out=zt, in_=zt, func=AF.Ln, bias=1.0), reads=[("acc", 0)], writes=[("acc", 0)])
                    P.add("act", lambda e, zt=zt: e.activation(out=zt, in_=zt, func=AF.Exp, scale=-1.0), reads=[("acc", 0)], writes=[("acc", 0)])
                    P.add("dve", lambda e, zt=zt, zr=zr, tl=tl, zs=zs: e.tensor_tensor(out=zs[:, tl, :], in0=zr, in1=zt, op=ALU.mult),
                          reads=[("acc", 0)], writes=[("zs", gp, tl)])
                Rk = ("R", gp)
                P.add("act", lambda e, R=R: e.activation(out=flat(R), in_=PF[1][:, 384:400], func=AF.Ln, bias=epsc[:, 0:1], scale=1.0),
                      reads=[("pf", 1), "epsc"], writes=[Rk])
                P.add("act", lambda e, R=R: e.activation(out=flat(R), in_=flat(R), func=AF.Exp, scale=-0.5), reads=[Rk], writes=[Rk])
                hs = slice(2 * hp, 2 * hp + 2)
                ts = slice(4 * gi, 4 * gi + 4)
                rk, rq = R[:, :, 0:2], R[:, :, 2:4]
                scv = lambda q_, SC=SC: SC[:, q_, :].rearrange("p (a b) -> p a b", a=4)
                T1, CKBG, CKD, CQ, UL, UA, BIAS, LN = (scv(i) for i in range(8))
                bt, egs, ekds, gcs = BETA[:, ts, hs], EG[:, ts, hs], EKD[:, ts, hs], GC[:, ts, hs]
                sk = lambda i: ("SC", gp, i)
                P.add("dve", lambda e, T1=T1, rk=rk, bt=bt: e.tensor_tensor(out=T1, in0=rk, in1=bt, op=ALU.mult), reads=[Rk, "BETA"], writes=[sk(0)])
                P.add("dve", lambda e, CKBG=CKBG, T1=T1, egs=egs: e.tensor_tensor(out=CKBG, in0=T1, in1=egs, op=ALU.mult), reads=[sk(0), "EG"], writes=[sk(1)])
                P.add("dve", lambda e, CKD=CKD, rk=rk, ekds=ekds: e.tensor_tensor(out=CKD, in0=rk, in1=ekds, op=ALU.mult), reads=[Rk, "EKD"], writes=[sk(2)])
                P.add("dve", lambda e, CQ=CQ, rq=rq, egs=egs: e.scalar_tensor_tensor(out=CQ, in0=rq, scalar=DKS, in1=egs, op0=ALU.mult, op1=ALU.mult),
                      reads=[Rk, "EG"], writes=[sk(3)])
                P.add("act", lambda e, UL=UL, T1=T1: e.activation(out=UL, in_=T1, func=AF.Ln), reads=[sk(0)], writes=[sk(4)])
                P.add("dve", lambda e, UL=UL, gcs=gcs: e.tensor_tensor(out=UL, in0=UL, in1=gcs, op=ALU.add), reads=[sk(4), "GC"], writes=[sk(4)])
                P.add("act", lambda e, UA=UA, rq=rq: e.activation(out=UA, in_=rq, func=AF.Ln, scale=DKS), reads=[Rk], writes=[sk(5)])
                P.add("dve", lambda e, UA=UA, gcs=gcs: e.tensor_tensor(out=UA, in0=UA, in1=gcs, op=ALU.add), reads=[sk(5), "GC"], writes=[sk(5)])
                P.add("act", lambda e, BIAS=BIAS, rk=rk: e.activation(out=BIAS, in_=rk, func=AF.Ln), reads=[Rk], writes=[sk(6)])
                P.add("dve", lambda e, BIAS=BIAS, gcs=gcs: e.tensor_tensor(out=BIAS, in0=BIAS, in1=gcs, op=ALU.subtract), reads=[sk(6), "GC"], writes=[sk(6)])
                FE[gi] = P.end_capture()
                SCK = [sk(i) for i in range(7)]
                for tl in range(4):
                    t = 4 * gi + tl
                    stageA = []
                    for c in range(2):
                        P.begin_capture()
                        hh = c
                        cs = 2 * (t % 2) + c
                        pbk, pbo = cs // 2, (cs % 2) * 512
                        sc1 = lambda q_, tl=tl, hh=hh, SC=SC: SC[:, q_, tl * 2 + hh:tl * 2 + hh + 1]
                        pb, pc = PB[pbk], PF[2 + cs]
                        kd, kbg, vb, E, MA = (g[k, cs] for k in ("kd", "kbg", "vb", "E", "MA"))
                        ksT = sT[:, hh, tl, 0:128]
                        vsT = sT[:, hh, tl, 256:384]
                        P.add("pe", lambda e, pb=pb, ksT=ksT, pbo=pbo: e.transpose(out=pb[:, pbo:pbo + 128], in_=ksT, identity=identb[:]),
                              reads=[("sT", gp, hh, 0), "identb"], writes=[("pb", pbk)])
                        P.add("pe", lambda e, pb=pb, vsT=vsT, pbo=pbo: e.transpose(out=pb[:, pbo + 128:pbo + 256], in_=vsT, identity=identb[:]),
                              reads=[("sT", gp, hh, 2), "identb"], writes=[("pb", pbk)])
                        P.add("act", lambda e, pb=pb, kd=kd, sc1=sc1, pbo=pbo: e.activation(out=kd, in_=pb[:, pbo:pbo + 128], func=AF.Copy, scale=sc1(2)),
                              reads=[("pb", pbk)] + SCK, writes=[("kd", cs)])
                        P.add("dve", lambda e, pb=pb, kbg=kbg, sc1=sc1, pbo=pbo: e.tensor_scalar(out=kbg, in0=pb[:, pbo:pbo + 128], scalar1=sc1(1), scalar2=None, op0=ALU.mult),
                              reads=[("pb", pbk)] + SCK, writes=[("kbg", cs)])
                        hcol = 2 * hp + hh
                        P.add("dve", lambda e, pb=pb, vb=vb, t=t, hcol=hcol, pbo=pbo: e.tensor_scalar(
                            out=vb, in0=pb[:, pbo + 128:pbo + 256], scalar1=BETA[:, t, hcol:hcol + 1], scalar2=None, op0=ALU.mult),
                            reads=[("pb", pbk), "BETA"], writes=[("vb", cs)])
                        P.add("pe", lambda e, pc=pc, ksT=ksT, hh=hh, tl=tl, sT=sT: e.matmul(pc[:, 0:256], ksT, sT[:, hh, tl, 0:256], start=True, stop=True,
                                                                                              skip_group_check=True),
                              reads=[("sT", gp, hh, 0), ("sT", gp, hh, 1)], writes=[("pf", 2 + cs)])
                        P.add("act", lambda e, E=E, sc1=sc1: e.activation(out=E[:, 0:128], in_=identf[:, :], func=AF.Copy, scale=sc1(4)),
                              reads=["identf"] + SCK, writes=[("E", cs)])
                        P.add("act", lambda e, E=E, sc1=sc1: e.activation(out=E[:, 128:256], in_=identf[:, :], func=AF.Copy, scale=sc1(5)),
                              reads=["identf", ("E", cs)] + SCK, writes=[("E", cs)])
                        P.add("pe", lambda e, pc=pc, E=E: e.matmul(pc[:, 256:512], self.onesf[:, :], E[:, :], start=True, stop=False, skip_group_check=True),
                              reads=[("E", cs), "onesf"], writes=[("pf", 2 + cs)])
                        P.add("pe", lambda e, pc=pc: e.matmul(pc[:, 256:512], identf[:, :], self.masks[:, :], start=False, stop=True, skip_group_check=True),
                              reads=["identf", "masks"], writes=[("pf", 2 + cs)])
                        P.add("act", lambda e, pc=pc, E=E, sc1=sc1: e.activation(out=E[:, :], in_=pc[:, 256:512], func=AF.Exp, bias=sc1(6)),
                              reads=[("pf", 2 + cs)] + SCK, writes=[("E", cs)])
                        P.add("dve", lambda e, pc=pc, E=E, MA=MA: e.tensor_tensor(out=MA[:, :], in0=pc[:, 0:256], in1=E[:, :], op=ALU.mult),
                              reads=[("pf", 2 + cs), ("E", cs)], writes=[("MA", cs)])
                        Lb, DD, TM = g["Lb", cs], g["DD", cs], g["TM", cs]
                        P.add("pe", lambda e, pb=pb, MA=MA, pbo=pbo: e.transpose(out=pb[:, pbo + 256:pbo + 384], in_=MA[:, 0:128], identity=identb[:]),
                              reads=[("MA", cs), "identb"], writes=[("pb", pbk)])
                        P.add("act", lambda e, pb=pb, Lb=Lb, pbo=pbo: e.activation(out=Lb, in_=pb[:, pbo + 256:pbo + 384], func=AF.Copy),
                              reads=[("pb", pbk)], writes=[("Lb", cs)])
                        P.add("dve", lambda e, TM=TM, Lb=Lb: e.tensor_tensor(out=TM[:, 0:128], in0=Lb, in1=NEGM[:, 0, 0:128], op=ALU.mult),
                              reads=[("Lb", cs), "NEGM"], writes=[("TM", cs)])
                        P.add("dve", lambda e, TM=TM, MA=MA: e.tensor_tensor(out=TM[:, 128:256], in0=MA[:, 0:128], in1=NEGM[:, 0, 128:256], op=ALU.mult),
                              reads=[("MA", cs), "NEGM", ("TM", cs)], writes=[("TM", cs)])
                        P.add("dve", lambda e, TM=TM, DD=DD: e.tensor_tensor(out=DD[:, 0, 0:128], in0=identb[:, :], in1=TM[:, 0:128], op=ALU.subtract),
                              reads=[("TM", cs), "identb"], writes=[("DD", cs, 0)])
                        P.add("dve", lambda e, TM=TM, DD=DD: e.tensor_tensor(out=DD[:, 0, 128:256], in0=identb[:, :], in1=TM[:, 128:256], op=ALU.subtract),
                              reads=[("TM", cs), "identb", ("DD", cs, 0)], writes=[("DD", cs, 0)])
                        stageA.append(P.end_capture())
                    P.begin_capture()
                    for lev in range(1, 7):
                        pi, po = (lev - 1) % 2, lev % 2
                        for c in range(2):
                            cs = 2 * (t % 2) + c
                            pc = PF[2 + cs]
                            MA, Lb, DD, QQ = (g[k_, cs] for k_ in ("MA", "Lb", "DD", "QQ"))
                            P.add("pe", lambda e, pc=pc, MA=MA, DD=DD, pi=pi: e.matmul(pc[:, 0:128], MA[:, 0:128], DD[:, pi, 0:128], start=True, stop=True,
                                                                                        skip_group_check=True),
                                  reads=[("MA", cs), ("DD", cs, pi)], writes=[("pf", 2 + cs)])
                            P.add("pe", lambda e, pc=pc, Lb=Lb, DD=DD, pi=pi: e.matmul(pc[:, 128:256], Lb, DD[:, pi, 128:256], start=True, stop=True,
                                                                                        skip_group_check=True),
                                  reads=[("Lb", cs), ("DD", cs, pi)], writes=[("pf", 2 + cs)])
                            P.add("dve", lambda e, pc=pc, QQ=QQ, lev=lev: e.tensor_tensor(out=QQ[:, :], in0=pc[:, 0:256], in1=NEGM[:, lev, :], op=ALU.mult),
                                  reads=[("pf", 2 + cs), "NEGM"], writes=[("QQ", cs)])
                        for c in range(2):
                            cs = 2 * (t % 2) + c
                            pc = PF[2 + cs]
                            DD, QQ = (g[k_, cs] for k_ in ("DD", "QQ"))
                            P.add("pe", lambda e, pc=pc, QQ=QQ, DD=DD, pi=pi: e.matmul(pc[:, 256:384], DD[:, pi, 128:256], QQ[:, 0:128], start=True, stop=True,
                                                                                        skip_group_check=True),
                                  reads=[("QQ", cs), ("DD", cs, pi)], writes=[("pf", 2 + cs)])
                            P.add("pe", lambda e, pc=pc, QQ=QQ, DD=DD, pi=pi: e.matmul(pc[:, 384:512], DD[:, pi, 0:128], QQ[:, 128:256], start=True, stop=True,
                                                                                        skip_group_check=True),
                                  reads=[("QQ", cs), ("DD", cs, pi)], writes=[("pf", 2 + cs)])
                            P.add("dve", lambda e, pc=pc, DD=DD, pi=pi, po=po: e.tensor_tensor(out=DD[:, po, :], in0=DD[:, pi, :], in1=pc[:, 256:512], op=ALU.subtract),
                                  reads=[("pf", 2 + cs), ("DD", cs, pi)], writes=[("DD", cs, po)])
                    for c in range(2):
                        cs = 2 * (t % 2) + c
                        pc = PF[2 + cs]
                        kbg, vb, DD, u, wT = (g[k, cs] for k in ("kbg", "vb", "DD", "u", "wT"))
                        P.add("pe", lambda e, pc=pc, DD=DD, vb=vb: e.matmul(pc[:, 0:128], DD[:, 0, 128:256], vb, start=True, stop=True, skip_group_check=True),
                              reads=[("DD", cs, 0), ("vb", cs)], writes=[("pf", 2 + cs)])
                        P.add("pe", lambda e, pc=pc, DD=DD, kbg=kbg: e.matmul(pc[:, 128:256], kbg, DD[:, 0, 128:256], start=True, stop=True, skip_group_check=True),
                              reads=[("DD", cs, 0), ("kbg", cs)], writes=[("pf", 2 + cs)])
                        P.add("act", lambda e, pc=pc, u=u: e.activation(out=u, in_=pc[:, 0:128], func=AF.Copy), reads=[("pf", 2 + cs)], writes=[("u", cs)])
                        P.add("dve", lambda e, pc=pc, wT=wT: e.tensor_copy(out=wT, in_=pc[:, 128:256]), reads=[("pf", 2 + cs)], writes=[("wT", cs)])
                    PREP[t] = Prog.merge(stageA) + P.end_capture()
                    scans = []
                    for c in range(2):
                        P.begin_capture()
                        hh = c
                        h = 2 * hp + hh
                        cs = 2 * (t % 2) + c
                        pbk, pbo = cs // 2, (cs % 2) * 512
                        ps_, pb = PF[2 + cs], PB[pbk]
                        kd, MA, u, wT, vn, o, og, psm = (g[k, cs] for k in ("kd", "MA", "u", "wT", "vn", "o", "og", "ps"))
                        sc1 = lambda q_, tl=tl, hh=hh, SC=SC: SC[:, q_, tl * 2 + hh:tl * 2 + hh + 1]
                        qsT = sT[:, hh, tl, 128:256]
                        PK = ("pf", 2 + cs)
                        P.add("pe", lambda e, ps_=ps_, wT=wT, hh=hh: e.matmul(ps_[:, 0:128], wT, Sb[:, hh, :], start=True, stop=True, skip_group_check=True),
                              reads=[("wT", cs), ("Sb", hh)], writes=[PK])
                        P.add("pe", lambda e, ps_=ps_, qsT=qsT, hh=hh: e.matmul(ps_[:, 128:256], qsT, Sb[:, hh, :], start=True, stop=True, skip_group_check=True),
                              reads=[("sT", gp, hh, 1), ("Sb", hh)], writes=[PK])
                        P.add("dve", lambda e, ps_=ps_, u=u, vn=vn: e.tensor_tensor(out=vn, in0=u, in1=ps_[:, 0:128], op=ALU.subtract),
                              reads=[PK, ("u", cs)], writes=[("vn", cs)])
                        P.add("pe", lambda e, ps_=ps_, MA=MA, vn=vn: e.matmul(ps_[:, 256:384], MA[:, 128:256], vn, start=True, stop=True, skip_group_check=True),
                              reads=[("MA", cs), ("vn", cs)], writes=[PK])
                        P.add("pe", lambda e, ps_=ps_, kd=kd, vn=vn: e.matmul(ps_[:, 384:512], kd, vn, start=True, stop=True, skip_group_check=True),
                              reads=[("kd", cs), ("vn", cs)], writes=[PK])
                        P.add("act", lambda e, ps_=ps_, o=o: e.activation(out=o, in_=ps_[:, 256:384], func=AF.Copy), reads=[PK], writes=[("o", cs)])
                        P.add("dve", lambda e, ps_=ps_, o=o, sc1=sc1: e.scalar_tensor_tensor(out=o, in0=ps_[:, 128:256], scalar=sc1(3), in1=o, op0=ALU.mult, op1=ALU.add),
                              reads=[PK, ("o", cs)] + SCK, writes=[("o", cs)])
                        P.add("dve", lambda e, ps_=ps_, hh=hh, t=t, h=h: e.scalar_tensor_tensor(
                            out=Sf[:, hh, :], in0=Sf[:, hh, :], scalar=EGL[:, t, h:h + 1], in1=ps_[:, 384:512], op0=ALU.mult, op1=ALU.add),
                            reads=[PK, ("Sf", hh), "EGL"], writes=[("Sf", hh)])
                        P.add("act", lambda e, hh=hh: e.activation(out=Sb[:, hh, :], in_=Sf[:, hh, :], func=AF.Copy), reads=[("Sf", hh)], writes=[("Sb", hh)])
                        P.add("act", lambda e, o=o, og=og, psm=psm: e.activation(out=og, in_=o, func=AF.Square, accum_out=psm[:, 0:1]),
                              reads=[("o", cs)], writes=[("og", cs), ("psm", cs)])
                        P.add("act", lambda e, psm=psm: e.activation(out=psm[:, 1:2], in_=psm[:, 0:1], func=AF.Ln, bias=epsc[:, 0:1], scale=1.0 / 128),
                              reads=[("psm", cs), "epsc"], writes=[("psm", cs)])
                        P.add("act", lambda e, psm=psm: e.activation(out=psm[:, 1:2], in_=psm[:, 1:2], func=AF.Exp, scale=-0.5), reads=[("psm", cs)], writes=[("psm", cs)])
                        P.add("dve", lambda e, o=o, og=og, psm=psm, tl=tl, hh=hh, zs=zs: e.scalar_tensor_tensor(
                            out=og, in0=o, scalar=psm[:, 1:2], in1=zs[:, tl, hh * 128:(hh + 1) * 128], op0=ALU.mult, op1=ALU.mult),
                            reads=[("o", cs), ("psm", cs), ("zs", gp, tl)], writes=[("og", cs)])
                        P.add("pe", lambda e, pb=pb, og=og, pbo=pbo: e.transpose(out=pb[:, pbo + 384:pbo + 512], in_=og, identity=identb[:]),
                              reads=[("og", cs), "identb"], writes=[("pb", pbk)])
                        P.add("act", lambda e, pb=pb, hh=hh, t=t, pbo=pbo: e.activation(out=U[:, hh, t * 128:(t + 1) * 128], in_=pb[:, pbo + 384:pbo + 512], func=AF.Copy,
                                                                                        scale=self.gdc[:, j, 0:1]),
                              reads=[("pb", pbk), ("gdc", j)], writes=[("U", hh, t // 4)])
                        scans.append(P.end_capture())
                    SCAN[t] = Prog.merge(scans)
            wv = so[0]
            for t in range(NT):
                P.begin_capture()
                for dh in range(2):
                    pf = PF[0]
                    P.begin_atomic()
                    for hc in range(2):
                        P.add("pe", lambda e, pf=pf, wv=wv, hc=hc, t=t, dh=dh: e.matmul(
                            pf[:, :], U[:, hc, t * 128:(t + 1) * 128], wv[:, hc, dh * 512:(dh + 1) * 512], start=(hc == 0), stop=(hc == 1)),
                            reads=[so[1], ("U", hc, t // 4)], writes=[("pf", 0)])
                    P.add("dve", lambda e, pf=pf, t=t, dh=dh: e.tensor_tensor(
                        out=X[:, t, dh * 512:(dh + 1) * 512], in0=X[:, t, dh * 512:(dh + 1) * 512], in1=pf[:, :], op=ALU.add),
                        reads=[("pf", 0), ("x", t)], writes=[("x", t)])
                    P.end_atomic()
                OUT[t] = P.end_capture()
            return slabs, so, FE, PREP, SCAN, OUT

        def zero_state():
            P.add("pool", lambda e: e.memset(flat(Sf), 0.0), reads=XN0, writes=[("Sf", 0), ("Sf", 1)])
            P.add("pool", lambda e: e.memset(flat(Sb), 0.0), reads=XN0, writes=[("Sb", 0), ("Sb", 1)])

        cur = capture_pair(0)
        P.replay([cur[2][0]])
        zero_state()
        for hp in range(4):
            slabs, so, FE, PREP, SCAN, OUT = cur
            fe_parts = {}
            for gi in range(1, 4):
                L = FE[gi]
                n = (len(L) + 2) // 3
                for k in range(3):
                    fe_parts[4 * (gi - 1) + 1 + k] = L[k * n:(k + 1) * n]
            for s_ in range(NT):
                lists = [PREP[s_]]
                if s_ >= 1:
                    lists.append(SCAN[s_ - 1])
                if s_ in fe_parts:
                    lists.append(fe_parts[s_])
                if s_ >= 2:
                    lists.append(OUT[s_ - 2])
                P.replay(lists)
            for which in ("q", "k", "v", "z"):
                self.ws.done(slabs[which][2])
            tail = Prog.merge([SCAN[NT - 1]]) + Prog.merge([OUT[NT - 2]]) + Prog.merge([OUT[NT - 1]])
            if hp < 3:
                cur = capture_pair(hp + 1)
                P.replay([tail, cur[2][0]])
            else:
                P.replay([tail])
            self.ws.done(so[2])
            if hp < 3:
                zero_state()

    def load_x(self, s):
        P = self.P
        X = self.X
        xv = self.d["x"][s].rearrange("(t p) d -> p t d", p=128)
        for q in range(4):
            P.add("sp", lambda e, q=q: e.dma_start(out=X[:, 4 * q:4 * q + 4, :], in_=xv[:, 4 * q:4 * q + 4, :]),
                  writes=[("x", t) for t in range(4 * q, 4 * q + 4)], dma=("xl", q))

    def store_x(self, s):
        P = self.P
        X = self.X
        ov = self.d["out"][s].rearrange("(t p) d -> p t d", p=128)
        ids = []
        for q in range(4):
            ids.append(P.add("sp", lambda e, q=q: e.dma_start(out=ov[:, 4 * q:4 * q + 4, :], in_=X[:, 4 * q:4 * q + 4, :]),
                             reads=[("x", t) for t in range(4 * q, 4 * q + 4)], writes=[("xst", q)], dma=("xs", q)))
        return ids

    def rmsnorm(self, gbase):
        P = self.P
        X, xs, ss, rstd, xnT = self.X, self.xs, self.ss, self.rstd, self.xnT
        identb, gcol = self.identb, self.gcol
        import os
        DBG = int(os.environ.get("K_DBG", "9"))
        for t in range(NT):
            P.add("act", lambda e, t=t: e.activation(out=xs[:, 1, :], in_=X[:, t, :], func=AF.Square,
                                                     accum_out=ss[:, t:t + 1]),
                  reads=[("x", t)], writes=[("xs", 1), ("ss", t)])
        P.add("act", lambda e: e.activation(out=rstd[:], in_=ss[:], func=AF.Ln, bias=self.epsc[:, 0:1], scale=1.0 / D),
              reads=[("ss", t) for t in range(NT)] + ["epsc"], writes=["rstd"])
        P.add("act", lambda e: e.activation(out=rstd[:], in_=rstd[:], func=AF.Exp, scale=-0.5), reads=["rstd"], writes=["rstd"])
        if DBG < 2:
            return
        for t in range(NT if DBG >= 6 else 1):
            b = t % 2
            pb = self.PB[b]
            P.add("act", lambda e, t=t, b=b: e.activation(out=xs[:, b, :], in_=X[:, t, :], func=AF.Copy,
                                                          scale=rstd[:, t:t + 1]),
                  reads=[("x", t), "rstd"], writes=[("xs", b)])
            if DBG < 4:
                continue
            for kc in range(8):
                P.add("pe", lambda e, b=b, kc=kc, pb=pb: e.transpose(out=pb[:, kc * 128:(kc + 1) * 128],
                                                                      in_=xs[:, b, kc * 128:(kc + 1) * 128],
                                                                      identity=identb[:]),
                      reads=[("xs", b), "identb"], writes=[("pb", b)])
            if DBG < 5:
                continue
            for kc in range(8):
                eng = "dve" if b == 0 else "act"
                if eng == "dve":
                    fn = lambda e, t=t, kc=kc, pb=pb: e.tensor_scalar(
                        out=xnT[:, kc, t * 128:(t + 1) * 128], in0=pb[:, kc * 128:(kc + 1) * 128],
                        scalar1=gcol[:, gbase + kc:gbase + kc + 1], scalar2=None, op0=ALU.mult)
                else:
                    fn = lambda e, t=t, kc=kc, pb=pb: e.activation(
                        out=xnT[:, kc, t * 128:(t + 1) * 128], in_=pb[:, kc * 128:(kc + 1) * 128],
                        func=AF.Copy, scale=gcol[:, gbase + kc:gbase + kc + 1])
                P.add(eng, fn, reads=[("pb", b), "gcol"], writes=[("xnT", t, kc)])

    def mlp(self, l):
        P = self.P
        X, xnT, U, ring = self.X, self.xnT, self.hT, self.ring
        w1 = self.d["mlp_w_in"][l].rearrange("(kc p) f -> p kc f", p=128)
        w2 = self.d["mlp_w_out"][l].rearrange("(fc p) d -> p fc d", p=128)
        self.rmsnorm(32 + 8 * l)
        self.arena_barrier()
        pfi = 0
        for fg in range(4):
            slabs = [self.ws.get(w1[:, :, fg * 1024 + s2 * 512: fg * 1024 + (s2 + 1) * 512], 512) for s2 in range(2)]
            for fc in range(8):
                slot, wkey, _ = slabs[fc // 4]
                off = (fc % 4) * 128
                for tt in range(4):
                    bank = pfi % 4
                    pfi += 1
                    pf = self.PF[bank]
                    for kc in range(8):
                        P.add("pe", lambda e, pf=pf, slot=slot, kc=kc, off=off, tt=tt: e.matmul(
                            pf[:, :], slot[:, kc, off:off + 128], xnT[:, kc, tt * 512:(tt + 1) * 512],
                            start=(kc == 0), stop=(kc == 7)),
                            reads=[wkey] + [("xnT", t, kc) for t in range(4 * tt, 4 * tt + 4)],
                            writes=[("pf", bank)])
                    rb = pfi % 2
                    P.add("act", lambda e, pf=pf, rb=rb: e.activation(out=self.rtmp[:, rb, :], in_=pf[:, :], func=AF.Relu),
                          reads=[("pf", bank)], writes=[("xs", rb)])
                    P.add("dve", lambda e, fc=fc, tt=tt, rb=rb: e.tensor_tensor(
                        out=U[:, fc, tt * 512:(tt + 1) * 512], in0=self.rtmp[:, rb, :], in1=self.rtmp[:, rb, :],
                        op=ALU.mult),
                        reads=[("xs", rb)], writes=[("hT", fc, tt)])
            for sl in slabs:
                self.ws.done(sl[2])
            slabs2 = [self.ws.get(w2[:, fg * 8:(fg + 1) * 8, dh * 512:(dh + 1) * 512], 512) for dh in range(2)]
            for t in range(NT):
                for dh in range(2):
                    slot, wkey, _ = slabs2[dh]
                    bank = pfi % 4
                    pfi += 1
                    pf = self.PF[bank]
                    for fc in range(8):
                        P.add("pe", lambda e, pf=pf, slot=slot, fc=fc, t=t: e.matmul(
                            pf[:, :], U[:, fc, t * 128:(t + 1) * 128], slot[:, fc, :],
                            start=(fc == 0), stop=(fc == 7)),
                            reads=[wkey, ("hT", fc, t // 4)], writes=[("pf", bank)])
                    P.add("dve", lambda e, pf=pf, t=t, dh=dh: e.tensor_tensor(
                        out=X[:, t, dh * 512:(dh + 1) * 512], in0=X[:, t, dh * 512:(dh + 1) * 512], in1=pf[:, :],
                        op=ALU.add),
                        reads=[("pf", bank), ("x", t)], writes=[("x", t)])
            for sl in slabs2:
                self.ws.done(sl[2])

    def build(self):
        P = self.P
        self.setup_consts()
        kinds = {k for k, _ in self.layers}
        if "gd" in kinds:
            self.setup_gd_static()
            self.setup_gd_levelmasks()
            for jj in sorted({l // 2 for (k, l) in self.layers if k == "gd"}):
                self.setup_gd_consts(jj)
        if "da" in kinds:
            self.setup_da_static()
            for (k, l) in self.layers:
                if k == "da":
                    self.setup_da_consts(l // 2, l)
        last_stores = []
        for s in range(self.n_seq):
            self.load_x(s)
            for l in self.layers:
                if l[0] == "mlp":
                    self.mlp(l[1])
                elif l[0] == "norm":
                    self.rmsnorm(32 + 8 * l[1])
                elif l[0] == "da":
                    self.diffattn(l[1])
                elif l[0] == "gd":
                    self.gdn(l[1])
            last_stores = self.store_x(s)
        P.add("sp", None, reads=[("xst", q) for q in range(4)])


def layer_plan():
    plan = []
    for i in range(DEPTH):
        plan.append(("da" if i % 2 == 0 else "gd", i))
        plan.append(("mlp", i))
    return plan


def build_program(n_seq=SEQ_PER_CORE, layers=None):
    if layers is None:
        layers = layer_plan()
    nc = bass.Bass("TRN2", target_bir_lowering=False)
    with ExitStack() as es:
        b = Builder(nc, Prog(nc, dry=True), None, n_seq, layers, es)
        b.build()
        future = list(b.ws.requests)
        P = Prog(nc, dry=False)
        b.P = P
        b.ws = WStream(P, b.ring, b.NSLOT * 2, future)
        b.build()
        P.emit(es)
    return nc


WEIGHT_NAMES = ["mix_norm", "mlp_norm", "mlp_w_in", "mlp_w_out", "da_w_in", "da_q_norm", "da_k_norm",
                "da_lambda_q1", "da_lambda_k1", "da_lambda_q2", "da_lambda_k2", "da_sub_norm", "da_w_out",
                "gd_w_in", "gd_conv_w", "gd_a_log", "gd_dt_bias", "gd_out_norm", "gd_w_out"]


def run(inputs, n_seq=SEQ_PER_CORE, layers=None, ncores=NCORES, trace=False):
    nc = build_program(n_seq, layers)
    x = np.ascontiguousarray(np.asarray(inputs["x"], dtype=np.float32))
    weights = {k: np.ascontiguousarray(np.asarray(inputs[k], dtype=np.float32)) for k in WEIGHT_NAMES}
    in_maps = []
    for c in range(ncores):
        m = {"x": x[c * n_seq:(c + 1) * n_seq]}
        m.update(weights)
        in_maps.append(m)
    res = run_bass_kernel_spmd(nc, in_maps, core_ids=list(range(ncores)), trace=trace)
    out = np.concatenate([r["out"] for r in res.results], axis=0)
    return out, res


def kernel(**inputs):
    out, _ = run(inputs)
    return out
```

```python
import math
from contextlib import ExitStack

import numpy as np
import concourse.bass as bass
import concourse.mybir as mybir
from concourse.bass_utils import run_bass_kernel_spmd

F32 = mybir.dt.float32
BF16 = mybir.dt.bfloat16
AF = mybir.ActivationFunctionType
ALU = mybir.AluOpType
AX = mybir.AxisListType

D = 1024
S = 2048
NT = S // 128
DFF = 4096
DEPTH = 4
EPS = 1e-6
NCORES = 8
SEQ_PER_CORE = 4
GD_IN = 4 * 1024 + 16


class Op:
    __slots__ = ("id", "eng", "fn", "deps", "dma", "seq", "signal")


class Prog:
    ENGS = ("pe", "act", "dve", "pool", "sp")

    def __init__(self, nc, dry=False):
        self.nc = nc
        self.dry = dry
        self.ops = []
        self.by_eng = {e: [] for e in self.ENGS}
        self.lw = {}
        self.rd = {}
        self.dma_groups = {}
        self.group_all = set()
        self.psum_last = {}
        self.arena_names = set()
        self.cap = None
        self.atom = None

    def begin_capture(self):
        self.cap = []

    def begin_atomic(self):
        if self.cap is not None:
            self.atom = []

    def end_atomic(self):
        if self.cap is not None:
            self.cap.append(self.atom)
            self.atom = None

    def end_capture(self):
        c, self.cap = self.cap, None
        return c

    @staticmethod
    def merge(lists):
        lists = [L for L in lists if L]
        idx = [0] * len(lists)
        out = []
        total = sum(len(L) for L in lists)
        nel = total
        done_el = 0
        while done_el < nel:
            done_el += 1
            best, bf = None, None
            for i, L in enumerate(lists):
                if idx[i] < len(L):
                    f = (idx[i] + 0.5) / len(L)
                    if bf is None or f < bf:
                        best, bf = i, f
            el = lists[best][idx[best]]
            idx[best] += 1
            total -= 1
            if isinstance(el, list):
                out.extend(el)
                total += len(el)
            else:
                out.append(el)
                total += 1
        return out

    def replay(self, lists):
        for rec in self.merge(lists):
            self.add(*rec)

    def add(self, eng, fn, reads=(), writes=(), dma=None):
        if self.dry:
            return None
        if self.cap is not None:
            rec = (eng, fn, tuple(reads), tuple(writes), dma)
            if self.atom is not None:
                self.atom.append(rec)
            else:
                self.cap.append(rec)
            return None
        op = Op()
        op.id = len(self.ops)
        op.eng = eng
        op.fn = fn
        op.dma = dma
        op.seq = 0
        op.signal = False
        if self.arena_names:
            for k in tuple(reads) + tuple(writes):
                nm = k[0] if isinstance(k, tuple) else k
                if nm in self.arena_names:
                    reads = tuple(reads) + ("ARENA",)
                    break
        deps = set()
        for k in reads:
            w = self.lw.get(k)
            if w is not None:
                deps.add(w)
        for k in writes:
            w = self.lw.get(k)
            if w is not None:
                deps.add(w)
            for r in self.rd.get(k, ()):
                deps.add(r)
        for k in reads:
            self.rd.setdefault(k, []).append(op.id)
        for k in writes:
            self.lw[k] = op.id
            self.rd[k] = []
        for k in tuple(reads) + tuple(writes):
            if isinstance(k, tuple) and k[0] in ("pf", "pb"):
                last = self.psum_last.setdefault(k, {})
                for eng2, oid in last.items():
                    if eng2 != eng:
                        deps.add(oid)
                last[eng] = op.id
        deps.discard(op.id)
        if eng == "pe" and dma is None:
            deps = {d for d in deps if not (self.ops[d].eng == "pe" and self.ops[d].dma is None)}
        op.deps = deps
        self.ops.append(op)
        self.by_eng[eng].append(op)
        if dma is not None:
            self.dma_groups.setdefault(dma, []).append(op.id)
        return op.id

    def emit(self, es):
        nc = self.nc
        ops = self.ops
        for op in ops:
            for d in op.deps:
                ops[d].signal = True
        for e in self.ENGS:
            c = 0
            for op in self.by_eng[e]:
                if op.dma is None and op.signal:
                    c += 1
                    op.seq = c
        for g, ids in self.dma_groups.items():
            for i, oid in enumerate(ids):
                ops[oid].seq = i + 1
        eng_sem = {e: es.enter_context(nc.semaphore("s_" + e)) for e in self.ENGS}
        dma_sem = {g: es.enter_context(nc.semaphore("d_" + str(g))) for g in self.dma_groups}
        block = es.enter_context(nc.Block())

        def emit_engine(ename, e):
            waited = {}
            for op in self.by_eng[ename]:
                need = {}
                for d in op.deps:
                    dop = ops[d]
                    if dop.dma is not None:
                        sem = dma_sem[dop.dma]
                        if dop.dma in self.group_all:
                            val = 16 * len(self.dma_groups[dop.dma])
                        else:
                            val = 16 * dop.seq
                    else:
                        sem = eng_sem[dop.eng]
                        val = dop.seq
                    key = id(sem)
                    if key not in need or need[key][1] < val:
                        need[key] = (sem, val)
                for key, (sem, val) in need.items():
                    if waited.get(key, 0) < val:
                        e.wait_ge(sem, val)
                        waited[key] = val
                if op.fn is None:
                    continue
                inst = op.fn(e)
                if op.dma is not None:
                    inst.then_inc(dma_sem[op.dma], 16)
                elif op.signal:
                    inst.then_inc(eng_sem[ename], 1)

        @block.tensor
        def _(e):
            emit_engine("pe", e)

        @block.scalar
        def _(e):
            emit_engine("act", e)

        @block.vector
        def _(e):
            emit_engine("dve", e)

        @block.gpsimd
        def _(e):
            emit_engine("pool", e)

        @block.sync
        def _(e):
            emit_engine("sp", e)


class WStream:
    UNIT = 2048

    def __init__(self, P, ring, nunits, lookahead_list=None):
        self.P = P
        self.ring = ring
        self.nu = nunits
        self.future = lookahead_list
        self.requests = []
        self.issued = 0
        self.released = set()
        self.head = 0
        self.occ = [None] * nunits
        self.units = []
        self.prev = []
        if lookahead_list is not None:
            for (src, w) in lookahead_list:
                self._place(w)

    @staticmethod
    def _nelem(w):
        return w[0] * w[1] if isinstance(w, tuple) else 8 * w

    def _place(self, w):
        n = (self._nelem(w) + self.UNIT - 1) // self.UNIT
        if self.head + n > self.nu:
            self.head = 0
        j = len(self.units)
        us = list(range(self.head, self.head + n))
        self.prev.append({self.occ[u] for u in us if self.occ[u] is not None})
        for u in us:
            self.occ[u] = j
        self.units.append((self.head, n))
        self.head = (self.head + n) % self.nu

    def view(self, idx, w):
        u0, n = self.units[idx]
        ne = self._nelem(w)
        v = self.ring[:, u0 * self.UNIT:u0 * self.UNIT + ne]
        if isinstance(w, tuple):
            return v.rearrange("p (a b) -> p a b", a=w[0])
        return v.rearrange("p (a b) -> p a b", a=8)

    def _pump(self):
        if self.P.dry:
            return
        while self.issued < len(self.future):
            j = self.issued
            if not all(pj in self.released for pj in self.prev[j]):
                break
            src, w = self.future[j]
            dst = self.view(j, w)
            self.P.add("pool", lambda e, dst=dst, src=src: e.dma_start(out=dst, in_=src),
                       writes=[("w", j)] + [("w", pj) for pj in self.prev[j]], dma=("wu", self.units[j][0]))
            self.issued += 1

    def get(self, src, w):
        i = len(self.requests)
        self.requests.append((src, w))
        if self.future is None:
            self._place(w)
        if self.P.dry:
            return self.view(i, w), ("w", i), i
        self._pump()
        assert self.issued > i, "weight ring deadlock: release slabs before requesting more"
        return self.view(i, w), ("w", i), i

    def done(self, idx):
        self.released.add(idx)
        self._pump()


class Builder:
    def __init__(self, nc, P, ws_future, n_seq, layers, es):
        self.nc = nc
        self.P = P
        self.n_seq = n_seq
        self.layers = layers
        self.es = es
        self.ws_future = ws_future
        self.alloc()

    def sb(self, name, shape, dt):
        return self.es.enter_context(self.nc.sbuf_tensor(name, shape, dt))

    def carve(self, shape, dt):
        esz = 4 if dt == F32 else 2
        n = 1
        for s_ in shape:
            n *= s_
        nbytes = (n * esz + 3) // 4 * 4
        off = self._aoff
        assert off + nbytes <= self.ARENA_BYTES, "arena overflow"
        self._aoff = off + nbytes
        v = self.arena[:, off // 4:(off + nbytes) // 4]
        if dt != F32:
            v = v.bitcast(dt)
        v = v[:, 0:n]
        if len(shape) == 2:
            v = v.rearrange("p (a b) -> p a b", a=shape[0])
        elif len(shape) == 3:
            v = v.rearrange("p (a b c) -> p a b c", a=shape[0], b=shape[1])
        return v

    def alloc(self):
        nc = self.nc
        n_seq = self.n_seq
        d = {}
        d["x"] = nc.dram_tensor("x", [n_seq, S, D], F32, kind="ExternalInput").ap()
        d["out"] = nc.dram_tensor("out", [n_seq, S, D], F32, kind="ExternalOutput").ap()
        specs = [
            ("mix_norm", [4, D]), ("mlp_norm", [4, D]), ("mlp_w_in", [4, D, DFF]), ("mlp_w_out", [4, DFF, D]),
            ("da_w_in", [2, D, 3072]), ("da_q_norm", [2, 64]), ("da_k_norm", [2, 64]),
            ("da_lambda_q1", [2, 64]), ("da_lambda_k1", [2, 64]), ("da_lambda_q2", [2, 64]), ("da_lambda_k2", [2, 64]),
            ("da_sub_norm", [2, 128]), ("da_w_out", [2, D, D]),
            ("gd_w_in", [2, D, GD_IN]), ("gd_conv_w", [2, 4, 3072]), ("gd_a_log", [2, 8]), ("gd_dt_bias", [2, 8]),
            ("gd_out_norm", [2, 128]), ("gd_w_out", [2, D, D]),
        ]
        for name, shape in specs:
            d[name] = nc.dram_tensor(name, shape, F32, kind="ExternalInput").ap()
        self.d = d
        self.NSLOT = 4
        self.X = self.sb("X", [128, NT, D], F32)
        self.xnT = self.sb("xnT", [128, 8, S], BF16)
        self.U = self.sb("U", [128, 2, S], BF16)
        self.ring = self.sb("ring", [128, self.NSLOT * 4096], BF16)
        self.xs_f = self.sb("xs_f", [128, 2, 512], F32)
        self.xs = self.xs_f[:, :, :].rearrange("p a b -> p (a b)").bitcast(BF16).rearrange("p (a b) -> p a b", a=2)
        self.rtmp = self.xs_f
        self.ss = self.sb("ss", [128, NT], F32)
        self.rstd = self.sb("rstd", [128, NT], F32)
        self.epsc = self.sb("epsc", [128, 4], F32)
        self.identb = self.sb("identb", [128, 128], BF16)
        self.identf = self.sb("identf", [128, 128], F32)
        self.crow = self.xs_f[0:64, 1, 0:128]
        self.gcol = self.sb("gcol", [128, 64], F32)
        self.ARENA_BYTES = 35840 + 24576
        self.arena = self.sb("arena", [128, self.ARENA_BYTES // 4], F32)
        self._aoff = 0
        self.hT = self.carve([8, S], BF16)
        self._aoff = 0
        self.qT = self.carve([2, S], BF16)
        self.kTz = self.carve([2, 2, S], BF16)
        self.vaug = self.carve([NT, 2, 130], BF16)
        self.pT = self.carve([3, 512], BF16)
        self.qraw = self.carve([2, 512], BF16)
        self.qsq = self.carve([2, 512], BF16)
        self.qrs = self.carve([1, 512], F32)
        self.accS = self.carve([4, 512], F32)
        self.osb = self.carve([2, 4, 128], F32)
        self.obf = self.carve([2, 4, 128], BF16)
        self.fsm = self.carve([2, 24], F32)
        self.da_arena_end = self._aoff
        self._aoff = 0
        g = {}
        g["BA"] = self.carve([NT, 16], F32)
        for nm in ("BETA", "GC", "EG", "EGL", "EKD"):
            g[nm] = self.carve([NT, 8], F32)
        g["GLB"] = g["BA"][:, :, :].rearrange("p a b -> p (a b)")[:, 0:NT * 8].rearrange("p (a b) -> p a b", a=NT)
        g["raw"] = self.carve([2, 516], F32)
        g["acc"] = self.carve([1, 512], F32)
        g["halo"] = self.carve([6, 4], F32)
        g["sT"] = [self.carve([2, 4, 3 * 128], BF16) for _ in range(2)]
        g["sq"] = self.carve([2, 512], BF16)
        g["zs"] = [self.carve([4, 256], BF16) for _ in range(2)]
        g["R"] = [self.carve([4, 4], F32) for _ in range(2)]
        g["SC"] = [self.carve([8, 8], F32) for _ in range(2)]
        g["Sf"] = self.carve([2, 128], F32)
        g["Sb"] = self.carve([2, 128], BF16)
        self.NCH = 4
        for c in range(self.NCH):
            g["kd", c] = self.carve([128], BF16)
            g["kbg", c] = self.carve([128], BF16)
            g["vb", c] = self.carve([128], BF16)
            g["E", c] = self.carve([256], F32)
            g["MA", c] = self.carve([256], BF16)
            g["Lb", c] = self.carve([128], BF16)
            g["QQ", c] = self.carve([256], BF16)
            g["TM", c] = self.carve([256], BF16)
            g["DD", c] = self.carve([2, 256], BF16)
            g["u", c] = self.carve([128], F32)
            g["wT", c] = self.carve([128], BF16)
            g["vn", c] = self.carve([128], BF16)
            g["o", c] = self.carve([128], F32)
            g["og", c] = self.carve([128], BF16)
            g["ps", c] = self.carve([8], F32)
        self.g = g
        self.gd_arena_end = self._aoff
        assert max(self.da_arena_end, self.gd_arena_end) <= self.ARENA_BYTES
        self.convw = self.sb("convw", [128, 2, 96], F32)
        self.gdc = self.sb("gdc", [128, 2, 4], F32)
        self.gsm = self.sb("gsm", [128, 2, 16], F32)
        self.NEGM = self.sb("NEGM", [128, 7, 256], BF16)
        self.trif = self.sb("trif", [128, 128], F32)
        self.sellast = self.sb("sellast", [128, 128], F32)
        self.onesf = self.sb("onesf", [128, 128], F32)
        self.masks = self.sb("masks", [128, 256], F32)
        self.onecol = self.sb("onecol", [128, 2], BF16)
        self.onesbd = self.sb("onesbd", [128, 128], BF16)
        self.negmask = self.sb("negmask", [128, 128], BF16)
        self.cda = self.sb("cda", [128, 16], F32)
        self.lamt = self.xs_f[:, 0, 256:512].rearrange("p (a b) -> p a b", a=4)
        self.lamp = self.sb("lamp", [128, 4], F32)
        self.scr_f = self.xs_f[:, 0, 0:128]
        self.scr_f2 = self.xs_f[:, 0, 128:256]
        self.PF = [self.es.enter_context(nc.psum_tensor("pf%d" % i, [128, 512], F32)) for i in range(6)]
        self.PB = [self.es.enter_context(nc.psum_tensor("pb%d" % i, [128, 1024], BF16)) for i in range(2)]
        self.ws = WStream(self.P, self.ring, self.NSLOT * 2, self.ws_future)
        self.dummy = self.sb("abar", [128, 2], F32)
        self.ARENA_NAMES = {"hT", "qT", "kT", "kTz", "accS", "va", "pT", "qraw", "qsq", "qrs", "osb", "obf", "fsm", "fss", "frs", "vones",
                            "BA", "BETA", "GC", "EG", "EGL", "EKD", "raw", "acc", "halo", "sT", "sq", "zs", "R", "SC", "Sf", "Sb",
                            "kd", "kbg", "vb", "E", "MA", "Lb", "QQ", "TM", "DD", "u", "wT", "vn", "o", "og", "psm"}

    def arena_barrier(self):
        self.P.arena_names = self.ARENA_NAMES
        dummy = self.dummy
        self.P.add("pool", lambda e: e.memset(dummy[:], 0.0), writes=["ARENA"])

    def setup_consts(self):
        P = self.P
        nc = self.nc
        identf, identb = self.identf, self.identb
        P.add("pool", lambda e: e.memset(identf[:], 0.0), writes=["identf"])
        P.add("pool", lambda e: e.memset(self.epsc[:], EPS), writes=["epsc"])
        P.add("pool", lambda e: e.affine_select(out=identf[:], in_=identf[:], pattern=[[-1, 128]],
                                                compare_op=ALU.not_equal, fill=1.0, base=0,
                                                channel_multiplier=1),
              reads=["identf"], writes=["identf"])
        P.add("dve", lambda e: e.tensor_copy(out=identb[:], in_=identf[:]), reads=["identf"], writes=["identb"])
        crow, gcol = self.crow, self.gcol
        mixr = self.d["mix_norm"].rearrange("l (kc p) -> (l kc) p", p=128)
        mlpr = self.d["mlp_norm"].rearrange("l (kc p) -> (l kc) p", p=128)
        P.add("sp", lambda e: e.dma_start(out=crow[0:32, :], in_=mixr), writes=[("xs", 1)], dma="c0")
        P.add("sp", lambda e: e.dma_start(out=crow[32:64, :], in_=mlpr), writes=[("xs", 1)], dma="c1")
        pf = self.PF[0]
        P.add("pe", lambda e: e.transpose(out=pf[:, 0:64], in_=crow[0:64, :], identity=identf[0:64, 0:64]),
              reads=[("xs", 1), "identf"], writes=[("pf", 0)])
        P.add("dve", lambda e: e.tensor_copy(out=gcol[:, :], in_=pf[:, 0:64]), reads=[("pf", 0)], writes=["gcol"])

    def setup_da_consts(self, j, l):
        P = self.P
        d = self.d
        cda, lamt, lamp = self.cda, self.lamt, self.lamp
        c0 = 5 * j
        col = lambda ap: ap.rearrange("(p o) -> p o", o=1)
        for half in range(2):
            P.add("sp", lambda e, half=half: e.dma_start(out=cda[64 * half:64 * half + 64, c0:c0 + 1], in_=col(d["da_q_norm"][j])),
                  writes=[("cda", j)], dma="cq%d%d" % (j, half))
            P.add("sp", lambda e, half=half: e.dma_start(out=cda[64 * half:64 * half + 64, c0 + 1:c0 + 2], in_=col(d["da_k_norm"][j])),
                  writes=[("cda", j)], dma="ck%d%d" % (j, half))
        P.add("sp", lambda e: e.dma_start(out=cda[:, c0 + 4:c0 + 5], in_=col(d["da_sub_norm"][j])),
              writes=[("cda", j)], dma="cs%d" % j)
        for i, nm in enumerate(["da_lambda_q1", "da_lambda_k1", "da_lambda_q2", "da_lambda_k2"]):
            P.add("sp", lambda e, i=i, nm=nm: e.dma_start(out=lamt[:, i, :], in_=d[nm][j:j + 1, :].partition_broadcast(128)),
                  writes=[("xs", 0)], dma="cl%d%d" % (j, i))
        lam_init = 0.8 - 0.6 * math.exp(-0.3 * l)
        for i in range(2):
            P.add("dve", lambda e, i=i: e.tensor_tensor(out=lamt[:, 2 * i, :], in0=lamt[:, 2 * i, :], in1=lamt[:, 2 * i + 1, :], op=ALU.mult),
                  reads=[("xs", 0)], writes=[("xs", 0)])
            P.add("dve", lambda e, i=i: e.reduce_sum(out=lamp[:, i:i + 1], in_=lamt[:, 2 * i, :], axis=AX.X),
                  reads=[("xs", 0)], writes=["lamp"])
        P.add("act", lambda e: e.activation(out=lamp[:, 0:2], in_=lamp[:, 0:2], func=AF.Exp), reads=["lamp"], writes=["lamp"])
        P.add("dve", lambda e: e.tensor_tensor(out=cda[:, c0 + 2:c0 + 3], in0=lamp[:, 0:1], in1=lamp[:, 1:2], op=ALU.subtract),
              reads=["lamp"], writes=[("cda", j)])
        P.add("dve", lambda e: e.tensor_scalar(out=cda[:, c0 + 2:c0 + 3], in0=cda[:, c0 + 2:c0 + 3], scalar1=lam_init, scalar2=None, op0=ALU.add),
              reads=[("cda", j)], writes=[("cda", j)])
        P.add("dve", lambda e: e.tensor_scalar(out=cda[:, c0 + 3:c0 + 4], in0=cda[:, c0 + 2:c0 + 3], scalar1=-1.0, scalar2=None, op0=ALU.mult),
              reads=[("cda", j)], writes=[("cda", j)])
        P.add("dve", lambda e: e.tensor_scalar(out=cda[:, c0:c0 + 1], in0=cda[:, c0:c0 + 1], scalar1=0.125, scalar2=None, op0=ALU.mult),
              reads=[("cda", j)], writes=[("cda", j)])
        P.add("dve", lambda e: e.tensor_scalar(out=cda[:, c0 + 4:c0 + 5], in0=cda[:, c0 + 4:c0 + 5], scalar1=1.0 - lam_init, scalar2=None, op0=ALU.mult),
              reads=[("cda", j)], writes=[("cda", j)])

    def setup_da_static(self):
        P = self.P
        onesbd, negmask, vaug = self.onesbd, self.negmask, self.vaug
        scr = self.scr_f
        P.add("pool", lambda e: e.memset(scr[:], 0.0), writes=[("xs", 0)])
        P.add("pool", lambda e: e.memset(scr[0:64, 0:64], 1.0 / 64), reads=[("xs", 0)], writes=[("xs", 0)])
        P.add("pool", lambda e: e.memset(scr[64:128, 64:128], 1.0 / 64), reads=[("xs", 0)], writes=[("xs", 0)])
        P.add("dve", lambda e: e.tensor_copy(out=onesbd[:], in_=scr[:]), reads=[("xs", 0)], writes=["onesbd"])
        scr2 = self.scr_f2
        P.add("pool", lambda e: e.memset(scr2[:], 0.0), writes=[("xs", 0)])
        P.add("pool", lambda e: e.affine_select(out=scr2[:], in_=scr2[:], pattern=[[1, 128]], compare_op=ALU.is_ge,
                                                fill=-30000.0, base=0, channel_multiplier=-1),
              reads=[("xs", 0)], writes=[("xs", 0)])
        P.add("dve", lambda e: e.tensor_copy(out=negmask[:], in_=scr2[:]), reads=[("xs", 0)], writes=["negmask"])

    def diffattn(self, l):
        P = self.P
        j = l // 2
        X, xnT, U, ring = self.X, self.xnT, self.U, self.ring
        qT, kTz, vaug, pT, accS = self.qT, self.kTz, self.vaug, self.pT, self.accS
        qraw, qsq, qrs = self.qraw, self.qsq, self.qrs
        cda, onesbd, negmask, identb = self.cda, self.onesbd, self.negmask, self.identb
        osb, obf, fsm = self.osb, self.obf, self.fsm
        PF, PB = self.PF, self.PB
        c0 = 5 * j
        win = self.d["da_w_in"][j].rearrange("(kc p) f -> p kc f", p=128)
        self.rmsnorm(8 * l)
        self.arena_barrier()
        P.add("pool", lambda e: e.memset(vaug[:, :, :, 128:130], 1.0), reads=[("xnT", 0, 0)], writes=["vones"])
        P.add("pool", lambda e: e.memset(kTz[64:128, 0, :, :], 0.0), reads=[("xnT", 0, 0)], writes=["kTz"])
        P.add("pool", lambda e: e.memset(kTz[0:64, 1, :, :], 0.0), reads=[("xnT", 0, 0)], writes=["kTz"])
        cnt = {"pf": 0, "nb": 0, "pt": 0, "fin": 0}
        def record_proj(hp):
            sq = self.ws.get(win[:, :, hp * 256:(hp + 1) * 256], 256)
            sk = self.ws.get(win[:, :, 1024 + hp * 256:1024 + (hp + 1) * 256], 256)
            sv = self.ws.get(win[:, :, 2048 + hp * 256:2048 + (hp + 1) * 256], 256)
            jobs = [(which, slab, gcolq, hh, tt) for which, slab, gcolq in (("q", sq, c0), ("k", sk, c0 + 1))
                    for hh in range(2) for tt in range(4)]
            jstate = {}

            def emit_proj(i):
                which, slab, gcolq, hh, tt = jobs[i]
                slot, wkey, _ = slab
                bank = cnt["pf"] % 2
                cnt["pf"] += 1
                nb = cnt["nb"] % 2
                cnt["nb"] += 1
                jstate[i] = (bank, nb)
                pf = PF[bank]
                for kc in range(8):
                    P.add("pe", lambda e, pf=pf, slot=slot, kc=kc, hh=hh, tt=tt: e.matmul(
                        pf[:, :], slot[:, kc, hh * 128:(hh + 1) * 128], xnT[:, kc, tt * 512:(tt + 1) * 512],
                        start=(kc == 0), stop=(kc == 7)),
                        reads=[wkey] + [("xnT", t, kc) for t in range(4 * tt, 4 * tt + 4)], writes=[("pf", bank)])

            def emit_norm(i):
                which, slab, gcolq, hh, tt = jobs[i]
                bank, nb = jstate[i]
                pf = PF[bank]
                P.add("act", lambda e, pf=pf, nb=nb: e.activation(out=qraw[:, nb, :], in_=pf[:, :], func=AF.Copy),
                      reads=[("pf", bank)], writes=[("qraw", nb)])
                P.add("dve", lambda e, nb=nb: e.tensor_tensor(out=qsq[:, nb, :], in0=qraw[:, nb, :], in1=qraw[:, nb, :], op=ALU.mult),
                      reads=[("qraw", nb)], writes=[("qsq", nb)])
                mbank = 2 + nb
                pm = PF[mbank]
                P.add("pe", lambda e, pm=pm, nb=nb: e.matmul(pm[:, :], onesbd[:, :], qsq[:, nb, :], start=True, stop=True),
                      reads=[("qsq", nb), "onesbd"], writes=[("pf", mbank)])
                P.add("act", lambda e, pm=pm, nb=nb: e.activation(out=qrs[:, 0, :], in_=pm[:, :], func=AF.Ln, bias=self.epsc[:, 0:1], scale=1.0),
                      reads=[("pf", mbank), "epsc"], writes=[("qrs", 0)])
                P.add("act", lambda e, nb=nb: e.activation(out=qrs[:, 0, :], in_=qrs[:, 0, :], func=AF.Exp, scale=-0.5),
                      reads=[("qrs", 0)], writes=[("qrs", 0)])
                if which == "q":
                    P.add("dve", lambda e, nb=nb, hh=hh, tt=tt, gcolq=gcolq: e.scalar_tensor_tensor(
                        out=qT[:, hh, tt * 512:(tt + 1) * 512], in0=qraw[:, nb, :], scalar=cda[:, gcolq:gcolq + 1],
                        in1=qrs[:, 0, :], op0=ALU.mult, op1=ALU.mult),
                        reads=[("qraw", nb), ("qrs", 0), ("cda", j)], writes=[("qT", hh, tt)])
                else:
                    for c in range(2):
                        pl, ph = 64 * c, 64 * c + 64
                        P.add("dve", lambda e, nb=nb, hh=hh, tt=tt, gcolq=gcolq, c=c, pl=pl, ph=ph: e.scalar_tensor_tensor(
                            out=kTz[pl:ph, c, hh, tt * 512:(tt + 1) * 512], in0=qraw[pl:ph, nb, :], scalar=cda[pl:ph, gcolq:gcolq + 1],
                            in1=qrs[pl:ph, 0, :], op0=ALU.mult, op1=ALU.mult),
                            reads=[("qraw", nb), ("qrs", 0), ("cda", j), "kTz"], writes=[("kT", hh, tt, c)])

            emit_proj(0)
            for i in range(len(jobs)):
                if i + 1 < len(jobs):
                    emit_proj(i + 1)
                emit_norm(i)
            slot, wkey, _ = sv
            for t in range(NT):
                bank = cnt["pf"] % 2
                cnt["pf"] += 1
                pf = PF[bank]
                for kc in range(8):
                    P.add("pe", lambda e, pf=pf, slot=slot, kc=kc, t=t: e.matmul(
                        pf[:, 0:256], xnT[:, kc, t * 128:(t + 1) * 128], slot[:, kc, 0:256],
                        start=(kc == 0), stop=(kc == 7)),
                        reads=[wkey, ("xnT", t, kc)], writes=[("pf", bank)])
                P.add("act", lambda e, pf=pf, t=t: e.activation(
                    out=vaug[:, t, :, 0:128], in_=pf[:, 0:256].rearrange("p (h d) -> p h d", h=2), func=AF.Copy),
                    reads=[("pf", bank), "vones"], writes=[("va", t)])
            return sq, sk, sv

        P.begin_capture()
        nxt = record_proj(0)
        P.replay([P.end_capture()])
        for sl in nxt:
            self.ws.done(sl[2])
        for hp in range(4):
            ATT, FIN = [], []
            for hh in range(2):
                h = 2 * hp + hh
                for qt in range(4):
                    P.begin_capture()
                    first_in_bank = {}
                    units = [(c, kb) for c in range(2) for kb in range(4 * qt + 4)]
                    ubank = {}

                    def emit_scores(u):
                        c, kb = units[u]
                        r = kb - 4 * qt
                        col0 = max(r, 0) * 128
                        bank = cnt["pf"] % 2
                        cnt["pf"] += 1
                        ubank[u] = bank
                        ps = PF[bank]
                        krd = [("kT", hh, kb // 4, c)]
                        qrd = [("qT", hh, qt)]
                        if r >= 0:
                            P.add("pe", lambda e, ps=ps, c=c, hh=hh, kb=kb, qt=qt, col0=col0: e.matmul(
                                ps[:, col0:col0 + 128], kTz[:, c, hh, kb * 128:(kb + 1) * 128],
                                qT[:, hh, qt * 512 + col0:qt * 512 + col0 + 128], start=True, stop=False,
                                skip_group_check=True),
                                reads=krd + qrd, writes=[("pf", bank)])
                            P.add("pe", lambda e, ps=ps, col0=col0: e.matmul(
                                ps[:, col0:col0 + 128], identb[:, :], negmask[:, :], start=False, stop=True,
                                skip_group_check=True),
                                reads=["identb", "negmask"], writes=[("pf", bank)])
                            if col0 + 128 < 512:
                                P.add("pe", lambda e, ps=ps, c=c, hh=hh, kb=kb, qt=qt, col0=col0: e.matmul(
                                    ps[:, col0 + 128:512], kTz[:, c, hh, kb * 128:(kb + 1) * 128],
                                    qT[:, hh, qt * 512 + col0 + 128:qt * 512 + 512], start=True, stop=True,
                                    skip_group_check=True),
                                    reads=krd + qrd, writes=[("pf", bank)])
                        else:
                            P.add("pe", lambda e, ps=ps, c=c, hh=hh, kb=kb, qt=qt: e.matmul(
                                ps[:, :], kTz[:, c, hh, kb * 128:(kb + 1) * 128],
                                qT[:, hh, qt * 512:qt * 512 + 512], start=True, stop=True, skip_group_check=True),
                                reads=krd + qrd, writes=[("pf", bank)])

                    def emit_exp_pv(u):
                        c, kb = units[u]
                        r = kb - 4 * qt
                        col0 = max(r, 0) * 128
                        bank = ubank[u]
                        ps = PF[bank]
                        pi = cnt["pt"] % 3
                        cnt["pt"] += 1
                        P.add("act", lambda e, ps=ps, pi=pi, col0=col0: e.activation(
                            out=pT[:, pi, col0:512], in_=ps[:, col0:512], func=AF.Exp),
                            reads=[("pf", bank)], writes=[("pT", pi)])
                        for rr in range(max(r, 0), 4):
                            abank = 2 + 2 * c + rr // 2
                            off = (rr % 2) * 256
                            st = abank not in first_in_bank
                            first_in_bank[abank] = True
                            P.add("pe", lambda e, abank=abank, off=off, pi=pi, rr=rr, kb=kb, st=st, hh=hh, qt=qt: e.matmul(
                                PF[abank][:, off:off + 129], pT[:, pi, rr * 128:(rr + 1) * 128], vaug[:, kb, hh, 0:129],
                                start=st, stop=(kb == 4 * qt + rr), skip_group_check=True),
                                reads=[("pT", pi), ("va", kb), "vones"], writes=[("pf", abank)])

                    emit_scores(0)
                    for u in range(len(units)):
                        if u + 1 < len(units):
                            emit_scores(u + 1)
                        emit_exp_pv(u)
                    for b in range(4):
                        srcv = PF[2 + b][:, :].rearrange("p (s w) -> p s w", s=2)[:, :, 0:129]
                        dstv = accS[:, b, :].rearrange("p (s w) -> p s w", s=2)[:, :, 0:129]
                        if b % 2 == 0:
                            P.add("act", lambda e, srcv=srcv, dstv=dstv: e.activation(out=dstv, in_=srcv, func=AF.Copy),
                                  reads=[("pf", 2 + b)], writes=[("accS", b)])
                        else:
                            P.add("dve", lambda e, srcv=srcv, dstv=dstv: e.tensor_copy(out=dstv, in_=srcv),
                                  reads=[("pf", 2 + b)], writes=[("accS", b)])
                    ATT.append(P.end_capture())
                    P.begin_capture()
                    fi = cnt["fin"] % 2
                    cnt["fin"] += 1
                    AK = [("accS", b) for b in range(4)]
                    FK = ("fsm", fi)
                    slots = accS[:, :, :].rearrange("p b (s w) -> p (b s) w", s=2)
                    P.add("dve", lambda e, fi=fi, slots=slots: e.reciprocal(out=fsm[:, fi, 0:8], in_=slots[:, :, 128]),
                          reads=AK, writes=[FK])
                    P.add("dve", lambda e, fi=fi: e.tensor_scalar(out=fsm[:, fi, 8:12], in0=fsm[:, fi, 4:8], scalar1=cda[:, c0 + 3:c0 + 4], scalar2=None, op0=ALU.mult),
                          reads=[FK, ("cda", j)], writes=[FK])
                    for rr in range(4):
                        P.add("act", lambda e, fi=fi, rr=rr, slots=slots: e.activation(out=osb[:, fi, rr, :], in_=slots[:, rr, 0:128], func=AF.Copy,
                                                                                        scale=fsm[:, fi, rr:rr + 1]),
                              reads=AK + [FK], writes=[("osb", fi, rr)])
                    for rr in range(4):
                        P.add("dve", lambda e, fi=fi, rr=rr, slots=slots: e.scalar_tensor_tensor(
                            out=osb[:, fi, rr, :], in0=slots[:, 4 + rr, 0:128], scalar=fsm[:, fi, 8 + rr:9 + rr], in1=osb[:, fi, rr, :],
                            op0=ALU.mult, op1=ALU.add),
                            reads=AK + [FK, ("osb", fi, rr)], writes=[("osb", fi, rr)])
                    for rr in range(4):
                        P.add("act", lambda e, fi=fi, rr=rr: e.activation(out=obf[:, fi, rr, :], in_=osb[:, fi, rr, :], func=AF.Square,
                                                                          accum_out=fsm[:, fi, 12 + rr:13 + rr]),
                              reads=[("osb", fi, rr)], writes=[("obf", fi, rr), ("fss", fi, rr)])
                    P.add("act", lambda e, fi=fi: e.activation(out=fsm[:, fi, 16:20], in_=fsm[:, fi, 12:16], func=AF.Ln,
                                                               bias=self.epsc[:, 0:1], scale=1.0 / 128),
                          reads=[("fss", fi, rr) for rr in range(4)] + ["epsc"], writes=[("frs", fi)])
                    P.add("act", lambda e, fi=fi: e.activation(out=fsm[:, fi, 16:20], in_=fsm[:, fi, 16:20], func=AF.Exp, scale=-0.5),
                          reads=[("frs", fi)], writes=[("frs", fi)])
                    for rr in range(4):
                        P.add("dve", lambda e, fi=fi, rr=rr: e.tensor_scalar(out=obf[:, fi, rr, :], in0=osb[:, fi, rr, :], scalar1=fsm[:, fi, 16 + rr:17 + rr],
                                                                              scalar2=None, op0=ALU.mult),
                              reads=[("osb", fi, rr), ("frs", fi), ("obf", fi, rr)], writes=[("obf", fi, rr)])
                    pb = PB[fi]
                    for rr in range(4):
                        P.add("pe", lambda e, pb=pb, fi=fi, rr=rr: e.transpose(out=pb[:, rr * 128:(rr + 1) * 128], in_=obf[:, fi, rr, :], identity=identb[:]),
                              reads=[("obf", fi, rr), "identb"], writes=[("pb", fi)])
                    P.add("dve", lambda e, pb=pb, hh=hh, qt=qt: e.tensor_scalar(
                        out=U[:, hh, qt * 512:(qt + 1) * 512], in0=pb[:, 0:512], scalar1=cda[:, c0 + 4:c0 + 5], scalar2=None, op0=ALU.mult),
                        reads=[("pb", fi), ("cda", j)], writes=[("U", hh, qt)])
                    FIN.append(P.end_capture())
            P.replay([ATT[0]])
            for i in range(1, 8):
                P.replay([ATT[i], FIN[i - 1]])
            wsrc = self.d["da_w_out"][j][hp * 256:(hp + 1) * 256, :].rearrange("(kc p) f -> p kc f", p=128)
            so = self.ws.get(wsrc, (2, 1024))
            wv = so[0]
            P.begin_capture()
            for t in range(NT):
                for dh in range(2):
                    bank = 4 + (cnt["pf"] % 2)
                    cnt["pf"] += 1
                    pf = PF[bank]
                    for hc in range(2):
                        P.add("pe", lambda e, pf=pf, wv=wv, hc=hc, t=t, dh=dh: e.matmul(
                            pf[:, :], U[:, hc, t * 128:(t + 1) * 128], wv[:, hc, dh * 512:(dh + 1) * 512], start=(hc == 0), stop=(hc == 1)),
                            reads=[so[1], ("U", hc, t // 4)], writes=[("pf", bank)])
                    P.add("dve", lambda e, pf=pf, t=t, dh=dh: e.tensor_tensor(
                        out=X[:, t, dh * 512:(dh + 1) * 512], in0=X[:, t, dh * 512:(dh + 1) * 512], in1=pf[:, :], op=ALU.add),
                        reads=[("pf", bank), ("x", t)], writes=[("x", t)])
            tail = FIN[7] + P.end_capture()
            if hp < 3:
                P.begin_capture()
                nxt = record_proj(hp + 1)
                P.replay([tail, P.end_capture()])
                for sl in nxt:
                    self.ws.done(sl[2])
            else:
                P.replay([tail])
            self.ws.done(so[2])

    def setup_gd_static(self):
        P = self.P
        trif, sellast, onesf, masks, onecol = self.trif, self.sellast, self.onesf, self.masks, self.onecol
        P.add("pool", lambda e: e.memset(onesf[:], 1.0), writes=["onesf"])
        P.add("pool", lambda e: e.memset(onecol[:], 1.0), writes=["onecol"])
        P.add("pool", lambda e: e.memset(trif[:], 1.0), writes=["trif"])
        P.add("pool", lambda e: e.affine_select(out=trif[:], in_=trif[:], pattern=[[1, 128]], compare_op=ALU.is_ge,
                                                fill=0.0, base=0, channel_multiplier=-1), reads=["trif"], writes=["trif"])
        P.add("pool", lambda e: e.memset(sellast[:], 1.0), writes=["sellast"])
        P.add("pool", lambda e: e.affine_select(out=sellast[:], in_=sellast[:], pattern=[[0, 128]], compare_op=ALU.is_ge,
                                                fill=0.0, base=-127, channel_multiplier=1), reads=["sellast"], writes=["sellast"])
        P.add("pool", lambda e: e.memset(masks[:], 0.0), writes=["masks"])
        P.add("pool", lambda e: e.affine_select(out=masks[:, 0:128], in_=masks[:, 0:128], pattern=[[1, 128]], compare_op=ALU.is_ge,
                                                fill=-30000.0, base=-1, channel_multiplier=-1), reads=["masks"], writes=["masks"])
        P.add("pool", lambda e: e.affine_select(out=masks[:, 128:256], in_=masks[:, 128:256], pattern=[[1, 128]], compare_op=ALU.is_ge,
                                                fill=-30000.0, base=0, channel_multiplier=-1), reads=["masks"], writes=["masks"])

    def setup_gd_levelmasks(self):
        P = self.P
        NEGM = self.NEGM
        scrA = lambda nb: self.xs_f[0:nb, 0, 0:128]
        scrC = lambda nb: self.xs_f[0:nb, 0, 128:256]
        K0 = [("xs", 0)]
        for k in range(7):
            B, half = 2 ** (k + 1), 2 ** k
            nb = 128 // B
            A, C = scrA(nb), scrC(nb)
            P.add("pool", lambda e, nb=nb: e.memset(self.xs_f[0:nb, 0, 0:256], 1.0), writes=K0)
            P.add("pool", lambda e, A=A, B=B, half=half: e.affine_select(out=A, in_=A, pattern=[[1, 128]], compare_op=ALU.is_ge, fill=0.0,
                                                                       base=-half, channel_multiplier=-B), reads=K0, writes=K0)
            P.add("pool", lambda e, A=A, B=B: e.affine_select(out=A, in_=A, pattern=[[-1, 128]], compare_op=ALU.is_ge, fill=0.0,
                                                             base=B - 1, channel_multiplier=B), reads=K0, writes=K0)
            P.add("pool", lambda e, C=C, B=B: e.affine_select(out=C, in_=C, pattern=[[1, 128]], compare_op=ALU.is_ge, fill=0.0,
                                                             base=0, channel_multiplier=-B), reads=K0, writes=K0)
            P.add("pool", lambda e, C=C, B=B, half=half: e.affine_select(out=C, in_=C, pattern=[[-1, 128]], compare_op=ALU.is_ge, fill=0.0,
                                                                       base=half - 1, channel_multiplier=B), reads=K0, writes=K0)
            pf = self.PF[1]
            P.add("pe", lambda e, pf=pf, A=A, C=C: e.matmul(pf[:, 0:128], A, C, start=True, stop=True, skip_group_check=True),
                  reads=K0, writes=[("pf", 1)])
            P.add("pe", lambda e, pf=pf, A=A, C=C: e.matmul(pf[:, 128:256], C, A, start=True, stop=True, skip_group_check=True),
                  reads=K0, writes=[("pf", 1)])
            P.add("act", lambda e, pf=pf, k=k: e.activation(out=NEGM[:, k, :], in_=pf[:, 0:256], func=AF.Copy),
                  reads=[("pf", 1)], writes=["NEGM"])

    def setup_gd_consts(self, j):
        P = self.P
        d = self.d
        convw, gdc, identf = self.convw, self.gdc, self.identf
        crow96 = self.xs_f[0:96, 1, 0:128]
        rows = d["gd_conv_w"][j].rearrange("k (c p) -> (k c) p", p=128)
        P.add("sp", lambda e: e.dma_start(out=crow96, in_=rows), writes=[("xs", 1)], dma="gcw%d" % j)
        pf = self.PF[0]
        P.add("pe", lambda e: e.transpose(out=pf[:, 0:96], in_=crow96, identity=identf[0:96, 0:96]),
              reads=[("xs", 1), "identf"], writes=[("pf", 0)])
        P.add("dve", lambda e: e.tensor_copy(out=convw[:, j, :], in_=pf[:, 0:96]), reads=[("pf", 0)], writes=[("convw", j)])
        col = lambda ap: ap.rearrange("(p o) -> p o", o=1)
        P.add("sp", lambda e: e.dma_start(out=gdc[:, j, 0:1], in_=col(d["gd_out_norm"][j])), writes=[("gdc", j)], dma="gon%d" % j)
        gsm = self.gsm
        P.add("sp", lambda e: e.dma_start(out=gsm[:, j, 0:8], in_=d["gd_a_log"][j:j + 1, :].partition_broadcast(128)),
              writes=[("gsm", j)], dma="gal%d" % j)
        P.add("sp", lambda e: e.dma_start(out=gsm[:, j, 8:16], in_=d["gd_dt_bias"][j:j + 1, :].partition_broadcast(128)),
              writes=[("gsm", j)], dma="gdt%d" % j)
        P.add("act", lambda e: e.activation(out=gsm[:, j, 0:8], in_=gsm[:, j, 0:8], func=AF.Exp), reads=[("gsm", j)], writes=[("gsm", j)])
        P.add("dve", lambda e: e.tensor_scalar(out=gsm[:, j, 0:8], in0=gsm[:, j, 0:8], scalar1=-1.0, scalar2=None, op0=ALU.mult),
              reads=[("gsm", j)], writes=[("gsm", j)])

    def gdn(self, l):
        P = self.P
        j = l // 2
        g = self.g
        X, xnT, U, ring = self.X, self.xnT, self.U, self.ring
        identb, identf = self.identb, self.identf
        PF, PB = self.PF, self.PB
        epsc = self.epsc
        DKS = 128.0 ** -0.5
        win = self.d["gd_w_in"][j].rearrange("(kc p) f -> p kc f", p=128)
        self.rmsnorm(8 * l)
        self.arena_barrier()
        XN0 = [("xnT", 0, 0)]
        cnt = {"pf": 0, "rb": 0, "sq": 0}
        BA, BETA, GC, EG, GLB, EGL, EKD = (g[k] for k in ("BA", "BETA", "GC", "EG", "GLB", "EGL", "EKD"))
        flat = lambda v: v.rearrange("p a b -> p (a b)")

        sba = self.ws.get(win[:, :, 4096:4112], 16)
        slot_ba, wkey_ba, _ = sba
        pf_ba = PF[0]
        for t in range(NT):
            for kc in range(8):
                P.add("pe", lambda e, t=t, kc=kc: e.matmul(pf_ba[:, t * 16:(t + 1) * 16], xnT[:, kc, t * 128:(t + 1) * 128],
                                                          slot_ba[:, kc, 0:16], start=(kc == 0), stop=(kc == 7),
                                                          skip_group_check=True),
                      reads=[wkey_ba, ("xnT", t, kc)], writes=[("pf", 0)])
        self.ws.done(sba[2])
        P.add("dve", lambda e: e.tensor_copy(out=flat(BA), in_=pf_ba[:, 0:256]), reads=[("pf", 0)] + XN0, writes=["BA"])
        P.add("act", lambda e: e.activation(out=BETA, in_=BA[:, :, 0:8], func=AF.Exp, scale=-1.0), reads=["BA"] + XN0, writes=["BETA"])
        P.add("act", lambda e: e.activation(out=BETA, in_=BETA, func=AF.Ln, bias=1.0), reads=["BETA"], writes=["BETA"])
        P.add("act", lambda e: e.activation(out=BETA, in_=BETA, func=AF.Exp, scale=-1.0), reads=["BETA"], writes=["BETA"])
        gsm = self.gsm
        for t in range(NT):
            P.add("dve", lambda e, t=t: e.tensor_tensor(out=GC[:, t, :], in0=BA[:, t, 8:16], in1=gsm[:, j, 8:16], op=ALU.add),
                  reads=["BA", ("gsm", j)] + XN0, writes=["GC"])
        P.add("act", lambda e: e.activation(out=GC, in_=GC, func=AF.Exp), reads=["GC"], writes=["GC"])
        P.add("act", lambda e: e.activation(out=GC, in_=GC, func=AF.Ln, bias=1.0), reads=["GC"], writes=["GC"])
        for t in range(NT):
            P.add("dve", lambda e, t=t: e.tensor_tensor(out=GLB[:, t, :], in0=GC[:, t, :], in1=gsm[:, j, 0:8], op=ALU.mult),
                  reads=["GC", "BETA", ("gsm", j)] + XN0, writes=["BA"])
        pf1 = PF[1]
        for t in range(NT):
            P.add("pe", lambda e, t=t: e.matmul(pf1[:, t * 8:(t + 1) * 8], self.trif[:, :], GLB[:, t, :], start=True, stop=True,
                                                skip_group_check=True),
                  reads=["BA", "trif"], writes=[("pf", 1)])
        P.add("dve", lambda e: e.tensor_copy(out=flat(GC), in_=pf1[:, 0:128]), reads=[("pf", 1)], writes=["GC"])
        P.add("act", lambda e: e.activation(out=EG, in_=GC, func=AF.Exp), reads=["GC"] + XN0, writes=["EG"])
        for t in range(NT):
            P.add("pe", lambda e, t=t: e.matmul(pf1[:, 128 + t * 8:128 + (t + 1) * 8], self.sellast[:, :], GC[:, t, :], start=True, stop=True,
                                                skip_group_check=True),
                  reads=["GC", "sellast"], writes=[("pf", 1)])
        P.add("dve", lambda e: e.tensor_copy(out=flat(GLB), in_=pf1[:, 128:256]), reads=[("pf", 1)], writes=["BA"])
        P.add("act", lambda e: e.activation(out=EGL, in_=GLB, func=AF.Exp), reads=["BA"] + XN0, writes=["EGL"])
        P.add("dve", lambda e: e.tensor_tensor(out=EKD, in0=GLB, in1=GC, op=ALU.subtract), reads=["BA", "GC"] + XN0, writes=["EKD"])
        P.add("act", lambda e: e.activation(out=EKD, in_=EKD, func=AF.Exp), reads=["EKD"], writes=["EKD"])

        raw, acc, halo, sq, Sf, Sb = (g[k] for k in ("raw", "acc", "halo", "sq", "Sf", "Sb"))
        convw = self.convw
        NEGM = self.NEGM
        def capture_pair(hp):
            slabs = {}
            for wi, which in enumerate(("q", "k", "v", "z")):
                slabs[which] = self.ws.get(win[:, :, wi * 1024 + hp * 256: wi * 1024 + (hp + 1) * 256], 256)
            wsrc = self.d["gd_w_out"][j][hp * 256:(hp + 1) * 256, :].rearrange("(kc p) f -> p kc f", p=128)
            so = self.ws.get(wsrc, (2, 1024))
            FE, PREP, SCAN, OUT = {}, {}, {}, {}
            for gi in range(4):
                gp = gi % 2
                sT, zs, R, SC = g["sT"][gp], g["zs"][gp], g["R"][gp], g["SC"][gp]
                P.begin_capture()
                if gi == 0:
                    P.add("pool", lambda e: e.memset(flat(halo), 0.0), reads=XN0, writes=[("halo", i) for i in range(6)])
                for wi, which in enumerate(("k", "q", "v")):
                    slot, wkey, _ = slabs[which]
                    cbase = {"q": 0, "k": 8, "v": 16}[which]
                    for hh in range(2):
                        h = 2 * hp + hh
                        pf = PF[0]
                        rb = cnt["rb"] % 2
                        cnt["rb"] += 1
                        hi = wi * 2 + hh
                        P.add("dve", lambda e, rb=rb, hi=hi: e.tensor_copy(out=raw[:, rb, 0:3], in_=halo[:, hi, 0:3]),
                              reads=[("halo", hi)], writes=[("raw", rb)])
                        P.begin_atomic()
                        for kc in range(8):
                            P.add("pe", lambda e, pf=pf, slot=slot, kc=kc, hh=hh, gi=gi: e.matmul(
                                pf[:, :], slot[:, kc, hh * 128:(hh + 1) * 128], xnT[:, kc, gi * 512:(gi + 1) * 512],
                                start=(kc == 0), stop=(kc == 7)),
                                reads=[wkey] + [("xnT", t, kc) for t in range(4 * gi, 4 * gi + 4)], writes=[("pf", 0)])
                        P.add("act", lambda e, pf=pf, rb=rb: e.activation(out=raw[:, rb, 3:515], in_=pf[:, :], func=AF.Copy),
                              reads=[("pf", 0), ("raw", rb)], writes=[("raw", rb)])
                        P.end_atomic()
                        if gi < 3:
                            P.add("dve", lambda e, rb=rb, hi=hi: e.tensor_copy(out=halo[:, hi, 0:3], in_=raw[:, rb, 512:515]),
                                  reads=[("raw", rb)], writes=[("halo", hi)])
                        cc = cbase + h
                        P.add("dve", lambda e, rb=rb, cc=cc: e.tensor_scalar(
                            out=acc[:, 0, :], in0=raw[:, rb, 0:512], scalar1=convw[:, j, cc:cc + 1], scalar2=None, op0=ALU.mult),
                            reads=[("raw", rb), ("convw", j)], writes=[("acc", 0)])
                        for tap in range(1, 4):
                            P.add("dve", lambda e, rb=rb, cc=cc, tap=tap: e.scalar_tensor_tensor(
                                out=acc[:, 0, :], in0=raw[:, rb, tap:tap + 512], scalar=convw[:, j, tap * 24 + cc:tap * 24 + cc + 1],
                                in1=acc[:, 0, :], op0=ALU.mult, op1=ALU.add),
                                reads=[("raw", rb), ("acc", 0), ("convw", j)], writes=[("acc", 0)])
                        sgb = raw[:, rb, 0:512]
                        P.add("act", lambda e, sgb=sgb: e.activation(out=sgb, in_=acc[:, 0, :], func=AF.Exp, scale=-1.0),
                              reads=[("acc", 0), ("raw", rb), ("halo", hi)], writes=[("raw", rb)])
                        P.add("act", lambda e, sgb=sgb: e.activation(out=sgb, in_=sgb, func=AF.Ln, bias=1.0), reads=[("raw", rb)], writes=[("raw", rb)])
                        P.add("act", lambda e, sgb=sgb: e.activation(out=sgb, in_=sgb, func=AF.Exp, scale=-1.0), reads=[("raw", rb)], writes=[("raw", rb)])
                        P.add("dve", lambda e, sgb=sgb, hh=hh, wi=wi, sT=sT: e.tensor_tensor(
                            out=sT[:, hh, :, wi * 128:(wi + 1) * 128], in0=acc[:, 0, :].rearrange("p (a b) -> p a b", a=4),
                            in1=sgb.rearrange("p (a b) -> p a b", a=4), op=ALU.mult),
                            reads=[("acc", 0), ("raw", rb)], writes=[("sT", gp, hh, wi)])
                        if which in ("k", "q"):
                            sb_ = cnt["sq"] % 2
                            cnt["sq"] += 1
                            P.add("pool", lambda e, hh=hh, wi=wi, sb_=sb_, sT=sT: e.tensor_tensor(
                                out=sq[:, sb_, :].rearrange("p (a b) -> p a b", a=4), in0=sT[:, hh, :, wi * 128:(wi + 1) * 128],
                                in1=sT[:, hh, :, wi * 128:(wi + 1) * 128], op=ALU.mult),
                                reads=[("sT", gp, hh, wi)], writes=[("sq", sb_)])
                            for tl in range(4):
                                colr = tl * 4 + wi * 2 + hh
                                P.add("pe", lambda e, sb_=sb_, tl=tl, colr=colr: e.matmul(
                                    PF[1][:, 384 + colr:384 + colr + 1], sq[:, sb_, tl * 128:(tl + 1) * 128], self.onecol[:, 0:1],
                                    start=True, stop=True, skip_group_check=True),
                                    reads=[("sq", sb_), "onecol"], writes=[("pf", 1)])
                slot, wkey, _ = slabs["z"]
                for tl in range(4):
                    t = 4 * gi + tl
                    pf = PF[1]
                    for kc in range(8):
                        P.add("pe", lambda e, pf=pf, slot=slot, kc=kc, t=t: e.matmul(
                            pf[:, 0:256], xnT[:, kc, t * 128:(t + 1) * 128], slot[:, kc, 0:256], start=(kc == 0), stop=(kc == 7),
                            skip_group_check=True),
                            reads=[wkey, ("xnT", t, kc)], writes=[("pf", 1)])
                    zt = acc[:, 0, 0:256]
                    zr = acc[:, 0, 256:512]
                    P.add("act", lambda e, pf=pf, zt=zt: e.activation(out=zt, in_=pf[:, 0:256], func=AF.Exp, scale=-1.0),
                          reads=[("pf", 1)], writes=[("acc", 0)])
                    P.add("act", lambda e, pf=pf, zr=zr: e.activation(out=zr, in_=pf[:, 0:256], func=AF.Copy),
                          reads=[("pf", 1), ("acc", 0)], writes=[("acc", 0)])
                    P.add("act", lambda e, zt=zt: e.activation(out=zt, in_=zt, func=AF.Ln, bias=1.0), reads=[("acc", 0)], writes=[("acc", 0)])
                    P.add("act", lambda e, zt=zt: e.activation(out=zt, in_=zt, func=AF.Exp, scale=-1.0), reads=[("acc", 0)], writes=[("acc", 0)])
                    P.add("dve", lambda e, zt=zt, zr=zr, tl=tl, zs=zs: e.tensor_tensor(out=zs[:, tl, :], in0=zr, in1=zt, op=ALU.mult),
                          reads=[("acc", 0)], writes=[("zs", gp, tl)])
                Rk = ("R", gp)
                P.add("act", lambda e, R=R: e.activation(out=flat(R), in_=PF[1][:, 384:400], func=AF.Ln, bias=epsc[:, 0:1], scale=1.0),
                      reads=[("pf", 1), "epsc"], writes=[Rk])
                P.add("act", lambda e, R=R: e.activation(out=flat(R), in_=flat(R), func=AF.Exp, scale=-0.5), reads=[Rk], writes=[Rk])
                hs = slice(2 * hp, 2 * hp + 2)
                ts = slice(4 * gi, 4 * gi + 4)
                rk, rq = R[:, :, 0:2], R[:, :, 2:4]
                scv = lambda q_, SC=SC: SC[:, q_, :].rearrange("p (a b) -> p a b", a=4)
                T1, CKBG, CKD, CQ, UL, UA, BIAS, LN = (scv(i) for i in range(8))
                bt, egs, ekds, gcs = BETA[:, ts, hs], EG[:, ts, hs], EKD[:, ts, hs], GC[:, ts, hs]
                sk = lambda i: ("SC", gp, i)
                P.add("dve", lambda e, T1=T1, rk=rk, bt=bt: e.tensor_tensor(out=T1, in0=rk, in1=bt, op=ALU.mult), reads=[Rk, "BETA"], writes=[sk(0)])
                P.add("dve", lambda e, CKBG=CKBG, T1=T1, egs=egs: e.tensor_tensor(out=CKBG, in0=T1, in1=egs, op=ALU.mult), reads=[sk(0), "EG"], writes=[sk(1)])
                P.add("dve", lambda e, CKD=CKD, rk=rk, ekds=ekds: e.tensor_tensor(out=CKD, in0=rk, in1=ekds, op=ALU.mult), reads=[Rk, "EKD"], writes=[sk(2)])
                P.add("dve", lambda e, CQ=CQ, rq=rq, egs=egs: e.scalar_tensor_tensor(out=CQ, in0=rq, scalar=DKS, in1=egs, op0=ALU.mult, op1=ALU.mult),
                      reads=[Rk, "EG"], writes=[sk(3)])
                P.add("act", lambda e, UL=UL, T1=T1: e.activation(out=UL, in_=T1, func=AF.Ln), reads=[sk(0)], writes=[sk(4)])
                P.add("dve", lambda e, UL=UL, gcs=gcs: e.tensor_tensor(out=UL, in0=UL, in1=gcs, op=ALU.add), reads=[sk(4), "GC"], writes=[sk(4)])
                P.add("act", lambda e, UA=UA, rq=rq: e.activation(out=UA, in_=rq, func=AF.Ln, scale=DKS), reads=[Rk], writes=[sk(5)])
                P.add("dve", lambda e, UA=UA, gcs=gcs: e.tensor_tensor(out=UA, in0=UA, in1=gcs, op=ALU.add), reads=[sk(5), "GC"], writes=[sk(5)])
                P.add("act", lambda e, BIAS=BIAS, rk=rk: e.activation(out=BIAS, in_=rk, func=AF.Ln), reads=[Rk], writes=[sk(6)])
                P.add("dve", lambda e, BIAS=BIAS, gcs=gcs: e.tensor_tensor(out=BIAS, in0=BIAS, in1=gcs, op=ALU.subtract), reads=[sk(6), "GC"], writes=[sk(6)])
                FE[gi] = P.end_capture()
                SCK = [sk(i) for i in range(7)]
                for tl in range(4):
                    t = 4 * gi + tl
                    stageA = []
                    for c in range(2):
                        P.begin_capture()
                        hh = c
                        cs = 2 * (t % 2) + c
                        pbk, pbo = cs // 2, (cs % 2) * 512
                        sc1 = lambda q_, tl=tl, hh=hh, SC=SC: SC[:, q_, tl * 2 + hh:tl * 2 + hh + 1]
                        pb, pc = PB[pbk], PF[2 + cs]
                        kd, kbg, vb, E, MA = (g[k, cs] for k in ("kd", "kbg", "vb", "E", "MA"))
                        ksT = sT[:, hh, tl, 0:128]
                        vsT = sT[:, hh, tl, 256:384]
                        P.add("pe", lambda e, pb=pb, ksT=ksT, pbo=pbo: e.transpose(out=pb[:, pbo:pbo + 128], in_=ksT, identity=identb[:]),
                              reads=[("sT", gp, hh, 0), "identb"], writes=[("pb", pbk)])
                        P.add("pe", lambda e, pb=pb, vsT=vsT, pbo=pbo: e.transpose(out=pb[:, pbo + 128:pbo + 256], in_=vsT, identity=identb[:]),
                              reads=[("sT", gp, hh, 2), "identb"], writes=[("pb", pbk)])
                        P.add("act", lambda e, pb=pb, kd=kd, sc1=sc1, pbo=pbo: e.activation(out=kd, in_=pb[:, pbo:pbo + 128], func=AF.Copy, scale=sc1(2)),
                              reads=[("pb", pbk)] + SCK, writes=[("kd", cs)])
                        P.add("dve", lambda e, pb=pb, kbg=kbg, sc1=sc1, pbo=pbo: e.tensor_scalar(out=kbg, in0=pb[:, pbo:pbo + 128], scalar1=sc1(1), scalar2=None, op0=ALU.mult),
                              reads=[("pb", pbk)] + SCK, writes=[("kbg", cs)])
                        hcol = 2 * hp + hh
                        P.add("dve", lambda e, pb=pb, vb=vb, t=t, hcol=hcol, pbo=pbo: e.tensor_scalar(
                            out=vb, in0=pb[:, pbo + 128:pbo + 256], scalar1=BETA[:, t, hcol:hcol + 1], scalar2=None, op0=ALU.mult),
                            reads=[("pb", pbk), "BETA"], writes=[("vb", cs)])
                        P.add("pe", lambda e, pc=pc, ksT=ksT, hh=hh, tl=tl, sT=sT: e.matmul(pc[:, 0:256], ksT, sT[:, hh, tl, 0:256], start=True, stop=True,
                                                                                              skip_group_check=True),
                              reads=[("sT", gp, hh, 0), ("sT", gp, hh, 1)], writes=[("pf", 2 + cs)])
                        P.add("act", lambda e, E=E, sc1=sc1: e.activation(out=E[:, 0:128], in_=identf[:, :], func=AF.Copy, scale=sc1(4)),
                              reads=["identf"] + SCK, writes=[("E", cs)])
                        P.add("act", lambda e, E=E, sc1=sc1: e.activation(out=E[:, 128:256], in_=identf[:, :], func=AF.Copy, scale=sc1(5)),
                              reads=["identf", ("E", cs)] + SCK, writes=[("E", cs)])
                        P.add("pe", lambda e, pc=pc, E=E: e.matmul(pc[:, 256:512], self.onesf[:, :], E[:, :], start=True, stop=False, skip_group_check=True),
                              reads=[("E", cs), "onesf"], writes=[("pf", 2 + cs)])
                        P.add("pe", lambda e, pc=pc: e.matmul(pc[:, 256:512], identf[:, :], self.masks[:, :], start=False, stop=True, skip_group_check=True),
                              reads=["identf", "masks"], writes=[("pf", 2 + cs)])
                        P.add("act", lambda e, pc=pc, E=E, sc1=sc1: e.activation(out=E[:, :], in_=pc[:, 256:512], func=AF.Exp, bias=sc1(6)),
                              reads=[("pf", 2 + cs)] + SCK, writes=[("E", cs)])
                        P.add("dve", lambda e, pc=pc, E=E, MA=MA: e.tensor_tensor(out=MA[:, :], in0=pc[:, 0:256], in1=E[:, :], op=ALU.mult),
                              reads=[("pf", 2 + cs), ("E", cs)], writes=[("MA", cs)])
                        Lb, DD, TM = g["Lb", cs], g["DD", cs], g["TM", cs]
                        P.add("pe", lambda e, pb=pb, MA=MA, pbo=pbo: e.transpose(out=pb[:, pbo + 256:pbo + 384], in_=MA[:, 0:128], identity=identb[:]),
                              reads=[("MA", cs), "identb"], writes=[("pb", pbk)])
                        P.add("act", lambda e, pb=pb, Lb=Lb, pbo=pbo: e.activation(out=Lb, in_=pb[:, pbo + 256:pbo + 384], func=AF.Copy),
                              reads=[("pb", pbk)], writes=[("Lb", cs)])
                        P.add("dve", lambda e, TM=TM, Lb=Lb: e.tensor_tensor(out=TM[:, 0:128], in0=Lb, in1=NEGM[:, 0, 0:128], op=ALU.mult),
                              reads=[("Lb", cs), "NEGM"], writes=[("TM", cs)])
                        P.add("dve", lambda e, TM=TM, MA=MA: e.tensor_tensor(out=TM[:, 128:256], in0=MA[:, 0:128], in1=NEGM[:, 0, 128:256], op=ALU.mult),
                              reads=[("MA", cs), "NEGM", ("TM", cs)], writes=[("TM", cs)])
                        P.add("dve", lambda e, TM=TM, DD=DD: e.tensor_tensor(out=DD[:, 0, 0:128], in0=identb[:, :], in1=TM[:, 0:128], op=ALU.subtract),
                              reads=[("TM", cs), "identb"], writes=[("DD", cs, 0)])
                        P.add("dve", lambda e, TM=TM, DD=DD: e.tensor_tensor(out=DD[:, 0, 128:256], in0=identb[:, :], in1=TM[:, 128:256], op=ALU.subtract),
                              reads=[("TM", cs), "identb", ("DD", cs, 0)], writes=[("DD", cs, 0)])
                        stageA.append(P.end_capture())
                    P.begin_capture()
                    for lev in range(1, 7):
                        pi, po = (lev - 1) % 2, lev % 2
                        for c in range(2):
                            cs = 2 * (t % 2) + c
                            pc = PF[2 + cs]
                            MA, Lb, DD, QQ = (g[k_, cs] for k_ in ("MA", "Lb", "DD", "QQ"))
                            P.add("pe", lambda e, pc=pc, MA=MA, DD=DD, pi=pi: e.matmul(pc[:, 0:128], MA[:, 0:128], DD[:, pi, 0:128], start=True, stop=True,
                                                                                        skip_group_check=True),
                                  reads=[("MA", cs), ("DD", cs, pi)], writes=[("pf", 2 + cs)])
                            P.add("pe", lambda e, pc=pc, Lb=Lb, DD=DD, pi=pi: e.matmul(pc[:, 128:256], Lb, DD[:, pi, 128:256], start=True, stop=True,
                                                                                        skip_group_check=True),
                                  reads=[("Lb", cs), ("DD", cs, pi)], writes=[("pf", 2 + cs)])
                            P.add("dve", lambda e, pc=pc, QQ=QQ, lev=lev: e.tensor_tensor(out=QQ[:, :], in0=pc[:, 0:256], in1=NEGM[:, lev, :], op=ALU.mult),
                                  reads=[("pf", 2 + cs), "NEGM"], writes=[("QQ", cs)])
                        for c in range(2):
                            cs = 2 * (t % 2) + c
                            pc = PF[2 + cs]
                            DD, QQ = (g[k_, cs] for k_ in ("DD", "QQ"))
                            P.add("pe", lambda e, pc=pc, QQ=QQ, DD=DD, pi=pi: e.matmul(pc[:, 256:384], DD[:, pi, 128:256], QQ[:, 0:128], start=True, stop=True,
                                                                                        skip_group_check=True),
                                  reads=[("QQ", cs), ("DD", cs, pi)], writes=[("pf", 2 + cs)])
                            P.add("pe", lambda e, pc=pc, QQ=QQ, DD=DD, pi=pi: e.matmul(pc[:, 384:512], DD[:, pi, 0:128], QQ[:, 128:256], start=True, stop=True,
                                                                                        skip_group_check=True),
                                  reads=[("QQ", cs), ("DD", cs, pi)], writes=[("pf", 2 + cs)])
                            P.add("dve", lambda e, pc=pc, DD=DD, pi=pi, po=po: e.tensor_tensor(out=DD[:, po, :], in0=DD[:, pi, :], in1=pc[:, 256:512], op=ALU.subtract),
                                  reads=[("pf", 2 + cs), ("DD", cs, pi)], writes=[("DD", cs, po)])
                    for c in range(2):
                        cs = 2 * (t % 2) + c
                        pc = PF[2 + cs]
                        kbg, vb, DD, u, wT = (g[k, cs] for k in ("kbg", "vb", "DD", "u", "wT"))
                        P.add("pe", lambda e, pc=pc, DD=DD, vb=vb: e.matmul(pc[:, 0:128], DD[:, 0, 128:256], vb, start=True, stop=True, skip_group_check=True),
                              reads=[("DD", cs, 0), ("vb", cs)], writes=[("pf", 2 + cs)])
                        P.add("pe", lambda e, pc=pc, DD=DD, kbg=kbg: e.matmul(pc[:, 128:256], kbg, DD[:, 0, 128:256], start=True, stop=True, skip_group_check=True),
                              reads=[("DD", cs, 0), ("kbg", cs)], writes=[("pf", 2 + cs)])
                        P.add("act", lambda e, pc=pc, u=u: e.activation(out=u, in_=pc[:, 0:128], func=AF.Copy), reads=[("pf", 2 + cs)], writes=[("u", cs)])
                        P.add("dve", lambda e, pc=pc, wT=wT: e.tensor_copy(out=wT, in_=pc[:, 128:256]), reads=[("pf", 2 + cs)], writes=[("wT", cs)])
                    PREP[t] = Prog.merge(stageA) + P.end_capture()
                    scans = []
                    for c in range(2):
                        P.begin_capture()
                        hh = c
                        h = 2 * hp + hh
                        cs = 2 * (t % 2) + c
                        pbk, pbo = cs // 2, (cs % 2) * 512
                        ps_, pb = PF[2 + cs], PB[pbk]
                        kd, MA, u, wT, vn, o, og, psm = (g[k, cs] for k in ("kd", "MA", "u", "wT", "vn", "o", "og", "ps"))
                        sc1 = lambda q_, tl=tl, hh=hh, SC=SC: SC[:, q_, tl * 2 + hh:tl * 2 + hh + 1]
                        qsT = sT[:, hh, tl, 128:256]
                        PK = ("pf", 2 + cs)
                        P.add("pe", lambda e, ps_=ps_, wT=wT, hh=hh: e.matmul(ps_[:, 0:128], wT, Sb[:, hh, :], start=True, stop=True, skip_group_check=True),
                              reads=[("wT", cs), ("Sb", hh)], writes=[PK])
                        P.add("pe", lambda e, ps_=ps_, qsT=qsT, hh=hh: e.matmul(ps_[:, 128:256], qsT, Sb[:, hh, :], start=True, stop=True, skip_group_check=True),
                              reads=[("sT", gp, hh, 1), ("Sb", hh)], writes=[PK])
                        P.add("dve", lambda e, ps_=ps_, u=u, vn=vn: e.tensor_tensor(out=vn, in0=u, in1=ps_[:, 0:128], op=ALU.subtract),
                              reads=[PK, ("u", cs)], writes=[("vn", cs)])
                        P.add("pe", lambda e, ps_=ps_, MA=MA, vn=vn: e.matmul(ps_[:, 256:384], MA[:, 128:256], vn, start=True, stop=True, skip_group_check=True),
                              reads=[("MA", cs), ("vn", cs)], writes=[PK])
                        P.add("pe", lambda e, ps_=ps_, kd=kd, vn=vn: e.matmul(ps_[:, 384:512], kd, vn, start=True, stop=True, skip_group_check=True),
                              reads=[("kd", cs), ("vn", cs)], writes=[PK])
                        P.add("act", lambda e, ps_=ps_, o=o: e.activation(out=o, in_=ps_[:, 256:384], func=AF.Copy), reads=[PK], writes=[("o", cs)])
                        P.add("dve", lambda e, ps_=ps_, o=o, sc1=sc1: e.scalar_tensor_tensor(out=o, in0=ps_[:, 128:256], scalar=sc1(3), in1=o, op0=ALU.mult, op1=ALU.add),
                              reads=[PK, ("o", cs)] + SCK, writes=[("o", cs)])
                        P.add("dve", lambda e, ps_=ps_, hh=hh, t=t, h=h: e.scalar_tensor_tensor(
                            out=Sf[:, hh, :], in0=Sf[:, hh, :], scalar=EGL[:, t, h:h + 1], in1=ps_[:, 384:512], op0=ALU.mult, op1=ALU.add),
                            reads=[PK, ("Sf", hh), "EGL"], writes=[("Sf", hh)])
                        P.add("act", lambda e, hh=hh: e.activation(out=Sb[:, hh, :], in_=Sf[:, hh, :], func=AF.Copy), reads=[("Sf", hh)], writes=[("Sb", hh)])
                        P.add("act", lambda e, o=o, og=og, psm=psm: e.activation(out=og, in_=o, func=AF.Square, accum_out=psm[:, 0:1]),
                              reads=[("o", cs)], writes=[("og", cs), ("psm", cs)])
                        P.add("act", lambda e, psm=psm: e.activation(out=psm[:, 1:2], in_=psm[:, 0:1], func=AF.Ln, bias=epsc[:, 0:1], scale=1.0 / 128),
                              reads=[("psm", cs), "epsc"], writes=[("psm", cs)])
                        P.add("act", lambda e, psm=psm: e.activation(out=psm[:, 1:2], in_=psm[:, 1:2], func=AF.Exp, scale=-0.5), reads=[("psm", cs)], writes=[("psm", cs)])
                        P.add("dve", lambda e, o=o, og=og, psm=psm, tl=tl, hh=hh, zs=zs: e.scalar_tensor_tensor(
                            out=og, in0=o, scalar=psm[:, 1:2], in1=zs[:, tl, hh * 128:(hh + 1) * 128], op0=ALU.mult, op1=ALU.mult),
                            reads=[("o", cs), ("psm", cs), ("zs", gp, tl)], writes=[("og", cs)])
                        P.add("pe", lambda e, pb=pb, og=og, pbo=pbo: e.transpose(out=pb[:, pbo + 384:pbo + 512], in_=og, identity=identb[:]),
                              reads=[("og", cs), "identb"], writes=[("pb", pbk)])
                        P.add("act", lambda e, pb=pb, hh=hh, t=t, pbo=pbo: e.activation(out=U[:, hh, t * 128:(t + 1) * 128], in_=pb[:, pbo + 384:pbo + 512], func=AF.Copy,
                                                                                        scale=self.gdc[:, j, 0:1]),
                              reads=[("pb", pbk), ("gdc", j)], writes=[("U", hh, t // 4)])
                        scans.append(P.end_capture())
                    SCAN[t] = Prog.merge(scans)
            wv = so[0]
            for t in range(NT):
                P.begin_capture()
                for dh in range(2):
                    pf = PF[0]
                    P.begin_atomic()
                    for hc in range(2):
                        P.add("pe", lambda e, pf=pf, wv=wv, hc=hc, t=t, dh=dh: e.matmul(
                            pf[:, :], U[:, hc, t * 128:(t + 1) * 128], wv[:, hc, dh * 512:(dh + 1) * 512], start=(hc == 0), stop=(hc == 1)),
                            reads=[so[1], ("U", hc, t // 4)], writes=[("pf", 0)])
                    P.add("dve", lambda e, pf=pf, t=t, dh=dh: e.tensor_tensor(
                        out=X[:, t, dh * 512:(dh + 1) * 512], in0=X[:, t, dh * 512:(dh + 1) * 512], in1=pf[:, :], op=ALU.add),
                        reads=[("pf", 0), ("x", t)], writes=[("x", t)])
                    P.end_atomic()
                OUT[t] = P.end_capture()
            return slabs, so, FE, PREP, SCAN, OUT

        def zero_state():
            P.add("pool", lambda e: e.memset(flat(Sf), 0.0), reads=XN0, writes=[("Sf", 0), ("Sf", 1)])
            P.add("pool", lambda e: e.memset(flat(Sb), 0.0), reads=XN0, writes=[("Sb", 0), ("Sb", 1)])

        cur = capture_pair(0)
        P.replay([cur[2][0]])
        zero_state()
        for hp in range(4):
            slabs, so, FE, PREP, SCAN, OUT = cur
            fe_parts = {}
            for gi in range(1, 4):
                L = FE[gi]
                n = (len(L) + 2) // 3
                for k in range(3):
                    fe_parts[4 * (gi - 1) + 1 + k] = L[k * n:(k + 1) * n]
            for s_ in range(NT):
                lists = [PREP[s_]]
                if s_ >= 1:
                    lists.append(SCAN[s_ - 1])
                if s_ in fe_parts:
                    lists.append(fe_parts[s_])
                if s_ >= 2:
                    lists.append(OUT[s_ - 2])
                P.replay(lists)
            for which in ("q", "k", "v", "z"):
                self.ws.done(slabs[which][2])
            tail = Prog.merge([SCAN[NT - 1]]) + Prog.merge([OUT[NT - 2]]) + Prog.merge([OUT[NT - 1]])
            if hp < 3:
                cur = capture_pair(hp + 1)
                P.replay([tail, cur[2][0]])
            else:
                P.replay([tail])
            self.ws.done(so[2])
            if hp < 3:
                zero_state()

    def load_x(self, s):
        P = self.P
        X = self.X
        xv = self.d["x"][s].rearrange("(t p) d -> p t d", p=128)
        for q in range(4):
            P.add("sp", lambda e, q=q: e.dma_start(out=X[:, 4 * q:4 * q + 4, :], in_=xv[:, 4 * q:4 * q + 4, :]),
                  writes=[("x", t) for t in range(4 * q, 4 * q + 4)], dma=("xl", q))

    def store_x(self, s):
        P = self.P
        X = self.X
        ov = self.d["out"][s].rearrange("(t p) d -> p t d", p=128)
        ids = []
        for q in range(4):
            ids.append(P.add("sp", lambda e, q=q: e.dma_start(out=ov[:, 4 * q:4 * q + 4, :], in_=X[:, 4 * q:4 * q + 4, :]),
                             reads=[("x", t) for t in range(4 * q, 4 * q + 4)], writes=[("xst", q)], dma=("xs", q)))
        return ids

    def rmsnorm(self, gbase):
        P = self.P
        X, xs, ss, rstd, xnT = self.X, self.xs, self.ss, self.rstd, self.xnT
        identb, gcol = self.identb, self.gcol
        import os
        DBG = int(os.environ.get("K_DBG", "9"))
        for t in range(NT):
            P.add("act", lambda e, t=t: e.activation(out=xs[:, 1, :], in_=X[:, t, :], func=AF.Square,
                                                     accum_out=ss[:, t:t + 1]),
                  reads=[("x", t)], writes=[("xs", 1), ("ss", t)])
        P.add("act", lambda e: e.activation(out=rstd[:], in_=ss[:], func=AF.Ln, bias=self.epsc[:, 0:1], scale=1.0 / D),
              reads=[("ss", t) for t in range(NT)] + ["epsc"], writes=["rstd"])
        P.add("act", lambda e: e.activation(out=rstd[:], in_=rstd[:], func=AF.Exp, scale=-0.5), reads=["rstd"], writes=["rstd"])
        if DBG < 2:
            return
        for t in range(NT if DBG >= 6 else 1):
            b = t % 2
            pb = self.PB[b]
            P.add("act", lambda e, t=t, b=b: e.activation(out=xs[:, b, :], in_=X[:, t, :], func=AF.Copy,
                                                          scale=rstd[:, t:t + 1]),
                  reads=[("x", t), "rstd"], writes=[("xs", b)])
            if DBG < 4:
                continue
            for kc in range(8):
                P.add("pe", lambda e, b=b, kc=kc, pb=pb: e.transpose(out=pb[:, kc * 128:(kc + 1) * 128],
                                                                      in_=xs[:, b, kc * 128:(kc + 1) * 128],
                                                                      identity=identb[:]),
                      reads=[("xs", b), "identb"], writes=[("pb", b)])
            if DBG < 5:
                continue
            for kc in range(8):
                eng = "dve" if b == 0 else "act"
                if eng == "dve":
                    fn = lambda e, t=t, kc=kc, pb=pb: e.tensor_scalar(
                        out=xnT[:, kc, t * 128:(t + 1) * 128], in0=pb[:, kc * 128:(kc + 1) * 128],
                        scalar1=gcol[:, gbase + kc:gbase + kc + 1], scalar2=None, op0=ALU.mult)
                else:
                    fn = lambda e, t=t, kc=kc, pb=pb: e.activation(
                        out=xnT[:, kc, t * 128:(t + 1) * 128], in_=pb[:, kc * 128:(kc + 1) * 128],
                        func=AF.Copy, scale=gcol[:, gbase + kc:gbase + kc + 1])
                P.add(eng, fn, reads=[("pb", b), "gcol"], writes=[("xnT", t, kc)])

    def mlp(self, l):
        P = self.P
        X, xnT, U, ring = self.X, self.xnT, self.hT, self.ring
        w1 = self.d["mlp_w_in"][l].rearrange("(kc p) f -> p kc f", p=128)
        w2 = self.d["mlp_w_out"][l].rearrange("(fc p) d -> p fc d", p=128)
        self.rmsnorm(32 + 8 * l)
        self.arena_barrier()
        pfi = 0
        for fg in range(4):
            slabs = [self.ws.get(w1[:, :, fg * 1024 + s2 * 512: fg * 1024 + (s2 + 1) * 512], 512) for s2 in range(2)]
            for fc in range(8):
                slot, wkey, _ = slabs[fc // 4]
                off = (fc % 4) * 128
                for tt in range(4):
                    bank = pfi % 4
                    pfi += 1
                    pf = self.PF[bank]
                    for kc in range(8):
                        P.add("pe", lambda e, pf=pf, slot=slot, kc=kc, off=off, tt=tt: e.matmul(
                            pf[:, :], slot[:, kc, off:off + 128], xnT[:, kc, tt * 512:(tt + 1) * 512],
                            start=(kc == 0), stop=(kc == 7)),
                            reads=[wkey] + [("xnT", t, kc) for t in range(4 * tt, 4 * tt + 4)],
                            writes=[("pf", bank)])
                    rb = pfi % 2
                    P.add("act", lambda e, pf=pf, rb=rb: e.activation(out=self.rtmp[:, rb, :], in_=pf[:, :], func=AF.Relu),
                          reads=[("pf", bank)], writes=[("xs", rb)])
                    P.add("dve", lambda e, fc=fc, tt=tt, rb=rb: e.tensor_tensor(
                        out=U[:, fc, tt * 512:(tt + 1) * 512], in0=self.rtmp[:, rb, :], in1=self.rtmp[:, rb, :],
                        op=ALU.mult),
                        reads=[("xs", rb)], writes=[("hT", fc, tt)])
            for sl in slabs:
                self.ws.done(sl[2])
            slabs2 = [self.ws.get(w2[:, fg * 8:(fg + 1) * 8, dh * 512:(dh + 1) * 512], 512) for dh in range(2)]
            for t in range(NT):
                for dh in range(2):
                    slot, wkey, _ = slabs2[dh]
                    bank = pfi % 4
                    pfi += 1
                    pf = self.PF[bank]
                    for fc in range(8):
                        P.add("pe", lambda e, pf=pf, slot=slot, fc=fc, t=t: e.matmul(
                            pf[:, :], U[:, fc, t * 128:(t + 1) * 128], slot[:, fc, :],
                            start=(fc == 0), stop=(fc == 7)),
                            reads=[wkey, ("hT", fc, t // 4)], writes=[("pf", bank)])
                    P.add("dve", lambda e, pf=pf, t=t, dh=dh: e.tensor_tensor(
                        out=X[:, t, dh * 512:(dh + 1) * 512], in0=X[:, t, dh * 512:(dh + 1) * 512], in1=pf[:, :],
                        op=ALU.add),
                        reads=[("pf", bank), ("x", t)], writes=[("x", t)])
            for sl in slabs2:
                self.ws.done(sl[2])

    def build(self):
        P = self.P
        self.setup_consts()
        kinds = {k for k, _ in self.layers}
        if "gd" in kinds:
            self.setup_gd_static()
            self.setup_gd_levelmasks()
            for jj in sorted({l // 2 for (k, l) in self.layers if k == "gd"}):
                self.setup_gd_consts(jj)
        if "da" in kinds:
            self.setup_da_static()
            for (k, l) in self.layers:
                if k == "da":
                    self.setup_da_consts(l // 2, l)
        last_stores = []
        for s in range(self.n_seq):
            self.load_x(s)
            for l in self.layers:
                if l[0] == "mlp":
                    self.mlp(l[1])
                elif l[0] == "norm":
                    self.rmsnorm(32 + 8 * l[1])
                elif l[0] == "da":
                    self.diffattn(l[1])
                elif l[0] == "gd":
                    self.gdn(l[1])
            last_stores = self.store_x(s)
        P.add("sp", None, reads=[("xst", q) for q in range(4)])


def layer_plan():
    plan = []
    for i in range(DEPTH):
        plan.append(("da" if i % 2 == 0 else "gd", i))
        plan.append(("mlp", i))
    return plan


def build_program(n_seq=SEQ_PER_CORE, layers=None):
    if layers is None:
        layers = layer_plan()
    nc = bass.Bass("TRN2", target_bir_lowering=False)
    with ExitStack() as es:
        b = Builder(nc, Prog(nc, dry=True), None, n_seq, layers, es)
        b.build()
        future = list(b.ws.requests)
        P = Prog(nc, dry=False)
        b.P = P
        b.ws = WStream(P, b.ring, b.NSLOT * 2, future)
        b.build()
        P.emit(es)
    return nc


WEIGHT_NAMES = ["mix_norm", "mlp_norm", "mlp_w_in", "mlp_w_out", "da_w_in", "da_q_norm", "da_k_norm",
                "da_lambda_q1", "da_lambda_k1", "da_lambda_q2", "da_lambda_k2", "da_sub_norm", "da_w_out",
                "gd_w_in", "gd_conv_w", "gd_a_log", "gd_dt_bias", "gd_out_norm", "gd_w_out"]


def run(inputs, n_seq=SEQ_PER_CORE, layers=None, ncores=NCORES, trace=False):
    nc = build_program(n_seq, layers)
    x = np.ascontiguousarray(np.asarray(inputs["x"], dtype=np.float32))
    weights = {k: np.ascontiguousarray(np.asarray(inputs[k], dtype=np.float32)) for k in WEIGHT_NAMES}
    in_maps = []
    for c in range(ncores):
        m = {"x": x[c * n_seq:(c + 1) * n_seq]}
        m.update(weights)
        in_maps.append(m)
    res = run_bass_kernel_spmd(nc, in_maps, core_ids=list(range(ncores)), trace=trace)
    out = np.concatenate([r["out"] for r in res.results], axis=0)
    return out, res


def kernel(**inputs):
    out, _ = run(inputs)
    return out
```

```python
import math
from contextlib import ExitStack

import numpy as np
import concourse.bass as bass
import concourse.mybir as mybir
from concourse.bass_utils import run_bass_kernel_spmd

F32 = mybir.dt.float32
BF16 = mybir.dt.bfloat16
AF = mybir.ActivationFunctionType
ALU = mybir.AluOpType
AX = mybir.AxisListType

D = 1024
S = 2048
NT = S // 128
DFF = 4096
DEPTH = 4
EPS = 1e-6
NCORES = 8
SEQ_PER_CORE = 4
GD_IN = 4 * 1024 + 16


class Op:
    __slots__ = ("id", "eng", "fn", "deps", "dma", "seq", "signal")


class Prog:
    ENGS = ("pe", "act", "dve", "pool", "sp")

    def __init__(self, nc, dry=False):
        self.nc = nc
        self.dry = dry
        self.ops = []
        self.by_eng = {e: [] for e in self.ENGS}
        self.lw = {}
        self.rd = {}
        self.dma_groups = {}
        self.group_all = set()
        self.psum_last = {}
        self.arena_names = set()
        self.cap = None
        self.atom = None

    def begin_capture(self):
        self.cap = []

    def begin_atomic(self):
        if self.cap is not None:
            self.atom = []

    def end_atomic(self):
        if self.cap is not None:
            self.cap.append(self.atom)
            self.atom = None

    def end_capture(self):
        c, self.cap = self.cap, None
        return c

    @staticmethod
    def merge(lists):
        lists = [L for L in lists if L]
        idx = [0] * len(lists)
        out = []
        total = sum(len(L) for L in lists)
        nel = total
        done_el = 0
        while done_el < nel:
            done_el += 1
            best, bf = None, None
            for i, L in enumerate(lists):
                if idx[i] < len(L):
                    f = (idx[i] + 0.5) / len(L)
                    if bf is None or f < bf:
                        best, bf = i, f
            el = lists[best][idx[best]]
            idx[best] += 1
            total -= 1
            if isinstance(el, list):
                out.extend(el)
                total += len(el)
            else:
                out.append(el)
                total += 1
        return out

    def replay(self, lists):
        for rec in self.merge(lists):
            self.add(*rec)

    def add(self, eng, fn, reads=(), writes=(), dma=None):
        if self.dry:
            return None
        if self.cap is not None:
            rec = (eng, fn, tuple(reads), tuple(writes), dma)
            if self.atom is not None:
                self.atom.append(rec)
            else:
                self.cap.append(rec)
            return None
        op = Op()
        op.id = len(self.ops)
        op.eng = eng
        op.fn = fn
        op.dma = dma
        op.seq = 0
        op.signal = False
        if self.arena_names:
            for k in tuple(reads) + tuple(writes):
                nm = k[0] if isinstance(k, tuple) else k
                if nm in self.arena_names:
                    reads = tuple(reads) + ("ARENA",)
                    break
        deps = set()
        for k in reads:
            w = self.lw.get(k)
            if w is not None:
                deps.add(w)
        for k in writes:
            w = self.lw.get(k)
            if w is not None:
                deps.add(w)
            for r in self.rd.get(k, ()):
                deps.add(r)
        for k in reads:
            self.rd.setdefault(k, []).append(op.id)
        for k in writes:
            self.lw[k] = op.id
            self.rd[k] = []
        for k in tuple(reads) + tuple(writes):
            if isinstance(k, tuple) and k[0] in ("pf", "pb"):
                last = self.psum_last.setdefault(k, {})
                for eng2, oid in last.items():
                    if eng2 != eng:
                        deps.add(oid)
                last[eng] = op.id
        deps.discard(op.id)
        if eng == "pe" and dma is None:
            deps = {d for d in deps if not (self.ops[d].eng == "pe" and self.ops[d].dma is None)}
        op.deps = deps
        self.ops.append(op)
        self.by_eng[eng].append(op)
        if dma is not None:
            self.dma_groups.setdefault(dma, []).append(op.id)
        return op.id

    def emit(self, es):
        nc = self.nc
        ops = self.ops
        for op in ops:
            for d in op.deps:
                ops[d].signal = True
        for e in self.ENGS:
            c = 0
            for op in self.by_eng[e]:
                if op.dma is None and op.signal:
                    c += 1
                    op.seq = c
        for g, ids in self.dma_groups.items():
            for i, oid in enumerate(ids):
                ops[oid].seq = i + 1
        eng_sem = {e: es.enter_context(nc.semaphore("s_" + e)) for e in self.ENGS}
        dma_sem = {g: es.enter_context(nc.semaphore("d_" + str(g))) for g in self.dma_groups}
        block = es.enter_context(nc.Block())

        def emit_engine(ename, e):
            waited = {}
            for op in self.by_eng[ename]:
                need = {}
                for d in op.deps:
                    dop = ops[d]
                    if dop.dma is not None:
                        sem = dma_sem[dop.dma]
                        if dop.dma in self.group_all:
                            val = 16 * len(self.dma_groups[dop.dma])
                        else:
                            val = 16 * dop.seq
                    else:
                        sem = eng_sem[dop.eng]
                        val = dop.seq
                    key = id(sem)
                    if key not in need or need[key][1] < val:
                        need[key] = (sem, val)
                for key, (sem, val) in need.items():
                    if waited.get(key, 0) < val:
                        e.wait_ge(sem, val)
                        waited[key] = val
                if op.fn is None:
                    continue
                inst = op.fn(e)
                if op.dma is not None:
                    inst.then_inc(dma_sem[op.dma], 16)
                elif op.signal:
                    inst.then_inc(eng_sem[ename], 1)

        @block.tensor
        def _(e):
            emit_engine("pe", e)

        @block.scalar
        def _(e):
            emit_engine("act", e)

        @block.vector
        def _(e):
            emit_engine("dve", e)

        @block.gpsimd
        def _(e):
            emit_engine("pool", e)

        @block.sync
        def _(e):
            emit_engine("sp", e)


class WStream:
    UNIT = 2048

    def __init__(self, P, ring, nunits, lookahead_list=None):
        self.P = P
        self.ring = ring
        self.nu = nunits
        self.future = lookahead_list
        self.requests = []
        self.issued = 0
        self.released = set()
        self.head = 0
        self.occ = [None] * nunits
        self.units = []
        self.prev = []
        if lookahead_list is not None:
            for (src, w) in lookahead_list:
                self._place(w)

    @staticmethod
    def _nelem(w):
        return w[0] * w[1] if isinstance(w, tuple) else 8 * w

    def _place(self, w):
        n = (self._nelem(w) + self.UNIT - 1) // self.UNIT
        if self.head + n > self.nu:
            self.head = 0
        j = len(self.units)
        us = list(range(self.head, self.head + n))
        self.prev.append({self.occ[u] for u in us if self.occ[u] is not None})
        for u in us:
            self.occ[u] = j
        self.units.append((self.head, n))
        self.head = (self.head + n) % self.nu

    def view(self, idx, w):
        u0, n = self.units[idx]
        ne = self._nelem(w)
        v = self.ring[:, u0 * self.UNIT:u0 * self.UNIT + ne]
        if isinstance(w, tuple):
            return v.rearrange("p (a b) -> p a b", a=w[0])
        return v.rearrange("p (a b) -> p a b", a=8)

    def _pump(self):
        if self.P.dry:
            return
        while self.issued < len(self.future):
            j = self.issued
            if not all(pj in self.released for pj in self.prev[j]):
                break
            src, w = self.future[j]
            dst = self.view(j, w)
            self.P.add("pool", lambda e, dst=dst, src=src: e.dma_start(out=dst, in_=src),
                       writes=[("w", j)] + [("w", pj) for pj in self.prev[j]], dma=("wu", self.units[j][0]))
            self.issued += 1

    def get(self, src, w):
        i = len(self.requests)
        self.requests.append((src, w))
        if self.future is None:
            self._place(w)
        if self.P.dry:
            return self.view(i, w), ("w", i), i
        self._pump()
        assert self.issued > i, "weight ring deadlock: release slabs before requesting more"
        return self.view(i, w), ("w", i), i

    def done(self, idx):
        self.released.add(idx)
        self._pump()


class Builder:
    def __init__(self, nc, P, ws_future, n_seq, layers, es):
        self.nc = nc
        self.P = P
        self.n_seq = n_seq
        self.layers = layers
        self.es = es
        self.ws_future = ws_future
        self.alloc()

    def sb(self, name, shape, dt):
        return self.es.enter_context(self.nc.sbuf_tensor(name, shape, dt))

    def carve(self, shape, dt):
        esz = 4 if dt == F32 else 2
        n = 1
        for s_ in shape:
            n *= s_
        nbytes = (n * esz + 3) // 4 * 4
        off = self._aoff
        assert off + nbytes <= self.ARENA_BYTES, "arena overflow"
        self._aoff = off + nbytes
        v = self.arena[:, off // 4:(off + nbytes) // 4]
        if dt != F32:
            v = v.bitcast(dt)
        v = v[:, 0:n]
        if len(shape) == 2:
            v = v.rearrange("p (a b) -> p a b", a=shape[0])
        elif len(shape) == 3:
            v = v.rearrange("p (a b c) -> p a b c", a=shape[0], b=shape[1])
        return v

    def alloc(self):
        nc = self.nc
        n_seq = self.n_seq
        d = {}
        d["x"] = nc.dram_tensor("x", [n_seq, S, D], F32, kind="ExternalInput").ap()
        d["out"] = nc.dram_tensor("out", [n_seq, S, D], F32, kind="ExternalOutput").ap()
        specs = [
            ("mix_norm", [4, D]), ("mlp_norm", [4, D]), ("mlp_w_in", [4, D, DFF]), ("mlp_w_out", [4, DFF, D]),
            ("da_w_in", [2, D, 3072]), ("da_q_norm", [2, 64]), ("da_k_norm", [2, 64]),
            ("da_lambda_q1", [2, 64]), ("da_lambda_k1", [2, 64]), ("da_lambda_q2", [2, 64]), ("da_lambda_k2", [2, 64]),
            ("da_sub_norm", [2, 128]), ("da_w_out", [2, D, D]),
            ("gd_w_in", [2, D, GD_IN]), ("gd_conv_w", [2, 4, 3072]), ("gd_a_log", [2, 8]), ("gd_dt_bias", [2, 8]),
            ("gd_out_norm", [2, 128]), ("gd_w_out", [2, D, D]),
        ]
        for name, shape in specs:
            d[name] = nc.dram_tensor(name, shape, F32, kind="ExternalInput").ap()
        self.d = d
        self.NSLOT = 4
        self.X = self.sb("X", [128, NT, D], F32)
        self.xnT = self.sb("xnT", [128, 8, S], BF16)
        self.U = self.sb("U", [128, 2, S], BF16)
        self.ring = self.sb("ring", [128, self.NSLOT * 4096], BF16)
        self.xs_f = self.sb("xs_f", [128, 2, 512], F32)
        self.xs = self.xs_f[:, :, :].rearrange("p a b -> p (a b)").bitcast(BF16).rearrange("p (a b) -> p a b", a=2)
        self.rtmp = self.xs_f
        self.ss = self.sb("ss", [128, NT], F32)
        self.rstd = self.sb("rstd", [128, NT], F32)
        self.epsc = self.sb("epsc", [128, 4], F32)
        self.identb = self.sb("identb", [128, 128], BF16)
        self.identf = self.sb("identf", [128, 128], F32)
        self.crow = self.xs_f[0:64, 1, 0:128]
        self.gcol = self.sb("gcol", [128, 64], F32)
        self.ARENA_BYTES = 35840 + 24576
        self.arena = self.sb("arena", [128, self.ARENA_BYTES // 4], F32)
        self._aoff = 0
        self.hT = self.carve([8, S], BF16)
        self._aoff = 0
        self.qT = self.carve([2, S], BF16)
        self.kTz = self.carve([2, 2, S], BF16)
        self.vaug = self.carve([NT, 2, 130], BF16)
        self.pT = self.carve([3, 512], BF16)
        self.qraw = self.carve([2, 512], BF16)
        self.qsq = self.carve([2, 512], BF16)
        self.qrs = self.carve([1, 512], F32)
        self.accS = self.carve([4, 512], F32)
        self.osb = self.carve([2, 4, 128], F32)
        self.obf = self.carve([2, 4, 128], BF16)
        self.fsm = self.carve([2, 24], F32)
        self.da_arena_end = self._aoff
        self._aoff = 0
        g = {}
        g["BA"] = self.carve([NT, 16], F32)
        for nm in ("BETA", "GC", "EG", "EGL", "EKD"):
            g[nm] = self.carve([NT, 8], F32)
        g["GLB"] = g["BA"][:, :, :].rearrange("p a b -> p (a b)")[:, 0:NT * 8].rearrange("p (a b) -> p a b", a=NT)
        g["raw"] = self.carve([2, 516], F32)
        g["acc"] = self.carve([1, 512], F32)
        g["halo"] = self.carve([6, 4], F32)
        g["sT"] = [self.carve([2, 4, 3 * 128], BF16) for _ in range(2)]
        g["sq"] = self.carve([2, 512], BF16)
        g["zs"] = [self.carve([4, 256], BF16) for _ in range(2)]
        g["R"] = [self.carve([4, 4], F32) for _ in range(2)]
        g["SC"] = [self.carve([8, 8], F32) for _ in range(2)]
        g["Sf"] = self.carve([2, 128], F32)
        g["Sb"] = self.carve([2, 128], BF16)
        self.NCH = 4
        for c in range(self.NCH):
            g["kd", c] = self.carve([128], BF16)
            g["kbg", c] = self.carve([128], BF16)
            g["vb", c] = self.carve([128], BF16)
            g["E", c] = self.carve([256], F32)
            g["MA", c] = self.carve([256], BF16)
            g["Lb", c] = self.carve([128], BF16)
            g["QQ", c] = self.carve([256], BF16)
            g["TM", c] = self.carve([256], BF16)
            g["DD", c] = self.carve([2, 256], BF16)
            g["u", c] = self.carve([128], F32)
            g["wT", c] = self.carve([128], BF16)
            g["vn", c] = self.carve([128], BF16)
            g["o", c] = self.carve([128], F32)
            g["og", c] = self.carve([128], BF16)
            g["ps", c] = self.carve([8], F32)
        self.g = g
        self.gd_arena_end = self._aoff
        assert max(self.da_arena_end, self.gd_arena_end) <= self.ARENA_BYTES
        self.convw = self.sb("convw", [128, 2, 96], F32)
        self.gdc = self.sb("gdc", [128, 2, 4], F32)
        self.gsm = self.sb("gsm", [128, 2, 16], F32)
        self.NEGM = self.sb("NEGM", [128, 7, 256], BF16)
        self.trif = self.sb("trif", [128, 128], F32)
        self.sellast = self.sb("sellast", [128, 128], F32)
        self.onesf = self.sb("onesf", [128, 128], F32)
        self.masks = self.sb("masks", [128, 256], F32)
        self.onecol = self.sb("onecol", [128, 2], BF16)
        self.onesbd = self.sb("onesbd", [128, 128], BF16)
        self.negmask = self.sb("negmask", [128, 128], BF16)
        self.cda = self.sb("cda", [128, 16], F32)
        self.lamt = self.xs_f[:, 0, 256:512].rearrange("p (a b) -> p a b", a=4)
        self.lamp = self.sb("lamp", [128, 4], F32)
        self.scr_f = self.xs_f[:, 0, 0:128]
        self.scr_f2 = self.xs_f[:, 0, 128:256]
        self.PF = [self.es.enter_context(nc.psum_tensor("pf%d" % i, [128, 512], F32)) for i in range(6)]
        self.PB = [self.es.enter_context(nc.psum_tensor("pb%d" % i, [128, 1024], BF16)) for i in range(2)]
        self.ws = WStream(self.P, self.ring, self.NSLOT * 2, self.ws_future)
        self.dummy = self.sb("abar", [128, 2], F32)
        self.ARENA_NAMES = {"hT", "qT", "kT", "kTz", "accS", "va", "pT", "qraw", "qsq", "qrs", "osb", "obf", "fsm", "fss", "frs", "vones",
                            "BA", "BETA", "GC", "EG", "EGL", "EKD", "raw", "acc", "halo", "sT", "sq", "zs", "R", "SC", "Sf", "Sb",
                            "kd", "kbg", "vb", "E", "MA", "Lb", "QQ", "TM", "DD", "u", "wT", "vn", "o", "og", "psm"}

    def arena_barrier(self):
        self.P.arena_names = self.ARENA_NAMES
        dummy = self.dummy
        self.P.add("pool", lambda e: e.memset(dummy[:], 0.0), writes=["ARENA"])

    def setup_consts(self):
        P = self.P
        nc = self.nc
        identf, identb = self.identf, self.identb
        P.add("pool", lambda e: e.memset(identf[:], 0.0), writes=["identf"])
        P.add("pool", lambda e: e.memset(self.epsc[:], EPS), writes=["epsc"])
        P.add("pool", lambda e: e.affine_select(out=identf[:], in_=identf[:], pattern=[[-1, 128]],
                                                compare_op=ALU.not_equal, fill=1.0, base=0,
                                                channel_multiplier=1),
              reads=["identf"], writes=["identf"])
        P.add("dve", lambda e: e.tensor_copy(out=identb[:], in_=identf[:]), reads=["identf"], writes=["identb"])
        crow, gcol = self.crow, self.gcol
        mixr = self.d["mix_norm"].rearrange("l (kc p) -> (l kc) p", p=128)
        mlpr = self.d["mlp_norm"].rearrange("l (kc p) -> (l kc) p", p=128)
        P.add("sp", lambda e: e.dma_start(out=crow[0:32, :], in_=mixr), writes=[("xs", 1)], dma="c0")
        P.add("sp", lambda e: e.dma_start(out=crow[32:64, :], in_=mlpr), writes=[("xs", 1)], dma="c1")
        pf = self.PF[0]
        P.add("pe", lambda e: e.transpose(out=pf[:, 0:64], in_=crow[0:64, :], identity=identf[0:64, 0:64]),
              reads=[("xs", 1), "identf"], writes=[("pf", 0)])
        P.add("dve", lambda e: e.tensor_copy(out=gcol[:, :], in_=pf[:, 0:64]), reads=[("pf", 0)], writes=["gcol"])

    def setup_da_consts(self, j, l):
        P = self.P
        d = self.d
        cda, lamt, lamp = self.cda, self.lamt, self.lamp
        c0 = 5 * j
        col = lambda ap: ap.rearrange("(p o) -> p o", o=1)
        for half in range(2):
            P.add("sp", lambda e, half=half: e.dma_start(out=cda[64 * half:64 * half + 64, c0:c0 + 1], in_=col(d["da_q_norm"][j])),
                  writes=[("cda", j)], dma="cq%d%d" % (j, half))
            P.add("sp", lambda e, half=half: e.dma_start(out=cda[64 * half:64 * half + 64, c0 + 1:c0 + 2], in_=col(d["da_k_norm"][j])),
                  writes=[("cda", j)], dma="ck%d%d" % (j, half))
        P.add("sp", lambda e: e.dma_start(out=cda[:, c0 + 4:c0 + 5], in_=col(d["da_sub_norm"][j])),
              writes=[("cda", j)], dma="cs%d" % j)
        for i, nm in enumerate(["da_lambda_q1", "da_lambda_k1", "da_lambda_q2", "da_lambda_k2"]):
            P.add("sp", lambda e, i=i, nm=nm: e.dma_start(out=lamt[:, i, :], in_=d[nm][j:j + 1, :].partition_broadcast(128)),
                  writes=[("xs", 0)], dma="cl%d%d" % (j, i))
        lam_init = 0.8 - 0.6 * math.exp(-0.3 * l)
        for i in range(2):
            P.add("dve", lambda e, i=i: e.tensor_tensor(out=lamt[:, 2 * i, :], in0=lamt[:, 2 * i, :], in1=lamt[:, 2 * i + 1, :], op=ALU.mult),
                  reads=[("xs", 0)], writes=[("xs", 0)])
            P.add("dve", lambda e, i=i: e.reduce_sum(out=lamp[:, i:i + 1], in_=lamt[:, 2 * i, :], axis=AX.X),
                  reads=[("xs", 0)], writes=["lamp"])
        P.add("act", lambda e: e.activation(out=lamp[:, 0:2], in_=lamp[:, 0:2], func=AF.Exp), reads=["lamp"], writes=["lamp"])
        P.add("dve", lambda e: e.tensor_tensor(out=cda[:, c0 + 2:c0 + 3], in0=lamp[:, 0:1], in1=lamp[:, 1:2], op=ALU.subtract),
              reads=["lamp"], writes=[("cda", j)])
        P.add("dve", lambda e: e.tensor_scalar(out=cda[:, c0 + 2:c0 + 3], in0=cda[:, c0 + 2:c0 + 3], scalar1=lam_init, scalar2=None, op0=ALU.add),
              reads=[("cda", j)], writes=[("cda", j)])
        P.add("dve", lambda e: e.tensor_scalar(out=cda[:, c0 + 3:c0 + 4], in0=cda[:, c0 + 2:c0 + 3], scalar1=-1.0, scalar2=None, op0=ALU.mult),
              reads=[("cda", j)], writes=[("cda", j)])
        P.add("dve", lambda e: e.tensor_scalar(out=cda[:, c0:c0 + 1], in0=cda[:, c0:c0 + 1], scalar1=0.125, scalar2=None, op0=ALU.mult),
              reads=[("cda", j)], writes=[("cda", j)])
        P.add("dve", lambda e: e.tensor_scalar(out=cda[:, c0 + 4:c0 + 5], in0=cda[:, c0 + 4:c0 + 5], scalar1=1.0 - lam_init, scalar2=None, op0=ALU.mult),
              reads=[("cda", j)], writes=[("cda", j)])

    def setup_da_static(self):
        P = self.P
        onesbd, negmask, vaug = self.onesbd, self.negmask, self.vaug
        scr = self.scr_f
        P.add("pool", lambda e: e.memset(scr[:], 0.0), writes=[("xs", 0)])
        P.add("pool", lambda e: e.memset(scr[0:64, 0:64], 1.0 / 64), reads=[("xs", 0)], writes=[("xs", 0)])
        P.add("pool", lambda e: e.memset(scr[64:128, 64:128], 1.0 / 64), reads=[("xs", 0)], writes=[("xs", 0)])
        P.add("dve", lambda e: e.tensor_copy(out=onesbd[:], in_=scr[:]), reads=[("xs", 0)], writes=["onesbd"])
        scr2 = self.scr_f2
        P.add("pool", lambda e: e.memset(scr2[:], 0.0), writes=[("xs", 0)])
        P.add("pool", lambda e: e.affine_select(out=scr2[:], in_=scr2[:], pattern=[[1, 128]], compare_op=ALU.is_ge,
                                                fill=-30000.0, base=0, channel_multiplier=-1),
              reads=[("xs", 0)], writes=[("xs", 0)])
        P.add("dve", lambda e: e.tensor_copy(out=negmask[:], in_=scr2[:]), reads=[("xs", 0)], writes=["negmask"])

    def diffattn(self, l):
        P = self.P
        j = l // 2
        X, xnT, U, ring = self.X, self.xnT, self.U, self.ring
        qT, kTz, vaug, pT, accS = self.qT, self.kTz, self.vaug, self.pT, self.accS
        qraw, qsq, qrs = self.qraw, self.qsq, self.qrs
        cda, onesbd, negmask, identb = self.cda, self.onesbd, self.negmask, self.identb
        osb, obf, fsm = self.osb, self.obf, self.fsm
        PF, PB = self.PF, self.PB
        c0 = 5 * j
        win = self.d["da_w_in"][j].rearrange("(kc p) f -> p kc f", p=128)
        self.rmsnorm(8 * l)
        self.arena_barrier()
        P.add("pool", lambda e: e.memset(vaug[:, :, :, 128:130], 1.0), reads=[("xnT", 0, 0)], writes=["vones"])
        P.add("pool", lambda e: e.memset(kTz[64:128, 0, :, :], 0.0), reads=[("xnT", 0, 0)], writes=["kTz"])
        P.add("pool", lambda e: e.memset(kTz[0:64, 1, :, :], 0.0), reads=[("xnT", 0, 0)], writes=["kTz"])
        cnt = {"pf": 0, "nb": 0, "pt": 0, "fin": 0}
        def record_proj(hp):
            sq = self.ws.get(win[:, :, hp * 256:(hp + 1) * 256], 256)
            sk = self.ws.get(win[:, :, 1024 + hp * 256:1024 + (hp + 1) * 256], 256)
            sv = self.ws.get(win[:, :, 2048 + hp * 256:2048 + (hp + 1) * 256], 256)
            jobs = [(which, slab, gcolq, hh, tt) for which, slab, gcolq in (("q", sq, c0), ("k", sk, c0 + 1))
                    for hh in range(2) for tt in range(4)]
            jstate = {}

            def emit_proj(i):
                which, slab, gcolq, hh, tt = jobs[i]
                slot, wkey, _ = slab
                bank = cnt["pf"] % 2
                cnt["pf"] += 1
                nb = cnt["nb"] % 2
                cnt["nb"] += 1
                jstate[i] = (bank, nb)
                pf = PF[bank]
                for kc in range(8):
                    P.add("pe", lambda e, pf=pf, slot=slot, kc=kc, hh=hh, tt=tt: e.matmul(
                        pf[:, :], slot[:, kc, hh * 128:(hh + 1) * 128], xnT[:, kc, tt * 512:(tt + 1) * 512],
                        start=(kc == 0), stop=(kc == 7)),
                        reads=[wkey] + [("xnT", t, kc) for t in range(4 * tt, 4 * tt + 4)], writes=[("pf", bank)])

            def emit_norm(i):
                which, slab, gcolq, hh, tt = jobs[i]
                bank, nb = jstate[i]
                pf = PF[bank]
                P.add("act", lambda e, pf=pf, nb=nb: e.activation(out=qraw[:, nb, :], in_=pf[:, :], func=AF.Copy),
                      reads=[("pf", bank)], writes=[("qraw", nb)])
                P.add("dve", lambda e, nb=nb: e.tensor_tensor(out=qsq[:, nb, :], in0=qraw[:, nb, :], in1=qraw[:, nb, :], op=ALU.mult),
                      reads=[("qraw", nb)], writes=[("qsq", nb)])
                mbank = 2 + nb
                pm = PF[mbank]
                P.add("pe", lambda e, pm=pm, nb=nb: e.matmul(pm[:, :], onesbd[:, :], qsq[:, nb, :], start=True, stop=True),
                      reads=[("qsq", nb), "onesbd"], writes=[("pf", mbank)])
                P.add("act", lambda e, pm=pm, nb=nb: e.activation(out=qrs[:, 0, :], in_=pm[:, :], func=AF.Ln, bias=self.epsc[:, 0:1], scale=1.0),
                      reads=[("pf", mbank), "epsc"], writes=[("qrs", 0)])
                P.add("act", lambda e, nb=nb: e.activation(out=qrs[:, 0, :], in_=qrs[:, 0, :], func=AF.Exp, scale=-0.5),
                      reads=[("qrs", 0)], writes=[("qrs", 0)])
                if which == "q":
                    P.add("dve", lambda e, nb=nb, hh=hh, tt=tt, gcolq=gcolq: e.scalar_tensor_tensor(
                        out=qT[:, hh, tt * 512:(tt + 1) * 512], in0=qraw[:, nb, :], scalar=cda[:, gcolq:gcolq + 1],
                        in1=qrs[:, 0, :], op0=ALU.mult, op1=ALU.mult),
                        reads=[("qraw", nb), ("qrs", 0), ("cda", j)], writes=[("qT", hh, tt)])
                else:
                    for c in range(2):
                        pl, ph = 64 * c, 64 * c + 64
                        P.add("dve", lambda e, nb=nb, hh=hh, tt=tt, gcolq=gcolq, c=c, pl=pl, ph=ph: e.scalar_tensor_tensor(
                            out=kTz[pl:ph, c, hh, tt * 512:(tt + 1) * 512], in0=qraw[pl:ph, nb, :], scalar=cda[pl:ph, gcolq:gcolq + 1],
                            in1=qrs[pl:ph, 0, :], op0=ALU.mult, op1=ALU.mult),
                            reads=[("qraw", nb), ("qrs", 0), ("cda", j), "kTz"], writes=[("kT", hh, tt, c)])

            emit_proj(0)
            for i in range(len(jobs)):
                if i + 1 < len(jobs):
                    emit_proj(i + 1)
                emit_norm(i)
            slot, wkey, _ = sv
            for t in range(NT):
                bank = cnt["pf"] % 2
                cnt["pf"] += 1
                pf = PF[bank]
                for kc in range(8):
                    P.add("pe", lambda e, pf=pf, slot=slot, kc=kc, t=t: e.matmul(
                        pf[:, 0:256], xnT[:, kc, t * 128:(t + 1) * 128], slot[:, kc, 0:256],
                        start=(kc == 0), stop=(kc == 7)),
                        reads=[wkey, ("xnT", t, kc)], writes=[("pf", bank)])
                P.add("act", lambda e, pf=pf, t=t: e.activation(
                    out=vaug[:, t, :, 0:128], in_=pf[:, 0:256].rearrange("p (h d) -> p h d", h=2), func=AF.Copy),
                    reads=[("pf", bank), "vones"], writes=[("va", t)])
            return sq, sk, sv

        P.begin_capture()
        nxt = record_proj(0)
        P.replay([P.end_capture()])
        for sl in nxt:
            self.ws.done(sl[2])
        for hp in range(4):
            ATT, FIN = [], []
            for hh in range(2):
                h = 2 * hp + hh
                for qt in range(4):
                    P.begin_capture()
                    first_in_bank = {}
                    units = [(c, kb) for c in range(2) for kb in range(4 * qt + 4)]
                    ubank = {}

                    def emit_scores(u):
                        c, kb = units[u]
                        r = kb - 4 * qt
                        col0 = max(r, 0) * 128
                        bank = cnt["pf"] % 2
                        cnt["pf"] += 1
                        ubank[u] = bank
                        ps = PF[bank]
                        krd = [("kT", hh, kb // 4, c)]
                        qrd = [("qT", hh, qt)]
                        if r >= 0:
                            P.add("pe", lambda e, ps=ps, c=c, hh=hh, kb=kb, qt=qt, col0=col0: e.matmul(
                                ps[:, col0:col0 + 128], kTz[:, c, hh, kb * 128:(kb + 1) * 128],
                                qT[:, hh, qt * 512 + col0:qt * 512 + col0 + 128], start=True, stop=False,
                                skip_group_check=True),
                                reads=krd + qrd, writes=[("pf", bank)])
                            P.add("pe", lambda e, ps=ps, col0=col0: e.matmul(
                                ps[:, col0:col0 + 128], identb[:, :], negmask[:, :], start=False, stop=True,
                                skip_group_check=True),
                                reads=["identb", "negmask"], writes=[("pf", bank)])
                            if col0 + 128 < 512:
                                P.add("pe", lambda e, ps=ps, c=c, hh=hh, kb=kb, qt=qt, col0=col0: e.matmul(
                                    ps[:, col0 + 128:512], kTz[:, c, hh, kb * 128:(kb + 1) * 128],
                                    qT[:, hh, qt * 512 + col0 + 128:qt * 512 + 512], start=True, stop=True,
                                    skip_group_check=True),
                                    reads=krd + qrd, writes=[("pf", bank)])
                        else:
                            P.add("pe", lambda e, ps=ps, c=c, hh=hh, kb=kb, qt=qt: e.matmul(
                                ps[:, :], kTz[:, c, hh, kb * 128:(kb + 1) * 128],
                                qT[:, hh, qt * 512:qt * 512 + 512], start=True, stop=True, skip_group_check=True),
                                reads=krd + qrd, writes=[("pf", bank)])

                    def emit_exp_pv(u):
                        c, kb = units[u]
                        r = kb - 4 * qt
                        col0 = max(r, 0) * 128
                        bank = ubank[u]
                        ps = PF[bank]
                        pi = cnt["pt"] % 3
                        cnt["pt"] += 1
                        P.add("act", lambda e, ps=ps, pi=pi, col0=col0: e.activation(
                            out=pT[:, pi, col0:512], in_=ps[:, col0:512], func=AF.Exp),
                            reads=[("pf", bank)], writes=[("pT", pi)])
                        for rr in range(max(r, 0), 4):
                            abank = 2 + 2 * c + rr // 2
                            off = (rr % 2) * 256
                            st = abank not in first_in_bank
                            first_in_bank[abank] = True
                            P.add("pe", lambda e, abank=abank, off=off, pi=pi, rr=rr, kb=kb, st=st, hh=hh, qt=qt: e.matmul(
                                PF[abank][:, off:off + 129], pT[:, pi, rr * 128:(rr + 1) * 128], vaug[:, kb, hh, 0:129],
                                start=st, stop=(kb == 4 * qt + rr), skip_group_check=True),
                                reads=[("pT", pi), ("va", kb), "vones"], writes=[("pf", abank)])

                    emit_scores(0)
                    for u in range(len(units)):
                        if u + 1 < len(units):
                            emit_scores(u + 1)
                        emit_exp_pv(u)
                    for b in range(4):
                        srcv = PF[2 + b][:, :].rearrange("p (s w) -> p s w", s=2)[:, :, 0:129]
                        dstv = accS[:, b, :].rearrange("p (s w) -> p s w", s=2)[:, :, 0:129]
                        if b % 2 == 0:
                            P.add("act", lambda e, srcv=srcv, dstv=dstv: e.activation(out=dstv, in_=srcv, func=AF.Copy),
                                  reads=[("pf", 2 + b)], writes=[("accS", b)])
                        else:
                            P.add("dve", lambda e, srcv=srcv, dstv=dstv: e.tensor_copy(out=dstv, in_=srcv),
                                  reads=[("pf", 2 + b)], writes=[("accS", b)])
                    ATT.append(P.end_capture())
                    P.begin_capture()
                    fi = cnt["fin"] % 2
                    cnt["fin"] += 1
                    AK = [("accS", b) for b in range(4)]
                    FK = ("fsm", fi)
                    slots = accS[:, :, :].rearrange("p b (s w) -> p (b s) w", s=2)
                    P.add("dve", lambda e, fi=fi, slots=slots: e.reciprocal(out=fsm[:, fi, 0:8], in_=slots[:, :, 128]),
                          reads=AK, writes=[FK])
                    P.add("dve", lambda e, fi=fi: e.tensor_scalar(out=fsm[:, fi, 8:12], in0=fsm[:, fi, 4:8], scalar1=cda[:, c0 + 3:c0 + 4], scalar2=None, op0=ALU.mult),
                          reads=[FK, ("cda", j)], writes=[FK])
                    for rr in range(4):
                        P.add("act", lambda e, fi=fi, rr=rr, slots=slots: e.activation(out=osb[:, fi, rr, :], in_=slots[:, rr, 0:128], func=AF.Copy,
                                                                                        scale=fsm[:, fi, rr:rr + 1]),
                              reads=AK + [FK], writes=[("osb", fi, rr)])
                    for rr in range(4):
                        P.add("dve", lambda e, fi=fi, rr=rr, slots=slots: e.scalar_tensor_tensor(
                            out=osb[:, fi, rr, :], in0=slots[:, 4 + rr, 0:128], scalar=fsm[:, fi, 8 + rr:9 + rr], in1=osb[:, fi, rr, :],
                            op0=ALU.mult, op1=ALU.add),
                            reads=AK + [FK, ("osb", fi, rr)], writes=[("osb", fi, rr)])
                    for rr in range(4):
                        P.add("act", lambda e, fi=fi, rr=rr: e.activation(out=obf[:, fi, rr, :], in_=osb[:, fi, rr, :], func=AF.Square,
                                                                          accum_out=fsm[:, fi, 12 + rr:13 + rr]),
                              reads=[("osb", fi, rr)], writes=[("obf", fi, rr), ("fss", fi, rr)])
                    P.add("act", lambda e, fi=fi: e.activation(out=fsm[:, fi, 16:20], in_=fsm[:, fi, 12:16], func=AF.Ln,
                                                               bias=self.epsc[:, 0:1], scale=1.0 / 128),
                          reads=[("fss", fi, rr) for rr in range(4)] + ["epsc"], writes=[("frs", fi)])
                    P.add("act", lambda e, fi=fi: e.activation(out=fsm[:, fi, 16:20], in_=fsm[:, fi, 16:20], func=AF.Exp, scale=-0.5),
                          reads=[("frs", fi)], writes=[("frs", fi)])
                    for rr in range(4):
                        P.add("dve", lambda e, fi=fi, rr=rr: e.tensor_scalar(out=obf[:, fi, rr, :], in0=osb[:, fi, rr, :], scalar1=fsm[:, fi, 16 + rr:17 + rr],
                                                                              scalar2=None, op0=ALU.mult),
                              reads=[("osb", fi, rr), ("frs", fi), ("obf", fi, rr)], writes=[("obf", fi, rr)])
                    pb = PB[fi]
                    for rr in range(4):
                        P.add("pe", lambda e, pb=pb, fi=fi, rr=rr: e.transpose(out=pb[:, rr * 128:(rr + 1) * 128], in_=obf[:, fi, rr, :], identity=identb[:]),
                              reads=[("obf", fi, rr), "identb"], writes=[("pb", fi)])
                    P.add("dve", lambda e, pb=pb, hh=hh, qt=qt: e.tensor_scalar(
                        out=U[:, hh, qt * 512:(qt + 1) * 512], in0=pb[:, 0:512], scalar1=cda[:, c0 + 4:c0 + 5], scalar2=None, op0=ALU.mult),
                        reads=[("pb", fi), ("cda", j)], writes=[("U", hh, qt)])
                    FIN.append(P.end_capture())
            P.replay([ATT[0]])
            for i in range(1, 8):
                P.replay([ATT[i], FIN[i - 1]])
            wsrc = self.d["da_w_out"][j][hp * 256:(hp + 1) * 256, :].rearrange("(kc p) f -> p kc f", p=128)
            so = self.ws.get(wsrc, (2, 1024))
            wv = so[0]
            P.begin_capture()
            for t in range(NT):
                for dh in range(2):
                    bank = 4 + (cnt["pf"] % 2)
                    cnt["pf"] += 1
                    pf = PF[bank]
                    for hc in range(2):
                        P.add("pe", lambda e, pf=pf, wv=wv, hc=hc, t=t, dh=dh: e.matmul(
                            pf[:, :], U[:, hc, t * 128:(t + 1) * 128], wv[:, hc, dh * 512:(dh + 1) * 512], start=(hc == 0), stop=(hc == 1)),
                            reads=[so[1], ("U", hc, t // 4)], writes=[("pf", bank)])
                    P.add("dve", lambda e, pf=pf, t=t, dh=dh: e.tensor_tensor(
                        out=X[:, t, dh * 512:(dh + 1) * 512], in0=X[:, t, dh * 512:(dh + 1) * 512], in1=pf[:, :], op=ALU.add),
                        reads=[("pf", bank), ("x", t)], writes=[("x", t)])
            tail = FIN[7] + P.end_capture()
            if hp < 3:
                P.begin_capture()
                nxt = record_proj(hp + 1)
                P.replay([tail, P.end_capture()])
                for sl in nxt:
                    self.ws.done(sl[2])
            else:
                P.replay([tail])
            self.ws.done(so[2])

    def setup_gd_static(self):
        P = self.P
        trif, sellast, onesf, masks, onecol = self.trif, self.sellast, self.onesf, self.masks, self.onecol
        P.add("pool", lambda e: e.memset(onesf[:], 1.0), writes=["onesf"])
        P.add("pool", lambda e: e.memset(onecol[:], 1.0), writes=["onecol"])
        P.add("pool", lambda e: e.memset(trif[:], 1.0), writes=["trif"])
        P.add("pool", lambda e: e.affine_select(out=trif[:], in_=trif[:], pattern=[[1, 128]], compare_op=ALU.is_ge,
                                                fill=0.0, base=0, channel_multiplier=-1), reads=["trif"], writes=["trif"])
        P.add("pool", lambda e: e.memset(sellast[:], 1.0), writes=["sellast"])
        P.add("pool", lambda e: e.affine_select(out=sellast[:], in_=sellast[:], pattern=[[0, 128]], compare_op=ALU.is_ge,
                                                fill=0.0, base=-127, channel_multiplier=1), reads=["sellast"], writes=["sellast"])
        P.add("pool", lambda e: e.memset(masks[:], 0.0), writes=["masks"])
        P.add("pool", lambda e: e.affine_select(out=masks[:, 0:128], in_=masks[:, 0:128], pattern=[[1, 128]], compare_op=ALU.is_ge,
                                                fill=-30000.0, base=-1, channel_multiplier=-1), reads=["masks"], writes=["masks"])
        P.add("pool", lambda e: e.affine_select(out=masks[:, 128:256], in_=masks[:, 128:256], pattern=[[1, 128]], compare_op=ALU.is_ge,
                                                fill=-30000.0, base=0, channel_multiplier=-1), reads=["masks"], writes=["masks"])

    def setup_gd_levelmasks(self):
        P = self.P
        NEGM = self.NEGM
        scrA = lambda nb: self.xs_f[0:nb, 0, 0:128]
        scrC = lambda nb: self.xs_f[0:nb, 0, 128:256]
        K0 = [("xs", 0)]
        for k in range(7):
            B, half = 2 ** (k + 1), 2 ** k
            nb = 128 // B
            A, C = scrA(nb), scrC(nb)
            P.add("pool", lambda e, nb=nb: e.memset(self.xs_f[0:nb, 0, 0:256], 1.0), writes=K0)
            P.add("pool", lambda e, A=A, B=B, half=half: e.affine_select(out=A, in_=A, pattern=[[1, 128]], compare_op=ALU.is_ge, fill=0.0,
                                                                       base=-half, channel_multiplier=-B), reads=K0, writes=K0)
            P.add("pool", lambda e, A=A, B=B: e.affine_select(out=A, in_=A, pattern=[[-1, 128]], compare_op=ALU.is_ge, fill=0.0,
                                                             base=B - 1, channel_multiplier=B), reads=K0, writes=K0)
            P.add("pool", lambda e, C=C, B=B: e.affine_select(out=C, in_=C, pattern=[[1, 128]], compare_op=ALU.is_ge, fill=0.0,
                                                             base=0, channel_multiplier=-B), reads=K0, writes=K0)
            P.add("pool", lambda e, C=C, B=B, half=half: e.affine_select(out=C, in_=C, pattern=[[-1, 128]], compare_op=ALU.is_ge, fill=0.0,
                                                                       base=half - 1, channel_multiplier=B), reads=K0, writes=K0)
            pf = self.PF[1]
            P.add("pe", lambda e, pf=pf, A=A, C=C: e.matmul(pf[:, 0:128], A, C, start=True, stop=True, skip_group_check=True),
                  reads=K0, writes=[("pf", 1)])
            P.add("pe", lambda e, pf=pf, A=A, C=C: e.matmul(pf[:, 128:256], C, A, start=True, stop=True, skip_group_check=True),
                  reads=K0, writes=[("pf", 1)])
            P.add("act", lambda e, pf=pf, k=k: e.activation(out=NEGM[:, k, :], in_=pf[:, 0:256], func=AF.Copy),
                  reads=[("pf", 1)], writes=["NEGM"])

    def setup_gd_consts(self, j):
        P = self.P
        d = self.d
        convw, gdc, identf = self.convw, self.gdc, self.identf
        crow96 = self.xs_f[0:96, 1, 0:128]
        rows = d["gd_conv_w"][j].rearrange("k (c p) -> (k c) p", p=128)
        P.add("sp", lambda e: e.dma_start(out=crow96, in_=rows), writes=[("xs", 1)], dma="gcw%d" % j)
        pf = self.PF[0]
        P.add("pe", lambda e: e.transpose(out=pf[:, 0:96], in_=crow96, identity=identf[0:96, 0:96]),
              reads=[("xs", 1), "identf"], writes=[("pf", 0)])
        P.add("dve", lambda e: e.tensor_copy(out=convw[:, j, :], in_=pf[:, 0:96]), reads=[("pf", 0)], writes=[("convw", j)])
        col = lambda ap: ap.rearrange("(p o) -> p o", o=1)
        P.add("sp", lambda e: e.dma_start(out=gdc[:, j, 0:1], in_=col(d["gd_out_norm"][j])), writes=[("gdc", j)], dma="gon%d" % j)
        gsm = self.gsm
        P.add("sp", lambda e: e.dma_start(out=gsm[:, j, 0:8], in_=d["gd_a_log"][j:j + 1, :].partition_broadcast(128)),
              writes=[("gsm", j)], dma="gal%d" % j)
        P.add("sp", lambda e: e.dma_start(out=gsm[:, j, 8:16], in_=d["gd_dt_bias"][j:j + 1, :].partition_broadcast(128)),
              writes=[("gsm", j)], dma="gdt%d" % j)
        P.add("act", lambda e: e.activation(out=gsm[:, j, 0:8], in_=gsm[:, j, 0:8], func=AF.Exp), reads=[("gsm", j)], writes=[("gsm", j)])
        P.add("dve", lambda e: e.tensor_scalar(out=gsm[:, j, 0:8], in0=gsm[:, j, 0:8], scalar1=-1.0, scalar2=None, op0=ALU.mult),
              reads=[("gsm", j)], writes=[("gsm", j)])

    def gdn(self, l):
        P = self.P
        j = l // 2
        g = self.g
        X, xnT, U, ring = self.X, self.xnT, self.U, self.ring
        identb, identf = self.identb, self.identf
        PF, PB = self.PF, self.PB
        epsc = self.epsc
        DKS = 128.0 ** -0.5
        win = self.d["gd_w_in"][j].rearrange("(kc p) f -> p kc f", p=128)
        self.rmsnorm(8 * l)
        self.arena_barrier()
        XN0 = [("xnT", 0, 0)]
        cnt = {"pf": 0, "rb": 0, "sq": 0}
        BA, BETA, GC, EG, GLB, EGL, EKD = (g[k] for k in ("BA", "BETA", "GC", "EG", "GLB", "EGL", "EKD"))
        flat = lambda v: v.rearrange("p a b -> p (a b)")

        sba = self.ws.get(win[:, :, 4096:4112], 16)
        slot_ba, wkey_ba, _ = sba
        pf_ba = PF[0]
        for t in range(NT):
            for kc in range(8):
                P.add("pe", lambda e, t=t, kc=kc: e.matmul(pf_ba[:, t * 16:(t + 1) * 16], xnT[:, kc, t * 128:(t + 1) * 128],
                                                          slot_ba[:, kc, 0:16], start=(kc == 0), stop=(kc == 7),
                                                          skip_group_check=True),
                      reads=[wkey_ba, ("xnT", t, kc)], writes=[("pf", 0)])
        self.ws.done(sba[2])
        P.add("dve", lambda e: e.tensor_copy(out=flat(BA), in_=pf_ba[:, 0:256]), reads=[("pf", 0)] + XN0, writes=["BA"])
        P.add("act", lambda e: e.activation(out=BETA, in_=BA[:, :, 0:8], func=AF.Exp, scale=-1.0), reads=["BA"] + XN0, writes=["BETA"])
        P.add("act", lambda e: e.activation(out=BETA, in_=BETA, func=AF.Ln, bias=1.0), reads=["BETA"], writes=["BETA"])
        P.add("act", lambda e: e.activation(out=BETA, in_=BETA, func=AF.Exp, scale=-1.0), reads=["BETA"], writes=["BETA"])
        gsm = self.gsm
        for t in range(NT):
            P.add("dve", lambda e, t=t: e.tensor_tensor(out=GC[:, t, :], in0=BA[:, t, 8:16], in1=gsm[:, j, 8:16], op=ALU.add),
                  reads=["BA", ("gsm", j)] + XN0, writes=["GC"])
        P.add("act", lambda e: e.activation(out=GC, in_=GC, func=AF.Exp), reads=["GC"], writes=["GC"])
        P.add("act", lambda e: e.activation(out=GC, in_=GC, func=AF.Ln, bias=1.0), reads=["GC"], writes=["GC"])
        for t in range(NT):
            P.add("dve", lambda e, t=t: e.tensor_tensor(out=GLB[:, t, :], in0=GC[:, t, :], in1=gsm[:, j, 0:8], op=ALU.mult),
                  reads=["GC", "BETA", ("gsm", j)] + XN0, writes=["BA"])
        pf1 = PF[1]
        for t in range(NT):
            P.add("pe", lambda e, t=t: e.matmul(pf1[:, t * 8:(t + 1) * 8], self.trif[:, :], GLB[:, t, :], start=True, stop=True,
                                                skip_group_check=True),
                  reads=["BA", "trif"], writes=[("pf", 1)])
        P.add("dve", lambda e: e.tensor_copy(out=flat(GC), in_=pf1[:, 0:128]), reads=[("pf", 1)], writes=["GC"])
        P.add("act", lambda e: e.activation(out=EG, in_=GC, func=AF.Exp), reads=["GC"] + XN0, writes=["EG"])
        for t in range(NT):
            P.add("pe", lambda e, t=t: e.matmul(pf1[:, 128 + t * 8:128 + (t + 1) * 8], self.sellast[:, :], GC[:, t, :], start=True, stop=True,
                                                skip_group_check=True),
                  reads=["GC", "sellast"], writes=[("pf", 1)])
        P.add("dve", lambda e: e.tensor_copy(out=flat(GLB), in_=pf1[:, 128:256]), reads=[("pf", 1)], writes=["BA"])
        P.add("act", lambda e: e.activation(out=EGL, in_=GLB, func=AF.Exp), reads=["BA"] + XN0, writes=["EGL"])
        P.add("dve", lambda e: e.tensor_tensor(out=EKD, in0=GLB, in1=GC, op=ALU.subtract), reads=["BA", "GC"] + XN0, writes=["EKD"])
        P.add("act", lambda e: e.activation(out=EKD, in_=EKD, func=AF.Exp), reads=["EKD"], writes=["EKD"])

        raw, acc, halo, sq, Sf, Sb = (g[k] for k in ("raw", "acc", "halo", "sq", "Sf", "Sb"))
        convw = self.convw
        NEGM = self.NEGM
        def capture_pair(hp):
            slabs = {}
            for wi, which in enumerate(("q", "k", "v", "z")):
                slabs[which] = self.ws.get(win[:, :, wi * 1024 + hp * 256: wi * 1024 + (hp + 1) * 256], 256)
            wsrc = self.d["gd_w_out"][j][hp * 256:(hp + 1) * 256, :].rearrange("(kc p) f -> p kc f", p=128)
            so = self.ws.get(wsrc, (2, 1024))
            FE, PREP, SCAN, OUT = {}, {}, {}, {}
            for gi in range(4):
                gp = gi % 2
                sT, zs, R, SC = g["sT"][gp], g["zs"][gp], g["R"][gp], g["SC"][gp]
                P.begin_capture()
                if gi == 0:
                    P.add("pool", lambda e: e.memset(flat(halo), 0.0), reads=XN0, writes=[("halo", i) for i in range(6)])
                for wi, which in enumerate(("k", "q", "v")):
                    slot, wkey, _ = slabs[which]
                    cbase = {"q": 0, "k": 8, "v": 16}[which]
                    for hh in range(2):
                        h = 2 * hp + hh
                        pf = PF[0]
                        rb = cnt["rb"] % 2
                        cnt["rb"] += 1
                        hi = wi * 2 + hh
                        P.add("dve", lambda e, rb=rb, hi=hi: e.tensor_copy(out=raw[:, rb, 0:3], in_=halo[:, hi, 0:3]),
                              reads=[("halo", hi)], writes=[("raw", rb)])
                        P.begin_atomic()
                        for kc in range(8):
                            P.add("pe", lambda e, pf=pf, slot=slot, kc=kc, hh=hh, gi=gi: e.matmul(
                                pf[:, :], slot[:, kc, hh * 128:(hh + 1) * 128], xnT[:, kc, gi * 512:(gi + 1) * 512],
                                start=(kc == 0), stop=(kc == 7)),
                                reads=[wkey] + [("xnT", t, kc) for t in range(4 * gi, 4 * gi + 4)], writes=[("pf", 0)])
                        P.add("act", lambda e, pf=pf, rb=rb: e.activation(out=raw[:, rb, 3:515], in_=pf[:, :], func=AF.Copy),
                              reads=[("pf", 0), ("raw", rb)], writes=[("raw", rb)])
                        P.end_atomic()
                        if gi < 3:
                            P.add("dve", lambda e, rb=rb, hi=hi: e.tensor_copy(out=halo[:, hi, 0:3], in_=raw[:, rb, 512:515]),
                                  reads=[("raw", rb)], writes=[("halo", hi)])
                        cc = cbase + h
                        P.add("dve", lambda e, rb=rb, cc=cc: e.tensor_scalar(
                            out=acc[:, 0, :], in0=raw[:, rb, 0:512], scalar1=convw[:, j, cc:cc + 1], scalar2=None, op0=ALU.mult),
                            reads=[("raw", rb), ("convw", j)], writes=[("acc", 0)])
                        for tap in range(1, 4):
                            P.add("dve", lambda e, rb=rb, cc=cc, tap=tap: e.scalar_tensor_tensor(
                                out=acc[:, 0, :], in0=raw[:, rb, tap:tap + 512], scalar=convw[:, j, tap * 24 + cc:tap * 24 + cc + 1],
                                in1=acc[:, 0, :], op0=ALU.mult, op1=ALU.add),
                                reads=[("raw", rb), ("acc", 0), ("convw", j)], writes=[("acc", 0)])
                        sgb = raw[:, rb, 0:512]
                        P.add("act", lambda e, sgb=sgb: e.activation(out=sgb, in_=acc[:, 0, :], func=AF.Exp, scale=-1.0),
                              reads=[("acc", 0), ("raw", rb), ("halo", hi)], writes=[("raw", rb)])
                        P.add("act", lambda e, sgb=sgb: e.activation(out=sgb, in_=sgb, func=AF.Ln, bias=1.0), reads=[("raw", rb)], writes=[("raw", rb)])
                        P.add("act", lambda e, sgb=sgb: e.activation(out=sgb, in_=sgb, func=AF.Exp, scale=-1.0), reads=[("raw", rb)], writes=[("raw", rb)])
                        P.add("dve", lambda e, sgb=sgb, hh=hh, wi=wi, sT=sT: e.tensor_tensor(
                            out=sT[:, hh, :, wi * 128:(wi + 1) * 128], in0=acc[:, 0, :].rearrange("p (a b) -> p a b", a=4),
                            in1=sgb.rearrange("p (a b) -> p a b", a=4), op=ALU.mult),
                            reads=[("acc", 0), ("raw", rb)], writes=[("sT", gp, hh, wi)])
                        if which in ("k", "q"):
                            sb_ = cnt["sq"] % 2
                            cnt["sq"] += 1
                            P.add("pool", lambda e, hh=hh, wi=wi, sb_=sb_, sT=sT: e.tensor_tensor(
                                out=sq[:, sb_, :].rearrange("p (a b) -> p a b", a=4), in0=sT[:, hh, :, wi * 128:(wi + 1) * 128],
                                in1=sT[:, hh, :, wi * 128:(wi + 1) * 128], op=ALU.mult),
                                reads=[("sT", gp, hh, wi)], writes=[("sq", sb_)])
                            for tl in range(4):
                                colr = tl * 4 + wi * 2 + hh
                                P.add("pe", lambda e, sb_=sb_, tl=tl, colr=colr: e.matmul(
                                    PF[1][:, 384 + colr:384 + colr + 1], sq[:, sb_, tl * 128:(tl + 1) * 128], self.onecol[:, 0:1],
                                    start=True, stop=True, skip_group_check=True),
                                    reads=[("sq", sb_), "onecol"], writes=[("pf", 1)])
                slot, wkey, _ = slabs["z"]
                for tl in range(4):
                    t = 4 * gi + tl
                    pf = PF[1]
                    for kc in range(8):
                        P.add("pe", lambda e, pf=pf, slot=slot, kc=kc, t=t: e.matmul(
                            pf[:, 0:256], xnT[:, kc, t * 128:(t + 1) * 128], slot[:, kc, 0:256], start=(kc == 0), stop=(kc == 7),
                            skip_group_check=True),
                            reads=[wkey, ("xnT", t, kc)], writes=[("pf", 1)])
                    zt = acc[:, 0, 0:256]
                    zr = acc[:, 0, 256:512]
                    P.add("act", lambda e, pf=pf, zt=zt: e.activation(out=zt, in_=pf[:, 0:256], func=AF.Exp, scale=-1.0),
                          reads=[("pf", 1)], writes=[("acc", 0)])
                    P.add("act", lambda e, pf=pf, zr=zr: e.activation(out=zr, in_=pf[:, 0:256], func=AF.Copy),
                          reads=[("pf", 1), ("acc", 0)], writes=[("acc", 0)])
                    P.add("act", lambda e, zt=zt: e.activation(out=zt, in_=zt, func=AF.Ln, bias=1.0), reads=[("acc", 0)], writes=[("acc", 0)])
                    P.add("act", lambda e, zt=zt: e.activation(out=zt, in_=zt, func=AF.Exp, scale=-1.0), reads=[("acc", 0)], writes=[("acc", 0)])
                    P.add("dve", lambda e, zt=zt, zr=zr, tl=tl, zs=zs: e.tensor_tensor(out=zs[:, tl, :], in0=zr, in1=zt, op=ALU.mult),
                          reads=[("acc", 0)], writes=[("zs", gp, tl)])
                Rk = ("R", gp)
                P.add("act", lambda e, R=R: e.activation(out=flat(R), in_=PF[1][:, 384:400], func=AF.Ln, bias=epsc[:, 0:1], scale=1.0),
                      reads=[("pf", 1), "epsc"], writes=[Rk])
                P.add("act", lambda e, R=R: e.activation(out=flat(R), in_=flat(R), func=AF.Exp, scale=-0.5), reads=[Rk], writes=[Rk])
                hs = slice(2 * hp, 2 * hp + 2)
                ts = slice(4 * gi, 4 * gi + 4)
                rk, rq = R[:, :, 0:2], R[:, :, 2:4]
                scv = lambda q_, SC=SC: SC[:, q_, :].rearrange("p (a b) -> p a b", a=4)
                T1, CKBG, CKD, CQ, UL, UA, BIAS, LN = (scv(i) for i in range(8))
                bt, egs, ekds, gcs = BETA[:, ts, hs], EG[:, ts, hs], EKD[:, ts, hs], GC[:, ts, hs]
                sk = lambda i: ("SC", gp, i)
                P.add("dve", lambda e, T1=T1, rk=rk, bt=bt: e.tensor_tensor(out=T1, in0=rk, in1=bt, op=ALU.mult), reads=[Rk, "BETA"], writes=[sk(0)])
                P.add("dve", lambda e, CKBG=CKBG, T1=T1, egs=egs: e.tensor_tensor(out=CKBG, in0=T1, in1=egs, op=ALU.mult), reads=[sk(0), "EG"], writes=[sk(1)])
                P.add("dve", lambda e, CKD=CKD, rk=rk, ekds=ekds: e.tensor_tensor(out=CKD, in0=rk, in1=ekds, op=ALU.mult), reads=[Rk, "EKD"], writes=[sk(2)])
                P.add("dve", lambda e, CQ=CQ, rq=rq, egs=egs: e.scalar_tensor_tensor(out=CQ, in0=rq, scalar=DKS, in1=egs, op0=ALU.mult, op1=ALU.mult),
                      reads=[Rk, "EG"], writes=[sk(3)])
                P.add("act", lambda e, UL=UL, T1=T1: e.activation(out=UL, in_=T1, func=AF.Ln), reads=[sk(0)], writes=[sk(4)])
                P.add("dve", lambda e, UL=UL, gcs=gcs: e.tensor_tensor(out=UL, in0=UL, in1=gcs, op=ALU.add), reads=[sk(4), "GC"], writes=[sk(4)])
                P.add("act", lambda e, UA=UA, rq=rq: e.activation(out=UA, in_=rq, func=AF.Ln, scale=DKS), reads=[Rk], writes=[sk(5)])
                P.add("dve", lambda e, UA=UA, gcs=gcs: e.tensor_tensor(out=UA, in0=UA, in1=gcs, op=ALU.add), reads=[sk(5), "GC"], writes=[sk(5)])
                P.add("act", lambda e, BIAS=BIAS, rk=rk: e.activation(out=BIAS, in_=rk, func=AF.Ln), reads=[Rk], writes=[sk(6)])
                P.add("dve", lambda e, BIAS=BIAS, gcs=gcs: e.tensor_tensor(out=BIAS, in0=BIAS, in1=gcs, op=ALU.subtract), reads=[sk(6), "GC"], writes=[sk(6)])
                FE[gi] = P.end_capture()
                SCK = [sk(i) for i in range(7)]
                for tl in range(4):
                    t = 4 * gi + tl
                    stageA = []
                    for c in range(2):
                        P.begin_capture()
                        hh = c
                        cs = 2 * (t % 2) + c
                        pbk, pbo = cs // 2, (cs % 2) * 512
                        sc1 = lambda q_, tl=tl, hh=hh, SC=SC: SC[:, q_, tl * 2 + hh:tl * 2 + hh + 1]
                        pb, pc = PB[pbk], PF[2 + cs]
                        kd, kbg, vb, E, MA = (g[k, cs] for k in ("kd", "kbg", "vb", "E", "MA"))
                        ksT = sT[:, hh, tl, 0:128]
                        vsT = sT[:, hh, tl, 256:384]
                        P.add("pe", lambda e, pb=pb, ksT=ksT, pbo=pbo: e.transpose(out=pb[:, pbo:pbo + 128], in_=ksT, identity=identb[:]),
                              reads=[("sT", gp, hh, 0), "identb"], writes=[("pb", pbk)])
                        P.add("pe", lambda e, pb=pb, vsT=vsT, pbo=pbo: e.transpose(out=pb[:, pbo + 128:pbo + 256], in_=vsT, identity=identb[:]),
                              reads=[("sT", gp, hh, 2), "identb"], writes=[("pb", pbk)])
                        P.add("act", lambda e, pb=pb, kd=kd, sc1=sc1, pbo=pbo: e.activation(out=kd, in_=pb[:, pbo:pbo + 128], func=AF.Copy, scale=sc1(2)),
                              reads=[("pb", pbk)] + SCK, writes=[("kd", cs)])
                        P.add("dve", lambda e, pb=pb, kbg=kbg, sc1=sc1, pbo=pbo: e.tensor_scalar(out=kbg, in0=pb[:, pbo:pbo + 128], scalar1=sc1(1), scalar2=None, op0=ALU.mult),
                              reads=[("pb", pbk)] + SCK, writes=[("kbg", cs)])
                        hcol = 2 * hp + hh
                        P.add("dve", lambda e, pb=pb, vb=vb, t=t, hcol=hcol, pbo=pbo: e.tensor_scalar(
                            out=vb, in0=pb[:, pbo + 128:pbo + 256], scalar1=BETA[:, t, hcol:hcol + 1], scalar2=None, op0=ALU.mult),
                            reads=[("pb", pbk), "BETA"], writes=[("vb", cs)])
                        P.add("pe", lambda e, pc=pc, ksT=ksT, hh=hh, tl=tl, sT=sT: e.matmul(pc[:, 0:256], ksT, sT[:, hh, tl, 0:256], start=True, stop=True,
                                                                                              skip_group_check=True),
                              reads=[("sT", gp, hh, 0), ("sT", gp, hh, 1)], writes=[("pf", 2 + cs)])
                        P.add("act", lambda e, E=E, sc1=sc1: e.activation(out=E[:, 0:128], in_=identf[:, :], func=AF.Copy, scale=sc1(4)),
                              reads=["identf"] + SCK, writes=[("E", cs)])
                        P.add("act", lambda e, E=E, sc1=sc1: e.activation(out=E[:, 128:256], in_=identf[:, :], func=AF.Copy, scale=sc1(5)),
                              reads=["identf", ("E", cs)] + SCK, writes=[("E", cs)])
                        P.add("pe", lambda e, pc=pc, E=E: e.matmul(pc[:, 256:512], self.onesf[:, :], E[:, :], start=True, stop=False, skip_group_check=True),
                              reads=[("E", cs), "onesf"], writes=[("pf", 2 + cs)])
                        P.add("pe", lambda e, pc=pc: e.matmul(pc[:, 256:512], identf[:, :], self.masks[:, :], start=False, stop=True, skip_group_check=True),
                              reads=["identf", "masks"], writes=[("pf", 2 + cs)])
                        P.add("act", lambda e, pc=pc, E=E, sc1=sc1: e.activation(out=E[:, :], in_=pc[:, 256:512], func=AF.Exp, bias=sc1(6)),
                              reads=[("pf", 2 + cs)] + SCK, writes=[("E", cs)])
                        P.add("dve", lambda e, pc=pc, E=E, MA=MA: e.tensor_tensor(out=MA[:, :], in0=pc[:, 0:256], in1=E[:, :], op=ALU.mult),
                              reads=[("pf", 2 + cs), ("E", cs)], writes=[("MA", cs)])
                        Lb, DD, TM = g["Lb", cs], g["DD", cs], g["TM", cs]
                        P.add("pe", lambda e, pb=pb, MA=MA, pbo=pbo: e.transpose(out=pb[:, pbo + 256:pbo + 384], in_=MA[:, 0:128], identity=identb[:]),
                              reads=[("MA", cs), "identb"], writes=[("pb", pbk)])
                        P.add("act", lambda e, pb=pb, Lb=Lb, pbo=pbo: e.activation(out=Lb, in_=pb[:, pbo + 256:pbo + 384], func=AF.Copy),
                              reads=[("pb", pbk)], writes=[("Lb", cs)])
                        P.add("dve", lambda e, TM=TM, Lb=Lb: e.tensor_tensor(out=TM[:, 0:128], in0=Lb, in1=NEGM[:, 0, 0:128], op=ALU.mult),
                              reads=[("Lb", cs), "NEGM"], writes=[("TM", cs)])
                        P.add("dve", lambda e, TM=TM, MA=MA: e.tensor_tensor(out=TM[:, 128:256], in0=MA[:, 0:128], in1=NEGM[:, 0, 128:256], op=ALU.mult),
                              reads=[("MA", cs), "NEGM", ("TM", cs)], writes=[("TM", cs)])
                        P.add("dve", lambda e, TM=TM, DD=DD: e.tensor_tensor(out=DD[:, 0, 0:128], in0=identb[:, :], in1=TM[:, 0:128], op=ALU.subtract),
                              reads=[("TM", cs), "identb"], writes=[("DD", cs, 0)])
                        P.add("dve", lambda e, TM=TM, DD=DD: e.tensor_tensor(out=DD[:, 0, 128:256], in0=identb[:, :], in1=TM[:, 128:256], op=ALU.subtract),
                              reads=[("TM", cs), "identb", ("DD", cs, 0)], writes=[("DD", cs, 0)])
                        stageA.append(P.end_capture())
                    P.begin_capture()
                    for lev in range(1, 7):
                        pi, po = (lev - 1) % 2, lev % 2
                        for c in range(2):
                            cs = 2 * (t % 2) + c
                            pc = PF[2 + cs]
                            MA, Lb, DD, QQ = (g[k_, cs] for k_ in ("MA", "Lb", "DD", "QQ"))
                            P.add("pe", lambda e, pc=pc, MA=MA, DD=DD, pi=pi: e.matmul(pc[:, 0:128], MA[:, 0:128], DD[:, pi, 0:128], start=True, stop=True,
                                                                                        skip_group_check=True),
                                  reads=[("MA", cs), ("DD", cs, pi)], writes=[("pf", 2 + cs)])
                            P.add("pe", lambda e, pc=pc, Lb=Lb, DD=DD, pi=pi: e.matmul(pc[:, 128:256], Lb, DD[:, pi, 128:256], start=True, stop=True,
                                                                                        skip_group_check=True),
                                  reads=[("Lb", cs), ("DD", cs, pi)], writes=[("pf", 2 + cs)])
                            P.add("dve", lambda e, pc=pc, QQ=QQ, lev=lev: e.tensor_tensor(out=QQ[:, :], in0=pc[:, 0:256], in1=NEGM[:, lev, :], op=ALU.mult),
                                  reads=[("pf", 2 + cs), "NEGM"], writes=[("QQ", cs)])
                        for c in range(2):
                            cs = 2 * (t % 2) + c
                            pc = PF[2 + cs]
                            DD, QQ = (g[k_, cs] for k_ in ("DD", "QQ"))
                            P.add("pe", lambda e, pc=pc, QQ=QQ, DD=DD, pi=pi: e.matmul(pc[:, 256:384], DD[:, pi, 128:256], QQ[:, 0:128], start=True, stop=True,
                                                                                        skip_group_check=True),
                                  reads=[("QQ", cs), ("DD", cs, pi)], writes=[("pf", 2 + cs)])
                            P.add("pe", lambda e, pc=pc, QQ=QQ, DD=DD, pi=pi: e.matmul(pc[:, 384:512], DD[:, pi, 0:128], QQ[:, 128:256], start=True, stop=True,
                                                                                        skip_group_check=True),
                                  reads=[("QQ", cs), ("DD", cs, pi)], writes=[("pf", 2 + cs)])
                            P.add("dve", lambda e, pc=pc, DD=DD, pi=pi, po=po: e.tensor_tensor(out=DD[:, po, :], in0=DD[:, pi, :], in1=pc[:, 256:512], op=ALU.subtract),
                                  reads=[("pf", 2 + cs), ("DD", cs, pi)], writes=[("DD", cs, po)])
                    for c in range(2):
                        cs = 2 * (t % 2) + c
                        pc = PF[2 + cs]
                        kbg, vb, DD, u, wT = (g[k, cs] for k in ("kbg", "vb", "DD", "u", "wT"))
                        P.add("pe", lambda e, pc=pc, DD=DD, vb=vb: e.matmul(pc[:, 0:128], DD[:, 0, 128:256], vb, start=True, stop=True, skip_group_check=True),
                              reads=[("DD", cs, 0), ("vb", cs)], writes=[("pf", 2 + cs)])
                        P.add("pe", lambda e, pc=pc, DD=DD, kbg=kbg: e.matmul(pc[:, 128:256], kbg, DD[:, 0, 128:256], start=True, stop=True, skip_group_check=True),
                              reads=[("DD", cs, 0), ("kbg", cs)], writes=[("pf", 2 + cs)])
                        P.add("act", lambda e, pc=pc, u=u: e.activation(out=u, in_=pc[:, 0:128], func=AF.Copy), reads=[("pf", 2 + cs)], writes=[("u", cs)])
                        P.add("dve", lambda e, pc=pc, wT=wT: e.tensor_copy(out=wT, in_=pc[:, 128:256]), reads=[("pf", 2 + cs)], writes=[("wT", cs)])
                    PREP[t] = Prog.merge(stageA) + P.end_capture()
                    scans = []
                    for c in range(2):
                        P.begin_capture()
                        hh = c
                        h = 2 * hp + hh
                        cs = 2 * (t % 2) + c
                        pbk, pbo = cs // 2, (cs % 2) * 512
                        ps_, pb = PF[2 + cs], PB[pbk]
                        kd, MA, u, wT, vn, o, og, psm = (g[k, cs] for k in ("kd", "MA", "u", "wT", "vn", "o", "og", "ps"))
                        sc1 = lambda q_, tl=tl, hh=hh, SC=SC: SC[:, q_, tl * 2 + hh:tl * 2 + hh + 1]
                        qsT = sT[:, hh, tl, 128:256]
                        PK = ("pf", 2 + cs)
                        P.add("pe", lambda e, ps_=ps_, wT=wT, hh=hh: e.matmul(ps_[:, 0:128], wT, Sb[:, hh, :], start=True, stop=True, skip_group_check=True),
                              reads=[("wT", cs), ("Sb", hh)], writes=[PK])
                        P.add("pe", lambda e, ps_=ps_, qsT=qsT, hh=hh: e.matmul(ps_[:, 128:256], qsT, Sb[:, hh, :], start=True, stop=True, skip_group_check=True),
                              reads=[("sT", gp, hh, 1), ("Sb", hh)], writes=[PK])
                        P.add("dve", lambda e, ps_=ps_, u=u, vn=vn: e.tensor_tensor(out=vn, in0=u, in1=ps_[:, 0:128], op=ALU.subtract),
                              reads=[PK, ("u", cs)], writes=[("vn", cs)])
                        P.add("pe", lambda e, ps_=ps_, MA=MA, vn=vn: e.matmul(ps_[:, 256:384], MA[:, 128:256], vn, start=True, stop=True, skip_group_check=True),
                              reads=[("MA", cs), ("vn", cs)], writes=[PK])
                        P.add("pe", lambda e, ps_=ps_, kd=kd, vn=vn: e.matmul(ps_[:, 384:512], kd, vn, start=True, stop=True, skip_group_check=True),
                              reads=[("kd", cs), ("vn", cs)], writes=[PK])
                        P.add("act", lambda e, ps_=ps_, o=o: e.activation(out=o, in_=ps_[:, 256:384], func=AF.Copy), reads=[PK], writes=[("o", cs)])
                        P.add("dve", lambda e, ps_=ps_, o=o, sc1=sc1: e.scalar_tensor_tensor(out=o, in0=ps_[:, 128:256], scalar=sc1(3), in1=o, op0=ALU.mult, op1=ALU.add),
                              reads=[PK, ("o", cs)] + SCK, writes=[("o", cs)])
                        P.add("dve", lambda e, ps_=ps_, hh=hh, t=t, h=h: e.scalar_tensor_tensor(
                            out=Sf[:, hh, :], in0=Sf[:, hh, :], scalar=EGL[:, t, h:h + 1], in1=ps_[:, 384:512], op0=ALU.mult, op1=ALU.add),
                            reads=[PK, ("Sf", hh), "EGL"], writes=[("Sf", hh)])
                        P.add("act", lambda e, hh=hh: e.activation(out=Sb[:, hh, :], in_=Sf[:, hh, :], func=AF.Copy), reads=[("Sf", hh)], writes=[("Sb", hh)])
                        P.add("act", lambda e, o=o, og=og, psm=psm: e.activation(out=og, in_=o, func=AF.Square, accum_out=psm[:, 0:1]),
                              reads=[("o", cs)], writes=[("og", cs), ("psm", cs)])
                        P.add("act", lambda e, psm=psm: e.activation(out=psm[:, 1:2], in_=psm[:, 0:1], func=AF.Ln, bias=epsc[:, 0:1], scale=1.0 / 128),
                              reads=[("psm", cs), "epsc"], writes=[("psm", cs)])
                        P.add("act", lambda e, psm=psm: e.activation(out=psm[:, 1:2], in_=psm[:, 1:2], func=AF.Exp, scale=-0.5), reads=[("psm", cs)], writes=[("psm", cs)])
                        P.add("dve", lambda e, o=o, og=og, psm=psm, tl=tl, hh=hh, zs=zs: e.scalar_tensor_tensor(
                            out=og, in0=o, scalar=psm[:, 1:2], in1=zs[:, tl, hh * 128:(hh + 1) * 128], op0=ALU.mult, op1=ALU.mult),
                            reads=[("o", cs), ("psm", cs), ("zs", gp, tl)], writes=[("og", cs)])
                        P.add("pe", lambda e, pb=pb, og=og, pbo=pbo: e.transpose(out=pb[:, pbo + 384:pbo + 512], in_=og, identity=identb[:]),
                              reads=[("og", cs), "identb"], writes=[("pb", pbk)])
                        P.add("act", lambda e, pb=pb, hh=hh, t=t, pbo=pbo: e.activation(out=U[:, hh, t * 128:(t + 1) * 128], in_=pb[:, pbo + 384:pbo + 512], func=AF.Copy,
                                                                                        scale=self.gdc[:, j, 0:1]),
                              reads=[("pb", pbk), ("gdc", j)], writes=[("U", hh, t // 4)])
                        scans.append(P.end_capture())
                    SCAN[t] = Prog.merge(scans)
            wv = so[0]
            for t in range(NT):
                P.begin_capture()
                for dh in range(2):
                    pf = PF[0]
                    P.begin_atomic()
                    for hc in range(2):
                        P.add("pe", lambda e, pf=pf, wv=wv, hc=hc, t=t, dh=dh: e.matmul(
                            pf[:, :], U[:, hc, t * 128:(t + 1) * 128], wv[:, hc, dh * 512:(dh + 1) * 512], start=(hc == 0), stop=(hc == 1)),
                            reads=[so[1], ("U", hc, t // 4)], writes=[("pf", 0)])
                    P.add("dve", lambda e, pf=pf, t=t, dh=dh: e.tensor_tensor(
                        out=X[:, t, dh * 512:(dh + 1) * 512], in0=X[:, t, dh * 512:(dh + 1) * 512], in1=pf[:, :], op=ALU.add),
                        reads=[("pf", 0), ("x", t)], writes=[("x", t)])
                    P.end_atomic()
                OUT[t] = P.end_capture()
            return slabs, so, FE, PREP, SCAN, OUT

        def zero_state():
            P.add("pool", lambda e: e.memset(flat(Sf), 0.0), reads=XN0, writes=[("Sf", 0), ("Sf", 1)])
            P.add("pool", lambda e: e.memset(flat(Sb), 0.0), reads=XN0, writes=[("Sb", 0), ("Sb", 1)])

        cur = capture_pair(0)
        P.replay([cur[2][0]])
        zero_state()
        for hp in range(4):
            slabs, so, FE, PREP, SCAN, OUT = cur
            fe_parts = {}
            for gi in range(1, 4):
                L = FE[gi]
                n = (len(L) + 2) // 3
                for k in range(3):
                    fe_parts[4 * (gi - 1) + 1 + k] = L[k * n:(k + 1) * n]
            for s_ in range(NT):
                lists = [PREP[s_]]
                if s_ >= 1:
                    lists.append(SCAN[s_ - 1])
                if s_ in fe_parts:
                    lists.append(fe_parts[s_])
                if s_ >= 2:
                    lists.append(OUT[s_ - 2])
                P.replay(lists)
            for which in ("q", "k", "v", "z"):
                self.ws.done(slabs[which][2])
            tail = Prog.merge([SCAN[NT - 1]]) + Prog.merge([OUT[NT - 2]]) + Prog.merge([OUT[NT - 1]])
            if hp < 3:
                cur = capture_pair(hp + 1)
                P.replay([tail, cur[2][0]])
            else:
                P.replay([tail])
            self.ws.done(so[2])
            if hp < 3:
                zero_state()

    def load_x(self, s):
        P = self.P
        X = self.X
        xv = self.d["x"][s].rearrange("(t p) d -> p t d", p=128)
        for q in range(4):
            P.add("sp", lambda e, q=q: e.dma_start(out=X[:, 4 * q:4 * q + 4, :], in_=xv[:, 4 * q:4 * q + 4, :]),
                  writes=[("x", t) for t in range(4 * q, 4 * q + 4)], dma=("xl", q))

    def store_x(self, s):
        P = self.P
        X = self.X
        ov = self.d["out"][s].rearrange("(t p) d -> p t d", p=128)
        ids = []
        for q in range(4):
            ids.append(P.add("sp", lambda e, q=q: e.dma_start(out=ov[:, 4 * q:4 * q + 4, :], in_=X[:, 4 * q:4 * q + 4, :]),
                             reads=[("x", t) for t in range(4 * q, 4 * q + 4)], writes=[("xst", q)], dma=("xs", q)))
        return ids

    def rmsnorm(self, gbase):
        P = self.P
        X, xs, ss, rstd, xnT = self.X, self.xs, self.ss, self.rstd, self.xnT
        identb, gcol = self.identb, self.gcol
        import os
        DBG = int(os.environ.get("K_DBG", "9"))
        for t in range(NT):
            P.add("act", lambda e, t=t: e.activation(out=xs[:, 1, :], in_=X[:, t, :], func=AF.Square,
                                                     accum_out=ss[:, t:t + 1]),
                  reads=[("x", t)], writes=[("xs", 1), ("ss", t)])
        P.add("act", lambda e: e.activation(out=rstd[:], in_=ss[:], func=AF.Ln, bias=self.epsc[:, 0:1], scale=1.0 / D),
              reads=[("ss", t) for t in range(NT)] + ["epsc"], writes=["rstd"])
        P.add("act", lambda e: e.activation(out=rstd[:], in_=rstd[:], func=AF.Exp, scale=-0.5), reads=["rstd"], writes=["rstd"])
        if DBG < 2:
            return
        for t in range(NT if DBG >= 6 else 1):
            b = t % 2
            pb = self.PB[b]
            P.add("act", lambda e, t=t, b=b: e.activation(out=xs[:, b, :], in_=X[:, t, :], func=AF.Copy,
                                                          scale=rstd[:, t:t + 1]),
                  reads=[("x", t), "rstd"], writes=[("xs", b)])
            if DBG < 4:
                continue
            for kc in range(8):
                P.add("pe", lambda e, b=b, kc=kc, pb=pb: e.transpose(out=pb[:, kc * 128:(kc + 1) * 128],
                                                                      in_=xs[:, b, kc * 128:(kc + 1) * 128],
                                                                      identity=identb[:]),
                      reads=[("xs", b), "identb"], writes=[("pb", b)])
            if DBG < 5:
                continue
            gb = gcol[:, gbase:gbase + 8].unsqueeze(2).to_broadcast([128, 8, 128])
            P.add("dve", lambda e, t=t, pb=pb, gb=gb: e.tensor_tensor(
                out=xnT[:, :, t * 128:(t + 1) * 128], in0=pb[:, :].rearrange("p (k c) -> p k c", k=8), in1=gb, op=ALU.mult),
                reads=[("pb", b), "gcol"], writes=[("xnT", t, kc) for kc in range(8)])

    def mlp(self, l):
        P = self.P
        X, xnT, U, ring = self.X, self.xnT, self.hT, self.ring
        w1 = self.d["mlp_w_in"][l].rearrange("(kc p) f -> p kc f", p=128)
        w2 = self.d["mlp_w_out"][l].rearrange("(fc p) d -> p fc d", p=128)
        self.rmsnorm(32 + 8 * l)
        self.arena_barrier()
        pfi = 0
        for fg in range(4):
            slabs = [self.ws.get(w1[:, :, fg * 1024 + s2 * 512: fg * 1024 + (s2 + 1) * 512], 512) for s2 in range(2)]
            for fc in range(8):
                slot, wkey, _ = slabs[fc // 4]
                off = (fc % 4) * 128
                for tt in range(4):
                    bank = pfi % 4
                    pfi += 1
                    pf = self.PF[bank]
                    for kc in range(8):
                        P.add("pe", lambda e, pf=pf, slot=slot, kc=kc, off=off, tt=tt: e.matmul(
                            pf[:, :], slot[:, kc, off:off + 128], xnT[:, kc, tt * 512:(tt + 1) * 512],
                            start=(kc == 0), stop=(kc == 7)),
                            reads=[wkey] + [("xnT", t, kc) for t in range(4 * tt, 4 * tt + 4)],
                            writes=[("pf", bank)])
                    rb = pfi % 2
                    P.add("act", lambda e, pf=pf, rb=rb: e.activation(out=self.rtmp[:, rb, :], in_=pf[:, :], func=AF.Relu),
                          reads=[("pf", bank)], writes=[("xs", rb)])
                    P.add("dve", lambda e, fc=fc, tt=tt, rb=rb: e.tensor_tensor(
                        out=U[:, fc, tt * 512:(tt + 1) * 512], in0=self.rtmp[:, rb, :], in1=self.rtmp[:, rb, :],
                        op=ALU.mult),
                        reads=[("xs", rb)], writes=[("hT", fc, tt)])
            for sl in slabs:
                self.ws.done(sl[2])
            slabs2 = [self.ws.get(w2[:, fg * 8:(fg + 1) * 8, dh * 512:(dh + 1) * 512], 512) for dh in range(2)]
            for t in range(NT):
                for dh in range(2):
                    slot, wkey, _ = slabs2[dh]
                    bank = pfi % 4
                    pfi += 1
                    pf = self.PF[bank]
                    for fc in range(8):
                        P.add("pe", lambda e, pf=pf, slot=slot, fc=fc, t=t: e.matmul(
                            pf[:, :], U[:, fc, t * 128:(t + 1) * 128], slot[:, fc, :],
                            start=(fc == 0), stop=(fc == 7)),
                            reads=[wkey, ("hT", fc, t // 4)], writes=[("pf", bank)])
                    P.add("dve", lambda e, pf=pf, t=t, dh=dh: e.tensor_tensor(
                        out=X[:, t, dh * 512:(dh + 1) * 512], in0=X[:, t, dh * 512:(dh + 1) * 512], in1=pf[:, :],
                        op=ALU.add),
                        reads=[("pf", bank), ("x", t)], writes=[("x", t)])
            for sl in slabs2:
                self.ws.done(sl[2])

    def build(self):
        P = self.P
        self.setup_consts()
        kinds = {k for k, _ in self.layers}
        if "gd" in kinds:
            self.setup_gd_static()
            self.setup_gd_levelmasks()
            for jj in sorted({l // 2 for (k, l) in self.layers if k == "gd"}):
                self.setup_gd_consts(jj)
        if "da" in kinds:
            self.setup_da_static()
            for (k, l) in self.layers:
                if k == "da":
                    self.setup_da_consts(l // 2, l)
        last_stores = []
        for s in range(self.n_seq):
            self.load_x(s)
            for l in self.layers:
                if l[0] == "mlp":
                    self.mlp(l[1])
                elif l[0] == "norm":
                    self.rmsnorm(32 + 8 * l[1])
                elif l[0] == "da":
                    self.diffattn(l[1])
                elif l[0] == "gd":
                    self.gdn(l[1])
            last_stores = self.store_x(s)
        P.add("sp", None, reads=[("xst", q) for q in range(4)])


def layer_plan():
    plan = []
    for i in range(DEPTH):
        plan.append(("da" if i % 2 == 0 else "gd", i))
        plan.append(("mlp", i))
    return plan


def build_program(n_seq=SEQ_PER_CORE, layers=None):
    if layers is None:
        layers = layer_plan()
    nc = bass.Bass("TRN2", target_bir_lowering=False)
    with ExitStack() as es:
        b = Builder(nc, Prog(nc, dry=True), None, n_seq, layers, es)
        b.build()
        future = list(b.ws.requests)
        P = Prog(nc, dry=False)
        b.P = P
        b.ws = WStream(P, b.ring, b.NSLOT * 2, future)
        b.build()
        P.emit(es)
    return nc


WEIGHT_NAMES = ["mix_norm", "mlp_norm", "mlp_w_in", "mlp_w_out", "da_w_in", "da_q_norm", "da_k_norm",
                "da_lambda_q1", "da_lambda_k1", "da_lambda_q2", "da_lambda_k2", "da_sub_norm", "da_w_out",
                "gd_w_in", "gd_conv_w", "gd_a_log", "gd_dt_bias", "gd_out_norm", "gd_w_out"]


def run(inputs, n_seq=SEQ_PER_CORE, layers=None, ncores=NCORES, trace=False):
    nc = build_program(n_seq, layers)
    x = np.ascontiguousarray(np.asarray(inputs["x"], dtype=np.float32))
    weights = {k: np.ascontiguousarray(np.asarray(inputs[k], dtype=np.float32)) for k in WEIGHT_NAMES}
    in_maps = []
    for c in range(ncores):
        m = {"x": x[c * n_seq:(c + 1) * n_seq]}
        m.update(weights)
        in_maps.append(m)
    res = run_bass_kernel_spmd(nc, in_maps, core_ids=list(range(ncores)), trace=trace)
    out = np.concatenate([r["out"] for r in res.results], axis=0)
    return out, res


def kernel(**inputs):
    out, _ = run(inputs)
    return out
```

```python
import math
from contextlib import ExitStack

import numpy as np
import concourse.bass as bass
import concourse.mybir as mybir
from concourse.bass_utils import run_bass_kernel_spmd

F32 = mybir.dt.float32
BF16 = mybir.dt.bfloat16
AF = mybir.ActivationFunctionType
ALU = mybir.AluOpType
AX = mybir.AxisListType

D = 1024
S = 2048
NT = S // 128
DFF = 4096
DEPTH = 4
EPS = 1e-6
NCORES = 8
SEQ_PER_CORE = 4
GD_IN = 4 * 1024 + 16


class Op:
    __slots__ = ("id", "eng", "fn", "deps", "dma", "seq", "signal")


class Prog:
    ENGS = ("pe", "act", "dve", "pool", "sp")

    def __init__(self, nc, dry=False):
        self.nc = nc
        self.dry = dry
        self.ops = []
        self.by_eng = {e: [] for e in self.ENGS}
        self.lw = {}
        self.rd = {}
        self.dma_groups = {}
        self.group_all = set()
        self.psum_last = {}
        self.arena_names = set()
        self.cap = None
        self.atom = None

    def begin_capture(self):
        self.cap = []

    def begin_atomic(self):
        if self.cap is not None:
            self.atom = []

    def end_atomic(self):
        if self.cap is not None:
            self.cap.append(self.atom)
            self.atom = None

    def end_capture(self):
        c, self.cap = self.cap, None
        return c

    @staticmethod
    def merge(lists):
        lists = [L for L in lists if L]
        idx = [0] * len(lists)
        out = []
        total = sum(len(L) for L in lists)
        nel = total
        done_el = 0
        while done_el < nel:
            done_el += 1
            best, bf = None, None
            for i, L in enumerate(lists):
                if idx[i] < len(L):
                    f = (idx[i] + 0.5) / len(L)
                    if bf is None or f < bf:
                        best, bf = i, f
            el = lists[best][idx[best]]
            idx[best] += 1
            total -= 1
            if isinstance(el, list):
                out.extend(el)
                total += len(el)
            else:
                out.append(el)
                total += 1
        return out

    def replay(self, lists):
        for rec in self.merge(lists):
            self.add(*rec)

    def add(self, eng, fn, reads=(), writes=(), dma=None):
        if self.dry:
            return None
        if self.cap is not None:
            rec = (eng, fn, tuple(reads), tuple(writes), dma)
            if self.atom is not None:
                self.atom.append(rec)
            else:
                self.cap.append(rec)
            return None
        op = Op()
        op.id = len(self.ops)
        op.eng = eng
        op.fn = fn
        op.dma = dma
        op.seq = 0
        op.signal = False
        if self.arena_names:
            for k in tuple(reads) + tuple(writes):
                nm = k[0] if isinstance(k, tuple) else k
                if nm in self.arena_names:
                    reads = tuple(reads) + ("ARENA",)
                    break
        deps = set()
        for k in reads:
            w = self.lw.get(k)
            if w is not None:
                deps.add(w)
        for k in writes:
            w = self.lw.get(k)
            if w is not None:
                deps.add(w)
            for r in self.rd.get(k, ()):
                deps.add(r)
        for k in reads:
            self.rd.setdefault(k, []).append(op.id)
        for k in writes:
            self.lw[k] = op.id
            self.rd[k] = []
        for k in tuple(reads) + tuple(writes):
            if isinstance(k, tuple) and k[0] in ("pf", "pb"):
                last = self.psum_last.setdefault(k, {})
                for eng2, oid in last.items():
                    if eng2 != eng:
                        deps.add(oid)
                last[eng] = op.id
        deps.discard(op.id)
        if eng == "pe" and dma is None:
            deps = {d for d in deps if not (self.ops[d].eng == "pe" and self.ops[d].dma is None)}
        op.deps = deps
        self.ops.append(op)
        self.by_eng[eng].append(op)
        if dma is not None:
            self.dma_groups.setdefault(dma, []).append(op.id)
        return op.id

    def emit(self, es):
        nc = self.nc
        ops = self.ops
        for op in ops:
            for d in op.deps:
                ops[d].signal = True
        for e in self.ENGS:
            c = 0
            for op in self.by_eng[e]:
                if op.dma is None and op.signal:
                    c += 1
                    op.seq = c
        for g, ids in self.dma_groups.items():
            for i, oid in enumerate(ids):
                ops[oid].seq = i + 1
        eng_sem = {e: es.enter_context(nc.semaphore("s_" + e)) for e in self.ENGS}
        dma_sem = {g: es.enter_context(nc.semaphore("d_" + str(g))) for g in self.dma_groups}
        block = es.enter_context(nc.Block())

        def emit_engine(ename, e):
            waited = {}
            for op in self.by_eng[ename]:
                need = {}
                for d in op.deps:
                    dop = ops[d]
                    if dop.dma is not None:
                        sem = dma_sem[dop.dma]
                        if dop.dma in self.group_all:
                            val = 16 * len(self.dma_groups[dop.dma])
                        else:
                            val = 16 * dop.seq
                    else:
                        sem = eng_sem[dop.eng]
                        val = dop.seq
                    key = id(sem)
                    if key not in need or need[key][1] < val:
                        need[key] = (sem, val)
                for key, (sem, val) in need.items():
                    if waited.get(key, 0) < val:
                        e.wait_ge(sem, val)
                        waited[key] = val
                if op.fn is None:
                    continue
                inst = op.fn(e)
                if op.dma is not None:
                    inst.then_inc(dma_sem[op.dma], 16)
                elif op.signal:
                    inst.then_inc(eng_sem[ename], 1)

        @block.tensor
        def _(e):
            emit_engine("pe", e)

        @block.scalar
        def _(e):
            emit_engine("act", e)

        @block.vector
        def _(e):
            emit_engine("dve", e)

        @block.gpsimd
        def _(e):
            emit_engine("pool", e)

        @block.sync
        def _(e):
            emit_engine("sp", e)


class WStream:
    UNIT = 2048

    def __init__(self, P, ring, nunits, lookahead_list=None):
        self.P = P
        self.ring = ring
        self.nu = nunits
        self.future = lookahead_list
        self.requests = []
        self.issued = 0
        self.released = set()
        self.head = 0
        self.occ = [None] * nunits
        self.units = []
        self.prev = []
        if lookahead_list is not None:
            for (src, w) in lookahead_list:
                self._place(w)

    @staticmethod
    def _nelem(w):
        return w[0] * w[1] if isinstance(w, tuple) else 8 * w

    def _place(self, w):
        n = (self._nelem(w) + self.UNIT - 1) // self.UNIT
        if self.head + n > self.nu:
            self.head = 0
        j = len(self.units)
        us = list(range(self.head, self.head + n))
        self.prev.append({self.occ[u] for u in us if self.occ[u] is not None})
        for u in us:
            self.occ[u] = j
        self.units.append((self.head, n))
        self.head = (self.head + n) % self.nu

    def view(self, idx, w):
        u0, n = self.units[idx]
        ne = self._nelem(w)
        v = self.ring[:, u0 * self.UNIT:u0 * self.UNIT + ne]
        if isinstance(w, tuple):
            return v.rearrange("p (a b) -> p a b", a=w[0])
        return v.rearrange("p (a b) -> p a b", a=8)

    def _pump(self):
        if self.P.dry:
            return
        while self.issued < len(self.future):
            j = self.issued
            if not all(pj in self.released for pj in self.prev[j]):
                break
            src, w = self.future[j]
            dst = self.view(j, w)
            self.P.add("pool", lambda e, dst=dst, src=src: e.dma_start(out=dst, in_=src),
                       writes=[("w", j)] + [("w", pj) for pj in self.prev[j]], dma=("wu", self.units[j][0]))
            self.issued += 1

    def get(self, src, w):
        i = len(self.requests)
        self.requests.append((src, w))
        if self.future is None:
            self._place(w)
        if self.P.dry:
            return self.view(i, w), ("w", i), i
        self._pump()
        assert self.issued > i, "weight ring deadlock: release slabs before requesting more"
        return self.view(i, w), ("w", i), i

    def done(self, idx):
        self.released.add(idx)
        self._pump()


class Builder:
    def __init__(self, nc, P, ws_future, n_seq, layers, es):
        self.nc = nc
        self.P = P
        self.n_seq = n_seq
        self.layers = layers
        self.es = es
        self.ws_future = ws_future
        self.alloc()

    def sb(self, name, shape, dt):
        return self.es.enter_context(self.nc.sbuf_tensor(name, shape, dt))

    def carve(self, shape, dt):
        esz = 4 if dt == F32 else 2
        n = 1
        for s_ in shape:
            n *= s_
        nbytes = (n * esz + 3) // 4 * 4
        off = self._aoff
        assert off + nbytes <= self.ARENA_BYTES, "arena overflow"
        self._aoff = off + nbytes
        v = self.arena[:, off // 4:(off + nbytes) // 4]
        if dt != F32:
            v = v.bitcast(dt)
        v = v[:, 0:n]
        if len(shape) == 2:
            v = v.rearrange("p (a b) -> p a b", a=shape[0])
        elif len(shape) == 3:
            v = v.rearrange("p (a b c) -> p a b c", a=shape[0], b=shape[1])
        return v

    def alloc(self):
        nc = self.nc
        n_seq = self.n_seq
        d = {}
        d["x"] = nc.dram_tensor("x", [n_seq, S, D], F32, kind="ExternalInput").ap()
        d["out"] = nc.dram_tensor("out", [n_seq, S, D], F32, kind="ExternalOutput").ap()
        specs = [
            ("mix_norm", [4, D]), ("mlp_norm", [4, D]), ("mlp_w_in", [4, D, DFF]), ("mlp_w_out", [4, DFF, D]),
            ("da_w_in", [2, D, 3072]), ("da_q_norm", [2, 64]), ("da_k_norm", [2, 64]),
            ("da_lambda_q1", [2, 64]), ("da_lambda_k1", [2, 64]), ("da_lambda_q2", [2, 64]), ("da_lambda_k2", [2, 64]),
            ("da_sub_norm", [2, 128]), ("da_w_out", [2, D, D]),
            ("gd_w_in", [2, D, GD_IN]), ("gd_conv_w", [2, 4, 3072]), ("gd_a_log", [2, 8]), ("gd_dt_bias", [2, 8]),
            ("gd_out_norm", [2, 128]), ("gd_w_out", [2, D, D]),
        ]
        for name, shape in specs:
            d[name] = nc.dram_tensor(name, shape, F32, kind="ExternalInput").ap()
        self.d = d
        self.NSLOT = 4
        self.X = self.sb("X", [128, NT, D], F32)
        self.xnT = self.sb("xnT", [128, 8, S], BF16)
        self.U = self.sb("U", [128, 2, S], BF16)
        self.ring = self.sb("ring", [128, self.NSLOT * 4096], BF16)
        self.xs_f = self.sb("xs_f", [128, 2, 512], F32)
        self.xs = self.xs_f[:, :, :].rearrange("p a b -> p (a b)").bitcast(BF16).rearrange("p (a b) -> p a b", a=2)
        self.rtmp = self.xs_f
        self.ss = self.sb("ss", [128, NT], F32)
        self.rstd = self.sb("rstd", [128, NT], F32)
        self.epsc = self.sb("epsc", [128, 4], F32)
        self.identb = self.sb("identb", [128, 128], BF16)
        self.identf = self.sb("identf", [128, 128], F32)
        self.crow = self.xs_f[0:64, 1, 0:128]
        self.gcol = self.sb("gcol", [128, 64], F32)
        self.ARENA_BYTES = 35840 + 24576
        self.arena = self.sb("arena", [128, self.ARENA_BYTES // 4], F32)
        self._aoff = 0
        self.hT = self.carve([8, S], BF16)
        self._aoff = 0
        self.qT = self.carve([2, S], BF16)
        self.kTz = self.carve([2, 2, S], BF16)
        self.vaug = self.carve([NT, 2, 130], BF16)
        self.pT = self.carve([3, 512], BF16)
        self.qraw = self.carve([2, 512], BF16)
        self.qsq = self.carve([2, 512], BF16)
        self.qrs = self.carve([1, 512], F32)
        self.accS = self.carve([4, 512], F32)
        self.osb = self.carve([2, 4, 128], F32)
        self.obf = self.carve([2, 4, 128], BF16)
        self.fsm = self.carve([2, 24], F32)
        self.da_arena_end = self._aoff
        self._aoff = 0
        g = {}
        g["BA"] = self.carve([NT, 16], F32)
        for nm in ("BETA", "GC", "EG", "EGL", "EKD"):
            g[nm] = self.carve([NT, 8], F32)
        g["GLB"] = g["BA"][:, :, :].rearrange("p a b -> p (a b)")[:, 0:NT * 8].rearrange("p (a b) -> p a b", a=NT)
        g["raw"] = self.carve([2, 516], F32)
        g["acc"] = self.carve([1, 512], F32)
        g["halo"] = self.carve([6, 4], F32)
        g["sT"] = [self.carve([2, 4, 3 * 128], BF16) for _ in range(2)]
        g["sq"] = self.carve([2, 512], BF16)
        g["zs"] = [self.carve([4, 256], BF16) for _ in range(2)]
        g["R"] = [self.carve([4, 4], F32) for _ in range(2)]
        g["SC"] = [self.carve([8, 8], F32) for _ in range(2)]
        g["Sf"] = self.carve([2, 128], F32)
        g["Sb"] = self.carve([2, 128], BF16)
        self.NCH = 4
        for c in range(self.NCH):
            g["kd", c] = self.carve([128], BF16)
            g["kbg", c] = self.carve([128], BF16)
            g["vb", c] = self.carve([128], BF16)
            g["E", c] = self.carve([256], F32)
            g["MA", c] = self.carve([256], BF16)
            g["Lb", c] = self.carve([128], BF16)
            g["QQ", c] = self.carve([256], BF16)
            g["TM", c] = self.carve([256], BF16)
            g["DD", c] = self.carve([2, 256], BF16)
            g["u", c] = self.carve([128], F32)
            g["wT", c] = self.carve([128], BF16)
            g["vn", c] = self.carve([128], BF16)
            g["o", c] = self.carve([128], F32)
            g["og", c] = self.carve([128], BF16)
            g["ps", c] = self.carve([8], F32)
        self.g = g
        self.gd_arena_end = self._aoff
        assert max(self.da_arena_end, self.gd_arena_end) <= self.ARENA_BYTES
        self.convw = self.sb("convw", [128, 2, 96], F32)
        self.gdc = self.sb("gdc", [128, 2, 4], F32)
        self.gsm = self.sb("gsm", [128, 2, 16], F32)
        self.NEGM = self.sb("NEGM", [128, 7, 256], BF16)
        self.trif = self.sb("trif", [128, 128], F32)
        self.sellast = self.sb("sellast", [128, 128], F32)
        self.onesf = self.sb("onesf", [128, 128], F32)
        self.masks = self.sb("masks", [128, 256], F32)
        self.onecol = self.sb("onecol", [128, 2], BF16)
        self.onesbd = self.sb("onesbd", [128, 128], BF16)
        self.negmask = self.sb("negmask", [128, 128], BF16)
        self.cda = self.sb("cda", [128, 16], F32)
        self.lamt = self.xs_f[:, 0, 256:512].rearrange("p (a b) -> p a b", a=4)
        self.lamp = self.sb("lamp", [128, 4], F32)
        self.scr_f = self.xs_f[:, 0, 0:128]
        self.scr_f2 = self.xs_f[:, 0, 128:256]
        self.PF = [self.es.enter_context(nc.psum_tensor("pf%d" % i, [128, 512], F32)) for i in range(6)]
        self.PB = [self.es.enter_context(nc.psum_tensor("pb%d" % i, [128, 1024], BF16)) for i in range(2)]
        self.ws = WStream(self.P, self.ring, self.NSLOT * 2, self.ws_future)
        self.dummy = self.sb("abar", [128, 2], F32)
        self.ARENA_NAMES = {"hT", "qT", "kT", "kTz", "accS", "va", "pT", "qraw", "qsq", "qrs", "osb", "obf", "fsm", "fss", "frs", "vones",
                            "BA", "BETA", "GC", "EG", "EGL", "EKD", "raw", "acc", "halo", "sT", "sq", "zs", "R", "SC", "Sf", "Sb",
                            "kd", "kbg", "vb", "E", "MA", "Lb", "QQ", "TM", "DD", "u", "wT", "vn", "o", "og", "psm"}

    def arena_barrier(self):
        self.P.arena_names = self.ARENA_NAMES
        dummy = self.dummy
        self.P.add("pool", lambda e: e.memset(dummy[:], 0.0), writes=["ARENA"])

    def setup_consts(self):
        P = self.P
        nc = self.nc
        identf, identb = self.identf, self.identb
        P.add("pool", lambda e: e.memset(identf[:], 0.0), writes=["identf"])
        P.add("pool", lambda e: e.memset(self.epsc[:], EPS), writes=["epsc"])
        P.add("pool", lambda e: e.affine_select(out=identf[:], in_=identf[:], pattern=[[-1, 128]],
                                                compare_op=ALU.not_equal, fill=1.0, base=0,
                                                channel_multiplier=1),
              reads=["identf"], writes=["identf"])
        P.add("dve", lambda e: e.tensor_copy(out=identb[:], in_=identf[:]), reads=["identf"], writes=["identb"])
        crow, gcol = self.crow, self.gcol
        mixr = self.d["mix_norm"].rearrange("l (kc p) -> (l kc) p", p=128)
        mlpr = self.d["mlp_norm"].rearrange("l (kc p) -> (l kc) p", p=128)
        P.add("sp", lambda e: e.dma_start(out=crow[0:32, :], in_=mixr), writes=[("xs", 1)], dma="c0")
        P.add("sp", lambda e: e.dma_start(out=crow[32:64, :], in_=mlpr), writes=[("xs", 1)], dma="c1")
        pf = self.PF[0]
        P.add("pe", lambda e: e.transpose(out=pf[:, 0:64], in_=crow[0:64, :], identity=identf[0:64, 0:64]),
              reads=[("xs", 1), "identf"], writes=[("pf", 0)])
        P.add("dve", lambda e: e.tensor_copy(out=gcol[:, :], in_=pf[:, 0:64]), reads=[("pf", 0)], writes=["gcol"])

    def setup_da_consts(self, j, l):
        P = self.P
        d = self.d
        cda, lamt, lamp = self.cda, self.lamt, self.lamp
        c0 = 5 * j
        col = lambda ap: ap.rearrange("(p o) -> p o", o=1)
        for half in range(2):
            P.add("sp", lambda e, half=half: e.dma_start(out=cda[64 * half:64 * half + 64, c0:c0 + 1], in_=col(d["da_q_norm"][j])),
                  writes=[("cda", j)], dma="cq%d%d" % (j, half))
            P.add("sp", lambda e, half=half: e.dma_start(out=cda[64 * half:64 * half + 64, c0 + 1:c0 + 2], in_=col(d["da_k_norm"][j])),
                  writes=[("cda", j)], dma="ck%d%d" % (j, half))
        P.add("sp", lambda e: e.dma_start(out=cda[:, c0 + 4:c0 + 5], in_=col(d["da_sub_norm"][j])),
              writes=[("cda", j)], dma="cs%d" % j)
        for i, nm in enumerate(["da_lambda_q1", "da_lambda_k1", "da_lambda_q2", "da_lambda_k2"]):
            P.add("sp", lambda e, i=i, nm=nm: e.dma_start(out=lamt[:, i, :], in_=d[nm][j:j + 1, :].partition_broadcast(128)),
                  writes=[("xs", 0)], dma="cl%d%d" % (j, i))
        lam_init = 0.8 - 0.6 * math.exp(-0.3 * l)
        for i in range(2):
            P.add("dve", lambda e, i=i: e.tensor_tensor(out=lamt[:, 2 * i, :], in0=lamt[:, 2 * i, :], in1=lamt[:, 2 * i + 1, :], op=ALU.mult),
                  reads=[("xs", 0)], writes=[("xs", 0)])
            P.add("dve", lambda e, i=i: e.reduce_sum(out=lamp[:, i:i + 1], in_=lamt[:, 2 * i, :], axis=AX.X),
                  reads=[("xs", 0)], writes=["lamp"])
        P.add("act", lambda e: e.activation(out=lamp[:, 0:2], in_=lamp[:, 0:2], func=AF.Exp), reads=["lamp"], writes=["lamp"])
        P.add("dve", lambda e: e.tensor_tensor(out=cda[:, c0 + 2:c0 + 3], in0=lamp[:, 0:1], in1=lamp[:, 1:2], op=ALU.subtract),
              reads=["lamp"], writes=[("cda", j)])
        P.add("dve", lambda e: e.tensor_scalar(out=cda[:, c0 + 2:c0 + 3], in0=cda[:, c0 + 2:c0 + 3], scalar1=lam_init, scalar2=None, op0=ALU.add),
              reads=[("cda", j)], writes=[("cda", j)])
        P.add("dve", lambda e: e.tensor_scalar(out=cda[:, c0 + 3:c0 + 4], in0=cda[:, c0 + 2:c0 + 3], scalar1=-1.0, scalar2=None, op0=ALU.mult),
              reads=[("cda", j)], writes=[("cda", j)])
        P.add("dve", lambda e: e.tensor_scalar(out=cda[:, c0:c0 + 1], in0=cda[:, c0:c0 + 1], scalar1=0.125, scalar2=None, op0=ALU.mult),
              reads=[("cda", j)], writes=[("cda", j)])
        P.add("dve", lambda e: e.tensor_scalar(out=cda[:, c0 + 4:c0 + 5], in0=cda[:, c0 + 4:c0 + 5], scalar1=1.0 - lam_init, scalar2=None, op0=ALU.mult),
              reads=[("cda", j)], writes=[("cda", j)])

    def setup_da_static(self):
        P = self.P
        onesbd, negmask, vaug = self.onesbd, self.negmask, self.vaug
        scr = self.scr_f
        P.add("pool", lambda e: e.memset(scr[:], 0.0), writes=[("xs", 0)])
        P.add("pool", lambda e: e.memset(scr[0:64, 0:64], 1.0 / 64), reads=[("xs", 0)], writes=[("xs", 0)])
        P.add("pool", lambda e: e.memset(scr[64:128, 64:128], 1.0 / 64), reads=[("xs", 0)], writes=[("xs", 0)])
        P.add("dve", lambda e: e.tensor_copy(out=onesbd[:], in_=scr[:]), reads=[("xs", 0)], writes=["onesbd"])
        scr2 = self.scr_f2
        P.add("pool", lambda e: e.memset(scr2[:], 0.0), writes=[("xs", 0)])
        P.add("pool", lambda e: e.affine_select(out=scr2[:], in_=scr2[:], pattern=[[1, 128]], compare_op=ALU.is_ge,
                                                fill=-30000.0, base=0, channel_multiplier=-1),
              reads=[("xs", 0)], writes=[("xs", 0)])
        P.add("dve", lambda e: e.tensor_copy(out=negmask[:], in_=scr2[:]), reads=[("xs", 0)], writes=["negmask"])

    def diffattn(self, l):
        P = self.P
        j = l // 2
        X, xnT, U, ring = self.X, self.xnT, self.U, self.ring
        qT, kTz, vaug, pT, accS = self.qT, self.kTz, self.vaug, self.pT, self.accS
        qraw, qsq, qrs = self.qraw, self.qsq, self.qrs
        cda, onesbd, negmask, identb = self.cda, self.onesbd, self.negmask, self.identb
        osb, obf, fsm = self.osb, self.obf, self.fsm
        PF, PB = self.PF, self.PB
        c0 = 5 * j
        win = self.d["da_w_in"][j].rearrange("(kc p) f -> p kc f", p=128)
        self.rmsnorm(8 * l)
        self.arena_barrier()
        P.add("pool", lambda e: e.memset(vaug[:, :, :, 128:130], 1.0), reads=[("xnT", 0, 0)], writes=["vones"])
        P.add("pool", lambda e: e.memset(kTz[64:128, 0, :, :], 0.0), reads=[("xnT", 0, 0)], writes=["kTz"])
        P.add("pool", lambda e: e.memset(kTz[0:64, 1, :, :], 0.0), reads=[("xnT", 0, 0)], writes=["kTz"])
        cnt = {"pf": 0, "nb": 0, "pt": 0, "fin": 0}
        def record_proj(hp):
            sq = self.ws.get(win[:, :, hp * 256:(hp + 1) * 256], 256)
            sk = self.ws.get(win[:, :, 1024 + hp * 256:1024 + (hp + 1) * 256], 256)
            sv = self.ws.get(win[:, :, 2048 + hp * 256:2048 + (hp + 1) * 256], 256)
            jobs = [(which, slab, gcolq, hh, tt) for which, slab, gcolq in (("q", sq, c0), ("k", sk, c0 + 1))
                    for hh in range(2) for tt in range(4)]
            jstate = {}

            def emit_proj(i):
                which, slab, gcolq, hh, tt = jobs[i]
                slot, wkey, _ = slab
                bank = cnt["pf"] % 2
                cnt["pf"] += 1
                nb = cnt["nb"] % 2
                cnt["nb"] += 1
                jstate[i] = (bank, nb)
                pf = PF[bank]
                for kc in range(8):
                    P.add("pe", lambda e, pf=pf, slot=slot, kc=kc, hh=hh, tt=tt: e.matmul(
                        pf[:, :], slot[:, kc, hh * 128:(hh + 1) * 128], xnT[:, kc, tt * 512:(tt + 1) * 512],
                        start=(kc == 0), stop=(kc == 7)),
                        reads=[wkey] + [("xnT", t, kc) for t in range(4 * tt, 4 * tt + 4)], writes=[("pf", bank)])

            def emit_norm(i):
                which, slab, gcolq, hh, tt = jobs[i]
                bank, nb = jstate[i]
                pf = PF[bank]
                P.add("act", lambda e, pf=pf, nb=nb: e.activation(out=qraw[:, nb, :], in_=pf[:, :], func=AF.Copy),
                      reads=[("pf", bank)], writes=[("qraw", nb)])
                P.add("dve", lambda e, nb=nb: e.tensor_tensor(out=qsq[:, nb, :], in0=qraw[:, nb, :], in1=qraw[:, nb, :], op=ALU.mult),
                      reads=[("qraw", nb)], writes=[("qsq", nb)])
                mbank = 2 + nb
                pm = PF[mbank]
                P.add("pe", lambda e, pm=pm, nb=nb: e.matmul(pm[:, :], onesbd[:, :], qsq[:, nb, :], start=True, stop=True),
                      reads=[("qsq", nb), "onesbd"], writes=[("pf", mbank)])
                P.add("act", lambda e, pm=pm, nb=nb: e.activation(out=qrs[:, 0, :], in_=pm[:, :], func=AF.Ln, bias=self.epsc[:, 0:1], scale=1.0),
                      reads=[("pf", mbank), "epsc"], writes=[("qrs", 0)])
                P.add("act", lambda e, nb=nb: e.activation(out=qrs[:, 0, :], in_=qrs[:, 0, :], func=AF.Exp, scale=-0.5),
                      reads=[("qrs", 0)], writes=[("qrs", 0)])
                if which == "q":
                    P.add("dve", lambda e, nb=nb, hh=hh, tt=tt, gcolq=gcolq: e.scalar_tensor_tensor(
                        out=qT[:, hh, tt * 512:(tt + 1) * 512], in0=qraw[:, nb, :], scalar=cda[:, gcolq:gcolq + 1],
                        in1=qrs[:, 0, :], op0=ALU.mult, op1=ALU.mult),
                        reads=[("qraw", nb), ("qrs", 0), ("cda", j)], writes=[("qT", hh, tt)])
                else:
                    for c in range(2):
                        pl, ph = 64 * c, 64 * c + 64
                        P.add("dve", lambda e, nb=nb, hh=hh, tt=tt, gcolq=gcolq, c=c, pl=pl, ph=ph: e.scalar_tensor_tensor(
                            out=kTz[pl:ph, c, hh, tt * 512:(tt + 1) * 512], in0=qraw[pl:ph, nb, :], scalar=cda[pl:ph, gcolq:gcolq + 1],
                            in1=qrs[pl:ph, 0, :], op0=ALU.mult, op1=ALU.mult),
                            reads=[("qraw", nb), ("qrs", 0), ("cda", j), "kTz"], writes=[("kT", hh, tt, c)])

            emit_proj(0)
            for i in range(len(jobs)):
                if i + 1 < len(jobs):
                    emit_proj(i + 1)
                emit_norm(i)
            slot, wkey, _ = sv
            for t in range(NT):
                bank = cnt["pf"] % 2
                cnt["pf"] += 1
                pf = PF[bank]
                for kc in range(8):
                    P.add("pe", lambda e, pf=pf, slot=slot, kc=kc, t=t: e.matmul(
                        pf[:, 0:256], xnT[:, kc, t * 128:(t + 1) * 128], slot[:, kc, 0:256],
                        start=(kc == 0), stop=(kc == 7)),
                        reads=[wkey, ("xnT", t, kc)], writes=[("pf", bank)])
                P.add("act", lambda e, pf=pf, t=t: e.activation(
                    out=vaug[:, t, :, 0:128], in_=pf[:, 0:256].rearrange("p (h d) -> p h d", h=2), func=AF.Copy),
                    reads=[("pf", bank), "vones"], writes=[("va", t)])
            return sq, sk, sv

        P.begin_capture()
        nxt = record_proj(0)
        P.replay([P.end_capture()])
        for sl in nxt:
            self.ws.done(sl[2])
        for hp in range(4):
            ATT, FIN = [], []
            for hh in range(2):
                h = 2 * hp + hh
                for qt in range(4):
                    P.begin_capture()
                    first_in_bank = {}
                    units = [(c, kb) for c in range(2) for kb in range(4 * qt + 4)]
                    ubank = {}

                    def emit_scores(u):
                        c, kb = units[u]
                        r = kb - 4 * qt
                        col0 = max(r, 0) * 128
                        bank = cnt["pf"] % 2
                        cnt["pf"] += 1
                        ubank[u] = bank
                        ps = PF[bank]
                        krd = [("kT", hh, kb // 4, c)]
                        qrd = [("qT", hh, qt)]
                        if r >= 0:
                            P.add("pe", lambda e, ps=ps, c=c, hh=hh, kb=kb, qt=qt, col0=col0: e.matmul(
                                ps[:, col0:col0 + 128], kTz[:, c, hh, kb * 128:(kb + 1) * 128],
                                qT[:, hh, qt * 512 + col0:qt * 512 + col0 + 128], start=True, stop=False,
                                skip_group_check=True),
                                reads=krd + qrd, writes=[("pf", bank)])
                            P.add("pe", lambda e, ps=ps, col0=col0: e.matmul(
                                ps[:, col0:col0 + 128], identb[:, :], negmask[:, :], start=False, stop=True,
                                skip_group_check=True),
                                reads=["identb", "negmask"], writes=[("pf", bank)])
                            if col0 + 128 < 512:
                                P.add("pe", lambda e, ps=ps, c=c, hh=hh, kb=kb, qt=qt, col0=col0: e.matmul(
                                    ps[:, col0 + 128:512], kTz[:, c, hh, kb * 128:(kb + 1) * 128],
                                    qT[:, hh, qt * 512 + col0 + 128:qt * 512 + 512], start=True, stop=True,
                                    skip_group_check=True),
                                    reads=krd + qrd, writes=[("pf", bank)])
                        else:
                            P.add("pe", lambda e, ps=ps, c=c, hh=hh, kb=kb, qt=qt: e.matmul(
                                ps[:, :], kTz[:, c, hh, kb * 128:(kb + 1) * 128],
                                qT[:, hh, qt * 512:qt * 512 + 512], start=True, stop=True, skip_group_check=True),
                                reads=krd + qrd, writes=[("pf", bank)])

                    def emit_exp_pv(u):
                        c, kb = units[u]
                        r = kb - 4 * qt
                        col0 = max(r, 0) * 128
                        bank = ubank[u]
                        ps = PF[bank]
                        pi = cnt["pt"] % 3
                        cnt["pt"] += 1
                        P.add("act", lambda e, ps=ps, pi=pi, col0=col0: e.activation(
                            out=pT[:, pi, col0:512], in_=ps[:, col0:512], func=AF.Exp),
                            reads=[("pf", bank)], writes=[("pT", pi)])
                        for rr in range(max(r, 0), 4):
                            abank = 2 + 2 * c + rr // 2
                            off = (rr % 2) * 256
                            st = abank not in first_in_bank
                            first_in_bank[abank] = True
                            P.add("pe", lambda e, abank=abank, off=off, pi=pi, rr=rr, kb=kb, st=st, hh=hh, qt=qt: e.matmul(
                                PF[abank][:, off:off + 129], pT[:, pi, rr * 128:(rr + 1) * 128], vaug[:, kb, hh, 0:129],
                                start=st, stop=(kb == 4 * qt + rr), skip_group_check=True),
                                reads=[("pT", pi), ("va", kb), "vones"], writes=[("pf", abank)])

                    emit_scores(0)
                    for u in range(len(units)):
                        if u + 1 < len(units):
                            emit_scores(u + 1)
                        emit_exp_pv(u)
                    for b in range(4):
                        srcv = PF[2 + b][:, :].rearrange("p (s w) -> p s w", s=2)[:, :, 0:129]
                        dstv = accS[:, b, :].rearrange("p (s w) -> p s w", s=2)[:, :, 0:129]
                        if b % 2 == 0:
                            P.add("act", lambda e, srcv=srcv, dstv=dstv: e.activation(out=dstv, in_=srcv, func=AF.Copy),
                                  reads=[("pf", 2 + b)], writes=[("accS", b)])
                        else:
                            P.add("dve", lambda e, srcv=srcv, dstv=dstv: e.tensor_copy(out=dstv, in_=srcv),
                                  reads=[("pf", 2 + b)], writes=[("accS", b)])
                    ATT.append(P.end_capture())
                    P.begin_capture()
                    fi = cnt["fin"] % 2
                    cnt["fin"] += 1
                    AK = [("accS", b) for b in range(4)]
                    FK = ("fsm", fi)
                    slots = accS[:, :, :].rearrange("p b (s w) -> p (b s) w", s=2)
                    P.add("dve", lambda e, fi=fi, slots=slots: e.reciprocal(out=fsm[:, fi, 0:8], in_=slots[:, :, 128]),
                          reads=AK, writes=[FK])
                    P.add("dve", lambda e, fi=fi: e.tensor_scalar(out=fsm[:, fi, 8:12], in0=fsm[:, fi, 4:8], scalar1=cda[:, c0 + 3:c0 + 4], scalar2=None, op0=ALU.mult),
                          reads=[FK, ("cda", j)], writes=[FK])
                    rb0 = fsm[:, fi, 0:4].unsqueeze(2).to_broadcast([128, 4, 128])
                    P.add("dve", lambda e, fi=fi, slots=slots, rb0=rb0: e.tensor_tensor(out=osb[:, fi, :, :], in0=slots[:, 0:4, 0:128], in1=rb0, op=ALU.mult),
                          reads=AK + [FK], writes=[("osb", fi, rr) for rr in range(4)])
                    for rr in range(4):
                        P.add("dve", lambda e, fi=fi, rr=rr, slots=slots: e.scalar_tensor_tensor(
                            out=osb[:, fi, rr, :], in0=slots[:, 4 + rr, 0:128], scalar=fsm[:, fi, 8 + rr:9 + rr], in1=osb[:, fi, rr, :],
                            op0=ALU.mult, op1=ALU.add),
                            reads=AK + [FK, ("osb", fi, rr)], writes=[("osb", fi, rr)])
                    for rr in range(4):
                        P.add("act", lambda e, fi=fi, rr=rr: e.activation(out=obf[:, fi, rr, :], in_=osb[:, fi, rr, :], func=AF.Square,
                                                                          accum_out=fsm[:, fi, 12 + rr:13 + rr]),
                              reads=[("osb", fi, rr)], writes=[("obf", fi, rr), ("fss", fi, rr)])
                    P.add("act", lambda e, fi=fi: e.activation(out=fsm[:, fi, 16:20], in_=fsm[:, fi, 12:16], func=AF.Ln,
                                                               bias=self.epsc[:, 0:1], scale=1.0 / 128),
                          reads=[("fss", fi, rr) for rr in range(4)] + ["epsc"], writes=[("frs", fi)])
                    P.add("act", lambda e, fi=fi: e.activation(out=fsm[:, fi, 16:20], in_=fsm[:, fi, 16:20], func=AF.Exp, scale=-0.5),
                          reads=[("frs", fi)], writes=[("frs", fi)])
                    rbs = fsm[:, fi, 16:20].unsqueeze(2).to_broadcast([128, 4, 128])
                    P.add("dve", lambda e, fi=fi, rbs=rbs: e.tensor_tensor(out=obf[:, fi, :, :], in0=osb[:, fi, :, :], in1=rbs, op=ALU.mult),
                          reads=[("osb", fi, rr) for rr in range(4)] + [("frs", fi)] + [("obf", fi, rr) for rr in range(4)],
                          writes=[("obf", fi, rr) for rr in range(4)])
                    pb = PB[fi]
                    for rr in range(4):
                        P.add("pe", lambda e, pb=pb, fi=fi, rr=rr: e.transpose(out=pb[:, rr * 128:(rr + 1) * 128], in_=obf[:, fi, rr, :], identity=identb[:]),
                              reads=[("obf", fi, rr), "identb"], writes=[("pb", fi)])
                    P.add("dve", lambda e, pb=pb, hh=hh, qt=qt: e.tensor_scalar(
                        out=U[:, hh, qt * 512:(qt + 1) * 512], in0=pb[:, 0:512], scalar1=cda[:, c0 + 4:c0 + 5], scalar2=None, op0=ALU.mult),
                        reads=[("pb", fi), ("cda", j)], writes=[("U", hh, qt)])
                    FIN.append(P.end_capture())
            P.replay([ATT[0]])
            for i in range(1, 8):
                P.replay([ATT[i], FIN[i - 1]])
            wsrc = self.d["da_w_out"][j][hp * 256:(hp + 1) * 256, :].rearrange("(kc p) f -> p kc f", p=128)
            so = self.ws.get(wsrc, (2, 1024))
            wv = so[0]
            P.begin_capture()
            for t in range(NT):
                for dh in range(2):
                    bank = 4 + (cnt["pf"] % 2)
                    cnt["pf"] += 1
                    pf = PF[bank]
                    for hc in range(2):
                        P.add("pe", lambda e, pf=pf, wv=wv, hc=hc, t=t, dh=dh: e.matmul(
                            pf[:, :], U[:, hc, t * 128:(t + 1) * 128], wv[:, hc, dh * 512:(dh + 1) * 512], start=(hc == 0), stop=(hc == 1)),
                            reads=[so[1], ("U", hc, t // 4)], writes=[("pf", bank)])
                    P.add("dve", lambda e, pf=pf, t=t, dh=dh: e.tensor_tensor(
                        out=X[:, t, dh * 512:(dh + 1) * 512], in0=X[:, t, dh * 512:(dh + 1) * 512], in1=pf[:, :], op=ALU.add),
                        reads=[("pf", bank), ("x", t)], writes=[("x", t)])
            tail = FIN[7] + P.end_capture()
            if hp < 3:
                P.begin_capture()
                nxt = record_proj(hp + 1)
                P.replay([tail, P.end_capture()])
                for sl in nxt:
                    self.ws.done(sl[2])
            else:
                P.replay([tail])
            self.ws.done(so[2])

    def setup_gd_static(self):
        P = self.P
        trif, sellast, onesf, masks, onecol = self.trif, self.sellast, self.onesf, self.masks, self.onecol
        P.add("pool", lambda e: e.memset(onesf[:], 1.0), writes=["onesf"])
        P.add("pool", lambda e: e.memset(onecol[:], 1.0), writes=["onecol"])
        P.add("pool", lambda e: e.memset(trif[:], 1.0), writes=["trif"])
        P.add("pool", lambda e: e.affine_select(out=trif[:], in_=trif[:], pattern=[[1, 128]], compare_op=ALU.is_ge,
                                                fill=0.0, base=0, channel_multiplier=-1), reads=["trif"], writes=["trif"])
        P.add("pool", lambda e: e.memset(sellast[:], 1.0), writes=["sellast"])
        P.add("pool", lambda e: e.affine_select(out=sellast[:], in_=sellast[:], pattern=[[0, 128]], compare_op=ALU.is_ge,
                                                fill=0.0, base=-127, channel_multiplier=1), reads=["sellast"], writes=["sellast"])
        P.add("pool", lambda e: e.memset(masks[:], 0.0), writes=["masks"])
        P.add("pool", lambda e: e.affine_select(out=masks[:, 0:128], in_=masks[:, 0:128], pattern=[[1, 128]], compare_op=ALU.is_ge,
                                                fill=-30000.0, base=-1, channel_multiplier=-1), reads=["masks"], writes=["masks"])
        P.add("pool", lambda e: e.affine_select(out=masks[:, 128:256], in_=masks[:, 128:256], pattern=[[1, 128]], compare_op=ALU.is_ge,
                                                fill=-30000.0, base=0, channel_multiplier=-1), reads=["masks"], writes=["masks"])

    def setup_gd_levelmasks(self):
        P = self.P
        NEGM = self.NEGM
        scrA = lambda nb: self.xs_f[0:nb, 0, 0:128]
        scrC = lambda nb: self.xs_f[0:nb, 0, 128:256]
        K0 = [("xs", 0)]
        for k in range(7):
            B, half = 2 ** (k + 1), 2 ** k
            nb = 128 // B
            A, C = scrA(nb), scrC(nb)
            P.add("pool", lambda e, nb=nb: e.memset(self.xs_f[0:nb, 0, 0:256], 1.0), writes=K0)
            P.add("pool", lambda e, A=A, B=B, half=half: e.affine_select(out=A, in_=A, pattern=[[1, 128]], compare_op=ALU.is_ge, fill=0.0,
                                                                       base=-half, channel_multiplier=-B), reads=K0, writes=K0)
            P.add("pool", lambda e, A=A, B=B: e.affine_select(out=A, in_=A, pattern=[[-1, 128]], compare_op=ALU.is_ge, fill=0.0,
                                                             base=B - 1, channel_multiplier=B), reads=K0, writes=K0)
            P.add("pool", lambda e, C=C, B=B: e.affine_select(out=C, in_=C, pattern=[[1, 128]], compare_op=ALU.is_ge, fill=0.0,
                                                             base=0, channel_multiplier=-B), reads=K0, writes=K0)
            P.add("pool", lambda e, C=C, B=B, half=half: e.affine_select(out=C, in_=C, pattern=[[-1, 128]], compare_op=ALU.is_ge, fill=0.0,
                                                                       base=half - 1, channel_multiplier=B), reads=K0, writes=K0)
            pf = self.PF[1]
            P.add("pe", lambda e, pf=pf, A=A, C=C: e.matmul(pf[:, 0:128], A, C, start=True, stop=True, skip_group_check=True),
                  reads=K0, writes=[("pf", 1)])
            P.add("pe", lambda e, pf=pf, A=A, C=C: e.matmul(pf[:, 128:256], C, A, start=True, stop=True, skip_group_check=True),
                  reads=K0, writes=[("pf", 1)])
            P.add("act", lambda e, pf=pf, k=k: e.activation(out=NEGM[:, k, :], in_=pf[:, 0:256], func=AF.Copy),
                  reads=[("pf", 1)], writes=["NEGM"])

    def setup_gd_consts(self, j):
        P = self.P
        d = self.d
        convw, gdc, identf = self.convw, self.gdc, self.identf
        crow96 = self.xs_f[0:96, 1, 0:128]
        rows = d["gd_conv_w"][j].rearrange("k (c p) -> (k c) p", p=128)
        P.add("sp", lambda e: e.dma_start(out=crow96, in_=rows), writes=[("xs", 1)], dma="gcw%d" % j)
        pf = self.PF[0]
        P.add("pe", lambda e: e.transpose(out=pf[:, 0:96], in_=crow96, identity=identf[0:96, 0:96]),
              reads=[("xs", 1), "identf"], writes=[("pf", 0)])
        P.add("dve", lambda e: e.tensor_copy(out=convw[:, j, :], in_=pf[:, 0:96]), reads=[("pf", 0)], writes=[("convw", j)])
        col = lambda ap: ap.rearrange("(p o) -> p o", o=1)
        P.add("sp", lambda e: e.dma_start(out=gdc[:, j, 0:1], in_=col(d["gd_out_norm"][j])), writes=[("gdc", j)], dma="gon%d" % j)
        gsm = self.gsm
        P.add("sp", lambda e: e.dma_start(out=gsm[:, j, 0:8], in_=d["gd_a_log"][j:j + 1, :].partition_broadcast(128)),
              writes=[("gsm", j)], dma="gal%d" % j)
        P.add("sp", lambda e: e.dma_start(out=gsm[:, j, 8:16], in_=d["gd_dt_bias"][j:j + 1, :].partition_broadcast(128)),
              writes=[("gsm", j)], dma="gdt%d" % j)
        P.add("act", lambda e: e.activation(out=gsm[:, j, 0:8], in_=gsm[:, j, 0:8], func=AF.Exp), reads=[("gsm", j)], writes=[("gsm", j)])
        P.add("dve", lambda e: e.tensor_scalar(out=gsm[:, j, 0:8], in0=gsm[:, j, 0:8], scalar1=-1.0, scalar2=None, op0=ALU.mult),
              reads=[("gsm", j)], writes=[("gsm", j)])

    def gdn(self, l):
        P = self.P
        j = l // 2
        g = self.g
        X, xnT, U, ring = self.X, self.xnT, self.U, self.ring
        identb, identf = self.identb, self.identf
        PF, PB = self.PF, self.PB
        epsc = self.epsc
        DKS = 128.0 ** -0.5
        win = self.d["gd_w_in"][j].rearrange("(kc p) f -> p kc f", p=128)
        self.rmsnorm(8 * l)
        self.arena_barrier()
        XN0 = [("xnT", 0, 0)]
        cnt = {"pf": 0, "rb": 0, "sq": 0}
        BA, BETA, GC, EG, GLB, EGL, EKD = (g[k] for k in ("BA", "BETA", "GC", "EG", "GLB", "EGL", "EKD"))
        flat = lambda v: v.rearrange("p a b -> p (a b)")

        sba = self.ws.get(win[:, :, 4096:4112], 16)
        slot_ba, wkey_ba, _ = sba
        pf_ba = PF[0]
        for t in range(NT):
            for kc in range(8):
                P.add("pe", lambda e, t=t, kc=kc: e.matmul(pf_ba[:, t * 16:(t + 1) * 16], xnT[:, kc, t * 128:(t + 1) * 128],
                                                          slot_ba[:, kc, 0:16], start=(kc == 0), stop=(kc == 7),
                                                          skip_group_check=True),
                      reads=[wkey_ba, ("xnT", t, kc)], writes=[("pf", 0)])
        self.ws.done(sba[2])
        P.add("dve", lambda e: e.tensor_copy(out=flat(BA), in_=pf_ba[:, 0:256]), reads=[("pf", 0)] + XN0, writes=["BA"])
        P.add("act", lambda e: e.activation(out=BETA, in_=BA[:, :, 0:8], func=AF.Exp, scale=-1.0), reads=["BA"] + XN0, writes=["BETA"])
        P.add("act", lambda e: e.activation(out=BETA, in_=BETA, func=AF.Ln, bias=1.0), reads=["BETA"], writes=["BETA"])
        P.add("act", lambda e: e.activation(out=BETA, in_=BETA, func=AF.Exp, scale=-1.0), reads=["BETA"], writes=["BETA"])
        gsm = self.gsm
        for t in range(NT):
            P.add("dve", lambda e, t=t: e.tensor_tensor(out=GC[:, t, :], in0=BA[:, t, 8:16], in1=gsm[:, j, 8:16], op=ALU.add),
                  reads=["BA", ("gsm", j)] + XN0, writes=["GC"])
        P.add("act", lambda e: e.activation(out=GC, in_=GC, func=AF.Exp), reads=["GC"], writes=["GC"])
        P.add("act", lambda e: e.activation(out=GC, in_=GC, func=AF.Ln, bias=1.0), reads=["GC"], writes=["GC"])
        for t in range(NT):
            P.add("dve", lambda e, t=t: e.tensor_tensor(out=GLB[:, t, :], in0=GC[:, t, :], in1=gsm[:, j, 0:8], op=ALU.mult),
                  reads=["GC", "BETA", ("gsm", j)] + XN0, writes=["BA"])
        pf1 = PF[1]
        for t in range(NT):
            P.add("pe", lambda e, t=t: e.matmul(pf1[:, t * 8:(t + 1) * 8], self.trif[:, :], GLB[:, t, :], start=True, stop=True,
                                                skip_group_check=True),
                  reads=["BA", "trif"], writes=[("pf", 1)])
        P.add("dve", lambda e: e.tensor_copy(out=flat(GC), in_=pf1[:, 0:128]), reads=[("pf", 1)], writes=["GC"])
        P.add("act", lambda e: e.activation(out=EG, in_=GC, func=AF.Exp), reads=["GC"] + XN0, writes=["EG"])
        for t in range(NT):
            P.add("pe", lambda e, t=t: e.matmul(pf1[:, 128 + t * 8:128 + (t + 1) * 8], self.sellast[:, :], GC[:, t, :], start=True, stop=True,
                                                skip_group_check=True),
                  reads=["GC", "sellast"], writes=[("pf", 1)])
        P.add("dve", lambda e: e.tensor_copy(out=flat(GLB), in_=pf1[:, 128:256]), reads=[("pf", 1)], writes=["BA"])
        P.add("act", lambda e: e.activation(out=EGL, in_=GLB, func=AF.Exp), reads=["BA"] + XN0, writes=["EGL"])
        P.add("dve", lambda e: e.tensor_tensor(out=EKD, in0=GLB, in1=GC, op=ALU.subtract), reads=["BA", "GC"] + XN0, writes=["EKD"])
        P.add("act", lambda e: e.activation(out=EKD, in_=EKD, func=AF.Exp), reads=["EKD"], writes=["EKD"])

        raw, acc, halo, sq, Sf, Sb = (g[k] for k in ("raw", "acc", "halo", "sq", "Sf", "Sb"))
        convw = self.convw
        NEGM = self.NEGM
        def capture_pair(hp):
            slabs = {}
            for wi, which in enumerate(("q", "k", "v", "z")):
                slabs[which] = self.ws.get(win[:, :, wi * 1024 + hp * 256: wi * 1024 + (hp + 1) * 256], 256)
            wsrc = self.d["gd_w_out"][j][hp * 256:(hp + 1) * 256, :].rearrange("(kc p) f -> p kc f", p=128)
            so = self.ws.get(wsrc, (2, 1024))
            FE, PREP, SCAN, OUT = {}, {}, {}, {}
            for gi in range(4):
                gp = gi % 2
                sT, zs, R, SC = g["sT"][gp], g["zs"][gp], g["R"][gp], g["SC"][gp]
                P.begin_capture()
                if gi == 0:
                    P.add("pool", lambda e: e.memset(flat(halo), 0.0), reads=XN0, writes=[("halo", i) for i in range(6)])
                for wi, which in enumerate(("k", "q", "v")):
                    slot, wkey, _ = slabs[which]
                    cbase = {"q": 0, "k": 8, "v": 16}[which]
                    for hh in range(2):
                        h = 2 * hp + hh
                        pf = PF[0]
                        rb = cnt["rb"] % 2
                        cnt["rb"] += 1
                        hi = wi * 2 + hh
                        P.add("dve", lambda e, rb=rb, hi=hi: e.tensor_copy(out=raw[:, rb, 0:3], in_=halo[:, hi, 0:3]),
                              reads=[("halo", hi)], writes=[("raw", rb)])
                        P.begin_atomic()
                        for kc in range(8):
                            P.add("pe", lambda e, pf=pf, slot=slot, kc=kc, hh=hh, gi=gi: e.matmul(
                                pf[:, :], slot[:, kc, hh * 128:(hh + 1) * 128], xnT[:, kc, gi * 512:(gi + 1) * 512],
                                start=(kc == 0), stop=(kc == 7)),
                                reads=[wkey] + [("xnT", t, kc) for t in range(4 * gi, 4 * gi + 4)], writes=[("pf", 0)])
                        P.add("act", lambda e, pf=pf, rb=rb: e.activation(out=raw[:, rb, 3:515], in_=pf[:, :], func=AF.Copy),
                              reads=[("pf", 0), ("raw", rb)], writes=[("raw", rb)])
                        P.end_atomic()
                        if gi < 3:
                            P.add("dve", lambda e, rb=rb, hi=hi: e.tensor_copy(out=halo[:, hi, 0:3], in_=raw[:, rb, 512:515]),
                                  reads=[("raw", rb)], writes=[("halo", hi)])
                        cc = cbase + h
                        P.add("dve", lambda e, rb=rb, cc=cc: e.tensor_scalar(
                            out=acc[:, 0, :], in0=raw[:, rb, 0:512], scalar1=convw[:, j, cc:cc + 1], scalar2=None, op0=ALU.mult),
                            reads=[("raw", rb), ("convw", j)], writes=[("acc", 0)])
                        for tap in range(1, 4):
                            P.add("dve", lambda e, rb=rb, cc=cc, tap=tap: e.scalar_tensor_tensor(
                                out=acc[:, 0, :], in0=raw[:, rb, tap:tap + 512], scalar=convw[:, j, tap * 24 + cc:tap * 24 + cc + 1],
                                in1=acc[:, 0, :], op0=ALU.mult, op1=ALU.add),
                                reads=[("raw", rb), ("acc", 0), ("convw", j)], writes=[("acc", 0)])
                        sgb = raw[:, rb, 0:512]
                        P.add("act", lambda e, sgb=sgb: e.activation(out=sgb, in_=acc[:, 0, :], func=AF.Exp, scale=-1.0),
                              reads=[("acc", 0), ("raw", rb), ("halo", hi)], writes=[("raw", rb)])
                        P.add("act", lambda e, sgb=sgb: e.activation(out=sgb, in_=sgb, func=AF.Ln, bias=1.0), reads=[("raw", rb)], writes=[("raw", rb)])
                        P.add("act", lambda e, sgb=sgb: e.activation(out=sgb, in_=sgb, func=AF.Exp, scale=-1.0), reads=[("raw", rb)], writes=[("raw", rb)])
                        P.add("dve", lambda e, sgb=sgb, hh=hh, wi=wi, sT=sT: e.tensor_tensor(
                            out=sT[:, hh, :, wi * 128:(wi + 1) * 128], in0=acc[:, 0, :].rearrange("p (a b) -> p a b", a=4),
                            in1=sgb.rearrange("p (a b) -> p a b", a=4), op=ALU.mult),
                            reads=[("acc", 0), ("raw", rb)], writes=[("sT", gp, hh, wi)])
                        if which in ("k", "q"):
                            sb_ = cnt["sq"] % 2
                            cnt["sq"] += 1
                            P.add("pool", lambda e, hh=hh, wi=wi, sb_=sb_, sT=sT: e.tensor_tensor(
                                out=sq[:, sb_, :].rearrange("p (a b) -> p a b", a=4), in0=sT[:, hh, :, wi * 128:(wi + 1) * 128],
                                in1=sT[:, hh, :, wi * 128:(wi + 1) * 128], op=ALU.mult),
                                reads=[("sT", gp, hh, wi)], writes=[("sq", sb_)])
                            for tl in range(4):
                                colr = tl * 4 + wi * 2 + hh
                                P.add("pe", lambda e, sb_=sb_, tl=tl, colr=colr: e.matmul(
                                    PF[1][:, 384 + colr:384 + colr + 1], sq[:, sb_, tl * 128:(tl + 1) * 128], self.onecol[:, 0:1],
                                    start=True, stop=True, skip_group_check=True),
                                    reads=[("sq", sb_), "onecol"], writes=[("pf", 1)])
                slot, wkey, _ = slabs["z"]
                for tl in range(4):
                    t = 4 * gi + tl
                    pf = PF[1]
                    for kc in range(8):
                        P.add("pe", lambda e, pf=pf, slot=slot, kc=kc, t=t: e.matmul(
                            pf[:, 0:256], xnT[:, kc, t * 128:(t + 1) * 128], slot[:, kc, 0:256], start=(kc == 0), stop=(kc == 7),
                            skip_group_check=True),
                            reads=[wkey, ("xnT", t, kc)], writes=[("pf", 1)])
                    zt = acc[:, 0, 0:256]
                    zr = acc[:, 0, 256:512]
                    P.add("act", lambda e, pf=pf, zt=zt: e.activation(out=zt, in_=pf[:, 0:256], func=AF.Exp, scale=-1.0),
                          reads=[("pf", 1)], writes=[("acc", 0)])
                    P.add("act", lambda e, pf=pf, zr=zr: e.activation(out=zr, in_=pf[:, 0:256], func=AF.Copy),
                          reads=[("pf", 1), ("acc", 0)], writes=[("acc", 0)])
                    P.add("act", lambda e, zt=zt: e.activation(out=zt, in_=zt, func=AF.Ln, bias=1.0), reads=[("acc", 0)], writes=[("acc", 0)])
                    P.add("act", lambda e, zt=zt: e.activation(out=zt, in_=zt, func=AF.Exp, scale=-1.0), reads=[("acc", 0)], writes=[("acc", 0)])
                    P.add("dve", lambda e, zt=zt, zr=zr, tl=tl, zs=zs: e.tensor_tensor(out=zs[:, tl, :], in0=zr, in1=zt, op=ALU.mult),
                          reads=[("acc", 0)], writes=[("zs", gp, tl)])
                Rk = ("R", gp)
                P.add("act", lambda e, R=R: e.activation(out=flat(R), in_=PF[1][:, 384:400], func=AF.Ln, bias=epsc[:, 0:1], scale=1.0),
                      reads=[("pf", 1), "epsc"], writes=[Rk])
                P.add("act", lambda e, R=R: e.activation(out=flat(R), in_=flat(R), func=AF.Exp, scale=-0.5), reads=[Rk], writes=[Rk])
                hs = slice(2 * hp, 2 * hp + 2)
                ts = slice(4 * gi, 4 * gi + 4)
                rk, rq = R[:, :, 0:2], R[:, :, 2:4]
                scv = lambda q_, SC=SC: SC[:, q_, :].rearrange("p (a b) -> p a b", a=4)
                T1, CKBG, CKD, CQ, UL, UA, BIAS, LN = (scv(i) for i in range(8))
                bt, egs, ekds, gcs = BETA[:, ts, hs], EG[:, ts, hs], EKD[:, ts, hs], GC[:, ts, hs]
                sk = lambda i: ("SC", gp, i)
                P.add("dve", lambda e, T1=T1, rk=rk, bt=bt: e.tensor_tensor(out=T1, in0=rk, in1=bt, op=ALU.mult), reads=[Rk, "BETA"], writes=[sk(0)])
                P.add("dve", lambda e, CKBG=CKBG, T1=T1, egs=egs: e.tensor_tensor(out=CKBG, in0=T1, in1=egs, op=ALU.mult), reads=[sk(0), "EG"], writes=[sk(1)])
                P.add("dve", lambda e, CKD=CKD, rk=rk, ekds=ekds: e.tensor_tensor(out=CKD, in0=rk, in1=ekds, op=ALU.mult), reads=[Rk, "EKD"], writes=[sk(2)])
                P.add("dve", lambda e, CQ=CQ, rq=rq, egs=egs: e.scalar_tensor_tensor(out=CQ, in0=rq, scalar=DKS, in1=egs, op0=ALU.mult, op1=ALU.mult),
                      reads=[Rk, "EG"], writes=[sk(3)])
                P.add("act", lambda e, UL=UL, T1=T1: e.activation(out=UL, in_=T1, func=AF.Ln), reads=[sk(0)], writes=[sk(4)])
                P.add("dve", lambda e, UL=UL, gcs=gcs: e.tensor_tensor(out=UL, in0=UL, in1=gcs, op=ALU.add), reads=[sk(4), "GC"], writes=[sk(4)])
                P.add("act", lambda e, UA=UA, rq=rq: e.activation(out=UA, in_=rq, func=AF.Ln, scale=DKS), reads=[Rk], writes=[sk(5)])
                P.add("dve", lambda e, UA=UA, gcs=gcs: e.tensor_tensor(out=UA, in0=UA, in1=gcs, op=ALU.add), reads=[sk(5), "GC"], writes=[sk(5)])
                P.add("act", lambda e, BIAS=BIAS, rk=rk: e.activation(out=BIAS, in_=rk, func=AF.Ln), reads=[Rk], writes=[sk(6)])
                P.add("dve", lambda e, BIAS=BIAS, gcs=gcs: e.tensor_tensor(out=BIAS, in0=BIAS, in1=gcs, op=ALU.subtract), reads=[sk(6), "GC"], writes=[sk(6)])
                FE[gi] = P.end_capture()
                SCK = [sk(i) for i in range(7)]
                for tl in range(4):
                    t = 4 * gi + tl
                    stageA = []
                    for c in range(2):
                        P.begin_capture()
                        hh = c
                        cs = 2 * (t % 2) + c
                        pbk, pbo = cs // 2, (cs % 2) * 512
                        sc1 = lambda q_, tl=tl, hh=hh, SC=SC: SC[:, q_, tl * 2 + hh:tl * 2 + hh + 1]
                        pb, pc = PB[pbk], PF[2 + cs]
                        kd, kbg, vb, E, MA = (g[k, cs] for k in ("kd", "kbg", "vb", "E", "MA"))
                        ksT = sT[:, hh, tl, 0:128]
                        vsT = sT[:, hh, tl, 256:384]
                        P.add("pe", lambda e, pb=pb, ksT=ksT, pbo=pbo: e.transpose(out=pb[:, pbo:pbo + 128], in_=ksT, identity=identb[:]),
                              reads=[("sT", gp, hh, 0), "identb"], writes=[("pb", pbk)])
                        P.add("pe", lambda e, pb=pb, vsT=vsT, pbo=pbo: e.transpose(out=pb[:, pbo + 128:pbo + 256], in_=vsT, identity=identb[:]),
                              reads=[("sT", gp, hh, 2), "identb"], writes=[("pb", pbk)])
                        P.add("act", lambda e, pb=pb, kd=kd, sc1=sc1, pbo=pbo: e.activation(out=kd, in_=pb[:, pbo:pbo + 128], func=AF.Copy, scale=sc1(2)),
                              reads=[("pb", pbk)] + SCK, writes=[("kd", cs)])
                        P.add("dve", lambda e, pb=pb, kbg=kbg, sc1=sc1, pbo=pbo: e.tensor_scalar(out=kbg, in0=pb[:, pbo:pbo + 128], scalar1=sc1(1), scalar2=None, op0=ALU.mult),
                              reads=[("pb", pbk)] + SCK, writes=[("kbg", cs)])
                        hcol = 2 * hp + hh
                        P.add("dve", lambda e, pb=pb, vb=vb, t=t, hcol=hcol, pbo=pbo: e.tensor_scalar(
                            out=vb, in0=pb[:, pbo + 128:pbo + 256], scalar1=BETA[:, t, hcol:hcol + 1], scalar2=None, op0=ALU.mult),
                            reads=[("pb", pbk), "BETA"], writes=[("vb", cs)])
                        P.add("pe", lambda e, pc=pc, ksT=ksT, hh=hh, tl=tl, sT=sT: e.matmul(pc[:, 0:256], ksT, sT[:, hh, tl, 0:256], start=True, stop=True,
                                                                                              skip_group_check=True),
                              reads=[("sT", gp, hh, 0), ("sT", gp, hh, 1)], writes=[("pf", 2 + cs)])
                        P.add("act", lambda e, E=E, sc1=sc1: e.activation(out=E[:, 0:128], in_=identf[:, :], func=AF.Copy, scale=sc1(4)),
                              reads=["identf"] + SCK, writes=[("E", cs)])
                        P.add("act", lambda e, E=E, sc1=sc1: e.activation(out=E[:, 128:256], in_=identf[:, :], func=AF.Copy, scale=sc1(5)),
                              reads=["identf", ("E", cs)] + SCK, writes=[("E", cs)])
                        P.add("pe", lambda e, pc=pc, E=E: e.matmul(pc[:, 256:512], self.onesf[:, :], E[:, :], start=True, stop=False, skip_group_check=True),
                              reads=[("E", cs), "onesf"], writes=[("pf", 2 + cs)])
                        P.add("pe", lambda e, pc=pc: e.matmul(pc[:, 256:512], identf[:, :], self.masks[:, :], start=False, stop=True, skip_group_check=True),
                              reads=["identf", "masks"], writes=[("pf", 2 + cs)])
                        P.add("act", lambda e, pc=pc, E=E, sc1=sc1: e.activation(out=E[:, :], in_=pc[:, 256:512], func=AF.Exp, bias=sc1(6)),
                              reads=[("pf", 2 + cs)] + SCK, writes=[("E", cs)])
                        P.add("dve", lambda e, pc=pc, E=E, MA=MA: e.tensor_tensor(out=MA[:, :], in0=pc[:, 0:256], in1=E[:, :], op=ALU.mult),
                              reads=[("pf", 2 + cs), ("E", cs)], writes=[("MA", cs)])
                        Lb, DD, TM = g["Lb", cs], g["DD", cs], g["TM", cs]
                        P.add("pe", lambda e, pb=pb, MA=MA, pbo=pbo: e.transpose(out=pb[:, pbo + 256:pbo + 384], in_=MA[:, 0:128], identity=identb[:]),
                              reads=[("MA", cs), "identb"], writes=[("pb", pbk)])
                        P.add("act", lambda e, pb=pb, Lb=Lb, pbo=pbo: e.activation(out=Lb, in_=pb[:, pbo + 256:pbo + 384], func=AF.Copy),
                              reads=[("pb", pbk)], writes=[("Lb", cs)])
                        P.add("dve", lambda e, TM=TM, Lb=Lb: e.tensor_tensor(out=TM[:, 0:128], in0=Lb, in1=NEGM[:, 0, 0:128], op=ALU.mult),
                              reads=[("Lb", cs), "NEGM"], writes=[("TM", cs)])
                        P.add("dve", lambda e, TM=TM, MA=MA: e.tensor_tensor(out=TM[:, 128:256], in0=MA[:, 0:128], in1=NEGM[:, 0, 128:256], op=ALU.mult),
                              reads=[("MA", cs), "NEGM", ("TM", cs)], writes=[("TM", cs)])
                        P.add("dve", lambda e, TM=TM, DD=DD: e.tensor_tensor(out=DD[:, 0, 0:128], in0=identb[:, :], in1=TM[:, 0:128], op=ALU.subtract),
                              reads=[("TM", cs), "identb"], writes=[("DD", cs, 0)])
                        P.add("dve", lambda e, TM=TM, DD=DD: e.tensor_tensor(out=DD[:, 0, 128:256], in0=identb[:, :], in1=TM[:, 128:256], op=ALU.subtract),
                              reads=[("TM", cs), "identb", ("DD", cs, 0)], writes=[("DD", cs, 0)])
                        stageA.append(P.end_capture())
                    P.begin_capture()
                    for lev in range(1, 7):
                        pi, po = (lev - 1) % 2, lev % 2
                        for c in range(2):
                            cs = 2 * (t % 2) + c
                            pc = PF[2 + cs]
                            MA, Lb, DD, QQ = (g[k_, cs] for k_ in ("MA", "Lb", "DD", "QQ"))
                            P.add("pe", lambda e, pc=pc, MA=MA, DD=DD, pi=pi: e.matmul(pc[:, 0:128], MA[:, 0:128], DD[:, pi, 0:128], start=True, stop=True,
                                                                                        skip_group_check=True),
                                  reads=[("MA", cs), ("DD", cs, pi)], writes=[("pf", 2 + cs)])
                            P.add("pe", lambda e, pc=pc, Lb=Lb, DD=DD, pi=pi: e.matmul(pc[:, 128:256], Lb, DD[:, pi, 128:256], start=True, stop=True,
                                                                                        skip_group_check=True),
                                  reads=[("Lb", cs), ("DD", cs, pi)], writes=[("pf", 2 + cs)])
                            P.add("dve", lambda e, pc=pc, QQ=QQ, lev=lev: e.tensor_tensor(out=QQ[:, :], in0=pc[:, 0:256], in1=NEGM[:, lev, :], op=ALU.mult),
                                  reads=[("pf", 2 + cs), "NEGM"], writes=[("QQ", cs)])
                        for c in range(2):
                            cs = 2 * (t % 2) + c
                            pc = PF[2 + cs]
                            DD, QQ = (g[k_, cs] for k_ in ("DD", "QQ"))
                            P.add("pe", lambda e, pc=pc, QQ=QQ, DD=DD, pi=pi: e.matmul(pc[:, 256:384], DD[:, pi, 128:256], QQ[:, 0:128], start=True, stop=True,
                                                                                        skip_group_check=True),
                                  reads=[("QQ", cs), ("DD", cs, pi)], writes=[("pf", 2 + cs)])
                            P.add("pe", lambda e, pc=pc, QQ=QQ, DD=DD, pi=pi: e.matmul(pc[:, 384:512], DD[:, pi, 0:128], QQ[:, 128:256], start=True, stop=True,
                                                                                        skip_group_check=True),
                                  reads=[("QQ", cs), ("DD", cs, pi)], writes=[("pf", 2 + cs)])
                            P.add("dve", lambda e, pc=pc, DD=DD, pi=pi, po=po: e.tensor_tensor(out=DD[:, po, :], in0=DD[:, pi, :], in1=pc[:, 256:512], op=ALU.subtract),
                                  reads=[("pf", 2 + cs), ("DD", cs, pi)], writes=[("DD", cs, po)])
                    for c in range(2):
                        cs = 2 * (t % 2) + c
                        pc = PF[2 + cs]
                        kbg, vb, DD, u, wT = (g[k, cs] for k in ("kbg", "vb", "DD", "u", "wT"))
                        P.add("pe", lambda e, pc=pc, DD=DD, vb=vb: e.matmul(pc[:, 0:128], DD[:, 0, 128:256], vb, start=True, stop=True, skip_group_check=True),
                              reads=[("DD", cs, 0), ("vb", cs)], writes=[("pf", 2 + cs)])
                        P.add("pe", lambda e, pc=pc, DD=DD, kbg=kbg: e.matmul(pc[:, 128:256], kbg, DD[:, 0, 128:256], start=True, stop=True, skip_group_check=True),
                              reads=[("DD", cs, 0), ("kbg", cs)], writes=[("pf", 2 + cs)])
                        P.add("act", lambda e, pc=pc, u=u: e.activation(out=u, in_=pc[:, 0:128], func=AF.Copy), reads=[("pf", 2 + cs)], writes=[("u", cs)])
                        P.add("dve", lambda e, pc=pc, wT=wT: e.tensor_copy(out=wT, in_=pc[:, 128:256]), reads=[("pf", 2 + cs)], writes=[("wT", cs)])
                    PREP[t] = Prog.merge(stageA) + P.end_capture()
                    scans = []
                    for c in range(2):
                        P.begin_capture()
                        hh = c
                        h = 2 * hp + hh
                        cs = 2 * (t % 2) + c
                        pbk, pbo = cs // 2, (cs % 2) * 512
                        ps_, pb = PF[2 + cs], PB[pbk]
                        kd, MA, u, wT, vn, o, og, psm = (g[k, cs] for k in ("kd", "MA", "u", "wT", "vn", "o", "og", "ps"))
                        sc1 = lambda q_, tl=tl, hh=hh, SC=SC: SC[:, q_, tl * 2 + hh:tl * 2 + hh + 1]
                        qsT = sT[:, hh, tl, 128:256]
                        PK = ("pf", 2 + cs)
                        P.add("pe", lambda e, ps_=ps_, wT=wT, hh=hh: e.matmul(ps_[:, 0:128], wT, Sb[:, hh, :], start=True, stop=True, skip_group_check=True),
                              reads=[("wT", cs), ("Sb", hh)], writes=[PK])
                        P.add("pe", lambda e, ps_=ps_, qsT=qsT, hh=hh: e.matmul(ps_[:, 128:256], qsT, Sb[:, hh, :], start=True, stop=True, skip_group_check=True),
                              reads=[("sT", gp, hh, 1), ("Sb", hh)], writes=[PK])
                        P.add("dve", lambda e, ps_=ps_, u=u, vn=vn: e.tensor_tensor(out=vn, in0=u, in1=ps_[:, 0:128], op=ALU.subtract),
                              reads=[PK, ("u", cs)], writes=[("vn", cs)])
                        P.add("pe", lambda e, ps_=ps_, MA=MA, vn=vn: e.matmul(ps_[:, 256:384], MA[:, 128:256], vn, start=True, stop=True, skip_group_check=True),
                              reads=[("MA", cs), ("vn", cs)], writes=[PK])
                        P.add("pe", lambda e, ps_=ps_, kd=kd, vn=vn: e.matmul(ps_[:, 384:512], kd, vn, start=True, stop=True, skip_group_check=True),
                              reads=[("kd", cs), ("vn", cs)], writes=[PK])
                        P.add("act", lambda e, ps_=ps_, o=o: e.activation(out=o, in_=ps_[:, 256:384], func=AF.Copy), reads=[PK], writes=[("o", cs)])
                        P.add("dve", lambda e, ps_=ps_, o=o, sc1=sc1: e.scalar_tensor_tensor(out=o, in0=ps_[:, 128:256], scalar=sc1(3), in1=o, op0=ALU.mult, op1=ALU.add),
                              reads=[PK, ("o", cs)] + SCK, writes=[("o", cs)])
                        P.add("dve", lambda e, ps_=ps_, hh=hh, t=t, h=h: e.scalar_tensor_tensor(
                            out=Sf[:, hh, :], in0=Sf[:, hh, :], scalar=EGL[:, t, h:h + 1], in1=ps_[:, 384:512], op0=ALU.mult, op1=ALU.add),
                            reads=[PK, ("Sf", hh), "EGL"], writes=[("Sf", hh)])
                        P.add("act", lambda e, hh=hh: e.activation(out=Sb[:, hh, :], in_=Sf[:, hh, :], func=AF.Copy), reads=[("Sf", hh)], writes=[("Sb", hh)])
                        P.add("act", lambda e, o=o, og=og, psm=psm: e.activation(out=og, in_=o, func=AF.Square, accum_out=psm[:, 0:1]),
                              reads=[("o", cs)], writes=[("og", cs), ("psm", cs)])
                        P.add("act", lambda e, psm=psm: e.activation(out=psm[:, 1:2], in_=psm[:, 0:1], func=AF.Ln, bias=epsc[:, 0:1], scale=1.0 / 128),
                              reads=[("psm", cs), "epsc"], writes=[("psm", cs)])
                        P.add("act", lambda e, psm=psm: e.activation(out=psm[:, 1:2], in_=psm[:, 1:2], func=AF.Exp, scale=-0.5), reads=[("psm", cs)], writes=[("psm", cs)])
                        P.add("dve", lambda e, o=o, og=og, psm=psm, tl=tl, hh=hh, zs=zs: e.scalar_tensor_tensor(
                            out=og, in0=o, scalar=psm[:, 1:2], in1=zs[:, tl, hh * 128:(hh + 1) * 128], op0=ALU.mult, op1=ALU.mult),
                            reads=[("o", cs), ("psm", cs), ("zs", gp, tl)], writes=[("og", cs)])
                        P.add("pe", lambda e, pb=pb, og=og, pbo=pbo: e.transpose(out=pb[:, pbo + 384:pbo + 512], in_=og, identity=identb[:]),
                              reads=[("og", cs), "identb"], writes=[("pb", pbk)])
                        P.add("act", lambda e, pb=pb, hh=hh, t=t, pbo=pbo: e.activation(out=U[:, hh, t * 128:(t + 1) * 128], in_=pb[:, pbo + 384:pbo + 512], func=AF.Copy,
                                                                                        scale=self.gdc[:, j, 0:1]),
                              reads=[("pb", pbk), ("gdc", j)], writes=[("U", hh, t // 4)])
                        scans.append(P.end_capture())
                    SCAN[t] = Prog.merge(scans)
            wv = so[0]
            for t in range(NT):
                P.begin_capture()
                for dh in range(2):
                    pf = PF[0]
                    P.begin_atomic()
                    for hc in range(2):
                        P.add("pe", lambda e, pf=pf, wv=wv, hc=hc, t=t, dh=dh: e.matmul(
                            pf[:, :], U[:, hc, t * 128:(t + 1) * 128], wv[:, hc, dh * 512:(dh + 1) * 512], start=(hc == 0), stop=(hc == 1)),
                            reads=[so[1], ("U", hc, t // 4)], writes=[("pf", 0)])
                    P.add("dve", lambda e, pf=pf, t=t, dh=dh: e.tensor_tensor(
                        out=X[:, t, dh * 512:(dh + 1) * 512], in0=X[:, t, dh * 512:(dh + 1) * 512], in1=pf[:, :], op=ALU.add),
                        reads=[("pf", 0), ("x", t)], writes=[("x", t)])
                    P.end_atomic()
                OUT[t] = P.end_capture()
            return slabs, so, FE, PREP, SCAN, OUT

        def zero_state():
            P.add("pool", lambda e: e.memset(flat(Sf), 0.0), reads=XN0, writes=[("Sf", 0), ("Sf", 1)])
            P.add("pool", lambda e: e.memset(flat(Sb), 0.0), reads=XN0, writes=[("Sb", 0), ("Sb", 1)])

        cur = capture_pair(0)
        P.replay([cur[2][0]])
        zero_state()
        for hp in range(4):
            slabs, so, FE, PREP, SCAN, OUT = cur
            fe_parts = {}
            for gi in range(1, 4):
                L = FE[gi]
                n = (len(L) + 2) // 3
                for k in range(3):
                    fe_parts[4 * (gi - 1) + 1 + k] = L[k * n:(k + 1) * n]
            for s_ in range(NT):
                lists = [PREP[s_]]
                if s_ >= 1:
                    lists.append(SCAN[s_ - 1])
                if s_ in fe_parts:
                    lists.append(fe_parts[s_])
                if s_ >= 2:
                    lists.append(OUT[s_ - 2])
                P.replay(lists)
            for which in ("q", "k", "v", "z"):
                self.ws.done(slabs[which][2])
            tail = Prog.merge([SCAN[NT - 1]]) + Prog.merge([OUT[NT - 2]]) + Prog.merge([OUT[NT - 1]])
            if hp < 3:
                cur = capture_pair(hp + 1)
                P.replay([tail, cur[2][0]])
            else:
                P.replay([tail])
            self.ws.done(so[2])
            if hp < 3:
                zero_state()

    def load_x(self, s):
        P = self.P
        X = self.X
        xv = self.d["x"][s].rearrange("(t p) d -> p t d", p=128)
        for q in range(4):
            P.add("sp", lambda e, q=q: e.dma_start(out=X[:, 4 * q:4 * q + 4, :], in_=xv[:, 4 * q:4 * q + 4, :]),
                  writes=[("x", t) for t in range(4 * q, 4 * q + 4)], dma=("xl", q))

    def store_x(self, s):
        P = self.P
        X = self.X
        ov = self.d["out"][s].rearrange("(t p) d -> p t d", p=128)
        ids = []
        for q in range(4):
            ids.append(P.add("sp", lambda e, q=q: e.dma_start(out=ov[:, 4 * q:4 * q + 4, :], in_=X[:, 4 * q:4 * q + 4, :]),
                             reads=[("x", t) for t in range(4 * q, 4 * q + 4)], writes=[("xst", q)], dma=("xs", q)))
        return ids

    def rmsnorm(self, gbase):
        P = self.P
        X, xs, ss, rstd, xnT = self.X, self.xs, self.ss, self.rstd, self.xnT
        identb, gcol = self.identb, self.gcol
        import os
        DBG = int(os.environ.get("K_DBG", "9"))
        for t in range(NT):
            P.add("act", lambda e, t=t: e.activation(out=xs[:, 1, :], in_=X[:, t, :], func=AF.Square,
                                                     accum_out=ss[:, t:t + 1]),
                  reads=[("x", t)], writes=[("xs", 1), ("ss", t)])
        P.add("act", lambda e: e.activation(out=rstd[:], in_=ss[:], func=AF.Ln, bias=self.epsc[:, 0:1], scale=1.0 / D),
              reads=[("ss", t) for t in range(NT)] + ["epsc"], writes=["rstd"])
        P.add("act", lambda e: e.activation(out=rstd[:], in_=rstd[:], func=AF.Exp, scale=-0.5), reads=["rstd"], writes=["rstd"])
        if DBG < 2:
            return
        for t in range(NT if DBG >= 6 else 1):
            b = t % 2
            pb = self.PB[b]
            P.add("act", lambda e, t=t, b=b: e.activation(out=xs[:, b, :], in_=X[:, t, :], func=AF.Copy,
                                                          scale=rstd[:, t:t + 1]),
                  reads=[("x", t), "rstd"], writes=[("xs", b)])
            if DBG < 4:
                continue
            for kc in range(8):
                P.add("pe", lambda e, b=b, kc=kc, pb=pb: e.transpose(out=pb[:, kc * 128:(kc + 1) * 128],
                                                                      in_=xs[:, b, kc * 128:(kc + 1) * 128],
                                                                      identity=identb[:]),
                      reads=[("xs", b), "identb"], writes=[("pb", b)])
            if DBG < 5:
                continue
            gb = gcol[:, gbase:gbase + 8].unsqueeze(2).to_broadcast([128, 8, 128])
            P.add("dve", lambda e, t=t, pb=pb, gb=gb: e.tensor_tensor(
                out=xnT[:, :, t * 128:(t + 1) * 128], in0=pb[:, :].rearrange("p (k c) -> p k c", k=8), in1=gb, op=ALU.mult),
                reads=[("pb", b), "gcol"], writes=[("xnT", t, kc) for kc in range(8)])

    def mlp(self, l):
        P = self.P
        X, xnT, U, ring = self.X, self.xnT, self.hT, self.ring
        w1 = self.d["mlp_w_in"][l].rearrange("(kc p) f -> p kc f", p=128)
        w2 = self.d["mlp_w_out"][l].rearrange("(fc p) d -> p fc d", p=128)
        self.rmsnorm(32 + 8 * l)
        self.arena_barrier()
        pfi = 0
        for fg in range(4):
            slabs = [self.ws.get(w1[:, :, fg * 1024 + s2 * 512: fg * 1024 + (s2 + 1) * 512], 512) for s2 in range(2)]
            for fc in range(8):
                slot, wkey, _ = slabs[fc // 4]
                off = (fc % 4) * 128
                for tt in range(4):
                    bank = pfi % 4
                    pfi += 1
                    pf = self.PF[bank]
                    for kc in range(8):
                        P.add("pe", lambda e, pf=pf, slot=slot, kc=kc, off=off, tt=tt: e.matmul(
                            pf[:, :], slot[:, kc, off:off + 128], xnT[:, kc, tt * 512:(tt + 1) * 512],
                            start=(kc == 0), stop=(kc == 7)),
                            reads=[wkey] + [("xnT", t, kc) for t in range(4 * tt, 4 * tt + 4)],
                            writes=[("pf", bank)])
                    rb = pfi % 2
                    P.add("act", lambda e, pf=pf, rb=rb: e.activation(out=self.rtmp[:, rb, :], in_=pf[:, :], func=AF.Relu),
                          reads=[("pf", bank)], writes=[("xs", rb)])
                    P.add("dve", lambda e, fc=fc, tt=tt, rb=rb: e.tensor_tensor(
                        out=U[:, fc, tt * 512:(tt + 1) * 512], in0=self.rtmp[:, rb, :], in1=self.rtmp[:, rb, :],
                        op=ALU.mult),
                        reads=[("xs", rb)], writes=[("hT", fc, tt)])
            for sl in slabs:
                self.ws.done(sl[2])
            slabs2 = [self.ws.get(w2[:, fg * 8:(fg + 1) * 8, dh * 512:(dh + 1) * 512], 512) for dh in range(2)]
            for t in range(NT):
                for dh in range(2):
                    slot, wkey, _ = slabs2[dh]
                    bank = pfi % 4
                    pfi += 1
                    pf = self.PF[bank]
                    for fc in range(8):
                        P.add("pe", lambda e, pf=pf, slot=slot, fc=fc, t=t: e.matmul(
                            pf[:, :], U[:, fc, t * 128:(t + 1) * 128], slot[:, fc, :],
                            start=(fc == 0), stop=(fc == 7)),
                            reads=[wkey, ("hT", fc, t // 4)], writes=[("pf", bank)])
                    P.add("dve", lambda e, pf=pf, t=t, dh=dh: e.tensor_tensor(
                        out=X[:, t, dh * 512:(dh + 1) * 512], in0=X[:, t, dh * 512:(dh + 1) * 512], in1=pf[:, :],
                        op=ALU.add),
                        reads=[("pf", bank), ("x", t)], writes=[("x", t)])
            for sl in slabs2:
                self.ws.done(sl[2])

    def build(self):
        P = self.P
        self.setup_consts()
        kinds = {k for k, _ in self.layers}
        if "gd" in kinds:
            self.setup_gd_static()
            self.setup_gd_levelmasks()
            for jj in sorted({l // 2 for (k, l) in self.layers if k == "gd"}):
                self.setup_gd_consts(jj)
        if "da" in kinds:
            self.setup_da_static()
            for (k, l) in self.layers:
                if k == "da":
                    self.setup_da_consts(l // 2, l)
        last_stores = []
        for s in range(self.n_seq):
            self.load_x(s)
            for l in self.layers:
                if l[0] == "mlp":
                    self.mlp(l[1])
                elif l[0] == "norm":
                    self.rmsnorm(32 + 8 * l[1])
                elif l[0] == "da":
                    self.diffattn(l[1])
                elif l[0] == "gd":
                    self.gdn(l[1])
            last_stores = self.store_x(s)
        P.add("sp", None, reads=[("xst", q) for q in range(4)])


def layer_plan():
    plan = []
    for i in range(DEPTH):
        plan.append(("da" if i % 2 == 0 else "gd", i))
        plan.append(("mlp", i))
    return plan


def build_program(n_seq=SEQ_PER_CORE, layers=None):
    if layers is None:
        layers = layer_plan()
    nc = bass.Bass("TRN2", target_bir_lowering=False)
    with ExitStack() as es:
        b = Builder(nc, Prog(nc, dry=True), None, n_seq, layers, es)
        b.build()
        future = list(b.ws.requests)
        P = Prog(nc, dry=False)
        b.P = P
        b.ws = WStream(P, b.ring, b.NSLOT * 2, future)
        b.build()
        P.emit(es)
    return nc


WEIGHT_NAMES = ["mix_norm", "mlp_norm", "mlp_w_in", "mlp_w_out", "da_w_in", "da_q_norm", "da_k_norm",
                "da_lambda_q1", "da_lambda_k1", "da_lambda_q2", "da_lambda_k2", "da_sub_norm", "da_w_out",
                "gd_w_in", "gd_conv_w", "gd_a_log", "gd_dt_bias", "gd_out_norm", "gd_w_out"]


def run(inputs, n_seq=SEQ_PER_CORE, layers=None, ncores=NCORES, trace=False):
    nc = build_program(n_seq, layers)
    x = np.ascontiguousarray(np.asarray(inputs["x"], dtype=np.float32))
    weights = {k: np.ascontiguousarray(np.asarray(inputs[k], dtype=np.float32)) for k in WEIGHT_NAMES}
    in_maps = []
    for c in range(ncores):
        m = {"x": x[c * n_seq:(c + 1) * n_seq]}
        m.update(weights)
        in_maps.append(m)
    res = run_bass_kernel_spmd(nc, in_maps, core_ids=list(range(ncores)), trace=trace)
    out = np.concatenate([r["out"] for r in res.results], axis=0)
    return out, res


def kernel(**inputs):
    out, _ = run(inputs)
    return out
```

```python
import math
from contextlib import ExitStack

import numpy as np
import concourse.bass as bass
import concourse.mybir as mybir
from concourse.bass_utils import run_bass_kernel_spmd

F32 = mybir.dt.float32
BF16 = mybir.dt.bfloat16
AF = mybir.ActivationFunctionType
ALU = mybir.AluOpType
AX = mybir.AxisListType

D = 1024
S = 2048
NT = S // 128
DFF = 4096
DEPTH = 4
EPS = 1e-6
NCORES = 8
SEQ_PER_CORE = 4
GD_IN = 4 * 1024 + 16


class Op:
    __slots__ = ("id", "eng", "fn", "deps", "dma", "seq", "signal")


class Prog:
    ENGS = ("pe", "act", "dve", "pool", "sp")

    def __init__(self, nc, dry=False):
        self.nc = nc
        self.dry = dry
        self.ops = []
        self.by_eng = {e: [] for e in self.ENGS}
        self.lw = {}
        self.rd = {}
        self.dma_groups = {}
        self.group_all = set()
        self.psum_last = {}
        self.arena_names = set()
        self.cap = None
        self.atom = None

    def begin_capture(self):
        self.cap = []

    def begin_atomic(self):
        if self.cap is not None:
            self.atom = []

    def end_atomic(self):
        if self.cap is not None:
            self.cap.append(self.atom)
            self.atom = None

    def end_capture(self):
        c, self.cap = self.cap, None
        return c

    @staticmethod
    def merge(lists):
        lists = [L for L in lists if L]
        idx = [0] * len(lists)
        out = []
        total = sum(len(L) for L in lists)
        nel = total
        done_el = 0
        while done_el < nel:
            done_el += 1
            best, bf = None, None
            for i, L in enumerate(lists):
                if idx[i] < len(L):
                    f = (idx[i] + 0.5) / len(L)
                    if bf is None or f < bf:
                        best, bf = i, f
            el = lists[best][idx[best]]
            idx[best] += 1
            total -= 1
            if isinstance(el, list):
                out.extend(el)
                total += len(el)
            else:
                out.append(el)
                total += 1
        return out

    def replay(self, lists):
        for rec in self.merge(lists):
            self.add(*rec)

    def add(self, eng, fn, reads=(), writes=(), dma=None):
        if self.dry:
            return None
        if self.cap is not None:
            rec = (eng, fn, tuple(reads), tuple(writes), dma)
            if self.atom is not None:
                self.atom.append(rec)
            else:
                self.cap.append(rec)
            return None
        op = Op()
        op.id = len(self.ops)
        op.eng = eng
        op.fn = fn
        op.dma = dma
        op.seq = 0
        op.signal = False
        if self.arena_names:
            for k in tuple(reads) + tuple(writes):
                nm = k[0] if isinstance(k, tuple) else k
                if nm in self.arena_names:
                    reads = tuple(reads) + ("ARENA",)
                    break
        deps = set()
        for k in reads:
            w = self.lw.get(k)
            if w is not None:
                deps.add(w)
        for k in writes:
            w = self.lw.get(k)
            if w is not None:
                deps.add(w)
            for r in self.rd.get(k, ()):
                deps.add(r)
        for k in reads:
            self.rd.setdefault(k, []).append(op.id)
        for k in writes:
            self.lw[k] = op.id
            self.rd[k] = []
        for k in tuple(reads) + tuple(writes):
            if isinstance(k, tuple) and k[0] in ("pf", "pb"):
                last = self.psum_last.setdefault(k, {})
                for eng2, oid in last.items():
                    if eng2 != eng:
                        deps.add(oid)
                last[eng] = op.id
        deps.discard(op.id)
        if eng == "pe" and dma is None:
            deps = {d for d in deps if not (self.ops[d].eng == "pe" and self.ops[d].dma is None)}
        op.deps = deps
        self.ops.append(op)
        self.by_eng[eng].append(op)
        if dma is not None:
            self.dma_groups.setdefault(dma, []).append(op.id)
        return op.id

    def emit(self, es):
        nc = self.nc
        ops = self.ops
        for op in ops:
            for d in op.deps:
                ops[d].signal = True
        for e in self.ENGS:
            c = 0
            for op in self.by_eng[e]:
                if op.dma is None and op.signal:
                    c += 1
                    op.seq = c
        for g, ids in self.dma_groups.items():
            for i, oid in enumerate(ids):
                ops[oid].seq = i + 1
        eng_sem = {e: es.enter_context(nc.semaphore("s_" + e)) for e in self.ENGS}
        dma_sem = {g: es.enter_context(nc.semaphore("d_" + str(g))) for g in self.dma_groups}
        block = es.enter_context(nc.Block())

        def emit_engine(ename, e):
            waited = {}
            for op in self.by_eng[ename]:
                need = {}
                for d in op.deps:
                    dop = ops[d]
                    if dop.dma is not None:
                        sem = dma_sem[dop.dma]
                        if dop.dma in self.group_all:
                            val = 16 * len(self.dma_groups[dop.dma])
                        else:
                            val = 16 * dop.seq
                    else:
                        sem = eng_sem[dop.eng]
                        val = dop.seq
                    key = id(sem)
                    if key not in need or need[key][1] < val:
                        need[key] = (sem, val)
                for key, (sem, val) in need.items():
                    if waited.get(key, 0) < val:
                        e.wait_ge(sem, val)
                        waited[key] = val
                if op.fn is None:
                    continue
                inst = op.fn(e)
                if op.dma is not None:
                    inst.then_inc(dma_sem[op.dma], 16)
                elif op.signal:
                    inst.then_inc(eng_sem[ename], 1)

        @block.tensor
        def _(e):
            emit_engine("pe", e)

        @block.scalar
        def _(e):
            emit_engine("act", e)

        @block.vector
        def _(e):
            emit_engine("dve", e)

        @block.gpsimd
        def _(e):
            emit_engine("pool", e)

        @block.sync
        def _(e):
            emit_engine("sp", e)


class WStream:
    UNIT = 2048

    def __init__(self, P, ring, nunits, lookahead_list=None):
        self.P = P
        self.ring = ring
        self.nu = nunits
        self.future = lookahead_list
        self.requests = []
        self.issued = 0
        self.released = set()
        self.head = 0
        self.occ = [None] * nunits
        self.units = []
        self.prev = []
        if lookahead_list is not None:
            for (src, w) in lookahead_list:
                self._place(w)

    @staticmethod
    def _nelem(w):
        return w[0] * w[1] if isinstance(w, tuple) else 8 * w

    def _place(self, w):
        n = (self._nelem(w) + self.UNIT - 1) // self.UNIT
        if self.head + n > self.nu:
            self.head = 0
        j = len(self.units)
        us = list(range(self.head, self.head + n))
        self.prev.append({self.occ[u] for u in us if self.occ[u] is not None})
        for u in us:
            self.occ[u] = j
        self.units.append((self.head, n))
        self.head = (self.head + n) % self.nu

    def view(self, idx, w):
        u0, n = self.units[idx]
        ne = self._nelem(w)
        v = self.ring[:, u0 * self.UNIT:u0 * self.UNIT + ne]
        if isinstance(w, tuple):
            return v.rearrange("p (a b) -> p a b", a=w[0])
        return v.rearrange("p (a b) -> p a b", a=8)

    def _pump(self):
        if self.P.dry:
            return
        while self.issued < len(self.future):
            j = self.issued
            if not all(pj in self.released for pj in self.prev[j]):
                break
            src, w = self.future[j]
            dst = self.view(j, w)
            self.P.add("pool", lambda e, dst=dst, src=src: e.dma_start(out=dst, in_=src),
                       writes=[("w", j)] + [("w", pj) for pj in self.prev[j]], dma=("wu", self.units[j][0]))
            self.issued += 1

    def get(self, src, w):
        i = len(self.requests)
        self.requests.append((src, w))
        if self.future is None:
            self._place(w)
        if self.P.dry:
            return self.view(i, w), ("w", i), i
        self._pump()
        assert self.issued > i, "weight ring deadlock: release slabs before requesting more"
        return self.view(i, w), ("w", i), i

    def done(self, idx):
        self.released.add(idx)
        self._pump()


class Builder:
    def __init__(self, nc, P, ws_future, n_seq, layers, es):
        self.nc = nc
        self.P = P
        self.n_seq = n_seq
        self.layers = layers
        self.es = es
        self.ws_future = ws_future
        self.alloc()

    def sb(self, name, shape, dt):
        return self.es.enter_context(self.nc.sbuf_tensor(name, shape, dt))

    def carve(self, shape, dt):
        esz = 4 if dt == F32 else 2
        n = 1
        for s_ in shape:
            n *= s_
        nbytes = (n * esz + 3) // 4 * 4
        off = self._aoff
        assert off + nbytes <= self.ARENA_BYTES, "arena overflow"
        self._aoff = off + nbytes
        v = self.arena[:, off // 4:(off + nbytes) // 4]
        if dt != F32:
            v = v.bitcast(dt)
        v = v[:, 0:n]
        if len(shape) == 2:
            v = v.rearrange("p (a b) -> p a b", a=shape[0])
        elif len(shape) == 3:
            v = v.rearrange("p (a b c) -> p a b c", a=shape[0], b=shape[1])
        return v

    def alloc(self):
        nc = self.nc
        n_seq = self.n_seq
        d = {}
        d["x"] = nc.dram_tensor("x", [n_seq, S, D], F32, kind="ExternalInput").ap()
        d["out"] = nc.dram_tensor("out", [n_seq, S, D], F32, kind="ExternalOutput").ap()
        specs = [
            ("mix_norm", [4, D]), ("mlp_norm", [4, D]), ("mlp_w_in", [4, D, DFF]), ("mlp_w_out", [4, DFF, D]),
            ("da_w_in", [2, D, 3072]), ("da_q_norm", [2, 64]), ("da_k_norm", [2, 64]),
            ("da_lambda_q1", [2, 64]), ("da_lambda_k1", [2, 64]), ("da_lambda_q2", [2, 64]), ("da_lambda_k2", [2, 64]),
            ("da_sub_norm", [2, 128]), ("da_w_out", [2, D, D]),
            ("gd_w_in", [2, D, GD_IN]), ("gd_conv_w", [2, 4, 3072]), ("gd_a_log", [2, 8]), ("gd_dt_bias", [2, 8]),
            ("gd_out_norm", [2, 128]), ("gd_w_out", [2, D, D]),
        ]
        for name, shape in specs:
            d[name] = nc.dram_tensor(name, shape, F32, kind="ExternalInput").ap()
        self.d = d
        self.NSLOT = 4
        self.X = self.sb("X", [128, NT, D], F32)
        self.xnT = self.sb("xnT", [128, 8, S], BF16)
        self.U = self.sb("U", [128, 2, S], BF16)
        self.ring = self.sb("ring", [128, self.NSLOT * 4096], BF16)
        self.xs_f = self.sb("xs_f", [128, 2, 512], F32)
        self.xs = self.xs_f[:, :, :].rearrange("p a b -> p (a b)").bitcast(BF16).rearrange("p (a b) -> p a b", a=2)
        self.rtmp = self.xs_f
        self.ss = self.sb("ss", [128, NT], F32)
        self.rstd = self.sb("rstd", [128, NT], F32)
        self.epsc = self.sb("epsc", [128, 4], F32)
        self.identb = self.sb("identb", [128, 128], BF16)
        self.identf = self.sb("identf", [128, 128], F32)
        self.crow = self.xs_f[0:64, 1, 0:128]
        self.gcol = self.sb("gcol", [128, 64], F32)
        self.ARENA_BYTES = 35840 + 24576
        self.arena = self.sb("arena", [128, self.ARENA_BYTES // 4], F32)
        self._aoff = 0
        self.hT = self.carve([8, S], BF16)
        self._aoff = 0
        self.qT = self.carve([2, S], BF16)
        self.kTz = self.carve([2, 2, S], BF16)
        self.vaug = self.carve([NT, 2, 130], BF16)
        self.pT = self.carve([3, 512], BF16)
        self.qraw = self.carve([2, 512], BF16)
        self.qsq = self.carve([2, 512], BF16)
        self.qrs = self.carve([1, 512], F32)
        self.accS = self.carve([4, 512], F32)
        self.osb = self.carve([2, 4, 128], F32)
        self.obf = self.carve([2, 4, 128], BF16)
        self.fsm = self.carve([2, 24], F32)
        self.da_arena_end = self._aoff
        self._aoff = 0
        g = {}
        g["BA"] = self.carve([NT, 16], F32)
        for nm in ("BETA", "GC", "EG", "EGL", "EKD"):
            g[nm] = self.carve([NT, 8], F32)
        g["GLB"] = g["BA"][:, :, :].rearrange("p a b -> p (a b)")[:, 0:NT * 8].rearrange("p (a b) -> p a b", a=NT)
        g["raw"] = self.carve([2, 516], F32)
        g["acc"] = self.carve([1, 512], F32)
        g["halo"] = self.carve([6, 4], F32)
        g["sT"] = [self.carve([2, 4, 3 * 128], BF16) for _ in range(2)]
        g["sq"] = self.carve([2, 512], BF16)
        g["zs"] = [self.carve([4, 256], BF16) for _ in range(2)]
        g["R"] = [self.carve([4, 4], F32) for _ in range(2)]
        g["SC"] = [self.carve([8, 8], F32) for _ in range(2)]
        g["Sf"] = self.carve([2, 128], F32)
        g["Sb"] = self.carve([2, 128], BF16)
        self.NCH = 4
        for c in range(self.NCH):
            g["kd", c] = self.carve([128], BF16)
            g["kbg", c] = self.carve([128], BF16)
            g["vb", c] = self.carve([128], BF16)
            g["E", c] = self.carve([256], F32)
            g["MA", c] = self.carve([256], BF16)
            g["Lb", c] = self.carve([128], BF16)
            g["QQ", c] = self.carve([256], BF16)
            g["TM", c] = self.carve([256], BF16)
            g["DD", c] = self.carve([2, 256], BF16)
            g["u", c] = self.carve([128], F32)
            g["wT", c] = self.carve([128], BF16)
            g["vn", c] = self.carve([128], BF16)
            g["o", c] = self.carve([128], F32)
            g["og", c] = self.carve([128], BF16)
            g["ps", c] = self.carve([8], F32)
        self.g = g
        self.gd_arena_end = self._aoff
        assert max(self.da_arena_end, self.gd_arena_end) <= self.ARENA_BYTES
        self.convw = self.sb("convw", [128, 2, 96], F32)
        self.gdc = self.sb("gdc", [128, 2, 4], F32)
        self.gsm = self.sb("gsm", [128, 2, 16], F32)
        self.NEGM = self.sb("NEGM", [128, 7, 256], BF16)
        self.trif = self.sb("trif", [128, 128], F32)
        self.sellast = self.sb("sellast", [128, 128], F32)
        self.onesf = self.sb("onesf", [128, 128], F32)
        self.masks = self.sb("masks", [128, 256], F32)
        self.onecol = self.sb("onecol", [128, 2], BF16)
        self.onesbd = self.sb("onesbd", [128, 128], BF16)
        self.negmask = self.sb("negmask", [128, 128], BF16)
        self.cda = self.sb("cda", [128, 16], F32)
        self.lamt = self.xs_f[:, 0, 256:512].rearrange("p (a b) -> p a b", a=4)
        self.lamp = self.sb("lamp", [128, 4], F32)
        self.scr_f = self.xs_f[:, 0, 0:128]
        self.scr_f2 = self.xs_f[:, 0, 128:256]
        self.PF = [self.es.enter_context(nc.psum_tensor("pf%d" % i, [128, 512], F32)) for i in range(6)]
        self.PB = [self.es.enter_context(nc.psum_tensor("pb%d" % i, [128, 1024], BF16)) for i in range(2)]
        self.ws = WStream(self.P, self.ring, self.NSLOT * 2, self.ws_future)
        self.dummy = self.sb("abar", [128, 2], F32)
        self.ARENA_NAMES = {"hT", "qT", "kT", "kTz", "accS", "va", "pT", "qraw", "qsq", "qrs", "osb", "obf", "fsm", "fss", "frs", "vones",
                            "BA", "BETA", "GC", "EG", "EGL", "EKD", "raw", "acc", "halo", "sT", "sq", "zs", "R", "SC", "Sf", "Sb",
                            "kd", "kbg", "vb", "E", "MA", "Lb", "QQ", "TM", "DD", "u", "wT", "vn", "o", "og", "psm"}

    def arena_barrier(self):
        self.P.arena_names = self.ARENA_NAMES
        dummy = self.dummy
        self.P.add("pool", lambda e: e.memset(dummy[:], 0.0), writes=["ARENA"])

    def setup_consts(self):
        P = self.P
        nc = self.nc
        identf, identb = self.identf, self.identb
        P.add("pool", lambda e: e.memset(identf[:], 0.0), writes=["identf"])
        P.add("pool", lambda e: e.memset(self.epsc[:], EPS), writes=["epsc"])
        P.add("pool", lambda e: e.affine_select(out=identf[:], in_=identf[:], pattern=[[-1, 128]],
                                                compare_op=ALU.not_equal, fill=1.0, base=0,
                                                channel_multiplier=1),
              reads=["identf"], writes=["identf"])
        P.add("dve", lambda e: e.tensor_copy(out=identb[:], in_=identf[:]), reads=["identf"], writes=["identb"])
        crow, gcol = self.crow, self.gcol
        mixr = self.d["mix_norm"].rearrange("l (kc p) -> (l kc) p", p=128)
        mlpr = self.d["mlp_norm"].rearrange("l (kc p) -> (l kc) p", p=128)
        P.add("sp", lambda e: e.dma_start(out=crow[0:32, :], in_=mixr), writes=[("xs", 1)], dma="c0")
        P.add("sp", lambda e: e.dma_start(out=crow[32:64, :], in_=mlpr), writes=[("xs", 1)], dma="c1")
        pf = self.PF[0]
        P.add("pe", lambda e: e.transpose(out=pf[:, 0:64], in_=crow[0:64, :], identity=identf[0:64, 0:64]),
              reads=[("xs", 1), "identf"], writes=[("pf", 0)])
        P.add("dve", lambda e: e.tensor_copy(out=gcol[:, :], in_=pf[:, 0:64]), reads=[("pf", 0)], writes=["gcol"])

    def setup_da_consts(self, j, l):
        P = self.P
        d = self.d
        cda, lamt, lamp = self.cda, self.lamt, self.lamp
        c0 = 5 * j
        col = lambda ap: ap.rearrange("(p o) -> p o", o=1)
        for half in range(2):
            P.add("sp", lambda e, half=half: e.dma_start(out=cda[64 * half:64 * half + 64, c0:c0 + 1], in_=col(d["da_q_norm"][j])),
                  writes=[("cda", j)], dma="cq%d%d" % (j, half))
            P.add("sp", lambda e, half=half: e.dma_start(out=cda[64 * half:64 * half + 64, c0 + 1:c0 + 2], in_=col(d["da_k_norm"][j])),
                  writes=[("cda", j)], dma="ck%d%d" % (j, half))
        P.add("sp", lambda e: e.dma_start(out=cda[:, c0 + 4:c0 + 5], in_=col(d["da_sub_norm"][j])),
              writes=[("cda", j)], dma="cs%d" % j)
        for i, nm in enumerate(["da_lambda_q1", "da_lambda_k1", "da_lambda_q2", "da_lambda_k2"]):
            P.add("sp", lambda e, i=i, nm=nm: e.dma_start(out=lamt[:, i, :], in_=d[nm][j:j + 1, :].partition_broadcast(128)),
                  writes=[("xs", 0)], dma="cl%d%d" % (j, i))
        lam_init = 0.8 - 0.6 * math.exp(-0.3 * l)
        for i in range(2):
            P.add("dve", lambda e, i=i: e.tensor_tensor(out=lamt[:, 2 * i, :], in0=lamt[:, 2 * i, :], in1=lamt[:, 2 * i + 1, :], op=ALU.mult),
                  reads=[("xs", 0)], writes=[("xs", 0)])
            P.add("dve", lambda e, i=i: e.reduce_sum(out=lamp[:, i:i + 1], in_=lamt[:, 2 * i, :], axis=AX.X),
                  reads=[("xs", 0)], writes=["lamp"])
        P.add("act", lambda e: e.activation(out=lamp[:, 0:2], in_=lamp[:, 0:2], func=AF.Exp), reads=["lamp"], writes=["lamp"])
        P.add("dve", lambda e: e.tensor_tensor(out=cda[:, c0 + 2:c0 + 3], in0=lamp[:, 0:1], in1=lamp[:, 1:2], op=ALU.subtract),
              reads=["lamp"], writes=[("cda", j)])
        P.add("dve", lambda e: e.tensor_scalar(out=cda[:, c0 + 2:c0 + 3], in0=cda[:, c0 + 2:c0 + 3], scalar1=lam_init, scalar2=None, op0=ALU.add),
              reads=[("cda", j)], writes=[("cda", j)])
        P.add("dve", lambda e: e.tensor_scalar(out=cda[:, c0 + 3:c0 + 4], in0=cda[:, c0 + 2:c0 + 3], scalar1=-1.0, scalar2=None, op0=ALU.mult),
              reads=[("cda", j)], writes=[("cda", j)])
        P.add("dve", lambda e: e.tensor_scalar(out=cda[:, c0:c0 + 1], in0=cda[:, c0:c0 + 1], scalar1=0.125, scalar2=None, op0=ALU.mult),
              reads=[("cda", j)], writes=[("cda", j)])
        P.add("dve", lambda e: e.tensor_scalar(out=cda[:, c0 + 4:c0 + 5], in0=cda[:, c0 + 4:c0 + 5], scalar1=1.0 - lam_init, scalar2=None, op0=ALU.mult),
              reads=[("cda", j)], writes=[("cda", j)])

    def setup_da_static(self):
        P = self.P
        onesbd, negmask, vaug = self.onesbd, self.negmask, self.vaug
        scr = self.scr_f
        P.add("pool", lambda e: e.memset(scr[:], 0.0), writes=[("xs", 0)])
        P.add("pool", lambda e: e.memset(scr[0:64, 0:64], 1.0 / 64), reads=[("xs", 0)], writes=[("xs", 0)])
        P.add("pool", lambda e: e.memset(scr[64:128, 64:128], 1.0 / 64), reads=[("xs", 0)], writes=[("xs", 0)])
        P.add("dve", lambda e: e.tensor_copy(out=onesbd[:], in_=scr[:]), reads=[("xs", 0)], writes=["onesbd"])
        scr2 = self.scr_f2
        P.add("pool", lambda e: e.memset(scr2[:], 0.0), writes=[("xs", 0)])
        P.add("pool", lambda e: e.affine_select(out=scr2[:], in_=scr2[:], pattern=[[1, 128]], compare_op=ALU.is_ge,
                                                fill=-30000.0, base=0, channel_multiplier=-1),
              reads=[("xs", 0)], writes=[("xs", 0)])
        P.add("dve", lambda e: e.tensor_copy(out=negmask[:], in_=scr2[:]), reads=[("xs", 0)], writes=["negmask"])

    def diffattn(self, l):
        P = self.P
        j = l // 2
        X, xnT, U, ring = self.X, self.xnT, self.U, self.ring
        qT, kTz, vaug, pT, accS = self.qT, self.kTz, self.vaug, self.pT, self.accS
        qraw, qsq, qrs = self.qraw, self.qsq, self.qrs
        cda, onesbd, negmask, identb = self.cda, self.onesbd, self.negmask, self.identb
        osb, obf, fsm = self.osb, self.obf, self.fsm
        PF, PB = self.PF, self.PB
        c0 = 5 * j
        win = self.d["da_w_in"][j].rearrange("(kc p) f -> p kc f", p=128)
        self.rmsnorm(8 * l)
        self.arena_barrier()
        P.add("pool", lambda e: e.memset(vaug[:, :, :, 128:130], 1.0), reads=[("xnT", 0, 0)], writes=["vones"])
        P.add("pool", lambda e: e.memset(kTz[64:128, 0, :, :], 0.0), reads=[("xnT", 0, 0)], writes=["kTz"])
        P.add("pool", lambda e: e.memset(kTz[0:64, 1, :, :], 0.0), reads=[("xnT", 0, 0)], writes=["kTz"])
        cnt = {"pf": 0, "nb": 0, "pt": 0, "fin": 0}
        def record_proj(hp):
            sq = self.ws.get(win[:, :, hp * 256:(hp + 1) * 256], 256)
            sk = self.ws.get(win[:, :, 1024 + hp * 256:1024 + (hp + 1) * 256], 256)
            sv = self.ws.get(win[:, :, 2048 + hp * 256:2048 + (hp + 1) * 256], 256)
            jobs = [(which, slab, gcolq, hh, tt) for which, slab, gcolq in (("q", sq, c0), ("k", sk, c0 + 1))
                    for hh in range(2) for tt in range(4)]
            jstate = {}

            def emit_proj(i):
                which, slab, gcolq, hh, tt = jobs[i]
                slot, wkey, _ = slab
                bank = cnt["pf"] % 2
                cnt["pf"] += 1
                nb = cnt["nb"] % 2
                cnt["nb"] += 1
                jstate[i] = (bank, nb)
                pf = PF[bank]
                for kc in range(8):
                    P.add("pe", lambda e, pf=pf, slot=slot, kc=kc, hh=hh, tt=tt: e.matmul(
                        pf[:, :], slot[:, kc, hh * 128:(hh + 1) * 128], xnT[:, kc, tt * 512:(tt + 1) * 512],
                        start=(kc == 0), stop=(kc == 7)),
                        reads=[wkey] + [("xnT", t, kc) for t in range(4 * tt, 4 * tt + 4)], writes=[("pf", bank)])

            def emit_norm(i):
                which, slab, gcolq, hh, tt = jobs[i]
                bank, nb = jstate[i]
                pf = PF[bank]
                P.add("act", lambda e, pf=pf, nb=nb: e.activation(out=qraw[:, nb, :], in_=pf[:, :], func=AF.Copy),
                      reads=[("pf", bank)], writes=[("qraw", nb)])
                P.add("dve", lambda e, nb=nb: e.tensor_tensor(out=qsq[:, nb, :], in0=qraw[:, nb, :], in1=qraw[:, nb, :], op=ALU.mult),
                      reads=[("qraw", nb)], writes=[("qsq", nb)])
                mbank = 2 + nb
                pm = PF[mbank]
                P.add("pe", lambda e, pm=pm, nb=nb: e.matmul(pm[:, :], onesbd[:, :], qsq[:, nb, :], start=True, stop=True),
                      reads=[("qsq", nb), "onesbd"], writes=[("pf", mbank)])
                P.add("act", lambda e, pm=pm, nb=nb: e.activation(out=qrs[:, 0, :], in_=pm[:, :], func=AF.Ln, bias=self.epsc[:, 0:1], scale=1.0),
                      reads=[("pf", mbank), "epsc"], writes=[("qrs", 0)])
                P.add("act", lambda e, nb=nb: e.activation(out=qrs[:, 0, :], in_=qrs[:, 0, :], func=AF.Exp, scale=-0.5),
                      reads=[("qrs", 0)], writes=[("qrs", 0)])
                if which == "q":
                    P.add("dve", lambda e, nb=nb, hh=hh, tt=tt, gcolq=gcolq: e.scalar_tensor_tensor(
                        out=qT[:, hh, tt * 512:(tt + 1) * 512], in0=qraw[:, nb, :], scalar=cda[:, gcolq:gcolq + 1],
                        in1=qrs[:, 0, :], op0=ALU.mult, op1=ALU.mult),
                        reads=[("qraw", nb), ("qrs", 0), ("cda", j)], writes=[("qT", hh, tt)])
                else:
                    for c in range(2):
                        pl, ph = 64 * c, 64 * c + 64
                        P.add("dve", lambda e, nb=nb, hh=hh, tt=tt, gcolq=gcolq, c=c, pl=pl, ph=ph: e.scalar_tensor_tensor(
                            out=kTz[pl:ph, c, hh, tt * 512:(tt + 1) * 512], in0=qraw[pl:ph, nb, :], scalar=cda[pl:ph, gcolq:gcolq + 1],
                            in1=qrs[pl:ph, 0, :], op0=ALU.mult, op1=ALU.mult),
                            reads=[("qraw", nb), ("qrs", 0), ("cda", j), "kTz"], writes=[("kT", hh, tt, c)])

            emit_proj(0)
            for i in range(len(jobs)):
                if i + 1 < len(jobs):
                    emit_proj(i + 1)
                emit_norm(i)
            slot, wkey, _ = sv
            for t in range(NT):
                bank = cnt["pf"] % 2
                cnt["pf"] += 1
                pf = PF[bank]
                for kc in range(8):
                    P.add("pe", lambda e, pf=pf, slot=slot, kc=kc, t=t: e.matmul(
                        pf[:, 0:256], xnT[:, kc, t * 128:(t + 1) * 128], slot[:, kc, 0:256],
                        start=(kc == 0), stop=(kc == 7)),
                        reads=[wkey, ("xnT", t, kc)], writes=[("pf", bank)])
                P.add("act", lambda e, pf=pf, t=t: e.activation(
                    out=vaug[:, t, :, 0:128], in_=pf[:, 0:256].rearrange("p (h d) -> p h d", h=2), func=AF.Copy),
                    reads=[("pf", bank), "vones"], writes=[("va", t)])
            return sq, sk, sv

        P.begin_capture()
        nxt = record_proj(0)
        P.replay([P.end_capture()])
        for sl in nxt:
            self.ws.done(sl[2])
        for hp in range(4):
            ATT, FIN = [], []
            for hh in range(2):
                h = 2 * hp + hh
                for qt in range(4):
                    P.begin_capture()
                    first_in_bank = {}
                    units = [(c, kb) for c in range(2) for kb in range(4 * qt + 4)]
                    ubank = {}

                    def emit_scores(u):
                        c, kb = units[u]
                        r = kb - 4 * qt
                        col0 = max(r, 0) * 128
                        bank = cnt["pf"] % 2
                        cnt["pf"] += 1
                        ubank[u] = bank
                        ps = PF[bank]
                        krd = [("kT", hh, kb // 4, c)]
                        qrd = [("qT", hh, qt)]
                        if r >= 0:
                            P.add("pe", lambda e, ps=ps, c=c, hh=hh, kb=kb, qt=qt, col0=col0: e.matmul(
                                ps[:, col0:col0 + 128], kTz[:, c, hh, kb * 128:(kb + 1) * 128],
                                qT[:, hh, qt * 512 + col0:qt * 512 + col0 + 128], start=True, stop=False,
                                skip_group_check=True),
                                reads=krd + qrd, writes=[("pf", bank)])
                            P.add("pe", lambda e, ps=ps, col0=col0: e.matmul(
                                ps[:, col0:col0 + 128], identb[:, :], negmask[:, :], start=False, stop=True,
                                skip_group_check=True),
                                reads=["identb", "negmask"], writes=[("pf", bank)])
                            if col0 + 128 < 512:
                                P.add("pe", lambda e, ps=ps, c=c, hh=hh, kb=kb, qt=qt, col0=col0: e.matmul(
                                    ps[:, col0 + 128:512], kTz[:, c, hh, kb * 128:(kb + 1) * 128],
                                    qT[:, hh, qt * 512 + col0 + 128:qt * 512 + 512], start=True, stop=True,
                                    skip_group_check=True),
                                    reads=krd + qrd, writes=[("pf", bank)])
                        else:
                            P.add("pe", lambda e, ps=ps, c=c, hh=hh, kb=kb, qt=qt: e.matmul(
                                ps[:, :], kTz[:, c, hh, kb * 128:(kb + 1) * 128],
                                qT[:, hh, qt * 512:qt * 512 + 512], start=True, stop=True, skip_group_check=True),
                                reads=krd + qrd, writes=[("pf", bank)])

                    def emit_exp_pv(u):
                        c, kb = units[u]
                        r = kb - 4 * qt
                        col0 = max(r, 0) * 128
                        bank = ubank[u]
                        ps = PF[bank]
                        pi = cnt["pt"] % 3
                        cnt["pt"] += 1
                        P.add("act", lambda e, ps=ps, pi=pi, col0=col0: e.activation(
                            out=pT[:, pi, col0:512], in_=ps[:, col0:512], func=AF.Exp),
                            reads=[("pf", bank)], writes=[("pT", pi)])
                        for rr in range(max(r, 0), 4):
                            abank = 2 + 2 * c + rr // 2
                            off = (rr % 2) * 256
                            st = abank not in first_in_bank
                            first_in_bank[abank] = True
                            P.add("pe", lambda e, abank=abank, off=off, pi=pi, rr=rr, kb=kb, st=st, hh=hh, qt=qt: e.matmul(
                                PF[abank][:, off:off + 129], pT[:, pi, rr * 128:(rr + 1) * 128], vaug[:, kb, hh, 0:129],
                                start=st, stop=(kb == 4 * qt + rr), skip_group_check=True),
                                reads=[("pT", pi), ("va", kb), "vones"], writes=[("pf", abank)])

                    emit_scores(0)
                    for u in range(len(units)):
                        if u + 1 < len(units):
                            emit_scores(u + 1)
                        emit_exp_pv(u)
                    for b in range(4):
                        srcv = PF[2 + b][:, :].rearrange("p (s w) -> p s w", s=2)[:, :, 0:129]
                        dstv = accS[:, b, :].rearrange("p (s w) -> p s w", s=2)[:, :, 0:129]
                        if b % 2 == 0:
                            P.add("act", lambda e, srcv=srcv, dstv=dstv: e.activation(out=dstv, in_=srcv, func=AF.Copy),
                                  reads=[("pf", 2 + b)], writes=[("accS", b)])
                        else:
                            P.add("dve", lambda e, srcv=srcv, dstv=dstv: e.tensor_copy(out=dstv, in_=srcv),
                                  reads=[("pf", 2 + b)], writes=[("accS", b)])
                    ATT.append(P.end_capture())
                    P.begin_capture()
                    fi = cnt["fin"] % 2
                    cnt["fin"] += 1
                    AK = [("accS", b) for b in range(4)]
                    FK = ("fsm", fi)
                    slots = accS[:, :, :].rearrange("p b (s w) -> p (b s) w", s=2)
                    P.add("dve", lambda e, fi=fi, slots=slots: e.reciprocal(out=fsm[:, fi, 0:8], in_=slots[:, :, 128]),
                          reads=AK, writes=[FK])
                    P.add("dve", lambda e, fi=fi: e.tensor_scalar(out=fsm[:, fi, 8:12], in0=fsm[:, fi, 4:8], scalar1=cda[:, c0 + 3:c0 + 4], scalar2=None, op0=ALU.mult),
                          reads=[FK, ("cda", j)], writes=[FK])
                    rb0 = fsm[:, fi, 0:4].unsqueeze(2).to_broadcast([128, 4, 128])
                    P.add("dve", lambda e, fi=fi, slots=slots, rb0=rb0: e.tensor_tensor(out=osb[:, fi, :, :], in0=slots[:, 0:4, 0:128], in1=rb0, op=ALU.mult),
                          reads=AK + [FK], writes=[("osb", fi, rr) for rr in range(4)])
                    for rr in range(4):
                        P.add("dve", lambda e, fi=fi, rr=rr, slots=slots: e.scalar_tensor_tensor(
                            out=osb[:, fi, rr, :], in0=slots[:, 4 + rr, 0:128], scalar=fsm[:, fi, 8 + rr:9 + rr], in1=osb[:, fi, rr, :],
                            op0=ALU.mult, op1=ALU.add),
                            reads=AK + [FK, ("osb", fi, rr)], writes=[("osb", fi, rr)])
                    OK4 = [("osb", fi, rr) for rr in range(4)]
                    BK4 = [("obf", fi, rr) for rr in range(4)]
                    P.add("dve", lambda e, fi=fi: e.tensor_tensor(out=obf[:, fi, :, :], in0=osb[:, fi, :, :], in1=osb[:, fi, :, :], op=ALU.mult),
                          reads=OK4 + BK4, writes=BK4)
                    P.add("dve", lambda e, fi=fi: e.reduce_sum(out=fsm[:, fi, 12:16], in_=obf[:, fi, :, :], axis=AX.X),
                          reads=BK4, writes=[("fss", fi, rr) for rr in range(4)])
                    P.add("act", lambda e, fi=fi: e.activation(out=fsm[:, fi, 16:20], in_=fsm[:, fi, 12:16], func=AF.Ln,
                                                               bias=self.epsc[:, 0:1], scale=1.0 / 128),
                          reads=[("fss", fi, rr) for rr in range(4)] + ["epsc"], writes=[("frs", fi)])
                    P.add("act", lambda e, fi=fi: e.activation(out=fsm[:, fi, 16:20], in_=fsm[:, fi, 16:20], func=AF.Exp, scale=-0.5),
                          reads=[("frs", fi)], writes=[("frs", fi)])
                    rbs = fsm[:, fi, 16:20].unsqueeze(2).to_broadcast([128, 4, 128])
                    P.add("dve", lambda e, fi=fi, rbs=rbs: e.tensor_tensor(out=obf[:, fi, :, :], in0=osb[:, fi, :, :], in1=rbs, op=ALU.mult),
                          reads=[("osb", fi, rr) for rr in range(4)] + [("frs", fi)] + [("obf", fi, rr) for rr in range(4)],
                          writes=[("obf", fi, rr) for rr in range(4)])
                    pb = PB[fi]
                    for rr in range(4):
                        P.add("pe", lambda e, pb=pb, fi=fi, rr=rr: e.transpose(out=pb[:, rr * 128:(rr + 1) * 128], in_=obf[:, fi, rr, :], identity=identb[:]),
                              reads=[("obf", fi, rr), "identb"], writes=[("pb", fi)])
                    P.add("dve", lambda e, pb=pb, hh=hh, qt=qt: e.tensor_scalar(
                        out=U[:, hh, qt * 512:(qt + 1) * 512], in0=pb[:, 0:512], scalar1=cda[:, c0 + 4:c0 + 5], scalar2=None, op0=ALU.mult),
                        reads=[("pb", fi), ("cda", j)], writes=[("U", hh, qt)])
                    FIN.append(P.end_capture())
            P.replay([ATT[0]])
            for i in range(1, 8):
                P.replay([ATT[i], FIN[i - 1]])
            wsrc = self.d["da_w_out"][j][hp * 256:(hp + 1) * 256, :].rearrange("(kc p) f -> p kc f", p=128)
            so = self.ws.get(wsrc, (2, 1024))
            wv = so[0]
            P.begin_capture()
            for t in range(NT):
                for dh in range(2):
                    bank = 4 + (cnt["pf"] % 2)
                    cnt["pf"] += 1
                    pf = PF[bank]
                    for hc in range(2):
                        P.add("pe", lambda e, pf=pf, wv=wv, hc=hc, t=t, dh=dh: e.matmul(
                            pf[:, :], U[:, hc, t * 128:(t + 1) * 128], wv[:, hc, dh * 512:(dh + 1) * 512], start=(hc == 0), stop=(hc == 1)),
                            reads=[so[1], ("U", hc, t // 4)], writes=[("pf", bank)])
                    P.add("dve", lambda e, pf=pf, t=t, dh=dh: e.tensor_tensor(
                        out=X[:, t, dh * 512:(dh + 1) * 512], in0=X[:, t, dh * 512:(dh + 1) * 512], in1=pf[:, :], op=ALU.add),
                        reads=[("pf", bank), ("x", t)], writes=[("x", t)])
            tail = FIN[7] + P.end_capture()
            if hp < 3:
                P.begin_capture()
                nxt = record_proj(hp + 1)
                P.replay([tail, P.end_capture()])
                for sl in nxt:
                    self.ws.done(sl[2])
            else:
                P.replay([tail])
            self.ws.done(so[2])

    def setup_gd_static(self):
        P = self.P
        trif, sellast, onesf, masks, onecol = self.trif, self.sellast, self.onesf, self.masks, self.onecol
        P.add("pool", lambda e: e.memset(onesf[:], 1.0), writes=["onesf"])
        P.add("pool", lambda e: e.memset(onecol[:], 1.0), writes=["onecol"])
        P.add("pool", lambda e: e.memset(trif[:], 1.0), writes=["trif"])
        P.add("pool", lambda e: e.affine_select(out=trif[:], in_=trif[:], pattern=[[1, 128]], compare_op=ALU.is_ge,
                                                fill=0.0, base=0, channel_multiplier=-1), reads=["trif"], writes=["trif"])
        P.add("pool", lambda e: e.memset(sellast[:], 1.0), writes=["sellast"])
        P.add("pool", lambda e: e.affine_select(out=sellast[:], in_=sellast[:], pattern=[[0, 128]], compare_op=ALU.is_ge,
                                                fill=0.0, base=-127, channel_multiplier=1), reads=["sellast"], writes=["sellast"])
        P.add("pool", lambda e: e.memset(masks[:], 0.0), writes=["masks"])
        P.add("pool", lambda e: e.affine_select(out=masks[:, 0:128], in_=masks[:, 0:128], pattern=[[1, 128]], compare_op=ALU.is_ge,
                                                fill=-30000.0, base=-1, channel_multiplier=-1), reads=["masks"], writes=["masks"])
        P.add("pool", lambda e: e.affine_select(out=masks[:, 128:256], in_=masks[:, 128:256], pattern=[[1, 128]], compare_op=ALU.is_ge,
                                                fill=-30000.0, base=0, channel_multiplier=-1), reads=["masks"], writes=["masks"])

    def setup_gd_levelmasks(self):
        P = self.P
        NEGM = self.NEGM
        scrA = lambda nb: self.xs_f[0:nb, 0, 0:128]
        scrC = lambda nb: self.xs_f[0:nb, 0, 128:256]
        K0 = [("xs", 0)]
        for k in range(7):
            B, half = 2 ** (k + 1), 2 ** k
            nb = 128 // B
            A, C = scrA(nb), scrC(nb)
            P.add("pool", lambda e, nb=nb: e.memset(self.xs_f[0:nb, 0, 0:256], 1.0), writes=K0)
            P.add("pool", lambda e, A=A, B=B, half=half: e.affine_select(out=A, in_=A, pattern=[[1, 128]], compare_op=ALU.is_ge, fill=0.0,
                                                                       base=-half, channel_multiplier=-B), reads=K0, writes=K0)
            P.add("pool", lambda e, A=A, B=B: e.affine_select(out=A, in_=A, pattern=[[-1, 128]], compare_op=ALU.is_ge, fill=0.0,
                                                             base=B - 1, channel_multiplier=B), reads=K0, writes=K0)
            P.add("pool", lambda e, C=C, B=B: e.affine_select(out=C, in_=C, pattern=[[1, 128]], compare_op=ALU.is_ge, fill=0.0,
                                                             base=0, channel_multiplier=-B), reads=K0, writes=K0)
            P.add("pool", lambda e, C=C, B=B, half=half: e.affine_select(out=C, in_=C, pattern=[[-1, 128]], compare_op=ALU.is_ge, fill=0.0,
                                                                       base=half - 1, channel_multiplier=B), reads=K0, writes=K0)
            pf = self.PF[1]
            P.add("pe", lambda e, pf=pf, A=A, C=C: e.matmul(pf[:, 0:128], A, C, start=True, stop=True, skip_group_check=True),
                  reads=K0, writes=[("pf", 1)])
            P.add("pe", lambda e, pf=pf, A=A, C=C: e.matmul(pf[:, 128:256], C, A, start=True, stop=True, skip_group_check=True),
                  reads=K0, writes=[("pf", 1)])
            P.add("act", lambda e, pf=pf, k=k: e.activation(out=NEGM[:, k, :], in_=pf[:, 0:256], func=AF.Copy),
                  reads=[("pf", 1)], writes=["NEGM"])

    def setup_gd_consts(self, j):
        P = self.P
        d = self.d
        convw, gdc, identf = self.convw, self.gdc, self.identf
        crow96 = self.xs_f[0:96, 1, 0:128]
        rows = d["gd_conv_w"][j].rearrange("k (c p) -> (k c) p", p=128)
        P.add("sp", lambda e: e.dma_start(out=crow96, in_=rows), writes=[("xs", 1)], dma="gcw%d" % j)
        pf = self.PF[0]
        P.add("pe", lambda e: e.transpose(out=pf[:, 0:96], in_=crow96, identity=identf[0:96, 0:96]),
              reads=[("xs", 1), "identf"], writes=[("pf", 0)])
        P.add("dve", lambda e: e.tensor_copy(out=convw[:, j, :], in_=pf[:, 0:96]), reads=[("pf", 0)], writes=[("convw", j)])
        col = lambda ap: ap.rearrange("(p o) -> p o", o=1)
        P.add("sp", lambda e: e.dma_start(out=gdc[:, j, 0:1], in_=col(d["gd_out_norm"][j])), writes=[("gdc", j)], dma="gon%d" % j)
        gsm = self.gsm
        P.add("sp", lambda e: e.dma_start(out=gsm[:, j, 0:8], in_=d["gd_a_log"][j:j + 1, :].partition_broadcast(128)),
              writes=[("gsm", j)], dma="gal%d" % j)
        P.add("sp", lambda e: e.dma_start(out=gsm[:, j, 8:16], in_=d["gd_dt_bias"][j:j + 1, :].partition_broadcast(128)),
              writes=[("gsm", j)], dma="gdt%d" % j)
        P.add("act", lambda e: e.activation(out=gsm[:, j, 0:8], in_=gsm[:, j, 0:8], func=AF.Exp), reads=[("gsm", j)], writes=[("gsm", j)])
        P.add("dve", lambda e: e.tensor_scalar(out=gsm[:, j, 0:8], in0=gsm[:, j, 0:8], scalar1=-1.0, scalar2=None, op0=ALU.mult),
              reads=[("gsm", j)], writes=[("gsm", j)])

    def gdn(self, l):
        P = self.P
        j = l // 2
        g = self.g
        X, xnT, U, ring = self.X, self.xnT, self.U, self.ring
        identb, identf = self.identb, self.identf
        PF, PB = self.PF, self.PB
        epsc = self.epsc
        DKS = 128.0 ** -0.5
        win = self.d["gd_w_in"][j].rearrange("(kc p) f -> p kc f", p=128)
        self.rmsnorm(8 * l)
        self.arena_barrier()
        XN0 = [("xnT", 0, 0)]
        cnt = {"pf": 0, "rb": 0, "sq": 0}
        BA, BETA, GC, EG, GLB, EGL, EKD = (g[k] for k in ("BA", "BETA", "GC", "EG", "GLB", "EGL", "EKD"))
        flat = lambda v: v.rearrange("p a b -> p (a b)")

        sba = self.ws.get(win[:, :, 4096:4112], 16)
        slot_ba, wkey_ba, _ = sba
        pf_ba = PF[0]
        for t in range(NT):
            for kc in range(8):
                P.add("pe", lambda e, t=t, kc=kc: e.matmul(pf_ba[:, t * 16:(t + 1) * 16], xnT[:, kc, t * 128:(t + 1) * 128],
                                                          slot_ba[:, kc, 0:16], start=(kc == 0), stop=(kc == 7),
                                                          skip_group_check=True),
                      reads=[wkey_ba, ("xnT", t, kc)], writes=[("pf", 0)])
        self.ws.done(sba[2])
        P.add("dve", lambda e: e.tensor_copy(out=flat(BA), in_=pf_ba[:, 0:256]), reads=[("pf", 0)] + XN0, writes=["BA"])
        P.add("act", lambda e: e.activation(out=BETA, in_=BA[:, :, 0:8], func=AF.Exp, scale=-1.0), reads=["BA"] + XN0, writes=["BETA"])
        P.add("act", lambda e: e.activation(out=BETA, in_=BETA, func=AF.Ln, bias=1.0), reads=["BETA"], writes=["BETA"])
        P.add("act", lambda e: e.activation(out=BETA, in_=BETA, func=AF.Exp, scale=-1.0), reads=["BETA"], writes=["BETA"])
        gsm = self.gsm
        for t in range(NT):
            P.add("dve", lambda e, t=t: e.tensor_tensor(out=GC[:, t, :], in0=BA[:, t, 8:16], in1=gsm[:, j, 8:16], op=ALU.add),
                  reads=["BA", ("gsm", j)] + XN0, writes=["GC"])
        P.add("act", lambda e: e.activation(out=GC, in_=GC, func=AF.Exp), reads=["GC"], writes=["GC"])
        P.add("act", lambda e: e.activation(out=GC, in_=GC, func=AF.Ln, bias=1.0), reads=["GC"], writes=["GC"])
        for t in range(NT):
            P.add("dve", lambda e, t=t: e.tensor_tensor(out=GLB[:, t, :], in0=GC[:, t, :], in1=gsm[:, j, 0:8], op=ALU.mult),
                  reads=["GC", "BETA", ("gsm", j)] + XN0, writes=["BA"])
        pf1 = PF[1]
        for t in range(NT):
            P.add("pe", lambda e, t=t: e.matmul(pf1[:, t * 8:(t + 1) * 8], self.trif[:, :], GLB[:, t, :], start=True, stop=True,
                                                skip_group_check=True),
                  reads=["BA", "trif"], writes=[("pf", 1)])
        P.add("dve", lambda e: e.tensor_copy(out=flat(GC), in_=pf1[:, 0:128]), reads=[("pf", 1)], writes=["GC"])
        P.add("act", lambda e: e.activation(out=EG, in_=GC, func=AF.Exp), reads=["GC"] + XN0, writes=["EG"])
        for t in range(NT):
            P.add("pe", lambda e, t=t: e.matmul(pf1[:, 128 + t * 8:128 + (t + 1) * 8], self.sellast[:, :], GC[:, t, :], start=True, stop=True,
                                                skip_group_check=True),
                  reads=["GC", "sellast"], writes=[("pf", 1)])
        P.add("dve", lambda e: e.tensor_copy(out=flat(GLB), in_=pf1[:, 128:256]), reads=[("pf", 1)], writes=["BA"])
        P.add("act", lambda e: e.activation(out=EGL, in_=GLB, func=AF.Exp), reads=["BA"] + XN0, writes=["EGL"])
        P.add("dve", lambda e: e.tensor_tensor(out=EKD, in0=GLB, in1=GC, op=ALU.subtract), reads=["BA", "GC"] + XN0, writes=["EKD"])
        P.add("act", lambda e: e.activation(out=EKD, in_=EKD, func=AF.Exp), reads=["EKD"], writes=["EKD"])

        raw, acc, halo, sq, Sf, Sb = (g[k] for k in ("raw", "acc", "halo", "sq", "Sf", "Sb"))
        convw = self.convw
        NEGM = self.NEGM
        def capture_pair(hp):
            slabs = {}
            for wi, which in enumerate(("q", "k", "v", "z")):
                slabs[which] = self.ws.get(win[:, :, wi * 1024 + hp * 256: wi * 1024 + (hp + 1) * 256], 256)
            wsrc = self.d["gd_w_out"][j][hp * 256:(hp + 1) * 256, :].rearrange("(kc p) f -> p kc f", p=128)
            so = self.ws.get(wsrc, (2, 1024))
            FE, PREP, SCAN, OUT = {}, {}, {}, {}
            for gi in range(4):
                gp = gi % 2
                sT, zs, R, SC = g["sT"][gp], g["zs"][gp], g["R"][gp], g["SC"][gp]
                P.begin_capture()
                if gi == 0:
                    P.add("pool", lambda e: e.memset(flat(halo), 0.0), reads=XN0, writes=[("halo", i) for i in range(6)])
                for wi, which in enumerate(("k", "q", "v")):
                    slot, wkey, _ = slabs[which]
                    cbase = {"q": 0, "k": 8, "v": 16}[which]
                    for hh in range(2):
                        h = 2 * hp + hh
                        pf = PF[0]
                        rb = cnt["rb"] % 2
                        cnt["rb"] += 1
                        hi = wi * 2 + hh
                        P.add("dve", lambda e, rb=rb, hi=hi: e.tensor_copy(out=raw[:, rb, 0:3], in_=halo[:, hi, 0:3]),
                              reads=[("halo", hi)], writes=[("raw", rb)])
                        P.begin_atomic()
                        for kc in range(8):
                            P.add("pe", lambda e, pf=pf, slot=slot, kc=kc, hh=hh, gi=gi: e.matmul(
                                pf[:, :], slot[:, kc, hh * 128:(hh + 1) * 128], xnT[:, kc, gi * 512:(gi + 1) * 512],
                                start=(kc == 0), stop=(kc == 7)),
                                reads=[wkey] + [("xnT", t, kc) for t in range(4 * gi, 4 * gi + 4)], writes=[("pf", 0)])
                        P.add("act", lambda e, pf=pf, rb=rb: e.activation(out=raw[:, rb, 3:515], in_=pf[:, :], func=AF.Copy),
                              reads=[("pf", 0), ("raw", rb)], writes=[("raw", rb)])
                        P.end_atomic()
                        if gi < 3:
                            P.add("dve", lambda e, rb=rb, hi=hi: e.tensor_copy(out=halo[:, hi, 0:3], in_=raw[:, rb, 512:515]),
                                  reads=[("raw", rb)], writes=[("halo", hi)])
                        cc = cbase + h
                        P.add("dve", lambda e, rb=rb, cc=cc: e.tensor_scalar(
                            out=acc[:, 0, :], in0=raw[:, rb, 0:512], scalar1=convw[:, j, cc:cc + 1], scalar2=None, op0=ALU.mult),
                            reads=[("raw", rb), ("convw", j)], writes=[("acc", 0)])
                        for tap in range(1, 4):
                            P.add("dve", lambda e, rb=rb, cc=cc, tap=tap: e.scalar_tensor_tensor(
                                out=acc[:, 0, :], in0=raw[:, rb, tap:tap + 512], scalar=convw[:, j, tap * 24 + cc:tap * 24 + cc + 1],
                                in1=acc[:, 0, :], op0=ALU.mult, op1=ALU.add),
                                reads=[("raw", rb), ("acc", 0), ("convw", j)], writes=[("acc", 0)])
                        sgb = raw[:, rb, 0:512]
                        P.add("act", lambda e, sgb=sgb: e.activation(out=sgb, in_=acc[:, 0, :], func=AF.Exp, scale=-1.0),
                              reads=[("acc", 0), ("raw", rb), ("halo", hi)], writes=[("raw", rb)])
                        P.add("act", lambda e, sgb=sgb: e.activation(out=sgb, in_=sgb, func=AF.Ln, bias=1.0), reads=[("raw", rb)], writes=[("raw", rb)])
                        P.add("act", lambda e, sgb=sgb: e.activation(out=sgb, in_=sgb, func=AF.Exp, scale=-1.0), reads=[("raw", rb)], writes=[("raw", rb)])
                        P.add("dve", lambda e, sgb=sgb, hh=hh, wi=wi, sT=sT: e.tensor_tensor(
                            out=sT[:, hh, :, wi * 128:(wi + 1) * 128], in0=acc[:, 0, :].rearrange("p (a b) -> p a b", a=4),
                            in1=sgb.rearrange("p (a b) -> p a b", a=4), op=ALU.mult),
                            reads=[("acc", 0), ("raw", rb)], writes=[("sT", gp, hh, wi)])
                        if which in ("k", "q"):
                            sb_ = cnt["sq"] % 2
                            cnt["sq"] += 1
                            P.add("pool", lambda e, hh=hh, wi=wi, sb_=sb_, sT=sT: e.tensor_tensor(
                                out=sq[:, sb_, :].rearrange("p (a b) -> p a b", a=4), in0=sT[:, hh, :, wi * 128:(wi + 1) * 128],
                                in1=sT[:, hh, :, wi * 128:(wi + 1) * 128], op=ALU.mult),
                                reads=[("sT", gp, hh, wi)], writes=[("sq", sb_)])
                            for tl in range(4):
                                colr = tl * 4 + wi * 2 + hh
                                P.add("pe", lambda e, sb_=sb_, tl=tl, colr=colr: e.matmul(
                                    PF[1][:, 384 + colr:384 + colr + 1], sq[:, sb_, tl * 128:(tl + 1) * 128], self.onecol[:, 0:1],
                                    start=True, stop=True, skip_group_check=True),
                                    reads=[("sq", sb_), "onecol"], writes=[("pf", 1)])
                slot, wkey, _ = slabs["z"]
                for tl in range(4):
                    t = 4 * gi + tl
                    pf = PF[1]
                    for kc in range(8):
                        P.add("pe", lambda e, pf=pf, slot=slot, kc=kc, t=t: e.matmul(
                            pf[:, 0:256], xnT[:, kc, t * 128:(t + 1) * 128], slot[:, kc, 0:256], start=(kc == 0), stop=(kc == 7),
                            skip_group_check=True),
                            reads=[wkey, ("xnT", t, kc)], writes=[("pf", 1)])
                    zt = acc[:, 0, 0:256]
                    zr = acc[:, 0, 256:512]
                    P.add("act", lambda e, pf=pf, zt=zt: e.activation(out=zt, in_=pf[:, 0:256], func=AF.Exp, scale=-1.0),
                          reads=[("pf", 1)], writes=[("acc", 0)])
                    P.add("act", lambda e, pf=pf, zr=zr: e.activation(out=zr, in_=pf[:, 0:256], func=AF.Copy),
                          reads=[("pf", 1), ("acc", 0)], writes=[("acc", 0)])
                    P.add("act", lambda e, zt=zt: e.activation(out=zt, in_=zt, func=AF.Ln, bias=1.0), reads=[("acc", 0)], writes=[("acc", 0)])
                    P.add("act", lambda e, zt=zt: e.activation(out=zt, in_=zt, func=AF.Exp, scale=-1.0), reads=[("acc", 0)], writes=[("acc", 0)])
                    P.add("dve", lambda e, zt=zt, zr=zr, tl=tl, zs=zs: e.tensor_tensor(out=zs[:, tl, :], in0=zr, in1=zt, op=ALU.mult),
                          reads=[("acc", 0)], writes=[("zs", gp, tl)])
                Rk = ("R", gp)
                P.add("act", lambda e, R=R: e.activation(out=flat(R), in_=PF[1][:, 384:400], func=AF.Ln, bias=epsc[:, 0:1], scale=1.0),
                      reads=[("pf", 1), "epsc"], writes=[Rk])
                P.add("act", lambda e, R=R: e.activation(out=flat(R), in_=flat(R), func=AF.Exp, scale=-0.5), reads=[Rk], writes=[Rk])
                hs = slice(2 * hp, 2 * hp + 2)
                ts = slice(4 * gi, 4 * gi + 4)
                rk, rq = R[:, :, 0:2], R[:, :, 2:4]
                scv = lambda q_, SC=SC: SC[:, q_, :].rearrange("p (a b) -> p a b", a=4)
                T1, CKBG, CKD, CQ, UL, UA, BIAS, LN = (scv(i) for i in range(8))
                bt, egs, ekds, gcs = BETA[:, ts, hs], EG[:, ts, hs], EKD[:, ts, hs], GC[:, ts, hs]
                sk = lambda i: ("SC", gp, i)
                P.add("dve", lambda e, T1=T1, rk=rk, bt=bt: e.tensor_tensor(out=T1, in0=rk, in1=bt, op=ALU.mult), reads=[Rk, "BETA"], writes=[sk(0)])
                P.add("dve", lambda e, CKBG=CKBG, T1=T1, egs=egs: e.tensor_tensor(out=CKBG, in0=T1, in1=egs, op=ALU.mult), reads=[sk(0), "EG"], writes=[sk(1)])
                P.add("dve", lambda e, CKD=CKD, rk=rk, ekds=ekds: e.tensor_tensor(out=CKD, in0=rk, in1=ekds, op=ALU.mult), reads=[Rk, "EKD"], writes=[sk(2)])
                P.add("dve", lambda e, CQ=CQ, rq=rq, egs=egs: e.scalar_tensor_tensor(out=CQ, in0=rq, scalar=DKS, in1=egs, op0=ALU.mult, op1=ALU.mult),
                      reads=[Rk, "EG"], writes=[sk(3)])
                P.add("act", lambda e, UL=UL, T1=T1: e.activation(out=UL, in_=T1, func=AF.Ln), reads=[sk(0)], writes=[sk(4)])
                P.add("dve", lambda e, UL=UL, gcs=gcs: e.tensor_tensor(out=UL, in0=UL, in1=gcs, op=ALU.add), reads=[sk(4), "GC"], writes=[sk(4)])
                P.add("act", lambda e, UA=UA, rq=rq: e.activation(out=UA, in_=rq, func=AF.Ln, scale=DKS), reads=[Rk], writes=[sk(5)])
                P.add("dve", lambda e, UA=UA, gcs=gcs: e.tensor_tensor(out=UA, in0=UA, in1=gcs, op=ALU.add), reads=[sk(5), "GC"], writes=[sk(5)])
                P.add("act", lambda e, BIAS=BIAS, rk=rk: e.activation(out=BIAS, in_=rk, func=AF.Ln), reads=[Rk], writes=[sk(6)])
                P.add("dve", lambda e, BIAS=BIAS, gcs=gcs: e.tensor_tensor(out=BIAS, in0=BIAS, in1=gcs, op=ALU.subtract), reads=[sk(6), "GC"], writes=[sk(6)])
                FE[gi] = P.end_capture()
                SCK = [sk(i) for i in range(7)]
                for tl in range(4):
                    t = 4 * gi + tl
                    stageA = []
                    for c in range(2):
                        P.begin_capture()
                        hh = c
                        cs = 2 * (t % 2) + c
                        pbk, pbo = cs // 2, (cs % 2) * 512
                        sc1 = lambda q_, tl=tl, hh=hh, SC=SC: SC[:, q_, tl * 2 + hh:tl * 2 + hh + 1]
                        pb, pc = PB[pbk], PF[2 + cs]
                        kd, kbg, vb, E, MA = (g[k, cs] for k in ("kd", "kbg", "vb", "E", "MA"))
                        ksT = sT[:, hh, tl, 0:128]
                        vsT = sT[:, hh, tl, 256:384]
                        P.add("pe", lambda e, pb=pb, ksT=ksT, pbo=pbo: e.transpose(out=pb[:, pbo:pbo + 128], in_=ksT, identity=identb[:]),
                              reads=[("sT", gp, hh, 0), "identb"], writes=[("pb", pbk)])
                        P.add("pe", lambda e, pb=pb, vsT=vsT, pbo=pbo: e.transpose(out=pb[:, pbo + 128:pbo + 256], in_=vsT, identity=identb[:]),
                              reads=[("sT", gp, hh, 2), "identb"], writes=[("pb", pbk)])
                        P.add("act", lambda e, pb=pb, kd=kd, sc1=sc1, pbo=pbo: e.activation(out=kd, in_=pb[:, pbo:pbo + 128], func=AF.Copy, scale=sc1(2)),
                              reads=[("pb", pbk)] + SCK, writes=[("kd", cs)])
                        P.add("dve", lambda e, pb=pb, kbg=kbg, sc1=sc1, pbo=pbo: e.tensor_scalar(out=kbg, in0=pb[:, pbo:pbo + 128], scalar1=sc1(1), scalar2=None, op0=ALU.mult),
                              reads=[("pb", pbk)] + SCK, writes=[("kbg", cs)])
                        hcol = 2 * hp + hh
                        P.add("dve", lambda e, pb=pb, vb=vb, t=t, hcol=hcol, pbo=pbo: e.tensor_scalar(
                            out=vb, in0=pb[:, pbo + 128:pbo + 256], scalar1=BETA[:, t, hcol:hcol + 1], scalar2=None, op0=ALU.mult),
                            reads=[("pb", pbk), "BETA"], writes=[("vb", cs)])
                        P.add("pe", lambda e, pc=pc, ksT=ksT, hh=hh, tl=tl, sT=sT: e.matmul(pc[:, 0:256], ksT, sT[:, hh, tl, 0:256], start=True, stop=True,
                                                                                              skip_group_check=True),
                              reads=[("sT", gp, hh, 0), ("sT", gp, hh, 1)], writes=[("pf", 2 + cs)])
                        P.add("act", lambda e, E=E, sc1=sc1: e.activation(out=E[:, 0:128], in_=identf[:, :], func=AF.Copy, scale=sc1(4)),
                              reads=["identf"] + SCK, writes=[("E", cs)])
                        P.add("act", lambda e, E=E, sc1=sc1: e.activation(out=E[:, 128:256], in_=identf[:, :], func=AF.Copy, scale=sc1(5)),
                              reads=["identf", ("E", cs)] + SCK, writes=[("E", cs)])
                        P.add("pe", lambda e, pc=pc, E=E: e.matmul(pc[:, 256:512], self.onesf[:, :], E[:, :], start=True, stop=False, skip_group_check=True),
                              reads=[("E", cs), "onesf"], writes=[("pf", 2 + cs)])
                        P.add("pe", lambda e, pc=pc: e.matmul(pc[:, 256:512], identf[:, :], self.masks[:, :], start=False, stop=True, skip_group_check=True),
                              reads=["identf", "masks"], writes=[("pf", 2 + cs)])
                        P.add("act", lambda e, pc=pc, E=E, sc1=sc1: e.activation(out=E[:, :], in_=pc[:, 256:512], func=AF.Exp, bias=sc1(6)),
                              reads=[("pf", 2 + cs)] + SCK, writes=[("E", cs)])
                        P.add("dve", lambda e, pc=pc, E=E, MA=MA: e.tensor_tensor(out=MA[:, :], in0=pc[:, 0:256], in1=E[:, :], op=ALU.mult),
                              reads=[("pf", 2 + cs), ("E", cs)], writes=[("MA", cs)])
                        Lb, DD, TM = g["Lb", cs], g["DD", cs], g["TM", cs]
                        P.add("pe", lambda e, pb=pb, MA=MA, pbo=pbo: e.transpose(out=pb[:, pbo + 256:pbo + 384], in_=MA[:, 0:128], identity=identb[:]),
                              reads=[("MA", cs), "identb"], writes=[("pb", pbk)])
                        P.add("act", lambda e, pb=pb, Lb=Lb, pbo=pbo: e.activation(out=Lb, in_=pb[:, pbo + 256:pbo + 384], func=AF.Copy),
                              reads=[("pb", pbk)], writes=[("Lb", cs)])
                        P.add("dve", lambda e, TM=TM, Lb=Lb: e.tensor_tensor(out=TM[:, 0:128], in0=Lb, in1=NEGM[:, 0, 0:128], op=ALU.mult),
                              reads=[("Lb", cs), "NEGM"], writes=[("TM", cs)])
                        P.add("dve", lambda e, TM=TM, MA=MA: e.tensor_tensor(out=TM[:, 128:256], in0=MA[:, 0:128], in1=NEGM[:, 0, 128:256], op=ALU.mult),
                              reads=[("MA", cs), "NEGM", ("TM", cs)], writes=[("TM", cs)])
                        P.add("dve", lambda e, TM=TM, DD=DD: e.tensor_tensor(out=DD[:, 0, 0:128], in0=identb[:, :], in1=TM[:, 0:128], op=ALU.subtract),
                              reads=[("TM", cs), "identb"], writes=[("DD", cs, 0)])
                        P.add("dve", lambda e, TM=TM, DD=DD: e.tensor_tensor(out=DD[:, 0, 128:256], in0=identb[:, :], in1=TM[:, 128:256], op=ALU.subtract),
                              reads=[("TM", cs), "identb", ("DD", cs, 0)], writes=[("DD", cs, 0)])
                        stageA.append(P.end_capture())
                    P.begin_capture()
                    for lev in range(1, 7):
                        pi, po = (lev - 1) % 2, lev % 2
                        for c in range(2):
                            cs = 2 * (t % 2) + c
                            pc = PF[2 + cs]
                            MA, Lb, DD, QQ = (g[k_, cs] for k_ in ("MA", "Lb", "DD", "QQ"))
                            P.add("pe", lambda e, pc=pc, MA=MA, DD=DD, pi=pi: e.matmul(pc[:, 0:128], MA[:, 0:128], DD[:, pi, 0:128], start=True, stop=True,
                                                                                        skip_group_check=True),
                                  reads=[("MA", cs), ("DD", cs, pi)], writes=[("pf", 2 + cs)])
                            P.add("pe", lambda e, pc=pc, Lb=Lb, DD=DD, pi=pi: e.matmul(pc[:, 128:256], Lb, DD[:, pi, 128:256], start=True, stop=True,
                                                                                        skip_group_check=True),
                                  reads=[("Lb", cs), ("DD", cs, pi)], writes=[("pf", 2 + cs)])
                            P.add("dve", lambda e, pc=pc, QQ=QQ, lev=lev: e.tensor_tensor(out=QQ[:, :], in0=pc[:, 0:256], in1=NEGM[:, lev, :], op=ALU.mult),
                                  reads=[("pf", 2 + cs), "NEGM"], writes=[("QQ", cs)])
                        for c in range(2):
                            cs = 2 * (t % 2) + c
                            pc = PF[2 + cs]
                            DD, QQ = (g[k_, cs] for k_ in ("DD", "QQ"))
                            P.add("pe", lambda e, pc=pc, QQ=QQ, DD=DD, pi=pi: e.matmul(pc[:, 256:384], DD[:, pi, 128:256], QQ[:, 0:128], start=True, stop=True,
                                                                                        skip_group_check=True),
                                  reads=[("QQ", cs), ("DD", cs, pi)], writes=[("pf", 2 + cs)])
                            P.add("pe", lambda e, pc=pc, QQ=QQ, DD=DD, pi=pi: e.matmul(pc[:, 384:512], DD[:, pi, 0:128], QQ[:, 128:256], start=True, stop=True,
                                                                                        skip_group_check=True),
                                  reads=[("QQ", cs), ("DD", cs, pi)], writes=[("pf", 2 + cs)])
                            P.add("dve", lambda e, pc=pc, DD=DD, pi=pi, po=po: e.tensor_tensor(out=DD[:, po, :], in0=DD[:, pi, :], in1=pc[:, 256:512], op=ALU.subtract),
                                  reads=[("pf", 2 + cs), ("DD", cs, pi)], writes=[("DD", cs, po)])
                    for c in range(2):
                        cs = 2 * (t % 2) + c
                        pc = PF[2 + cs]
                        kbg, vb, DD, u, wT = (g[k, cs] for k in ("kbg", "vb", "DD", "u", "wT"))
                        P.add("pe", lambda e, pc=pc, DD=DD, vb=vb: e.matmul(pc[:, 0:128], DD[:, 0, 128:256], vb, start=True, stop=True, skip_group_check=True),
                              reads=[("DD", cs, 0), ("vb", cs)], writes=[("pf", 2 + cs)])
                        P.add("pe", lambda e, pc=pc, DD=DD, kbg=kbg: e.matmul(pc[:, 128:256], kbg, DD[:, 0, 128:256], start=True, stop=True, skip_group_check=True),
                              reads=[("DD", cs, 0), ("kbg", cs)], writes=[("pf", 2 + cs)])
                        P.add("act", lambda e, pc=pc, u=u: e.activation(out=u, in_=pc[:, 0:128], func=AF.Copy), reads=[("pf", 2 + cs)], writes=[("u", cs)])
                        P.add("dve", lambda e, pc=pc, wT=wT: e.tensor_copy(out=wT, in_=pc[:, 128:256]), reads=[("pf", 2 + cs)], writes=[("wT", cs)])
                    PREP[t] = Prog.merge(stageA) + P.end_capture()
                    scans = []
                    for c in range(2):
                        P.begin_capture()
                        hh = c
                        h = 2 * hp + hh
                        cs = 2 * (t % 2) + c
                        pbk, pbo = cs // 2, (cs % 2) * 512
                        ps_, pb = PF[2 + cs], PB[pbk]
                        kd, MA, u, wT, vn, o, og, psm = (g[k, cs] for k in ("kd", "MA", "u", "wT", "vn", "o", "og", "ps"))
                        sc1 = lambda q_, tl=tl, hh=hh, SC=SC: SC[:, q_, tl * 2 + hh:tl * 2 + hh + 1]
                        qsT = sT[:, hh, tl, 128:256]
                        PK = ("pf", 2 + cs)
                        P.add("pe", lambda e, ps_=ps_, wT=wT, hh=hh: e.matmul(ps_[:, 0:128], wT, Sb[:, hh, :], start=True, stop=True, skip_group_check=True),
                              reads=[("wT", cs), ("Sb", hh)], writes=[PK])
                        P.add("pe", lambda e, ps_=ps_, qsT=qsT, hh=hh: e.matmul(ps_[:, 128:256], qsT, Sb[:, hh, :], start=True, stop=True, skip_group_check=True),
                              reads=[("sT", gp, hh, 1), ("Sb", hh)], writes=[PK])
                        P.add("dve", lambda e, ps_=ps_, u=u, vn=vn: e.tensor_tensor(out=vn, in0=u, in1=ps_[:, 0:128], op=ALU.subtract),
                              reads=[PK, ("u", cs)], writes=[("vn", cs)])
                        P.add("pe", lambda e, ps_=ps_, MA=MA, vn=vn: e.matmul(ps_[:, 256:384], MA[:, 128:256], vn, start=True, stop=True, skip_group_check=True),
                              reads=[("MA", cs), ("vn", cs)], writes=[PK])
                        P.add("pe", lambda e, ps_=ps_, kd=kd, vn=vn: e.matmul(ps_[:, 384:512], kd, vn, start=True, stop=True, skip_group_check=True),
                              reads=[("kd", cs), ("vn", cs)], writes=[PK])
                        P.add("act", lambda e, ps_=ps_, o=o: e.activation(out=o, in_=ps_[:, 256:384], func=AF.Copy), reads=[PK], writes=[("o", cs)])
                        P.add("dve", lambda e, ps_=ps_, o=o, sc1=sc1: e.scalar_tensor_tensor(out=o, in0=ps_[:, 128:256], scalar=sc1(3), in1=o, op0=ALU.mult, op1=ALU.add),
                              reads=[PK, ("o", cs)] + SCK, writes=[("o", cs)])
                        P.add("dve", lambda e, ps_=ps_, hh=hh, t=t, h=h: e.scalar_tensor_tensor(
                            out=Sf[:, hh, :], in0=Sf[:, hh, :], scalar=EGL[:, t, h:h + 1], in1=ps_[:, 384:512], op0=ALU.mult, op1=ALU.add),
                            reads=[PK, ("Sf", hh), "EGL"], writes=[("Sf", hh)])
                        P.add("act", lambda e, hh=hh: e.activation(out=Sb[:, hh, :], in_=Sf[:, hh, :], func=AF.Copy), reads=[("Sf", hh)], writes=[("Sb", hh)])
                        P.add("act", lambda e, o=o, og=og, psm=psm: e.activation(out=og, in_=o, func=AF.Square, accum_out=psm[:, 0:1]),
                              reads=[("o", cs)], writes=[("og", cs), ("psm", cs)])
                        P.add("act", lambda e, psm=psm: e.activation(out=psm[:, 1:2], in_=psm[:, 0:1], func=AF.Ln, bias=epsc[:, 0:1], scale=1.0 / 128),
                              reads=[("psm", cs), "epsc"], writes=[("psm", cs)])
                        P.add("act", lambda e, psm=psm: e.activation(out=psm[:, 1:2], in_=psm[:, 1:2], func=AF.Exp, scale=-0.5), reads=[("psm", cs)], writes=[("psm", cs)])
                        P.add("dve", lambda e, o=o, og=og, psm=psm, tl=tl, hh=hh, zs=zs: e.scalar_tensor_tensor(
                            out=og, in0=o, scalar=psm[:, 1:2], in1=zs[:, tl, hh * 128:(hh + 1) * 128], op0=ALU.mult, op1=ALU.mult),
                            reads=[("o", cs), ("psm", cs), ("zs", gp, tl)], writes=[("og", cs)])
                        P.add("pe", lambda e, pb=pb, og=og, pbo=pbo: e.transpose(out=pb[:, pbo + 384:pbo + 512], in_=og, identity=identb[:]),
                              reads=[("og", cs), "identb"], writes=[("pb", pbk)])
                        P.add("act", lambda e, pb=pb, hh=hh, t=t, pbo=pbo: e.activation(out=U[:, hh, t * 128:(t + 1) * 128], in_=pb[:, pbo + 384:pbo + 512], func=AF.Copy,
                                                                                        scale=self.gdc[:, j, 0:1]),
                              reads=[("pb", pbk), ("gdc", j)], writes=[("U", hh, t // 4)])
                        scans.append(P.end_capture())
                    SCAN[t] = Prog.merge(scans)
            wv = so[0]
            for t in range(NT):
                P.begin_capture()
                for dh in range(2):
                    pf = PF[0]
                    P.begin_atomic()
                    for hc in range(2):
                        P.add("pe", lambda e, pf=pf, wv=wv, hc=hc, t=t, dh=dh: e.matmul(
                            pf[:, :], U[:, hc, t * 128:(t + 1) * 128], wv[:, hc, dh * 512:(dh + 1) * 512], start=(hc == 0), stop=(hc == 1)),
                            reads=[so[1], ("U", hc, t // 4)], writes=[("pf", 0)])
                    P.add("dve", lambda e, pf=pf, t=t, dh=dh: e.tensor_tensor(
                        out=X[:, t, dh * 512:(dh + 1) * 512], in0=X[:, t, dh * 512:(dh + 1) * 512], in1=pf[:, :], op=ALU.add),
                        reads=[("pf", 0), ("x", t)], writes=[("x", t)])
                    P.end_atomic()
                OUT[t] = P.end_capture()
            return slabs, so, FE, PREP, SCAN, OUT

        def zero_state():
            P.add("pool", lambda e: e.memset(flat(Sf), 0.0), reads=XN0, writes=[("Sf", 0), ("Sf", 1)])
            P.add("pool", lambda e: e.memset(flat(Sb), 0.0), reads=XN0, writes=[("Sb", 0), ("Sb", 1)])

        cur = capture_pair(0)
        P.replay([cur[2][0]])
        zero_state()
        for hp in range(4):
            slabs, so, FE, PREP, SCAN, OUT = cur
            fe_parts = {}
            for gi in range(1, 4):
                L = FE[gi]
                n = (len(L) + 2) // 3
                for k in range(3):
                    fe_parts[4 * (gi - 1) + 1 + k] = L[k * n:(k + 1) * n]
            for s_ in range(NT):
                lists = [PREP[s_]]
                if s_ >= 1:
                    lists.append(SCAN[s_ - 1])
                if s_ in fe_parts:
                    lists.append(fe_parts[s_])
                if s_ >= 2:
                    lists.append(OUT[s_ - 2])
                P.replay(lists)
            for which in ("q", "k", "v", "z"):
                self.ws.done(slabs[which][2])
            tail = Prog.merge([SCAN[NT - 1]]) + Prog.merge([OUT[NT - 2]]) + Prog.merge([OUT[NT - 1]])
            if hp < 3:
                cur = capture_pair(hp + 1)
                P.replay([tail, cur[2][0]])
            else:
                P.replay([tail])
            self.ws.done(so[2])
            if hp < 3:
                zero_state()

    def load_x(self, s):
        P = self.P
        X = self.X
        xv = self.d["x"][s].rearrange("(t p) d -> p t d", p=128)
        for q in range(4):
            P.add("sp", lambda e, q=q: e.dma_start(out=X[:, 4 * q:4 * q + 4, :], in_=xv[:, 4 * q:4 * q + 4, :]),
                  writes=[("x", t) for t in range(4 * q, 4 * q + 4)], dma=("xl", q))

    def store_x(self, s):
        P = self.P
        X = self.X
        ov = self.d["out"][s].rearrange("(t p) d -> p t d", p=128)
        ids = []
        for q in range(4):
            ids.append(P.add("sp", lambda e, q=q: e.dma_start(out=ov[:, 4 * q:4 * q + 4, :], in_=X[:, 4 * q:4 * q + 4, :]),
                             reads=[("x", t) for t in range(4 * q, 4 * q + 4)], writes=[("xst", q)], dma=("xs", q)))
        return ids

    def rmsnorm(self, gbase):
        P = self.P
        X, xs, ss, rstd, xnT = self.X, self.xs, self.ss, self.rstd, self.xnT
        identb, gcol = self.identb, self.gcol
        import os
        DBG = int(os.environ.get("K_DBG", "9"))
        for t in range(NT):
            P.add("act", lambda e, t=t: e.activation(out=xs[:, 1, :], in_=X[:, t, :], func=AF.Square,
                                                     accum_out=ss[:, t:t + 1]),
                  reads=[("x", t)], writes=[("xs", 1), ("ss", t)])
        P.add("act", lambda e: e.activation(out=rstd[:], in_=ss[:], func=AF.Ln, bias=self.epsc[:, 0:1], scale=1.0 / D),
              reads=[("ss", t) for t in range(NT)] + ["epsc"], writes=["rstd"])
        P.add("act", lambda e: e.activation(out=rstd[:], in_=rstd[:], func=AF.Exp, scale=-0.5), reads=["rstd"], writes=["rstd"])
        if DBG < 2:
            return
        for t in range(NT if DBG >= 6 else 1):
            b = t % 2
            pb = self.PB[b]
            P.add("act", lambda e, t=t, b=b: e.activation(out=xs[:, b, :], in_=X[:, t, :], func=AF.Copy,
                                                          scale=rstd[:, t:t + 1]),
                  reads=[("x", t), "rstd"], writes=[("xs", b)])
            if DBG < 4:
                continue
            for kc in range(8):
                P.add("pe", lambda e, b=b, kc=kc, pb=pb: e.transpose(out=pb[:, kc * 128:(kc + 1) * 128],
                                                                      in_=xs[:, b, kc * 128:(kc + 1) * 128],
                                                                      identity=identb[:]),
                      reads=[("xs", b), "identb"], writes=[("pb", b)])
            if DBG < 5:
                continue
            gb = gcol[:, gbase:gbase + 8].unsqueeze(2).to_broadcast([128, 8, 128])
            P.add("dve", lambda e, t=t, pb=pb, gb=gb: e.tensor_tensor(
                out=xnT[:, :, t * 128:(t + 1) * 128], in0=pb[:, :].rearrange("p (k c) -> p k c", k=8), in1=gb, op=ALU.mult),
                reads=[("pb", b), "gcol"], writes=[("xnT", t, kc) for kc in range(8)])

    def mlp(self, l):
        P = self.P
        X, xnT, U, ring = self.X, self.xnT, self.hT, self.ring
        w1 = self.d["mlp_w_in"][l].rearrange("(kc p) f -> p kc f", p=128)
        w2 = self.d["mlp_w_out"][l].rearrange("(fc p) d -> p fc d", p=128)
        self.rmsnorm(32 + 8 * l)
        self.arena_barrier()
        pfi = 0
        for fg in range(4):
            slabs = [self.ws.get(w1[:, :, fg * 1024 + s2 * 512: fg * 1024 + (s2 + 1) * 512], 512) for s2 in range(2)]
            for fc in range(8):
                slot, wkey, _ = slabs[fc // 4]
                off = (fc % 4) * 128
                for tt in range(4):
                    bank = pfi % 4
                    pfi += 1
                    pf = self.PF[bank]
                    for kc in range(8):
                        P.add("pe", lambda e, pf=pf, slot=slot, kc=kc, off=off, tt=tt: e.matmul(
                            pf[:, :], slot[:, kc, off:off + 128], xnT[:, kc, tt * 512:(tt + 1) * 512],
                            start=(kc == 0), stop=(kc == 7)),
                            reads=[wkey] + [("xnT", t, kc) for t in range(4 * tt, 4 * tt + 4)],
                            writes=[("pf", bank)])
                    rb = pfi % 2
                    P.add("act", lambda e, pf=pf, rb=rb: e.activation(out=self.rtmp[:, rb, :], in_=pf[:, :], func=AF.Relu),
                          reads=[("pf", bank)], writes=[("xs", rb)])
                    P.add("dve", lambda e, fc=fc, tt=tt, rb=rb: e.tensor_tensor(
                        out=U[:, fc, tt * 512:(tt + 1) * 512], in0=self.rtmp[:, rb, :], in1=self.rtmp[:, rb, :],
                        op=ALU.mult),
                        reads=[("xs", rb)], writes=[("hT", fc, tt)])
            for sl in slabs:
                self.ws.done(sl[2])
            slabs2 = [self.ws.get(w2[:, fg * 8:(fg + 1) * 8, dh * 512:(dh + 1) * 512], 512) for dh in range(2)]
            for t in range(NT):
                for dh in range(2):
                    slot, wkey, _ = slabs2[dh]
                    bank = pfi % 4
                    pfi += 1
                    pf = self.PF[bank]
                    for fc in range(8):
                        P.add("pe", lambda e, pf=pf, slot=slot, fc=fc, t=t: e.matmul(
                            pf[:, :], U[:, fc, t * 128:(t + 1) * 128], slot[:, fc, :],
                            start=(fc == 0), stop=(fc == 7)),
                            reads=[wkey, ("hT", fc, t // 4)], writes=[("pf", bank)])
                    P.add("dve", lambda e, pf=pf, t=t, dh=dh: e.tensor_tensor(
                        out=X[:, t, dh * 512:(dh + 1) * 512], in0=X[:, t, dh * 512:(dh + 1) * 512], in1=pf[:, :],
                        op=ALU.add),
                        reads=[("pf", bank), ("x", t)], writes=[("x", t)])
            for sl in slabs2:
                self.ws.done(sl[2])

    def build(self):
        P = self.P
        self.setup_consts()
        kinds = {k for k, _ in self.layers}
        if "gd" in kinds:
            self.setup_gd_static()
            self.setup_gd_levelmasks()
            for jj in sorted({l // 2 for (k, l) in self.layers if k == "gd"}):
                self.setup_gd_consts(jj)
        if "da" in kinds:
            self.setup_da_static()
            for (k, l) in self.layers:
                if k == "da":
                    self.setup_da_consts(l // 2, l)
        last_stores = []
        for s in range(self.n_seq):
            self.load_x(s)
            for l in self.layers:
                if l[0] == "mlp":
                    self.mlp(l[1])
                elif l[0] == "norm":
                    self.rmsnorm(32 + 8 * l[1])
                elif l[0] == "da":
                    self.diffattn(l[1])
                elif l[0] == "gd":
                    self.gdn(l[1])
            last_stores = self.store_x(s)
        P.add("sp", None, reads=[("xst", q) for q in range(4)])


def layer_plan():
    plan = []
    for i in range(DEPTH):
        plan.append(("da" if i % 2 == 0 else "gd", i))
        plan.append(("mlp", i))
    return plan


def build_program(n_seq=SEQ_PER_CORE, layers=None):
    if layers is None:
        layers = layer_plan()
    nc = bass.Bass("TRN2", target_bir_lowering=False)
    with ExitStack() as es:
        b = Builder(nc, Prog(nc, dry=True), None, n_seq, layers, es)
        b.build()
        future = list(b.ws.requests)
        P = Prog(nc, dry=False)
        b.P = P
        b.ws = WStream(P, b.ring, b.NSLOT * 2, future)
        b.build()
        P.emit(es)
    return nc


WEIGHT_NAMES = ["mix_norm", "mlp_norm", "mlp_w_in", "mlp_w_out", "da_w_in", "da_q_norm", "da_k_norm",
                "da_lambda_q1", "da_lambda_k1", "da_lambda_q2", "da_lambda_k2", "da_sub_norm", "da_w_out",
                "gd_w_in", "gd_conv_w", "gd_a_log", "gd_dt_bias", "gd_out_norm", "gd_w_out"]


def run(inputs, n_seq=SEQ_PER_CORE, layers=None, ncores=NCORES, trace=False):
    nc = build_program(n_seq, layers)
    x = np.ascontiguousarray(np.asarray(inputs["x"], dtype=np.float32))
    weights = {k: np.ascontiguousarray(np.asarray(inputs[k], dtype=np.float32)) for k in WEIGHT_NAMES}
    in_maps = []
    for c in range(ncores):
        m = {"x": x[c * n_seq:(c + 1) * n_seq]}
        m.update(weights)
        in_maps.append(m)
    res = run_bass_kernel_spmd(nc, in_maps, core_ids=list(range(ncores)), trace=trace)
    out = np.concatenate([r["out"] for r in res.results], axis=0)
    return out, res


def kernel(**inputs):
    out, _ = run(inputs)
    return out
```
